# Optimizing a Trainium2 kernel written in Bass

```python
import jax, jax.numpy as jnp
from jax import lax
import numpy as np

D_MODEL = 1024
BATCH = 4
SEQ = 4096
DEPTH = 4
DEC_BATCH = 32
DEC_SEQ = 8
PAST_LEN = 8192
PAGE_SIZE = 128

N_A = DEPTH // 2
N_B = DEPTH - N_A
CONV_WIDTH = 31
N_HEADS = 16
HEAD_DIM = D_MODEL // N_HEADS
WINDOWS = (128, 512, 2048)
DILATIONS = (1, 4, 16)
N_GROUPS = len(WINDOWS)
BLK = 128
D_FF = (8 * D_MODEL + 3 * 256 - 1) // (3 * 256) * 256
ROPE_THETA = 10000.0
EPS = 1e-6
SCALE = HEAD_DIM ** -0.5

kernel_name = 'yoco_conformer_dilated_swa_decoder'


def rms_norm(x, g):
    xf = x.astype(jnp.float32)
    y = xf * lax.rsqrt(jnp.mean(xf * xf, axis=-1, keepdims=True) + EPS)
    return (y * g.astype(jnp.float32)).astype(x.dtype)


def layer_norm(x, g, b):
    xf = x.astype(jnp.float32)
    mu = jnp.mean(xf, axis=-1, keepdims=True)
    xc = xf - mu
    y = xc * lax.rsqrt(jnp.mean(xc * xc, axis=-1, keepdims=True) + EPS)
    return (y * g.astype(jnp.float32) + b.astype(jnp.float32)).astype(x.dtype)


def rotary(x, pos):
    half = HEAD_DIM // 2
    inv = ROPE_THETA ** (-jnp.arange(half, dtype=jnp.float32) / half)
    ang = pos.astype(jnp.float32)[:, None] * inv[None, :]
    shp = (1, pos.shape[0]) + (1,) * (x.ndim - 3) + (half,)
    cos = jnp.cos(ang).reshape(shp)
    sin = jnp.sin(ang).reshape(shp)
    xf = x.astype(jnp.float32)
    x1, x2 = xf[..., :half], xf[..., half:]
    return jnp.concatenate([x1 * cos - x2 * sin, x2 * cos + x1 * sin], axis=-1).astype(x.dtype)


def conv_module(x, prev, norm_g, w_in, b_in, w_dw, b_dw, ln_g, ln_b, w_out, b_out):
    u = rms_norm(x, norm_g)
    a = u @ w_in + b_in
    glu = a[..., :D_MODEL] * jax.nn.sigmoid(a[..., D_MODEL:])
    ext = jnp.concatenate([prev.astype(glu.dtype), glu], axis=1)
    c = lax.conv_general_dilated(ext, w_dw[:, None, :].astype(ext.dtype), window_strides=(1,),
                                 padding='VALID', dimension_numbers=('NWC', 'WIO', 'NWC'),
                                 feature_group_count=D_MODEL) + b_dw
    h = jax.nn.silu(layer_norm(c, ln_g, ln_b))
    return h @ w_out + b_out, ext[:, -(CONV_WIDTH - 1):]


def swiglu(x, g, w_gate_up, w_down):
    gu = rms_norm(x, g) @ w_gate_up
    return (jax.nn.silu(gu[..., :D_FF]) * gu[..., D_FF:]) @ w_down


def shared_kv(s, pos, kv_norm, w_kv, k_norm):
    B, S, _ = s.shape
    kv = (rms_norm(s, kv_norm) @ w_kv).reshape(B, S, N_GROUPS, 2, N_HEADS, HEAD_DIM)
    k = rotary(rms_norm(kv[:, :, :, 0], k_norm[None, None, :, None, :]), pos)
    v = kv[:, :, :, 1]
    return k, v


def queries(x, pos, b_norm, w_q, q_norm):
    B, S, _ = x.shape
    q = (rms_norm(x, b_norm) @ w_q).reshape(B, S, N_GROUPS, N_HEADS, HEAD_DIM)
    return rotary(rms_norm(q, q_norm[None, None, :, None, :]), pos)


def to_blocks(x, r):
    B, S, H, hd = x.shape
    span = r * BLK
    Sp = -(-S // span) * span
    L = Sp // r
    x = jnp.pad(x, ((0, 0), (0, Sp - S), (0, 0), (0, 0)))
    x = x.reshape(B, L, r, H, hd).transpose(0, 2, 1, 3, 4)
    return x.reshape(B, r, L // BLK, BLK, H, hd)


def band_keys(k, v, r):
    def with_prev(xb):
        prev = jnp.concatenate([jnp.zeros_like(xb[:, :, :1]), xb[:, :, :-1]], axis=2)
        return jnp.concatenate([prev, xb], axis=3)
    return with_prev(to_blocks(k, r)), with_prev(to_blocks(v, r))


def band_attention(q, kb2, vb2, r, n_strided):
    B, S, H, hd = q.shape
    nb = kb2.shape[2]
    L = nb * BLK
    Sp = L * r
    qb = to_blocks(q, r)
    s = jnp.einsum('brnqhd,brnkhd->brnhqk', qb.astype(jnp.float32), kb2.astype(jnp.float32)) * SCALE
    qi = jnp.arange(BLK)[:, None]
    ki = jnp.arange(2 * BLK)[None, :]
    dist = BLK + qi - ki
    band = (dist >= 0) & (dist <= n_strided)
    after_start = (jnp.arange(nb)[:, None, None] > 0) | (ki[None] >= BLK)
    mask = band[None] & after_start
    s = jnp.where(mask[None, None, :, None], s, -jnp.inf)
    lse = jax.nn.logsumexp(s, axis=-1)
    p = jnp.exp(s - lse[..., None])
    o = jnp.einsum('brnhqk,brnkhd->brnqhd', p, vb2.astype(jnp.float32))
    o = o.reshape(B, r, L, H, hd).transpose(0, 2, 1, 3, 4).reshape(B, Sp, H, hd)[:, :S]
    lse = lse.transpose(0, 1, 2, 4, 3).reshape(B, r, L, H).transpose(0, 2, 1, 3).reshape(B, Sp, H)[:, :S]
    return o, lse


def gather_keys(buf, k_new, v_new, r, n_strided):
    Lb = buf.shape[1]
    T = k_new.shape[1]
    kall = jnp.concatenate([buf[:, :, 0].astype(k_new.dtype), k_new], axis=1)
    vall = jnp.concatenate([buf[:, :, 1].astype(v_new.dtype), v_new], axis=1)
    idx = Lb + jnp.arange(T)[:, None] - r * jnp.arange(n_strided + 1)[None, :]
    valid = idx >= 0
    idx = jnp.maximum(idx, 0)
    kg = kall[:, idx]
    vg = vall[:, idx]
    new_buf = jnp.stack([kall[:, -Lb:], vall[:, -Lb:]], axis=2)
    return (kg, vg, valid), new_buf


def gathered_attention(q, kg, vg, valid):
    s = jnp.einsum('bthd,btjhd->bhtj', q.astype(jnp.float32), kg.astype(jnp.float32)) * SCALE
    s = jnp.where(valid[None, None], s, -jnp.inf)
    lse = jax.nn.logsumexp(s, axis=-1)
    p = jnp.exp(s - lse[..., None])
    o = jnp.einsum('bhtj,btjhd->bthd', p, vg.astype(jnp.float32))
    return o, lse.transpose(0, 2, 1)


def merge_groups(outs, lses, w_o, dtype):
    o = jnp.stack(outs, axis=0)
    w = jax.nn.softmax(jnp.stack(lses, axis=0), axis=0)
    o = jnp.sum(w[..., None] * o, axis=0)
    B, S = o.shape[:2]
    return o.reshape(B, S, N_HEADS * HEAD_DIM).astype(dtype) @ w_o


def run_trunk(x, pos, conv_prev, kv_bufs, p):
    conv_states = []
    keyside, new_bufs = [], []
    for layer in range(DEPTH):
        if layer < N_A:
            h, st = conv_module(x, conv_prev[layer], p['a_norm'][layer], p['a_w_in'][layer], p['a_b_in'][layer],
                                p['a_w_dw'][layer], p['a_b_dw'][layer], p['a_ln_g'][layer], p['a_ln_b'][layer],
                                p['a_w_out'][layer], p['a_b_out'][layer])
            x = x + h
            conv_states.append(st)
        else:
            if layer == N_A:
                k, v = shared_kv(x, pos, p['kv_norm'], p['w_kv'], p['k_norm'])
                for g in range(N_GROUPS):
                    r = DILATIONS[g]
                    if kv_bufs is None:
                        keyside.append(band_keys(k[:, :, g], v[:, :, g], r))
                        lg = min(WINDOWS[g], x.shape[1])
                        new_bufs.append(jnp.stack([k[:, -lg:, g], v[:, -lg:, g]], axis=2))
                    else:
                        ks, nbuf = gather_keys(kv_bufs[g], k[:, :, g], v[:, :, g], r, WINDOWS[g] // r)
                        keyside.append(ks)
                        new_bufs.append(nbuf)
            j = layer - N_A
            q = queries(x, pos, p['b_norm'][j], p['w_q'][j], p['q_norm'][j])
            outs, lses = [], []
            for g in range(N_GROUPS):
                r = DILATIONS[g]
                if kv_bufs is None:
                    o, l = band_attention(q[:, :, g], keyside[g][0], keyside[g][1], r, WINDOWS[g] // r)
                else:
                    o, l = gathered_attention(q[:, :, g], *keyside[g])
                outs.append(o)
                lses.append(l)
            x = x + merge_groups(outs, lses, p['w_o'][j], x.dtype)
        x = x + swiglu(x, p['ffn_norm'][layer], p['w_gate_up'][layer], p['w_down'][layer])
    return x, jnp.stack(conv_states, axis=0), new_bufs


def setup_inputs(seed: int = 0) -> dict:
    key = jax.random.key(seed)
    ks = jax.random.split(key, 32)
    f32 = jnp.float32

    def nrm(k, shape, scale):
        return scale * jax.random.normal(k, shape, f32)

    GH = N_GROUPS * N_HEADS * HEAD_DIM
    return {
        'x_prompt': nrm(ks[0], (BATCH, SEQ, D_MODEL), 1.0),
        'x_sample': nrm(ks[1], (DEC_BATCH, DEC_SEQ, D_MODEL), 1.0),
        'cache_conv': nrm(ks[2], (N_A, DEC_BATCH, CONV_WIDTH - 1, D_MODEL), 0.5),
        'cache_kv_w128': nrm(ks[3], (DEC_BATCH, min(WINDOWS[0], PAST_LEN), 2, N_HEADS, HEAD_DIM), 1.0),
        'cache_kv_w512': nrm(ks[4], (DEC_BATCH, min(WINDOWS[1], PAST_LEN), 2, N_HEADS, HEAD_DIM), 1.0),
        'cache_kv_w2048': nrm(ks[5], (DEC_BATCH, min(WINDOWS[2], PAST_LEN), 2, N_HEADS, HEAD_DIM), 1.0),
        'a_norm': 1.0 + nrm(ks[6], (N_A, D_MODEL), 0.05),
        'a_w_in': nrm(ks[7], (N_A, D_MODEL, 2 * D_MODEL), D_MODEL ** -0.5),
        'a_b_in': nrm(ks[8], (N_A, 2 * D_MODEL), 0.02),
        'a_w_dw': nrm(ks[9], (N_A, CONV_WIDTH, D_MODEL), CONV_WIDTH ** -0.5),
        'a_b_dw': nrm(ks[10], (N_A, D_MODEL), 0.02),
        'a_ln_g': 1.0 + nrm(ks[11], (N_A, D_MODEL), 0.05),
        'a_ln_b': nrm(ks[12], (N_A, D_MODEL), 0.02),
        'a_w_out': nrm(ks[13], (N_A, D_MODEL, D_MODEL), D_MODEL ** -0.5),
        'a_b_out': nrm(ks[14], (N_A, D_MODEL), 0.02),
        'kv_norm': 1.0 + nrm(ks[15], (D_MODEL,), 0.05),
        'w_kv': nrm(ks[16], (D_MODEL, 2 * GH), D_MODEL ** -0.5),
        'k_norm': 1.0 + nrm(ks[17], (N_GROUPS, HEAD_DIM), 0.05),
        'b_norm': 1.0 + nrm(ks[18], (N_B, D_MODEL), 0.05),
        'w_q': nrm(ks[19], (N_B, D_MODEL, GH), D_MODEL ** -0.5),
        'q_norm': 1.0 + nrm(ks[20], (N_B, N_GROUPS, HEAD_DIM), 0.05),
        'w_o': nrm(ks[21], (N_B, N_HEADS * HEAD_DIM, D_MODEL), (N_HEADS * HEAD_DIM) ** -0.5),
        'ffn_norm': 1.0 + nrm(ks[22], (DEPTH, D_MODEL), 0.05),
        'w_gate_up': nrm(ks[23], (DEPTH, D_MODEL, 2 * D_FF), D_MODEL ** -0.5),
        'w_down': nrm(ks[24], (DEPTH, D_FF, D_MODEL), D_FF ** -0.5),
    }


def reference(x_prompt, x_sample, cache_conv, cache_kv_w128, cache_kv_w512, cache_kv_w2048,
              a_norm, a_w_in, a_b_in, a_w_dw, a_b_dw, a_ln_g, a_ln_b, a_w_out, a_b_out,
              kv_norm, w_kv, k_norm, b_norm, w_q, q_norm, w_o, ffn_norm, w_gate_up, w_down):
    p = dict(a_norm=a_norm, a_w_in=a_w_in, a_b_in=a_b_in, a_w_dw=a_w_dw, a_b_dw=a_b_dw,
             a_ln_g=a_ln_g, a_ln_b=a_ln_b, a_w_out=a_w_out, a_b_out=a_b_out,
             kv_norm=kv_norm, w_kv=w_kv, k_norm=k_norm, b_norm=b_norm, w_q=w_q, q_norm=q_norm,
             w_o=w_o, ffn_norm=ffn_norm, w_gate_up=w_gate_up, w_down=w_down)
    B, S, _ = x_prompt.shape
    T = x_sample.shape[1]
    pos_prompt = jnp.arange(S, dtype=jnp.int32)
    pos_sample = PAST_LEN + jnp.arange(T, dtype=jnp.int32)
    conv_zero = jnp.zeros((N_A, B, CONV_WIDTH - 1, D_MODEL), x_prompt.dtype)
    y_prompt, conv_p, bufs_p = run_trunk(x_prompt, pos_prompt, conv_zero, None, p)
    y_sample, conv_s, bufs_s = run_trunk(x_sample, pos_sample, cache_conv,
                                         (cache_kv_w128, cache_kv_w512, cache_kv_w2048), p)
    return (y_prompt, y_sample, conv_p, conv_s, bufs_p[0], bufs_p[1], bufs_p[2], bufs_s[0], bufs_s[1], bufs_s[2])
```

```python
import numpy as np
import ml_dtypes
import concourse.bass as bass
import concourse.mybir as mybir
from concourse.bass_utils import run_bass_kernel_spmd

F32 = mybir.dt.float32
BF16 = mybir.dt.bfloat16
AF = mybir.ActivationFunctionType
ALU = mybir.AluOpType

D = 1024
NCH = 8
SEQ = 4096
HALF = 2048
NS = 32
NT = HALF + NS
DFF = 2816
NFC = 22
CW = 31
EPS = 1e-6
PAST = 8192
WINS = (128, 512, 2048)
DILS = (1, 4, 16)
KTW = SEQ + NS

ENGS = ["pe", "act", "dve", "pool", "sp"]
SQK = tuple(f"sqb{i}" for i in range(8))


class Prog:
    def __init__(self):
        self.ops = {e: [] for e in ENGS}
        self.last_w = {}
        self.readers = {}
        self.chan_n = {}

    def _deps(self, reads, writes):
        deps = []
        for k in reads:
            t = self.last_w.get(k)
            if t is not None:
                deps.append(t)
        for k in writes:
            t = self.last_w.get(k)
            if t is not None:
                deps.append(t)
            deps.extend(self.readers.get(k, ()))
        return deps

    def _commit(self, tok, reads, writes):
        for k in reads:
            lst = self.readers.setdefault(k, [])
            src = tok[:2]
            lst[:] = [t for t in lst if t[:2] != src]
            lst.append(tok)
        for k in writes:
            self.last_w[k] = tok
            self.readers[k] = []

    def op(self, eng, fn, reads=(), writes=()):
        reads = tuple(reads)
        writes = tuple(writes) + tuple(k for k in reads if k.startswith("ps") and k[2:].isdigit())
        idx = len(self.ops[eng])
        tok = ("e", eng, idx)
        deps = [t for t in self._deps(reads, writes) if not (t[0] == "e" and t[1] == eng and (eng == "pe" or t[2] == idx))]
        deps += self._take_fence(eng, idx)
        for t in deps:
            if t[0] == "e":
                self.ops[t[1]][t[2]]["signal"] = True
        self.ops[eng].append({"fn": fn, "deps": deps, "signal": False, "dma": None})
        self._commit(tok, reads, writes)
        return tok

    def dma(self, queue, chan, fn, reads=(), writes=(), n=1):
        cnt = self.chan_n.get(chan, 0) + n
        self.chan_n[chan] = cnt
        tok = ("d", chan, cnt)
        deps = list(self._deps(reads, writes))
        deps += self._take_fence(queue, len(self.ops[queue]))
        for t in deps:
            if t[0] == "e":
                self.ops[t[1]][t[2]]["signal"] = True
        self.ops[queue].append({"fn": fn, "deps": deps, "signal": False, "dma": chan})
        self._commit(tok, reads, writes)
        return tok

    def fence(self):
        deps = [("e", E, len(self.ops[E]) - 1) for E in ENGS if self.ops[E] and self.ops[E][-1]["dma"] is None]
        for E in ENGS:
            if self.ops[E] and self.ops[E][-1]["dma"] is not None:
                for i in range(len(self.ops[E]) - 1, -1, -1):
                    if self.ops[E][i]["dma"] is None:
                        deps.append(("e", E, i))
                        break
        deps += [("d", ch, n) for ch, n in self.chan_n.items()]
        self.pending = {E: list(deps) for E in ENGS}

    def _take_fence(self, eng, idx):
        pend = getattr(self, "pending", None)
        if not pend or not pend.get(eng):
            return []
        d = pend[eng]
        pend[eng] = []
        return [t for t in d if not (t[0] == "e" and t[1] == eng and eng == "pe")]

    def emit(self, nc, block_engines, sems, chan_sems):
        sigcount = {}
        for e in ENGS:
            c = 0
            arr = []
            for o in self.ops[e]:
                if o["signal"] and o["dma"] is None:
                    c += 1
                arr.append(c)
            sigcount[e] = arr
        prog = self

        def make(e):
            def body(eng):
                seen = {}
                for o in prog.ops[e]:
                    for t in o["deps"]:
                        if t[0] == "e":
                            src = ("e", t[1])
                            cnt = sigcount[t[1]][t[2]]
                            sem = sems[t[1]]
                        else:
                            src = ("d", t[1])
                            cnt = 16 * t[2]
                            sem = chan_sems[t[1]]
                        if seen.get(src, 0) >= cnt:
                            continue
                        seen[src] = cnt
                        eng.wait_ge(sem, cnt)
                    r = o["fn"](eng)
                    if o["dma"] is not None:
                        for ins in r:
                            ins.then_inc(chan_sems[o["dma"]], 16)
                    elif o["signal"]:
                        r.then_inc(sems[e], 1)
                if e == "sp":
                    for ch, n in prog.chan_n.items():
                        if seen.get(("d", ch), 0) < 16 * n:
                            eng.wait_ge(chan_sems[ch], 16 * n)
            return body

        for e in ENGS:
            block_engines[e](make(e))


class Mem:
    def __init__(self, nc, base, size):
        self.nc = nc
        self.base = base
        self.size = size
        self.cur = 0
        self.n = 0
        Mem._gn = getattr(Mem, "_gn", 0) + 1000
        self.n = Mem._gn

    def alloc(self, shape, dtype, name=None):
        nbytes = int(np.prod(shape[1:])) * (4 if dtype == F32 else 2)
        nbytes = (nbytes + 63) // 64 * 64
        off = self.cur
        self.cur += nbytes
        assert self.cur <= self.size, f"SBUF overflow {self.cur} > {self.size} at {name}"
        self.n += 1
        self.offs = getattr(self, "offs", {})
        self.offs[name] = (self.base + off, nbytes)
        return self._reg(name, self.nc.alloc_sbuf_tensor_at(f"{name or 't'}_{self.n}", list(shape), dtype, offset=self.base + off))

    def _reg(self, name, h):
        self.names = getattr(self, "names", {})
        self.names[name] = h.name
        return h

    def mark(self):
        return self.cur

    def reset(self, m):
        self.cur = m


TILES_X = [(i * 512, 512) for i in range(4)]
TILES_Y = TILES_X + [(HALF, NS)]
GW = 30 + HALF + 4 * 38
GS0 = 30 + HALF


class Builder:
    def __init__(self, stop_after=None):
        self.stop_after = stop_after
        nc = self.nc = bass.Bass("TRN2", target_bir_lowering=False)
        self.P = Prog()
        self.din = {}
        self.dout = {}
        self._uid = 0

    def inp(self, name, shape, dtype=F32):
        self.din[name] = self.nc.dram_tensor(name, list(shape), dtype, kind="ExternalInput").ap()
        return self.din[name]

    def outp(self, name, shape, dtype=F32):
        self.dout[name] = self.nc.dram_tensor(name, list(shape), dtype, kind="ExternalOutput").ap()
        return self.dout[name]

    @staticmethod
    def uk(tk):
        return tuple(f"u:{tk}:{c}" for c in range(NCH))

    def uid(self):
        self._uid += 1
        return self._uid

    def ps(self):
        i = self._psi
        self._psi = (i + 1) % 8
        return self.psb[i], f"ps{i}"

    def wslot(self):
        i = self._wi
        self._wi = (i + 1) % len(self.wbufs)
        return self.wbufs[i], f"w{i}"

    def t32(self):
        i = self._t32i
        self._t32i = (i + 1) % len(self.T32)
        return self.T32[i], f"t32_{i}"

    def s32(self):
        i = self._s32i
        self._s32i = (i + 1) % len(self.S32)
        return self.S32[i], f"s32_{i}"

    def stage(self):
        i = self._stgi
        self._stgi = (i + 1) % 2
        return self.stg[i], f"stg{i}"

    def load_w(self, src_ap, shape_view):
        buf, key = self.wslot()
        a, b = shape_view
        dst = buf[:, 0:a * b].rearrange("p (a b) -> p a b", a=a)
        self.P.dma("pool", key, lambda e, dst=dst, src=src_ap: [e.dma_start(out=dst, in_=src)],
                   reads=(), writes=(key,))
        return dst, key

    PV_ROWS = {}

    @staticmethod
    def pv_layout():
        rows = {}
        n = 0

        def add(name, cnt=1):
            nonlocal n
            rows[name] = n
            n += cnt
        for L in range(2):
            add(f"a_norm{L}")
            add(f"b1_{L}")
            add(f"b2_{L}")
            add(f"wdw{L}", CW)
            add(f"b_dw{L}")
            add(f"ln_g{L}")
            add(f"ln_b{L}")
            add(f"b_out{L}")
        add("kv_norm")
        for j in range(2):
            add(f"b_norm{j}")
        for L in range(4):
            add(f"ffn_norm{L}")
        for g in range(3):
            add(f"k_norm{g}")
        for j in range(2):
            for g in range(3):
                add(f"q_norm{j}_{g}")
        return rows, n

    def pvs(self, name, c, off=0):
        i = self.pvrows[name] + off
        return self.pv[:, c, i:i + 1]

    def setup(self):
        nc = self.nc
        P = self.P
        self.pvrows, self.npv = self.pv_layout()
        self.inp("xin", [SEQ, D])
        self.inp("xs", [NS, D])
        self.inp("cconv", [2, 4, 30, D])
        self.inp("ckv0", [4, 128, 2048])
        self.inp("ckv1", [4, 512, 2048])
        self.inp("ckv2", [4, 2048, 2048])
        self.inp("pvec", [self.npv, D])
        self.inp("a_w_in", [2, D, 2 * D])
        self.inp("a_w_out", [2, D, D])
        self.inp("w_kv", [D, 6 * D])
        self.inp("w_q", [2, D, 3 * D])
        self.inp("w_o", [2, D, D])
        self.inp("w_gate_up", [4, D, 2 * DFF])
        self.inp("w_down", [4, DFF, D])
        self.inp("cmat", [6, 128, 128])
        self.inp("rot", [2, 128, KTW])
        self.inp("flag", [128, 1])
        self.inp("smaskin", [128, 544])
        self.outp("y", [NT, D])
        self.outp("convp", [2, 30, D])
        self.outp("convs", [2, 4, 30, D])
        self.outp("kvp0", [128, 2048])
        self.outp("kvp1", [512, 2048])
        self.outp("kvp2", [2048, 2048])
        self.outp("kvs0", [4, 128, 2048])
        self.outp("kvs1", [4, 512, 2048])
        self.outp("kvs2", [4, 2048, 2048])
        self.kt_scr = nc.dram_tensor("kt_scr", [3, 8, 128, KTW], BF16, kind="Internal").ap()
        self.v_scr = nc.dram_tensor("v_scr", [3, 8, KTW, 128], BF16, kind="Internal").ap()
        self.q_scr = nc.dram_tensor("q_scr", [3, 8, 128, NT], BF16, kind="Internal").ap()

        total = nc.sbuf_bytes_remaining - 64
        arena = nc.alloc_sbuf_tensor("arena", [128, total // 4], F32)
        base = nc.lookup_mloc(arena).addr
        self.M = M = Mem(nc, base, (total // 4) * 4)
        self.x = M.alloc([128, NCH, NT], F32, "x")
        self.u = M.alloc([128, NCH, NT], BF16, "u")
        self.G = M.alloc([128, NCH, GW], BF16, "glu")
        self.pv = M.alloc([128, NCH, self.npv], F32, "pv")
        self.ident = M.alloc([128, 128], F32, "ident")
        self.cb = M.alloc([128, 6, 128], BF16, "cb")
        self.maskf = M.alloc([128, 512], F32, "maskf")
        self.flag = M.alloc([128, 1], F32, "flag")
        self.sacc = M.alloc([128, 8, 2, NS], F32, "sacc")
        self.smask = M.alloc([128, 544], BF16, "smask")
        self.gstate = M.alloc([128, 2, NCH, 30], BF16, "gstate")
        self.g32 = M.alloc([128, 2, NCH, 30], F32, "g32")
        self.gs32 = M.alloc([128, 2, NCH, NS], F32, "gs32")
        self.wall = M.alloc([128, 6 * 2048], BF16, "wall")
        self.wbufs = [self.wall[:, i * 2048:(i + 1) * 2048] for i in range(6)]
        self.sqb = M.alloc([128, NCH, 512], BF16, "sqb")
        self.T32 = [M.alloc([128, 512], F32, f"t32_{i}") for i in range(4)]
        self._t32i = 0
        self.S32 = [M.alloc([128, 512], F32, f"s32_{i}") for i in range(4)]
        self._s32i = 0
        self.stg = [M.alloc([128, D], F32, f"stg{i}") for i in range(2)]
        self._stgi = 0
        self.hbuf = [M.alloc([128, 512], BF16, f"hbuf{i}") for i in range(4)]
        self._hi = 0
        self._di = 0
        self._sqi = 0
        self.persist_mark = M.mark()
        print("SBUF used", M.cur, "of", M.size)

        self.psb = [nc.alloc_psum_tensor(f"psb{i}", [128, 512], F32) for i in range(8)]
        self._psi = 0
        self._wi = 0

        cst = self.stg[1][:, 0:768].rearrange("p (a b) -> p a b", a=6)
        P.dma("sp", "stg1", lambda e: [e.dma_start(out=cst, in_=self.din["cmat"].rearrange("a p n -> p a n"))],
              writes=("stg1",))
        P.dma("sp", "c1", lambda e: [e.dma_start(out=self.flag[:], in_=self.din["flag"][:, :])], writes=("flag",))
        P.dma("pool", "c3", lambda e: [e.dma_start(out=self.smask[:], in_=self.din["smaskin"][:, :])], writes=("smask",))
        P.op("dve", lambda e: e.tensor_copy(out=self.cb[:], in_=cst), reads=("stg1",), writes=("cb",))
        P.op("dve", lambda e: e.tensor_copy(out=self.ident[:], in_=cst[:, 0, :]), reads=("stg1",), writes=("ident",))
        for q, src in enumerate([4, 4, 5, 5]):
            P.op("dve", lambda e, q=q, src=src: e.tensor_copy(out=self.maskf[:, q * 128:(q + 1) * 128], in_=cst[:, src, :]),
                 reads=("stg1",), writes=("maskf",))
        self.I_BF = self.cb[:, 0, :]
        self.ONES = self.cb[:, 1, :]
        self.BONES = self.cb[:, 2, :]
        self.SWAP = self.cb[:, 3, :]
        pst = self.stg[0]
        npv = self.npv
        P.dma("sp", "stg0", lambda e: [e.dma_start(out=pst[0:npv, :], in_=self.din["pvec"][:, :])], writes=("stg0",))
        for half in range(2):
            pt, pk = self.ps()

            def f(e, half=half, pt=pt):
                for cc in range(4):
                    c = half * 4 + cc
                    r = e.transpose(out=pt[:, cc * 128:cc * 128 + npv], in_=pst[0:npv, c * 128:(c + 1) * 128],
                                    identity=self.ident[0:npv, 0:npv])
                return r
            P.op("pe", f, reads=("stg0", "ident"), writes=(pk,))
            P.op("act", lambda e, half=half, pt=pt: e.activation(
                out=self.pv[:, half * 4:half * 4 + 4, :],
                in_=pt[:, :].rearrange("p (a b) -> p a b", a=4)[:, :, 0:npv], func=AF.Copy),
                reads=(pk,), writes=("pv",))
        P.op("dve", lambda e: e.memset(self.G[:, :, 0:30], 0.0), writes=("G:state",))

    def load_tokens(self, src_ap, nrows, col0):
        P = self.P
        st, sk = self.stage()
        P.dma("sp", sk, lambda e: [e.dma_start(out=st[0:nrows, :], in_=src_ap)], writes=(sk,))
        xkey = f"x:{col0 // 512}"
        for half in range(2):
            pt, pk = self.ps()

            def f(e, half=half, pt=pt):
                for cc in range(4):
                    c = half * 4 + cc
                    r = e.transpose(out=pt[:, cc * 128:cc * 128 + nrows], in_=st[0:nrows, c * 128:(c + 1) * 128],
                                    identity=self.ident[0:nrows, 0:nrows])
                return r
            P.op("pe", f, reads=(sk, "ident"), writes=(pk,))
            P.op("act", lambda e, half=half, pt=pt: e.activation(
                out=self.x[:, half * 4:half * 4 + 4, col0:col0 + nrows],
                in_=pt[:, :].rearrange("p (a b) -> p a b", a=4)[:, :, 0:nrows], func=AF.Copy),
                reads=(pk,), writes=(xkey,))

    def store_tokens(self, dst_ap, nrows, col0, src=None, skey=None):
        P = self.P
        src = self.x if src is None else src
        skey = f"x:{col0 // 512}" if skey is None else skey
        st, sk = self.stage()
        for half in range(2):
            pt, pk = self.ps()

            def f(e, half=half, pt=pt):
                for cc in range(4):
                    c = half * 4 + cc
                    r = e.transpose(out=pt[0:nrows, cc * 128:(cc + 1) * 128], in_=src[:, c, col0:col0 + nrows],
                                    identity=self.ident[:, :])
                return r
            P.op("pe", f, reads=(skey, "ident"), writes=(pk,))
            P.op("act", lambda e, half=half, pt=pt: e.activation(
                out=st[0:nrows, half * 512:(half + 1) * 512], in_=pt[0:nrows, :], func=AF.Copy),
                reads=(pk,), writes=(sk,))
        P.dma("sp", sk, lambda e: [e.dma_start(out=dst_ap, in_=st[0:nrows, :])], reads=(sk,))

    def rmsnorm(self, tiles, gname):
        P = self.P
        rs = {}
        for (c0, n) in tiles:
            tk = c0 // 512
            P.op("act", lambda e, c0=c0, n=n: e.activation(out=self.sqb[:, :, 0:n], in_=self.x[:, :, c0:c0 + n], func=AF.Square),
                 reads=(f"x:{tk}",), writes=SQK)
            pt, pk = self.ps()

            def f(e, pt=pt, n=n):
                for c in range(NCH):
                    r = e.matmul(pt[:, 0:n], lhsT=self.ONES, rhs=self.sqb[:, c, 0:n], start=(c == 0), stop=(c == NCH - 1))
                return r
            P.op("pe", f, reads=SQK + ("cb",), writes=(pk,))
            t, tkey = self.s32()
            P.op("act", lambda e, pt=pt, t=t, n=n: e.activation(out=t[:, 0:n], in_=pt[:, 0:n], func=AF.Sqrt, bias=EPS, scale=1.0 / D),
                 reads=(pk,), writes=(tkey,))
            P.op("dve", lambda e, t=t, n=n: e.reciprocal(out=t[:, 0:n], in_=t[:, 0:n]), reads=(tkey,), writes=(tkey,))

            def g(e, t=t, c0=c0, n=n):
                for c in range(NCH):
                    r = e.scalar_tensor_tensor(out=self.u[:, c, c0:c0 + n], in0=self.x[:, c, c0:c0 + n],
                                               scalar=self.pvs(gname, c), in1=t[:, 0:n], op0=ALU.mult, op1=ALU.mult)
                return r
            P.op("dve", g, reads=(tkey, f"x:{tk}", "pv"), writes=self.uk(tk))

    def conv_layer(self, L, isY, tiles=None):
        P = self.P
        if tiles is None:
            tiles = TILES_Y if isY else TILES_X
        w_in = self.din["a_w_in"]
        w_out = self.din["a_w_out"]
        G = self.G
        self.rmsnorm(tiles, f"a_norm{L}")
        if isY:
            P.op("dve", lambda e: e.tensor_copy(out=G[:, :, 0:30], in_=self.gstate[:, L, :, :]),
                 reads=("gstate",), writes=("G:state",))
            for s in range(4):
                st, sk = self.stage()
                P.dma("sp", sk, lambda e, st=st, s=s: [e.dma_start(out=st[0:30, :], in_=self.din["cconv"][L, s, :, :])], writes=(sk,))
                for half in range(2):
                    pt, pk = self.ps()

                    def f(e, half=half, pt=pt, st=st):
                        for cc in range(4):
                            c = half * 4 + cc
                            r = e.transpose(out=pt[:, cc * 128:cc * 128 + 30], in_=st[0:30, c * 128:(c + 1) * 128],
                                            identity=self.ident[0:30, 0:30])
                        return r
                    P.op("pe", f, reads=(sk, "ident"), writes=(pk,))
                    P.op("act", lambda e, half=half, pt=pt, s=s: e.activation(
                        out=G[:, half * 4:half * 4 + 4, GS0 + s * 38:GS0 + s * 38 + 30],
                        in_=pt[:, :].rearrange("p (a b) -> p a b", a=4)[:, :, 0:30], func=AF.Copy),
                        reads=(pk,), writes=("G:s",))
        for oc in range(NCH):
            wv, wk = self.wslot()
            wa = wv[:, 0:2048].rearrange("p (h k n) -> p h k n", h=2, k=NCH)
            src1 = w_in[L, :, oc * 128:(oc + 1) * 128].rearrange("(k p) n -> p k n", p=128)
            src2 = w_in[L, :, D + oc * 128:D + (oc + 1) * 128].rearrange("(k p) n -> p k n", p=128)
            P.dma("pool", wk, lambda e, wa=wa, src1=src1, src2=src2: [e.dma_start(out=wa[:, 0], in_=src1), e.dma_start(out=wa[:, 1], in_=src2)],
                  writes=(wk,), n=2)
            for (c0, n) in tiles:
                tk = c0 // 512
                p1, k1 = self.ps()
                p2, k2 = self.ps()

                def mm(e, p1=p1, p2=p2, wa=wa, c0=c0, n=n):
                    for k in range(NCH):
                        e.matmul(p1[:, 0:n], lhsT=wa[:, 0, k, :], rhs=self.u[:, k, c0:c0 + n], start=(k == 0), stop=(k == NCH - 1))
                    for k in range(NCH):
                        r = e.matmul(p2[:, 0:n], lhsT=wa[:, 1, k, :], rhs=self.u[:, k, c0:c0 + n], start=(k == 0), stop=(k == NCH - 1))
                    return r
                P.op("pe", mm, reads=(wk,) + self.uk(tk), writes=(k1, k2))
                sg, sgk = self.t32()
                P.op("act", lambda e, sg=sg, p2=p2, n=n, oc=oc: e.activation(out=sg[:, 0:n], in_=p2[:, 0:n], func=AF.Sigmoid,
                                                                       bias=self.pvs(f"b2_{L}", oc), scale=1.0),
                     reads=(k2, "pv"), writes=(sgk,))
                if n == 512:
                    gout = G[:, oc, 30 + c0:30 + c0 + n]
                    gkey = f"G:{tk}"
                else:
                    gout = G[:, oc, GS0:GS0 + 4 * 38].rearrange("p (s t) -> p s t", s=4)[:, :, 30:38]
                    gkey = "G:s"

                def glu(e, sg=sg, p1=p1, n=n, oc=oc, gout=gout, c0=c0):
                    if n == 512:
                        r = e.scalar_tensor_tensor(out=gout, in0=p1[:, 0:n], scalar=self.pvs(f"b1_{L}", oc), in1=sg[:, 0:n],
                                                   op0=ALU.add, op1=ALU.mult)
                        if c0 == 1536:
                            if isY:
                                r = e.scalar_tensor_tensor(out=self.g32[:, L, oc, :], in0=p1[:, 482:512], scalar=self.pvs(f"b1_{L}", oc),
                                                           in1=sg[:, 482:512], op0=ALU.add, op1=ALU.mult)
                            else:
                                r = e.scalar_tensor_tensor(out=self.gstate[:, L, oc, :], in0=p1[:, 482:512], scalar=self.pvs(f"b1_{L}", oc),
                                                           in1=sg[:, 482:512], op0=ALU.add, op1=ALU.mult)
                    else:
                        e.scalar_tensor_tensor(out=gout, in0=p1[:, 0:n].rearrange("p (s t) -> p s t", s=4), scalar=self.pvs(f"b1_{L}", oc),
                                               in1=sg[:, 0:n].rearrange("p (s t) -> p s t", s=4), op0=ALU.add, op1=ALU.mult)
                        r = e.scalar_tensor_tensor(out=self.gs32[:, L, oc, :], in0=p1[:, 0:n], scalar=self.pvs(f"b1_{L}", oc),
                                                   in1=sg[:, 0:n], op0=ALU.add, op1=ALU.mult)
                    return r
                wr = [gkey]
                if c0 == 1536:
                    wr.append("g32" if isY else "gstate")
                if n != 512:
                    wr.append("gs32")
                P.op("dve", glu, reads=(k1, sgk, "pv", "flag"), writes=tuple(wr))
                if c0 == 1536 and not isY:
                    P.op("dve", lambda e, oc=oc: e.tensor_scalar(out=self.gstate[:, L, oc, :], in0=self.gstate[:, L, oc, :],
                                                                 scalar1=self.flag[:, 0:1], scalar2=None, op0=ALU.mult),
                         reads=("gstate", "flag"), writes=("gstate",))
        gkeys_main = tuple(f"G:{i}" for i in range(4)) + ("G:state",)
        for c in range(NCH):
            i1 = 2 * self._di
            self._di = (self._di + 1) % 3
            sk1, sk2 = f"w{i1}", f"w{i1 + 1}"
            dg = self.wall[:, i1 * 2048:i1 * 2048 + CW * 128].rearrange("p (j n) -> p j n", j=CW)

            def mk(e, dg=dg, c=c):
                for j in range(CW):
                    r = e.activation(out=dg[:, j, :], in_=self.I_BF, func=AF.Identity, scale=self.pvs(f"wdw{L}", c, j))
                return r
            P.op("act", mk, reads=("cb", "pv"), writes=(sk1, sk2))
            for (c0, n) in tiles:
                tk = c0 // 512
                pt, pk = self.ps()
                if n == 512:
                    def cv(e, dg=dg, c=c, c0=c0, pt=pt):
                        for j in range(CW):
                            r = e.matmul(pt[:, 0:512], lhsT=dg[:, j, :], rhs=G[:, c, c0 + j:c0 + j + 512], start=(j == 0), stop=(j == CW - 1))
                        return r
                    rd = (sk1, sk2) + gkeys_main
                else:
                    def cv(e, dg=dg, c=c, pt=pt):
                        gsv = G[:, c, GS0:GS0 + 4 * 38].rearrange("p (s t) -> p s t", s=4)
                        for j in range(CW):
                            r = e.matmul(pt[:, 0:NS].rearrange("p (s t) -> p s t", s=4), lhsT=dg[:, j, :], rhs=gsv[:, :, j:j + 8],
                                         start=(j == 0), stop=(j == CW - 1))
                        return r
                    rd = (sk1, sk2, "G:s")
                P.op("pe", cv, reads=rd, writes=(pk,))
                P.op("act", lambda e, pt=pt, c=c, c0=c0, n=n: e.activation(out=self.u[:, c, c0:c0 + n], in_=pt[:, 0:n], func=AF.Identity,
                                                                     bias=self.pvs(f"b_dw{L}", c), scale=1.0),
                     reads=(pk, "pv"), writes=(f"u:{tk}:{c}",))
        if getattr(self, "stop_stage", None) == "B":
            return
        for (c0, n) in tiles:
            tk = c0 // 512
            ukey = self.uk(tk)
            P.op("act", lambda e, c0=c0, n=n: e.activation(out=self.sqb[:, :, 0:n], in_=self.u[:, :, c0:c0 + n], func=AF.Square),
                 reads=ukey, writes=SQK)
            psm, ksm = self.ps()
            psq, ksq = self.ps()

            def st(e, psm=psm, psq=psq, c0=c0, n=n):
                for c in range(NCH):
                    e.matmul(psm[:, 0:n], lhsT=self.ONES, rhs=self.u[:, c, c0:c0 + n], start=(c == 0), stop=(c == NCH - 1))
                for c in range(NCH):
                    r = e.matmul(psq[:, 0:n], lhsT=self.ONES, rhs=self.sqb[:, c, 0:n], start=(c == 0), stop=(c == NCH - 1))
                return r
            P.op("pe", st, reads=ukey + SQK + ("cb",), writes=(ksm, ksq))
            mu, kmu = self.s32()
            va, kva = self.s32()

            P.op("dve", lambda e, mu=mu, psm=psm, n=n: e.tensor_scalar(out=mu[:, 0:n], in0=psm[:, 0:n], scalar1=1.0 / D, scalar2=None, op0=ALU.mult),
                 reads=(ksm,), writes=(kmu,))
            P.op("dve", lambda e, mu=mu, va=va, n=n: e.tensor_tensor(out=va[:, 0:n], in0=mu[:, 0:n], in1=mu[:, 0:n], op=ALU.mult),
                 reads=(kmu,), writes=(kva,))
            P.op("dve", lambda e, va=va, psq=psq, n=n: e.scalar_tensor_tensor(out=va[:, 0:n], in0=psq[:, 0:n], scalar=1.0 / D, in1=va[:, 0:n],
                                                                         op0=ALU.mult, op1=ALU.subtract),
                 reads=(ksq, kva), writes=(kva,))
            P.op("act", lambda e, va=va, n=n: e.activation(out=va[:, 0:n], in_=va[:, 0:n], func=AF.Sqrt, bias=EPS, scale=1.0),
                 reads=(kva,), writes=(kva,))
            P.op("dve", lambda e, va=va, n=n: e.reciprocal(out=va[:, 0:n], in_=va[:, 0:n]), reads=(kva,), writes=(kva,))
            for c in range(NCH):
                t, tkey = self.t32()

                P.op("dve", lambda e, t=t, c=c, c0=c0, n=n, mu=mu: e.tensor_tensor(out=t[:, 0:n], in0=self.u[:, c, c0:c0 + n], in1=mu[:, 0:n], op=ALU.subtract),
                     reads=(ukey[c], kmu), writes=(tkey,))
                P.op("dve", lambda e, t=t, n=n, va=va: e.tensor_tensor(out=t[:, 0:n], in0=t[:, 0:n], in1=va[:, 0:n], op=ALU.mult),
                     reads=(tkey, kva), writes=(tkey,))
                P.op("act", lambda e, t=t, c=c, c0=c0, n=n: e.activation(out=self.u[:, c, c0:c0 + n], in_=t[:, 0:n], func=AF.Silu,
                                                                   bias=self.pvs(f"ln_b{L}", c), scale=self.pvs(f"ln_g{L}", c)),
                     reads=(tkey, "pv"), writes=(ukey[c],))
        if getattr(self, "stop_stage", None) == "C":
            return
        for oc in range(NCH):
            wv, wk = self.wslot()
            wa = wv[:, 0:1024].rearrange("p (k n) -> p k n", k=NCH)
            src = w_out[L, :, oc * 128:(oc + 1) * 128].rearrange("(k p) n -> p k n", p=128)
            P.dma("pool", wk, lambda e, wa=wa, src=src: [e.dma_start(out=wa, in_=src)], writes=(wk,))
            for (c0, n) in tiles:
                tk = c0 // 512
                pt, pk = self.ps()

                def mm(e, pt=pt, wa=wa, c0=c0, n=n):
                    for k in range(NCH):
                        r = e.matmul(pt[:, 0:n], lhsT=wa[:, k, :], rhs=self.u[:, k, c0:c0 + n], start=(k == 0), stop=(k == NCH - 1))
                    return r
                P.op("pe", mm, reads=(wk,) + self.uk(tk), writes=(pk,))
                P.op("dve", lambda e, pt=pt, oc=oc, c0=c0, n=n: e.scalar_tensor_tensor(
                    out=self.x[:, oc, c0:c0 + n], in0=pt[:, 0:n], scalar=self.pvs(f"b_out{L}", oc), in1=self.x[:, oc, c0:c0 + n],
                    op0=ALU.add, op1=ALU.add), reads=(pk, "pv", f"x:{tk}"), writes=(f"x:{tk}",))

    def ffn(self, L, tiles):
        P = self.P
        wgu = self.din["w_gate_up"]
        wdn = self.din["w_down"]
        self.rmsnorm(tiles, f"ffn_norm{L}")
        for fg in range(NFC // 2):
            f0 = fg * 256
            wgv, kg = self.wslot()
            wuv, ku = self.wslot()
            wdv, kd = self.wslot()
            wg = wgv[:, 0:2048].rearrange("p (k n) -> p k n", k=NCH)
            wu = wuv[:, 0:2048].rearrange("p (k n) -> p k n", k=NCH)
            wd = wdv[:, 0:2048].rearrange("p (j n) -> p j n", j=2)
            sg_ = wgu[L, :, f0:f0 + 256].rearrange("(k p) n -> p k n", p=128)
            su_ = wgu[L, :, DFF + f0:DFF + f0 + 256].rearrange("(k p) n -> p k n", p=128)
            sd_ = wdn[L, f0:f0 + 256, :].rearrange("(j p) n -> p j n", p=128)
            P.dma("pool", kg, lambda e, wg=wg, sg_=sg_: [e.dma_start(out=wg, in_=sg_)], writes=(kg,))
            P.dma("pool", ku, lambda e, wu=wu, su_=su_: [e.dma_start(out=wu, in_=su_)], writes=(ku,))
            P.dma("pool", kd, lambda e, wd=wd, sd_=sd_: [e.dma_start(out=wd, in_=sd_)], writes=(kd,))

            def gate_up_steps(c0, n):
                tk = c0 // 512
                hkeys = []
                hs = []
                steps = []
                for j in range(2):
                    sg, sgk = self.t32()
                    hb = self.hbuf[self._hi]
                    hk = f"h{self._hi}"
                    self._hi = (self._hi + 1) % len(self.hbuf)
                    hs.append(hb)
                    hkeys.append(hk)
                    bank = {}

                    def s_g(bank=bank, j=j):
                        pg, kpg = self.ps()
                        bank["g"] = (pg, kpg)

                        def mm(e, pg=pg, j=j, c0=c0, n=n, wg=wg):
                            for k in range(NCH):
                                r = e.matmul(pg[:, 0:n], lhsT=wg[:, k, j * 128:(j + 1) * 128], rhs=self.u[:, k, c0:c0 + n], start=(k == 0), stop=(k == NCH - 1))
                            return r
                        P.op("pe", mm, reads=(kg,) + self.uk(tk), writes=(kpg,))

                    def s_u(bank=bank, j=j):
                        pu, kpu = self.ps()
                        bank["u"] = (pu, kpu)

                        def mm(e, pu=pu, j=j, c0=c0, n=n, wu=wu):
                            for k in range(NCH):
                                r = e.matmul(pu[:, 0:n], lhsT=wu[:, k, j * 128:(j + 1) * 128], rhs=self.u[:, k, c0:c0 + n], start=(k == 0), stop=(k == NCH - 1))
                            return r
                        P.op("pe", mm, reads=(ku,) + self.uk(tk), writes=(kpu,))

                    def s_e(bank=bank, sg=sg, sgk=sgk, hb=hb, hk=hk):
                        pg, kpg = bank["g"]
                        pu, kpu = bank["u"]
                        P.op("act", lambda e, sg=sg, pg=pg, n=n: e.activation(out=sg[:, 0:n], in_=pg[:, 0:n], func=AF.Silu),
                             reads=(kpg,), writes=(sgk,))
                        P.op("dve", lambda e, hb=hb, sg=sg, pu=pu, n=n: e.tensor_tensor(out=hb[:, 0:n], in0=sg[:, 0:n], in1=pu[:, 0:n], op=ALU.mult),
                             reads=(sgk, kpu), writes=(hk,))
                    steps += [s_g, s_u, s_e]
                return steps, hs, hkeys

            def down_steps(c0, n, hs, hkeys):
                tk = c0 // 512
                steps = []
                for dc in range(NCH):
                    def s_d(dc=dc):
                        pd, kpd = self.ps()

                        def dn(e, pd=pd, dc=dc, hs=tuple(hs), n=n, wd=wd):
                            e.matmul(pd[:, 0:n], lhsT=wd[:, 0, dc * 128:(dc + 1) * 128], rhs=hs[0][:, 0:n], start=True, stop=False)
                            return e.matmul(pd[:, 0:n], lhsT=wd[:, 1, dc * 128:(dc + 1) * 128], rhs=hs[1][:, 0:n], start=False, stop=True)
                        P.op("pe", dn, reads=(kd,) + tuple(hkeys), writes=(kpd,))
                        P.op("dve", lambda e, pd=pd, dc=dc, c0=c0, n=n: e.tensor_tensor(out=self.x[:, dc, c0:c0 + n], in0=pd[:, 0:n],
                                                                                  in1=self.x[:, dc, c0:c0 + n], op=ALU.add),
                             reads=(kpd, f"x:{tk}"), writes=(f"x:{tk}",))
                    steps.append(s_d)
                return steps
            prev = None
            for (c0, n) in tiles:
                gu, hs, hkeys = gate_up_steps(c0, n)
                dn_ = down_steps(*prev) if prev is not None else []
                order = [gu[0]] + dn_[0:2] + [gu[1]] + dn_[2:4] + [gu[2], gu[3]] + dn_[4:6] + [gu[4]] + dn_[6:8] + [gu[5]]
                for st_ in order:
                    st_()
                prev = (c0, n, hs, hkeys)
            for st_ in down_steps(*prev):
                st_()

    def phase1_test(self):
        self.setup()
        xin = self.din["xin"]
        for sup in range(2):
            isY = sup == 1
            for tb in range(16):
                self.load_tokens(xin[sup * HALF + tb * 128: sup * HALF + (tb + 1) * 128, :], 128, tb * 128)
            if isY:
                self.load_tokens(self.din["xs"][:, :], NS, HALF)
            tiles = TILES_Y if isY else TILES_X
            for L in range(2):
                self.conv_layer(L, isY)
                self.ffn(L, tiles)
        for tb in range(16):
            self.store_tokens(self.dout["y"][tb * 128:(tb + 1) * 128, :], 128, tb * 128)
        self.store_tokens(self.dout["y"][HALF:HALF + NS, :], NS, HALF)

    def finish(self):
        nc = self.nc
        P = self.P
        chans = sorted(P.chan_n.keys())
        import contextlib
        with contextlib.ExitStack() as es:
            sems = {e: es.enter_context(nc.semaphore(f"sem_{e}")) for e in ENGS}
            csems = {ch: es.enter_context(nc.semaphore(f"semd_{ch}")) for ch in chans}
            block = es.enter_context(nc.Block())
            regs = {"pe": block.tensor, "act": block.scalar, "dve": block.vector, "pool": block.gpsimd, "sp": block.sync}
            P.emit(nc, regs, sems, csems)
        return nc


def _const_mats():
    p = np.arange(128)
    ident = np.eye(128, dtype=np.float32)
    ones = np.ones((128, 128), np.float32)
    bones = (p[:, None] // 64 == p[None, :] // 64).astype(np.float32)
    partner = (p // 64) * 64 + ((p % 64) + 32) % 64
    swap = np.zeros((128, 128), np.float32)
    swap[partner, p] = 1.0
    maskP = (p[None, :] <= p[:, None]).astype(np.float32)
    maskC = (p[:, None] <= p[None, :]).astype(np.float32)
    return np.stack([ident, ones, bones, swap, maskP, maskC]).astype(np.float32)


def _smask():
    m = np.zeros((128, 544), np.float32)
    k = np.arange(128)[:, None]
    q = np.arange(8)[None, :]
    for a in range(16):
        m[:, a * 8:(a + 1) * 8] = (k >= q)
        m[:, 128 + a * 2] = 1.0
        m[:, 128 + a * 2 + 1] = (np.arange(128) >= 1)
    t = np.arange(8)[:, None]
    M0 = (t <= q).astype(np.float32)
    M1 = ((t == q) | (t == q - 4)).astype(np.float32)
    M2 = (t == q).astype(np.float32)
    for g, Mg in enumerate((M0, M1, M2)):
        for a in range(16):
            m[0:8, 160 + g * 128 + a * 8:160 + g * 128 + (a + 1) * 8] = Mg
    return m


def _rot_tables(half_id):
    half = 32
    inv = np.float32(10000.0) ** (-(np.arange(half, dtype=np.float32) / np.float32(half)))
    pos = np.zeros(KTW, np.float32)
    l = np.arange(SEQ)
    if half_id == 1:
        pos[:SEQ] = l
    else:
        pos[:SEQ] = np.maximum(l - HALF, 0)
    pos[SEQ:] = np.tile(PAST + np.arange(8), 4)
    ang = (pos[:, None].astype(np.float32) * inv[None, :].astype(np.float32)).astype(np.float32)
    p = np.arange(128)
    idx = (p % 64) % 32
    cosT = np.cos(ang)[:, idx].T.astype(np.float32)
    sinT = np.sin(ang)[:, idx].T.astype(np.float32)
    sign = np.where((p % 64) < 32, -1.0, 1.0).astype(np.float32)
    return np.stack([cosT, sinT * sign[:, None]]).astype(np.float32)


def _pvec(inp):
    rows, n = Builder.pv_layout()
    pv = np.zeros((n, D), np.float32)
    for L in range(2):
        pv[rows[f"a_norm{L}"]] = inp["a_norm"][L]
        pv[rows[f"b1_{L}"]] = inp["a_b_in"][L][:D]
        pv[rows[f"b2_{L}"]] = inp["a_b_in"][L][D:]
        pv[rows[f"wdw{L}"]:rows[f"wdw{L}"] + CW] = inp["a_w_dw"][L]
        pv[rows[f"b_dw{L}"]] = inp["a_b_dw"][L]
        pv[rows[f"ln_g{L}"]] = inp["a_ln_g"][L]
        pv[rows[f"ln_b{L}"]] = inp["a_ln_b"][L]
        pv[rows[f"b_out{L}"]] = inp["a_b_out"][L]
    pv[rows["kv_norm"]] = inp["kv_norm"]
    for j in range(2):
        pv[rows[f"b_norm{j}"]] = inp["b_norm"][j]
    for L in range(4):
        pv[rows[f"ffn_norm{L}"]] = inp["ffn_norm"][L]
    for g in range(3):
        pv[rows[f"k_norm{g}"], :128] = np.tile(inp["k_norm"][g], 2)
    for j in range(2):
        for g in range(3):
            pv[rows[f"q_norm{j}_{g}"], :128] = np.tile(inp["q_norm"][j, g], 2)
    return pv


def make_in_maps(inp):
    inp = {k: np.asarray(v) for k, v in inp.items()}
    cmat = _const_mats()
    pvec = _pvec(inp)
    rots = [_rot_tables(0), _rot_tables(1)]
    shared = {
        "pvec": pvec, "cmat": cmat, "smaskin": _smask(),
        "a_w_in": np.ascontiguousarray(inp["a_w_in"], np.float32), "a_w_out": np.ascontiguousarray(inp["a_w_out"], np.float32),
        "w_kv": np.ascontiguousarray(inp["w_kv"], np.float32), "w_q": np.ascontiguousarray(inp["w_q"], np.float32),
        "w_o": np.ascontiguousarray(inp["w_o"], np.float32), "w_gate_up": np.ascontiguousarray(inp["w_gate_up"], np.float32),
        "w_down": np.ascontiguousarray(inp["w_down"], np.float32),
    }
    maps = []
    for c in range(8):
        b, h = c // 2, c % 2
        if h == 1:
            xin = np.ascontiguousarray(inp["x_prompt"][b], np.float32)
        else:
            xin = np.concatenate([np.zeros((HALF, D), np.float32), inp["x_prompt"][b, :HALF]], axis=0)
        m = dict(shared)
        m["xin"] = xin
        m["xs"] = np.ascontiguousarray(inp["x_sample"][4 * c:4 * c + 4].reshape(NS, D), np.float32)
        m["cconv"] = np.ascontiguousarray(inp["cache_conv"][:, 4 * c:4 * c + 4], np.float32)
        m["ckv0"] = np.ascontiguousarray(inp["cache_kv_w128"][4 * c:4 * c + 4].reshape(4, 128, 2048), np.float32)
        m["ckv1"] = np.ascontiguousarray(inp["cache_kv_w512"][4 * c:4 * c + 4].reshape(4, 512, 2048), np.float32)
        m["ckv2"] = np.ascontiguousarray(inp["cache_kv_w2048"][4 * c:4 * c + 4].reshape(4, 2048, 2048), np.float32)
        m["rot"] = rots[h]
        m["flag"] = np.full((128, 1), float(h), np.float32)
        maps.append(m)
    return maps


def _dbg_t0(self):
    self.setup()
    xin = self.din["xin"]
    self.load_tokens(xin[HALF:HALF + 128, :], 128, 0)
    self.store_tokens(self.dout["y"][0:128, :], 128, 0)


Builder.dbg_t0 = _dbg_t0


def _dbg_c0(self):
    self.setup()
    xin = self.din["xin"]
    tiles = [(0, 512)]
    for tb in range(4):
        self.load_tokens(xin[tb * 128:(tb + 1) * 128, :], 128, tb * 128)
    import os
    self.stop_stage = os.environ.get("STOP_STAGE")
    self.conv_layer(0, False, tiles=tiles)
    if self.stop_stage is None:
        self.ffn(0, tiles)
    for tb in range(4):
        self.store_tokens(self.dout["y"][tb * 128:(tb + 1) * 128, :], 128, tb * 128)
    self.dbg_sbuf = [self.M.names[k] for k in ("u", "glu", "pv", "x")]


Builder.dbg_c0 = _dbg_c0


def _ovl_G(self):
    off, nb = self.M.offs["glu"]
    return Mem(self.nc, off, nb)


def _load_rot(self, OM, col_src0, ncols_main, with_sample):
    P = self.P
    rc = OM.alloc([128, NT], F32, "rotc")
    rs = OM.alloc([128, NT], F32, "rots")
    rot = self.din["rot"]

    def f(e):
        r = [e.dma_start(out=rc[:, 0:HALF], in_=rot[0, :, col_src0:col_src0 + HALF]),
             e.dma_start(out=rs[:, 0:HALF], in_=rot[1, :, col_src0:col_src0 + HALF])]
        if with_sample:
            r += [e.dma_start(out=rc[:, HALF:NT], in_=rot[0, :, SEQ:SEQ + NS]),
                  e.dma_start(out=rs[:, HALF:NT], in_=rot[1, :, SEQ:SEQ + NS])]
        return r
    P.dma("sp", "rot", f, writes=("rot",), n=4 if with_sample else 2)
    return rc, rs


def _head_proj(self, tiles, w_src_fn, gain_fn, dst_fn, rc, rs, OM, out32_fn=None):
    P = self.P
    kst = [OM.alloc([128, 512], BF16, f"kst{i}") for i in range(2)]
    ksti = 0
    for g in range(3):
        for hp in range(8):
            wv, wk = self.wslot()
            wa = wv[:, 0:1024].rearrange("p (k n) -> p k n", k=NCH)
            src = w_src_fn(g, hp)
            P.dma("pool", wk, lambda e, wa=wa, src=src: [e.dma_start(out=wa, in_=src)], writes=(wk,))
            for (c0, n) in tiles:
                tk = c0 // 512
                pr, kpr = self.ps()

                def mm(e, pr=pr, wa=wa, c0=c0, n=n):
                    for k in range(NCH):
                        r = e.matmul(pr[:, 0:n], lhsT=wa[:, k, :], rhs=self.u[:, k, c0:c0 + n], start=(k == 0), stop=(k == NCH - 1))
                    return r
                P.op("pe", mm, reads=(wk,) + self.uk(tk), writes=(kpr,))
                si = self._sqi
                self._sqi = (si + 2) % 8
                sq = self.sqb[:, si, 0:n]
                xg = self.sqb[:, si + 1, 0:n]
                ksq, kxg = f"sqb{si}", f"sqb{si + 1}"
                P.op("act", lambda e, sq=sq, pr=pr, n=n: e.activation(out=sq, in_=pr[:, 0:n], func=AF.Square), reads=(kpr,), writes=(ksq,))
                P.op("act", lambda e, xg=xg, pr=pr, n=n, g=g: e.activation(out=xg, in_=pr[:, 0:n], func=AF.Identity, scale=gain_fn(g)),
                     reads=(kpr, "pv"), writes=(kxg,))
                p1, k1 = self.ps()
                p2, k2 = self.ps()
                P.op("pe", lambda e, p1=p1, sq=sq, n=n: e.matmul(p1[:, 0:n], lhsT=self.BONES, rhs=sq, start=True, stop=True),
                     reads=(ksq, "cb"), writes=(k1,))
                P.op("pe", lambda e, p2=p2, xg=xg, n=n: e.matmul(p2[:, 0:n], lhsT=self.SWAP, rhs=xg, start=True, stop=True),
                     reads=(kxg, "cb"), writes=(k2,))
                rstd, krs = self.s32()
                P.op("act", lambda e, rstd=rstd, p1=p1, n=n: e.activation(out=rstd[:, 0:n], in_=p1[:, 0:n], func=AF.Sqrt, bias=EPS, scale=1.0 / 64),
                     reads=(k1,), writes=(krs,))
                P.op("dve", lambda e, rstd=rstd, n=n: e.reciprocal(out=rstd[:, 0:n], in_=rstd[:, 0:n]), reads=(krs,), writes=(krs,))
                t1, kt1 = self.t32()
                t2, kt2 = self.t32()
                P.op("pool", lambda e, t1=t1, xg=xg, c0=c0, n=n: e.tensor_tensor(out=t1[:, 0:n], in0=xg, in1=rc[:, c0:c0 + n], op=ALU.mult),
                     reads=(kxg, "rot"), writes=(kt1,))
                P.op("dve", lambda e, t2=t2, p2=p2, c0=c0, n=n: e.tensor_tensor(out=t2[:, 0:n], in0=p2[:, 0:n], in1=rs[:, c0:c0 + n], op=ALU.mult),
                     reads=(k2, "rot"), writes=(kt2,))
                P.op("dve", lambda e, t1=t1, t2=t2, n=n: e.tensor_tensor(out=t1[:, 0:n], in0=t1[:, 0:n], in1=t2[:, 0:n], op=ALU.add),
                     reads=(kt1, kt2), writes=(kt1,))
                P.op("dve", lambda e, t1=t1, rstd=rstd, n=n: e.tensor_tensor(out=t1[:, 0:n], in0=t1[:, 0:n], in1=rstd[:, 0:n], op=ALU.mult),
                     reads=(kt1, krs), writes=(kt1,))
                ks = kst[ksti]
                kk = f"kst{ksti}"
                ksti = (ksti + 1) % 2
                P.op("act", lambda e, ks=ks, t1=t1, n=n: e.activation(out=ks[:, 0:n], in_=t1[:, 0:n], func=AF.Copy), reads=(kt1,), writes=(kk,))
                dst = dst_fn(g, hp, c0, n)
                P.dma("sp", kk, lambda e, ks=ks, dst=dst, n=n: [e.dma_start(out=dst, in_=ks[:, 0:n])], reads=(kk,))
                if out32_fn is not None:
                    out32_fn(g, hp, c0, n, t1, kt1)


def _kv_stage(self, isY):
    P = self.P
    P.fence()
    tiles = TILES_Y if isY else TILES_X
    self.rmsnorm(tiles, "kv_norm")
    OM = self.ovl_G()
    loc0 = HALF if isY else 0
    rc, rs = self.load_rot(OM, loc0, HALF, isY)
    w_kv = self.din["w_kv"]
    kt = self.kt_scr
    vs = self.v_scr

    def w_src(g, hp):
        c = g * 2048 + hp * 128
        return w_kv[:, c:c + 128].rearrange("(k p) n -> p k n", p=128)

    def gain(g):
        return self.pvs(f"k_norm{g}", 0)

    def dst(g, hp, c0, n):
        if n == 512:
            return kt[g, hp, :, loc0 + c0:loc0 + c0 + n]
        return kt[g, hp, :, SEQ:SEQ + NS]

    def out32(g, hp, c0, n, t1, kt1):
        import os
        if not isY or os.environ.get("NO_OUT32"):
            return
        W = WINS[g]
        if n == 512:
            blocks = [b for b in range(4) if c0 + b * 128 >= HALF - W]
            if not blocks:
                return
            pt, pk = self.ps()

            def tr(e, pt=pt, t1=t1, blocks=tuple(blocks)):
                for b in blocks:
                    r = e.transpose(out=pt[:, b * 128:(b + 1) * 128], in_=t1[:, b * 128:(b + 1) * 128], identity=self.ident[:, :])
                return r
            P.op("pe", tr, reads=(kt1, "ident"), writes=(pk,))
            st, sk = self.stage()
            b0, nb = blocks[0], len(blocks)
            P.op("act", lambda e, st=st, pt=pt, b0=b0, nb=nb: e.activation(out=st[:, b0 * 128:(b0 + nb) * 128], in_=pt[:, b0 * 128:(b0 + nb) * 128], func=AF.Copy),
                 reads=(pk,), writes=(sk,))
            row0 = c0 + b0 * 128 - (HALF - W)
            dsto = self.dout[f"kvp{g}"][row0:row0 + nb * 128, hp * 128:(hp + 1) * 128].rearrange("(b t) f -> t b f", t=128)
            P.dma("sp", sk, lambda e, st=st, dsto=dsto, b0=b0, nb=nb: [e.dma_start(out=dsto, in_=st[:, b0 * 128:(b0 + nb) * 128].rearrange("t (b f) -> t b f", b=nb))],
                  reads=(sk,))
        else:
            pt, pk = self.ps()
            P.op("pe", lambda e, pt=pt, t1=t1: e.transpose(out=pt[0:NS, 0:128], in_=t1[:, 0:NS], identity=self.ident[:, :]),
                 reads=(kt1, "ident"), writes=(pk,))
            st, sk = self.stage()
            P.op("act", lambda e, st=st, pt=pt: e.activation(out=st[0:NS, 0:128], in_=pt[0:NS, 0:128], func=AF.Copy), reads=(pk,), writes=(sk,))
            dsto = self.dout[f"kvs{g}"][:, W - 8:W, hp * 128:(hp + 1) * 128]

            def f(e, st=st, dsto=dsto):
                return [e.dma_start(out=dsto[s], in_=st[s * 8:(s + 1) * 8, 0:128]) for s in range(4)]
            P.dma("sp", sk, f, reads=(sk,), n=4)
    self.head_proj(tiles, w_src, gain, dst, rc, rs, OM, out32)

    vst = [OM.alloc([128, 256], BF16, f"vst{i}") for i in range(2)]
    vi = 0
    blocks = [(tb * 128, 128) for tb in range(16)] + ([(HALF, NS)] if isY else [])
    for g in range(3):
        W = WINS[g]
        for q in range(4):
            wv, wk = self.wslot()
            wa = wv[:, 0:2048].rearrange("p (k n) -> p k n", k=NCH)
            c = g * 2048 + 1024 + q * 256
            src = w_kv[:, c:c + 256].rearrange("(k p) n -> p k n", p=128)
            P.dma("pool", wk, lambda e, wa=wa, src=src: [e.dma_start(out=wa, in_=src)], writes=(wk,))
            for (c0, m) in blocks:
                tk = c0 // 512
                pt, pk = self.ps()

                def mm(e, pt=pt, wa=wa, c0=c0, m=m):
                    for k in range(NCH):
                        r = e.matmul(pt[0:m, 0:256], lhsT=self.u[:, k, c0:c0 + m], rhs=wa[:, k, :], start=(k == 0), stop=(k == NCH - 1))
                    return r
                P.op("pe", mm, reads=(wk,) + self.uk(tk), writes=(pk,))
                vb = vst[vi]
                vk = f"vst{vi}"
                vi = (vi + 1) % 2
                P.op("act", lambda e, vb=vb, pt=pt, m=m: e.activation(out=vb[0:m, :], in_=pt[0:m, 0:256], func=AF.Copy), reads=(pk,), writes=(vk,))
                if m == 128:
                    l0 = loc0 + c0
                else:
                    l0 = SEQ
                dstv = vs[g, 2 * q:2 * q + 2, l0:l0 + m, :].rearrange("h t f -> t h f")
                P.dma("sp", vk, lambda e, vb=vb, dstv=dstv, m=m: [e.dma_start(out=dstv, in_=vb[0:m, :].rearrange("t (h f) -> t h f", h=2))],
                      reads=(vk,))
                import os
                if isY and not os.environ.get("NO_VOUT"):
                    need = (m == NS) or (c0 >= HALF - W)
                    if need:
                        st, sk = self.stage()
                        P.op("dve", lambda e, st=st, pt=pt, m=m: e.tensor_copy(out=st[0:m, 0:256], in_=pt[0:m, 0:256]), reads=(pk,), writes=(sk,))
                        if m == 128:
                            row0 = c0 - (HALF - W)
                            dsto = self.dout[f"kvp{g}"][row0:row0 + 128, 1024 + q * 256:1024 + (q + 1) * 256]
                            P.dma("sp", sk, lambda e, st=st, dsto=dsto: [e.dma_start(out=dsto, in_=st[:, 0:256])], reads=(sk,))
                        else:
                            dsto = self.dout[f"kvs{g}"][:, W - 8:W, 1024 + q * 256:1024 + (q + 1) * 256]

                            def f(e, st=st, dsto=dsto):
                                return [e.dma_start(out=dsto[s], in_=st[s * 8:(s + 1) * 8, 0:256]) for s in range(4)]
                            P.dma("sp", sk, f, reads=(sk,), n=4)
    P.fence()


Builder.ovl_G = _ovl_G
Builder.load_rot = _load_rot
Builder.head_proj = _head_proj
Builder.kv_stage = _kv_stage


SCALE = 0.125


def sl(start, n, step):
    return slice(start, start + step * (n - 1) + 1, step)

KBASE = (HALF - 128, HALF - 512, 0)
NBO = (16, 4, 1)


def _q_stage(self, j):
    P = self.P
    P.fence()
    self.rmsnorm(TILES_Y, f"b_norm{j}")
    OM = self.ovl_G()
    rc, rs = self.load_rot(OM, HALF, HALF, True)
    w_q = self.din["w_q"]

    def w_src(g, hp):
        c = g * 1024 + hp * 128
        return w_q[j, :, c:c + 128].rearrange("(k p) n -> p k n", p=128)

    def gain(g):
        return self.pvs(f"q_norm{j}_{g}", 0)

    def dst(g, hp, c0, n):
        return self.q_scr[g, hp, :, c0:c0 + n]
    self.head_proj(TILES_Y, w_src, gain, dst, rc, rs, OM, None)
    P.fence()


def _attn_layer(self, j, STOP=""):
    P = self.P
    self.q_stage(j)
    if STOP == "Q":
        return
    self.sample_stage(j)
    if STOP == "S":
        return
    uo, ub = self.M.offs["u"]
    go, gb = self.M.offs["glu"]
    assert uo + ub == go
    OM = Mem(self.nc, uo, ub + gb)
    QT = [OM.alloc([128, NT], BF16, f"QT{g}") for g in range(3)]
    kw = [SEQ - KBASE[g] + NS for g in range(3)]
    KT = [OM.alloc([128, kw[g]], BF16, f"KT{g}") for g in range(3)]
    nblk = [DILS[g] * (NBO[g] + 1) for g in range(3)]
    VT = [OM.alloc([128, nblk[g], 128], BF16, f"VT{g}") for g in range(3)]
    acc = OM.alloc([128, 2, NT], F32, "acc")
    PT = [OM.alloc([128, 512], BF16, f"PT{i}") for i in range(3)]
    qo, qb = OM.offs["QT0"]
    assert OM.offs["QT1"][0] == qo + qb and qb == NT * 2
    oT = self.nc.alloc_sbuf_tensor_at(f"oT_{j}", [128, 2, NT], BF16, offset=qo)
    self.att = dict(QT=QT, KT=KT, VT=VT, acc=acc, PT=PT, OM=OM)
    pti = 0
    w_o = self.din["w_o"]
    for hp in range(8):
        for g in range(3):
            r = DILS[g]
            nbo = NBO[g]
            kb0 = KBASE[g]
            P.dma("sp", f"QT{g}", lambda e, g=g, hp=hp: [e.dma_start(out=QT[g][:, :], in_=self.q_scr[g, hp, :, :])], writes=(f"QT{g}",))
            P.dma("sp", f"KT{g}", lambda e, g=g, hp=hp, kb0=kb0: [e.dma_start(out=KT[g][:, :], in_=self.kt_scr[g, hp, :, kb0:KTW])],
                  writes=(f"KT{g}",))
            vsrc = self.v_scr[g, hp, kb0:kb0 + r * 128 * (nbo + 1), :].rearrange("(m i c) f -> i c m f", i=128, c=r)
            vdst = VT[g][:, :, :].rearrange("p (c m) f -> p c m f", c=r)
            if r == 1:
                P.dma("sp", f"VT{g}", lambda e, vdst=vdst, vsrc=vsrc: [e.dma_start(out=vdst[:, 0], in_=vsrc[:, 0])], writes=(f"VT{g}",))
            else:
                def fv(e, vdst=vdst, vsrc=vsrc, r=r):
                    return [e.dma_start(out=vdst[:, c], in_=vsrc[:, c]) for c in range(r)]
                P.dma("sp", f"VT{g}", fv, writes=(f"VT{g}",), n=r)
            def front(c, nn, g=g, r=r, nbo=nbo, kb0=kb0):
                nonlocal pti
                q0 = c + r * 128 * nn
                kcur = (HALF - kb0) + q0
                kprev = kcur - r * 128
                bprev = c * (nbo + 1) + nn
                pSa, kSa = self.ps()
                pSb, kSb = self.ps()

                def st(e, pSa=pSa, pSb=pSb, g=g, r=r, q0=q0, kcur=kcur, kprev=kprev):
                    for h, pS in enumerate((pSa, pSb)):
                        for kb, k0 in enumerate((kprev, kcur)):
                            rr = e.matmul(pS[:, kb * 128:(kb + 1) * 128],
                                          lhsT=KT[g][h * 64:(h + 1) * 64, sl(k0, 128, r)],
                                          rhs=QT[g][h * 64:(h + 1) * 64, sl(q0, 128, r)], start=True, stop=True)
                    return rr
                P.op("pe", st, reads=(f"KT{g}", f"QT{g}"), writes=(kSa, kSb))
                E, kE = self.t32()
                E4 = E[:, :].rearrange("p (k h q) -> p k h q", k=2, h=2)
                P.op("act", lambda e, E4=E4, pSa=pSa: e.activation(out=E4[:, :, 0, :], in_=pSa[:, 0:256].rearrange("p (k q) -> p k q", k=2),
                                                                func=AF.Exp, scale=SCALE), reads=(kSa,), writes=(kE,))
                P.op("act", lambda e, E4=E4, pSb=pSb: e.activation(out=E4[:, :, 1, :], in_=pSb[:, 0:256].rearrange("p (k q) -> p k q", k=2),
                                                                func=AF.Exp, scale=SCALE), reads=(kSb, kE), writes=(kE,))
                pt = PT[pti]
                kpt = f"PT{pti}"
                pti = (pti + 1) % 3
                if nn == 0:
                    def mk(e, pt=pt, E=E):
                        e.scalar_tensor_tensor(out=pt[:, 0:256], in0=E[:, 0:256], scalar=self.flag[:, 0:1], in1=self.maskf[:, 0:256],
                                               op0=ALU.mult, op1=ALU.mult)
                        return e.tensor_tensor(out=pt[:, 256:512], in0=E[:, 256:512], in1=self.maskf[:, 256:512], op=ALU.mult)
                else:
                    def mk(e, pt=pt, E=E):
                        return e.tensor_tensor(out=pt[:, :], in0=E[:, :], in1=self.maskf[:, :], op=ALU.mult)
                P.op("dve", mk, reads=(kE, "maskf", "flag"), writes=(kpt,))
                return (pt, kpt, bprev, q0)

            def back(ctx, g=g, r=r):
                pt, kpt, bprev, q0 = ctx
                pU, kU = self.ps()

                def pv(e, pU=pU, pt=pt, g=g, bprev=bprev):
                    for h in range(2):
                        for kb in range(2):
                            rhs = pt[:, (kb * 2 + h) * 128:(kb * 2 + h + 1) * 128]
                            e.matmul(pU[0:64, h * 128:(h + 1) * 128], lhsT=VT[g][:, bprev + kb, h * 64:(h + 1) * 64], rhs=rhs,
                                     start=(kb == 0), stop=(kb == 1))
                            rr = e.matmul(pU[64:128, h * 128:(h + 1) * 128], lhsT=self.ONES[:, 0:64], rhs=rhs,
                                          start=(kb == 0), stop=(kb == 1))
                    return rr
                P.op("pe", pv, reads=(kpt, f"VT{g}", "cb"), writes=(kU,))
                av = acc[:, :, sl(q0, 128, r)]
                pv3 = pU[:, 0:256].rearrange("p (h q) -> p h q", h=2)
                if g == 0:
                    P.op("dve", lambda e, av=av, pv3=pv3: e.tensor_copy(out=av, in_=pv3), reads=(kU,), writes=("acc",))
                else:
                    P.op("dve", lambda e, av=av, pv3=pv3: e.tensor_tensor(out=av, in0=pv3, in1=av, op=ALU.add), reads=(kU, "acc"), writes=("acc",))
            pend = None
            for c in range(r):
                for nn in range(nbo):
                    ctx = front(c, nn)
                    if pend is not None:
                        back(pend)
                    pend = ctx
            back(pend)
        self.sample_attn(j, hp)
        P.op("dve", lambda e: e.reciprocal(out=acc[64:128, :, :], in_=acc[64:128, :, :]), reads=("acc",), writes=("acc",))
        wv, wk = self.wslot()
        wo = wv[0:64, 0:2048].rearrange("p (h n) -> p h n", h=2)
        src = w_o[j, hp * 128:(hp + 1) * 128, :].rearrange("(h p) n -> p h n", p=64)
        P.dma("pool", wk, lambda e, wo=wo, src=src: [e.dma_start(out=wo, in_=src)], writes=(wk,))
        for (c0, n) in TILES_Y:
            tk = c0 // 512
            for h in range(2):
                pR, kR = self.ps()
                P.op("pe", lambda e, pR=pR, h=h, c0=c0, n=n: e.matmul(pR[0:64, 0:n], lhsT=self.ident[:, 64:128], rhs=acc[:, h, c0:c0 + n], start=True, stop=True),
                     reads=("acc", "ident"), writes=(kR,))
                P.op("dve", lambda e, pR=pR, h=h, c0=c0, n=n: e.tensor_tensor(out=oT[0:64, h, c0:c0 + n], in0=acc[0:64, h, c0:c0 + n], in1=pR[0:64, 0:n], op=ALU.mult),
                     reads=(kR, "acc"), writes=("QT0", "QT1"))
            for dc in range(NCH):
                pd, kd = self.ps()

                def wm(e, pd=pd, dc=dc, c0=c0, n=n, wo=wo):
                    e.matmul(pd[:, 0:n], lhsT=wo[0:64, 0, dc * 128:(dc + 1) * 128], rhs=oT[0:64, 0, c0:c0 + n], start=True, stop=False)
                    return e.matmul(pd[:, 0:n], lhsT=wo[0:64, 1, dc * 128:(dc + 1) * 128], rhs=oT[0:64, 1, c0:c0 + n], start=False, stop=True)
                P.op("pe", wm, reads=(wk, "QT0", "QT1"), writes=(kd,))
                P.op("dve", lambda e, pd=pd, dc=dc, c0=c0, n=n: e.tensor_tensor(out=self.x[:, dc, c0:c0 + n], in0=pd[:, 0:n], in1=self.x[:, dc, c0:c0 + n], op=ALU.add),
                     reads=(kd, f"x:{tk}"), writes=(f"x:{tk}",))
    P.fence()


def _sample_attn_stub(self, j, hp):
    acc = self.att["acc"]
    self.P.op("dve", lambda e: e.memset(acc[:, :, HALF:NT], 1.0), writes=("acc",))


Builder.q_stage = _q_stage
Builder.attn_layer = _attn_layer
Builder.sample_attn = _sample_attn_stub


def _sample_stage(self, j):
    P = self.P
    P.fence()
    uo, ub = self.M.offs["u"]
    go, gb = self.M.offs["glu"]
    OM = Mem(self.nc, uo, ub + gb)
    QS = OM.alloc([128, 24, NS], BF16, "QS")
    KN = OM.alloc([128, 24, NS], BF16, "KN")
    VN = [OM.alloc([8, 32, 128], BF16, f"VN{g}") for g in range(3)]
    KC = [OM.alloc([128, 1024], F32, f"KC{i}") for i in range(2)]
    VC = [OM.alloc([128, 1024], BF16, f"VC{i}") for i in range(2)]
    KCT = [OM.alloc([128, 8, 128], BF16, f"KCT{i}") for i in range(2)]
    ES = [OM.alloc([128, 128], F32, f"ES{i}") for i in range(2)]
    PS_ = [OM.alloc([128, 128], BF16, f"PS{i}") for i in range(2)]
    sacc = self.sacc
    P.dma("sp", "QS", lambda e: [e.dma_start(out=QS[:, :, :], in_=self.q_scr[:, :, :, HALF:NT].rearrange("g h p n -> p (g h) n"))], writes=("QS",))
    P.dma("sp", "KN", lambda e: [e.dma_start(out=KN[:, :, :], in_=self.kt_scr[:, :, :, SEQ:KTW].rearrange("g h p n -> p (g h) n"))], writes=("KN",))
    for g in range(3):
        def fvn(e, g=g):
            return [e.dma_start(out=VN[g][:, hp * 4:(hp + 1) * 4, :], in_=self.v_scr[g, hp, SEQ:KTW, :].rearrange("(s t) f -> t s f", t=8))
                    for hp in range(8)]
        P.dma("sp", f"VN{g}", fvn, writes=(f"VN{g}",), n=8)
    P.op("dve", lambda e: e.memset(sacc[:, :, :, :], 0.0), writes=("sacc",))
    ckv = [self.din["ckv0"], self.din["ckv1"], self.din["ckv2"]]
    it = 0
    pend = None

    def run(front, back):
        nonlocal pend
        ctx = front()
        if pend is not None:
            pend[0](pend[1])
        pend = (back, ctx)
    for s in range(4):
        for g in range(3):
            r = DILS[g]
            ntile = (1, 4, 8)[g]
            for c in range(ntile):
                if g == 0:
                    qcols = list(range(8))
                elif g == 1:
                    qcols = [c, c + 4]
                else:
                    qcols = [c]
                nq = len(qcols)
                q0, qstep = s * 8 + qcols[0], (qcols[1] - qcols[0]) if nq > 1 else 1
                b = it % 2
                it += 1

                def front(s=s, g=g, c=c, r=r, nq=nq, q0=q0, qstep=qstep, b=b):
                    kc, vc, kct, es, pb = KC[b], VC[b], KCT[b], ES[b], PS_[b]
                    rows = ckv[g][s, sl(c, 128, r), :]
                    P.dma("sp", f"KC{b}", lambda e, kc=kc, rows=rows: [e.dma_start(out=kc[:, :], in_=rows[:, 0:1024])], writes=(f"KC{b}",))
                    P.dma("pool", f"VC{b}", lambda e, vc=vc, rows=rows: [e.dma_start(out=vc[:, :], in_=rows[:, 1024:2048])], writes=(f"VC{b}",))
                    for half in range(2):
                        pt, pk = self.ps()

                        def tr(e, pt=pt, kc=kc, half=half):
                            for cc in range(4):
                                hp = half * 4 + cc
                                rr = e.transpose(out=pt[:, cc * 128:(cc + 1) * 128], in_=kc[:, hp * 128:(hp + 1) * 128], identity=self.ident[:, :])
                            return rr
                        P.op("pe", tr, reads=(f"KC{b}", "ident"), writes=(pk,))
                        P.op("act", lambda e, pt=pt, kct=kct, half=half: e.activation(out=kct[:, half * 4:half * 4 + 4, :],
                                                                                   in_=pt[:, :].rearrange("p (a k) -> p a k", a=4), func=AF.Copy),
                             reads=(pk,), writes=(f"KCT{b}",))
                    pSa, kSa = self.ps()
                    pSb, kSb = self.ps()

                    def st(e, pSa=pSa, pSb=pSb, kct=kct):
                        for h, pS in enumerate((pSa, pSb)):
                            for hp in range(8):
                                rr = e.matmul(pS[:, hp * nq:(hp + 1) * nq], lhsT=kct[h * 64:(h + 1) * 64, hp, :],
                                              rhs=QS[h * 64:(h + 1) * 64, g * 8 + hp, sl(q0, nq, qstep)], start=True, stop=True)
                        return rr
                    P.op("pe", st, reads=(f"KCT{b}", "QS"), writes=(kSa, kSb))
                    es4 = es[:, 0:16 * nq].rearrange("p (a h q) -> p a h q", a=8, h=2)
                    P.op("act", lambda e, es4=es4, pSa=pSa: e.activation(out=es4[:, :, 0, :], in_=pSa[:, 0:8 * nq].rearrange("p (a q) -> p a q", a=8),
                                                                      func=AF.Exp, scale=SCALE), reads=(kSa,), writes=(f"ES{b}",))
                    P.op("act", lambda e, es4=es4, pSb=pSb: e.activation(out=es4[:, :, 1, :], in_=pSb[:, 0:8 * nq].rearrange("p (a q) -> p a q", a=8),
                                                                      func=AF.Exp, scale=SCALE), reads=(kSb, f"ES{b}"), writes=(f"ES{b}",))
                    if g == 0:
                        msk = self.smask[:, 0:128]
                    elif g == 1:
                        msk = self.smask[:, 128:160]
                    else:
                        msk = None
                    if msk is not None:
                        P.op("dve", lambda e, pb=pb, es=es, msk=msk: e.tensor_tensor(out=pb[:, 0:16 * nq], in0=es[:, 0:16 * nq], in1=msk, op=ALU.mult),
                             reads=(f"ES{b}", "smask"), writes=(f"PS{b}",))
                    else:
                        P.op("dve", lambda e, pb=pb, es=es: e.tensor_copy(out=pb[:, 0:16 * nq], in_=es[:, 0:16 * nq]),
                             reads=(f"ES{b}",), writes=(f"PS{b}",))
                    return None

                def back(ctx, s=s, g=g, nq=nq, q0=q0, qstep=qstep, b=b):
                    vc, pb = VC[b], PS_[b]
                    pU, kU = self.ps()

                    def pv(e, pU=pU, pb=pb, vc=vc):
                        for hp in range(8):
                            for h in range(2):
                                col = (hp * 2 + h) * nq
                                e.matmul(pU[0:64, col:col + nq], lhsT=vc[:, hp * 128 + h * 64:hp * 128 + (h + 1) * 64], rhs=pb[:, col:col + nq], start=True, stop=True)
                                rr = e.matmul(pU[64:128, col:col + nq], lhsT=self.ONES[:, 0:64], rhs=pb[:, col:col + nq], start=True, stop=True)
                        return rr
                    P.op("pe", pv, reads=(f"PS{b}", f"VC{b}", "cb"), writes=(kU,))
                    sv = sacc[:, :, :, sl(q0, nq, qstep)].rearrange("p a h q -> p (a h) q")
                    P.op("dve", lambda e, sv=sv, pU=pU: e.tensor_tensor(out=sv, in0=pU[:, 0:16 * nq].rearrange("p (a q) -> p a q", a=16), in1=sv, op=ALU.add),
                         reads=(kU, "sacc"), writes=("sacc",))
                run(front, back)
            b = it % 2
            it += 1

            def frontn(s=s, g=g, b=b):
                es, pb = ES[b], PS_[b]
                pSa, kSa = self.ps()
                pSb, kSb = self.ps()

                def stn(e, pSa=pSa, pSb=pSb):
                    for h, pS in enumerate((pSa, pSb)):
                        for hp in range(8):
                            rr = e.matmul(pS[0:8, hp * 8:(hp + 1) * 8], lhsT=KN[h * 64:(h + 1) * 64, g * 8 + hp, s * 8:s * 8 + 8],
                                          rhs=QS[h * 64:(h + 1) * 64, g * 8 + hp, s * 8:s * 8 + 8], start=True, stop=True)
                    return rr
                P.op("pe", stn, reads=("KN", "QS"), writes=(kSa, kSb))
                esn = es[0:8, :].rearrange("p (a h q) -> p a h q", a=8, h=2)
                P.op("act", lambda e, esn=esn, pSa=pSa: e.activation(out=esn[:, :, 0, :], in_=pSa[0:8, 0:64].rearrange("p (a q) -> p a q", a=8),
                                                                  func=AF.Exp, scale=SCALE), reads=(kSa,), writes=(f"ES{b}",))
                P.op("act", lambda e, esn=esn, pSb=pSb: e.activation(out=esn[:, :, 1, :], in_=pSb[0:8, 0:64].rearrange("p (a q) -> p a q", a=8),
                                                                  func=AF.Exp, scale=SCALE), reads=(kSb, f"ES{b}"), writes=(f"ES{b}",))
                mo = 160 + g * 128
                P.op("dve", lambda e, pb=pb, es=es, mo=mo: e.tensor_tensor(out=pb[0:8, :], in0=es[0:8, :], in1=self.smask[0:8, mo:mo + 128], op=ALU.mult),
                     reads=(f"ES{b}", "smask"), writes=(f"PS{b}",))
                return None

            def backn(ctx, s=s, g=g, b=b):
                pb = PS_[b]
                pU, kU = self.ps()

                def pvn(e, pU=pU, pb=pb):
                    for hp in range(8):
                        for h in range(2):
                            col = (hp * 2 + h) * 8
                            e.matmul(pU[0:64, col:col + 8], lhsT=VN[g][0:8, hp * 4 + s, h * 64:(h + 1) * 64], rhs=pb[0:8, col:col + 8], start=True, stop=True)
                            rr = e.matmul(pU[64:128, col:col + 8], lhsT=self.ONES[0:8, 0:64], rhs=pb[0:8, col:col + 8], start=True, stop=True)
                    return rr
                P.op("pe", pvn, reads=(f"PS{b}", f"VN{g}", "cb"), writes=(kU,))
                sv = sacc[:, :, :, s * 8:s * 8 + 8].rearrange("p a h q -> p (a h) q")
                P.op("dve", lambda e, sv=sv, pU=pU: e.tensor_tensor(out=sv, in0=pU[:, 0:128].rearrange("p (a q) -> p a q", a=16), in1=sv, op=ALU.add),
                     reads=(kU, "sacc"), writes=("sacc",))
            run(frontn, backn)
    if pend is not None:
        pend[0](pend[1])
    P.fence()


def _sample_attn(self, j, hp):
    acc = self.att["acc"]
    self.P.op("dve", lambda e, hp=hp: e.tensor_copy(out=acc[:, :, HALF:NT], in_=self.sacc[:, hp, :, :]), reads=("sacc",), writes=("acc",))


Builder.sample_stage = _sample_stage
Builder.sample_attn = _sample_attn


def _conv_outputs(self, L):
    P = self.P
    st, sk = self.stage()
    for half in range(2):
        pt, pk = self.ps()

        def f(e, half=half, pt=pt):
            for cc in range(4):
                c = half * 4 + cc
                r = e.transpose(out=pt[0:30, cc * 128:(cc + 1) * 128], in_=self.g32[:, L, c, :], identity=self.ident[:, :])
            return r
        P.op("pe", f, reads=("g32", "ident"), writes=(pk,))
        P.op("act", lambda e, half=half, pt=pt, st=st: e.activation(out=st[0:30, half * 512:(half + 1) * 512], in_=pt[0:30, :], func=AF.Copy),
             reads=(pk,), writes=(sk,))
    P.dma("sp", sk, lambda e, st=st: [e.dma_start(out=self.dout["convp"][L, :, :], in_=st[0:30, :])], reads=(sk,))
    P.dma("sp", "d2d", lambda e: [e.dma_start(out=self.dout["convs"][L, :, 0:22, :], in_=self.din["cconv"][L, :, 8:30, :])])
    st, sk = self.stage()
    for half in range(2):
        pt, pk = self.ps()

        def f2(e, half=half, pt=pt):
            for cc in range(4):
                c = half * 4 + cc
                r = e.transpose(out=pt[0:NS, cc * 128:(cc + 1) * 128], in_=self.gs32[:, L, c, :], identity=self.ident[:, :])
            return r
        P.op("pe", f2, reads=("gs32", "ident"), writes=(pk,))
        P.op("act", lambda e, half=half, pt=pt, st=st: e.activation(out=st[0:NS, half * 512:(half + 1) * 512], in_=pt[0:NS, :], func=AF.Copy),
             reads=(pk,), writes=(sk,))

    def fs(e, st=st):
        return [e.dma_start(out=self.dout["convs"][L, s, 22:30, :], in_=st[s * 8:(s + 1) * 8, :]) for s in range(4)]
    P.dma("sp", sk, fs, reads=(sk,), n=4)


def _build_full(self):
    import os
    STOP = os.environ.get("KSTOP", "")
    P = self.P
    self.setup()
    xin = self.din["xin"]
    ck = [self.din["ckv0"], self.din["ckv1"], self.din["ckv2"]]
    for g in range(3):
        W = WINS[g]
        if os.environ.get("NO_D2D"):
            break
        for s in range(4):
            for r0 in range(0, W - 8, 512):
                nr = min(512, W - 8 - r0)
                P.dma("act", "d2d", lambda e, g=g, s=s, r0=r0, nr=nr: [e.dma_start(out=self.dout[f"kvs{g}"][s, r0:r0 + nr, :],
                                                                                  in_=ck[g][s, 8 + r0:8 + r0 + nr, :])])
    for sup in range(2):
        isY = sup == 1 and STOP != "XX"
        tiles = TILES_Y if isY else TILES_X
        for tb in range(16):
            self.load_tokens(xin[sup * HALF + tb * 128: sup * HALF + (tb + 1) * 128, :], 128, tb * 128)
        if isY:
            self.load_tokens(self.din["xs"][:, :], NS, HALF)
        for L in range(2):
            self.conv_layer(L, isY)
            if isY and not os.environ.get("NO_CONVOUT"):
                self.conv_outputs(L)
            self.ffn(L, tiles)
        self.kv_stage(isY)
        if STOP == "X":
            break
    for j in range(2):
        if STOP in ("X", "KV", "XX"):
            break
        self.attn_layer(j, STOP)
        if STOP in ("Q", "S", "A"):
            break
        self.ffn(2 + j, TILES_Y)
    for tb in range(16):
        self.store_tokens(self.dout["y"][tb * 128:(tb + 1) * 128, :], 128, tb * 128)
    self.store_tokens(self.dout["y"][HALF:HALF + NS, :], NS, HALF)


Builder.conv_outputs = _conv_outputs
Builder.build_full = _build_full

_CACHE = {}


def kernel(**inputs):
    if "nc" not in _CACHE:
        B = Builder()
        B.build_full()
        _CACHE["nc"] = B.finish()
        _CACHE["names"] = set(B.din.keys())
    nc = _CACHE["nc"]
    maps = make_in_maps(inputs)
    maps = [{k: v for k, v in m.items() if k in _CACHE["names"]} for m in maps]
    res = run_bass_kernel_spmd(nc, maps, core_ids=list(range(8)))
    R = res.results
    f32 = np.float32
    y_prompt = np.zeros((4, SEQ, D), f32)
    y_sample = np.zeros((32, 8, D), f32)
    conv_p = np.zeros((2, 4, 30, D), f32)
    conv_s = np.zeros((2, 32, 30, D), f32)
    kvp = [np.zeros((4, W, 2, 16, 64), f32) for W in WINS]
    kvs = [np.zeros((32, W, 2, 16, 64), f32) for W in WINS]
    for c in range(8):
        b, h = c // 2, c % 2
        r = R[c]
        y_prompt[b, h * HALF:(h + 1) * HALF] = r["y"][:HALF]
        y_sample[4 * c:4 * c + 4] = r["y"][HALF:].reshape(4, 8, D)
        conv_s[:, 4 * c:4 * c + 4] = r["convs"]
        for g in range(3):
            kvs[g][4 * c:4 * c + 4] = r[f"kvs{g}"].reshape(4, WINS[g], 2, 16, 64)
        if h == 1:
            conv_p[:, b] = r["convp"]
            for g in range(3):
                kvp[g][b] = r[f"kvp{g}"].reshape(WINS[g], 2, 16, 64)
    return (y_prompt, y_sample, conv_p, conv_s, kvp[0], kvp[1], kvp[2], kvs[0], kvs[1], kvs[2])
```

```python
import numpy as np
import ml_dtypes
import concourse.bass as bass
import concourse.mybir as mybir
from concourse.bass_utils import run_bass_kernel_spmd

F32 = mybir.dt.float32
BF16 = mybir.dt.bfloat16
AF = mybir.ActivationFunctionType
ALU = mybir.AluOpType

D = 1024
NCH = 8
SEQ = 4096
HALF = 2048
NS = 32
NT = HALF + NS
DFF = 2816
NFC = 22
CW = 31
EPS = 1e-6
PAST = 8192
WINS = (128, 512, 2048)
DILS = (1, 4, 16)
KTW = SEQ + NS

ENGS = ["pe", "act", "dve", "pool", "sp"]
SQK = tuple(f"sqb{i}" for i in range(8))


class Prog:
    def __init__(self):
        self.ops = {e: [] for e in ENGS}
        self.last_w = {}
        self.readers = {}
        self.chan_n = {}

    def _deps(self, reads, writes):
        deps = []
        for k in reads:
            t = self.last_w.get(k)
            if t is not None:
                deps.append(t)
        for k in writes:
            t = self.last_w.get(k)
            if t is not None:
                deps.append(t)
            deps.extend(self.readers.get(k, ()))
        return deps

    def _commit(self, tok, reads, writes):
        for k in reads:
            lst = self.readers.setdefault(k, [])
            src = tok[:2]
            lst[:] = [t for t in lst if t[:2] != src]
            lst.append(tok)
        for k in writes:
            self.last_w[k] = tok
            self.readers[k] = []

    def op(self, eng, fn, reads=(), writes=()):
        reads = tuple(reads)
        writes = tuple(writes) + tuple(k for k in reads if k.startswith("ps") and k[2:].isdigit())
        idx = len(self.ops[eng])
        tok = ("e", eng, idx)
        deps = [t for t in self._deps(reads, writes) if not (t[0] == "e" and t[1] == eng and (eng == "pe" or t[2] == idx))]
        deps += self._take_fence(eng, idx)
        for t in deps:
            if t[0] == "e":
                self.ops[t[1]][t[2]]["signal"] = True
        self.ops[eng].append({"fn": fn, "deps": deps, "signal": False, "dma": None})
        self._commit(tok, reads, writes)
        return tok

    def dma(self, queue, chan, fn, reads=(), writes=(), n=1):
        cnt = self.chan_n.get(chan, 0) + n
        self.chan_n[chan] = cnt
        tok = ("d", chan, cnt)
        deps = list(self._deps(reads, writes))
        deps += self._take_fence(queue, len(self.ops[queue]))
        for t in deps:
            if t[0] == "e":
                self.ops[t[1]][t[2]]["signal"] = True
        self.ops[queue].append({"fn": fn, "deps": deps, "signal": False, "dma": chan})
        self._commit(tok, reads, writes)
        return tok

    def fence(self):
        deps = [("e", E, len(self.ops[E]) - 1) for E in ENGS if self.ops[E] and self.ops[E][-1]["dma"] is None]
        for E in ENGS:
            if self.ops[E] and self.ops[E][-1]["dma"] is not None:
                for i in range(len(self.ops[E]) - 1, -1, -1):
                    if self.ops[E][i]["dma"] is None:
                        deps.append(("e", E, i))
                        break
        deps += [("d", ch, n) for ch, n in self.chan_n.items()]
        self.pending = {E: list(deps) for E in ENGS}

    def _take_fence(self, eng, idx):
        pend = getattr(self, "pending", None)
        if not pend or not pend.get(eng):
            return []
        d = pend[eng]
        pend[eng] = []
        return [t for t in d if not (t[0] == "e" and t[1] == eng and eng == "pe")]

    def emit(self, nc, block_engines, sems, chan_sems):
        sigcount = {}
        for e in ENGS:
            c = 0
            arr = []
            for o in self.ops[e]:
                if o["signal"] and o["dma"] is None:
                    c += 1
                arr.append(c)
            sigcount[e] = arr
        prog = self

        def make(e):
            def body(eng):
                seen = {}
                for o in prog.ops[e]:
                    for t in o["deps"]:
                        if t[0] == "e":
                            src = ("e", t[1])
                            cnt = sigcount[t[1]][t[2]]
                            sem = sems[t[1]]
                        else:
                            src = ("d", t[1])
                            cnt = 16 * t[2]
                            sem = chan_sems[t[1]]
                        if seen.get(src, 0) >= cnt:
                            continue
                        seen[src] = cnt
                        eng.wait_ge(sem, cnt)
                    r = o["fn"](eng)
                    if o["dma"] is not None:
                        for ins in r:
                            ins.then_inc(chan_sems[o["dma"]], 16)
                    elif o["signal"]:
                        r.then_inc(sems[e], 1)
                if e == "sp":
                    for ch, n in prog.chan_n.items():
                        if seen.get(("d", ch), 0) < 16 * n:
                            eng.wait_ge(chan_sems[ch], 16 * n)
            return body

        for e in ENGS:
            block_engines[e](make(e))


class Mem:
    def __init__(self, nc, base, size):
        self.nc = nc
        self.base = base
        self.size = size
        self.cur = 0
        self.n = 0
        Mem._gn = getattr(Mem, "_gn", 0) + 1000
        self.n = Mem._gn

    def alloc(self, shape, dtype, name=None):
        nbytes = int(np.prod(shape[1:])) * (4 if dtype == F32 else 2)
        nbytes = (nbytes + 63) // 64 * 64
        off = self.cur
        self.cur += nbytes
        assert self.cur <= self.size, f"SBUF overflow {self.cur} > {self.size} at {name}"
        self.n += 1
        self.offs = getattr(self, "offs", {})
        self.offs[name] = (self.base + off, nbytes)
        return self._reg(name, self.nc.alloc_sbuf_tensor_at(f"{name or 't'}_{self.n}", list(shape), dtype, offset=self.base + off))

    def _reg(self, name, h):
        self.names = getattr(self, "names", {})
        self.names[name] = h.name
        return h

    def mark(self):
        return self.cur

    def reset(self, m):
        self.cur = m


TILES_X = [(i * 512, 512) for i in range(4)]
TILES_Y = TILES_X + [(HALF, NS)]
GW = 30 + HALF + 4 * 38
GS0 = 30 + HALF


class Builder:
    def __init__(self, stop_after=None):
        self.stop_after = stop_after
        nc = self.nc = bass.Bass("TRN2", target_bir_lowering=False)
        self.P = Prog()
        self.din = {}
        self.dout = {}
        self._uid = 0

    def inp(self, name, shape, dtype=F32):
        self.din[name] = self.nc.dram_tensor(name, list(shape), dtype, kind="ExternalInput").ap()
        return self.din[name]

    def outp(self, name, shape, dtype=F32):
        self.dout[name] = self.nc.dram_tensor(name, list(shape), dtype, kind="ExternalOutput").ap()
        return self.dout[name]

    @staticmethod
    def uk(tk):
        return tuple(f"u:{tk}:{c}" for c in range(NCH))

    def uid(self):
        self._uid += 1
        return self._uid

    def ps(self):
        i = self._psi
        self._psi = (i + 1) % 8
        return self.psb[i], f"ps{i}"

    def wslot(self):
        i = self._wi
        self._wi = (i + 1) % len(self.wbufs)
        return self.wbufs[i], f"w{i}"

    def t32(self):
        i = self._t32i
        self._t32i = (i + 1) % len(self.T32)
        return self.T32[i], f"t32_{i}"

    def s32(self):
        i = self._s32i
        self._s32i = (i + 1) % len(self.S32)
        return self.S32[i], f"s32_{i}"

    def stage(self):
        i = self._stgi
        self._stgi = (i + 1) % 2
        return self.stg[i], f"stg{i}"

    def load_w(self, src_ap, shape_view):
        buf, key = self.wslot()
        a, b = shape_view
        dst = buf[:, 0:a * b].rearrange("p (a b) -> p a b", a=a)
        self.P.dma("pool", key, lambda e, dst=dst, src=src_ap: [e.dma_start(out=dst, in_=src)],
                   reads=(), writes=(key,))
        return dst, key

    PV_ROWS = {}

    @staticmethod
    def pv_layout():
        rows = {}
        n = 0

        def add(name, cnt=1):
            nonlocal n
            rows[name] = n
            n += cnt
        for L in range(2):
            add(f"a_norm{L}")
            add(f"b1_{L}")
            add(f"b2_{L}")
            add(f"wdw{L}", CW)
            add(f"b_dw{L}")
            add(f"ln_g{L}")
            add(f"ln_b{L}")
            add(f"b_out{L}")
        add("kv_norm")
        for j in range(2):
            add(f"b_norm{j}")
        for L in range(4):
            add(f"ffn_norm{L}")
        for g in range(3):
            add(f"k_norm{g}")
        for j in range(2):
            for g in range(3):
                add(f"q_norm{j}_{g}")
        return rows, n

    def pvs(self, name, c, off=0):
        i = self.pvrows[name] + off
        return self.pv[:, c, i:i + 1]

    def setup(self):
        nc = self.nc
        P = self.P
        self.pvrows, self.npv = self.pv_layout()
        self.inp("xin", [SEQ, D])
        self.inp("xs", [NS, D])
        self.inp("cconv", [2, 4, 30, D])
        self.inp("ckv0", [4, 128, 2048])
        self.inp("ckv1", [4, 512, 2048])
        self.inp("ckv2", [4, 2048, 2048])
        self.inp("pvec", [self.npv, D])
        self.inp("a_w_in", [2, D, 2 * D])
        self.inp("a_w_out", [2, D, D])
        self.inp("w_kv", [D, 6 * D])
        self.inp("w_q", [2, D, 3 * D])
        self.inp("w_o", [2, D, D])
        self.inp("w_gate_up", [4, D, 2 * DFF])
        self.inp("w_down", [4, DFF, D])
        self.inp("cmat", [6, 128, 128])
        self.inp("rot", [2, 128, KTW])
        self.inp("flag", [128, 1])
        self.inp("smaskin", [128, 544])
        self.outp("y", [NT, D])
        self.outp("convp", [2, 30, D])
        self.outp("convs", [2, 4, 30, D])
        self.outp("kvp0", [128, 2048])
        self.outp("kvp1", [512, 2048])
        self.outp("kvp2", [2048, 2048])
        self.outp("kvs0", [4, 128, 2048])
        self.outp("kvs1", [4, 512, 2048])
        self.outp("kvs2", [4, 2048, 2048])
        self.kt_scr = nc.dram_tensor("kt_scr", [3, 8, 128, KTW], BF16, kind="Internal").ap()
        self.v_scr = nc.dram_tensor("v_scr", [3, 8, KTW, 128], BF16, kind="Internal").ap()
        self.q_scr = nc.dram_tensor("q_scr", [3, 8, 128, NT], BF16, kind="Internal").ap()

        total = nc.sbuf_bytes_remaining - 64
        arena = nc.alloc_sbuf_tensor("arena", [128, total // 4], F32)
        base = nc.lookup_mloc(arena).addr
        self.M = M = Mem(nc, base, (total // 4) * 4)
        self.x = M.alloc([128, NCH, NT], F32, "x")
        self.u = M.alloc([128, NCH, NT], BF16, "u")
        self.G = M.alloc([128, NCH, GW], BF16, "glu")
        self.pv = M.alloc([128, NCH, self.npv], F32, "pv")
        self.ident = M.alloc([128, 128], F32, "ident")
        self.cb = M.alloc([128, 6, 128], BF16, "cb")
        self.maskf = M.alloc([128, 512], F32, "maskf")
        self.flag = M.alloc([128, 1], F32, "flag")
        self.sacc = M.alloc([128, 8, 2, NS], F32, "sacc")
        self.smask = M.alloc([128, 544], BF16, "smask")
        self.gstate = M.alloc([128, 2, NCH, 30], BF16, "gstate")
        self.g32 = M.alloc([128, 2, NCH, 30], F32, "g32")
        self.gs32 = M.alloc([128, 2, NCH, NS], F32, "gs32")
        self.wall = M.alloc([128, 6 * 2048], BF16, "wall")
        self.wbufs = [self.wall[:, i * 2048:(i + 1) * 2048] for i in range(6)]
        self.sqb = M.alloc([128, NCH, 512], BF16, "sqb")
        self.T32 = [M.alloc([128, 512], F32, f"t32_{i}") for i in range(4)]
        self._t32i = 0
        self.S32 = [M.alloc([128, 512], F32, f"s32_{i}") for i in range(4)]
        self._s32i = 0
        self.stg = [M.alloc([128, D], F32, f"stg{i}") for i in range(2)]
        self._stgi = 0
        self.hbuf = [M.alloc([128, 512], BF16, f"hbuf{i}") for i in range(4)]
        self._hi = 0
        self._di = 0
        self._sqi = 0
        self.persist_mark = M.mark()
        print("SBUF used", M.cur, "of", M.size)

        self.psb = [nc.alloc_psum_tensor(f"psb{i}", [128, 512], F32) for i in range(8)]
        self._psi = 0
        self._wi = 0

        cst = self.stg[1][:, 0:768].rearrange("p (a b) -> p a b", a=6)
        P.dma("sp", "stg1", lambda e: [e.dma_start(out=cst, in_=self.din["cmat"].rearrange("a p n -> p a n"))],
              writes=("stg1",))
        P.dma("sp", "c1", lambda e: [e.dma_start(out=self.flag[:], in_=self.din["flag"][:, :])], writes=("flag",))
        P.dma("pool", "c3", lambda e: [e.dma_start(out=self.smask[:], in_=self.din["smaskin"][:, :])], writes=("smask",))
        P.op("dve", lambda e: e.tensor_copy(out=self.cb[:], in_=cst), reads=("stg1",), writes=("cb",))
        P.op("dve", lambda e: e.tensor_copy(out=self.ident[:], in_=cst[:, 0, :]), reads=("stg1",), writes=("ident",))
        for q, src in enumerate([4, 4, 5, 5]):
            P.op("dve", lambda e, q=q, src=src: e.tensor_copy(out=self.maskf[:, q * 128:(q + 1) * 128], in_=cst[:, src, :]),
                 reads=("stg1",), writes=("maskf",))
        self.I_BF = self.cb[:, 0, :]
        self.ONES = self.cb[:, 1, :]
        self.BONES = self.cb[:, 2, :]
        self.SWAP = self.cb[:, 3, :]
        pst = self.stg[0]
        npv = self.npv
        P.dma("sp", "stg0", lambda e: [e.dma_start(out=pst[0:npv, :], in_=self.din["pvec"][:, :])], writes=("stg0",))
        for half in range(2):
            pt, pk = self.ps()

            def f(e, half=half, pt=pt):
                for cc in range(4):
                    c = half * 4 + cc
                    r = e.transpose(out=pt[:, cc * 128:cc * 128 + npv], in_=pst[0:npv, c * 128:(c + 1) * 128],
                                    identity=self.ident[0:npv, 0:npv])
                return r
            P.op("pe", f, reads=("stg0", "ident"), writes=(pk,))
            P.op("act", lambda e, half=half, pt=pt: e.activation(
                out=self.pv[:, half * 4:half * 4 + 4, :],
                in_=pt[:, :].rearrange("p (a b) -> p a b", a=4)[:, :, 0:npv], func=AF.Copy),
                reads=(pk,), writes=("pv",))
        P.op("dve", lambda e: e.memset(self.G[:, :, 0:30], 0.0), writes=("G:state",))

    def load_tokens(self, src_ap, nrows, col0):
        P = self.P
        st, sk = self.stage()
        P.dma("sp", sk, lambda e: [e.dma_start(out=st[0:nrows, :], in_=src_ap)], writes=(sk,))
        xkey = f"x:{col0 // 512}"
        for half in range(2):
            pt, pk = self.ps()

            def f(e, half=half, pt=pt):
                for cc in range(4):
                    c = half * 4 + cc
                    r = e.transpose(out=pt[:, cc * 128:cc * 128 + nrows], in_=st[0:nrows, c * 128:(c + 1) * 128],
                                    identity=self.ident[0:nrows, 0:nrows])
                return r
            P.op("pe", f, reads=(sk, "ident"), writes=(pk,))
            P.op("act", lambda e, half=half, pt=pt: e.activation(
                out=self.x[:, half * 4:half * 4 + 4, col0:col0 + nrows],
                in_=pt[:, :].rearrange("p (a b) -> p a b", a=4)[:, :, 0:nrows], func=AF.Copy),
                reads=(pk,), writes=(xkey,))

    def store_tokens(self, dst_ap, nrows, col0, src=None, skey=None):
        P = self.P
        src = self.x if src is None else src
        skey = f"x:{col0 // 512}" if skey is None else skey
        st, sk = self.stage()
        for half in range(2):
            pt, pk = self.ps()

            def f(e, half=half, pt=pt):
                for cc in range(4):
                    c = half * 4 + cc
                    r = e.transpose(out=pt[0:nrows, cc * 128:(cc + 1) * 128], in_=src[:, c, col0:col0 + nrows],
                                    identity=self.ident[:, :])
                return r
            P.op("pe", f, reads=(skey, "ident"), writes=(pk,))
            P.op("act", lambda e, half=half, pt=pt: e.activation(
                out=st[0:nrows, half * 512:(half + 1) * 512], in_=pt[0:nrows, :], func=AF.Copy),
                reads=(pk,), writes=(sk,))
        P.dma("sp", sk, lambda e: [e.dma_start(out=dst_ap, in_=st[0:nrows, :])], reads=(sk,))

    def rmsnorm(self, tiles, gname):
        P = self.P
        rs = {}
        for (c0, n) in tiles:
            tk = c0 // 512
            P.op("act", lambda e, c0=c0, n=n: e.activation(out=self.sqb[:, :, 0:n], in_=self.x[:, :, c0:c0 + n], func=AF.Square),
                 reads=(f"x:{tk}",), writes=SQK)
            pt, pk = self.ps()

            def f(e, pt=pt, n=n):
                for c in range(NCH):
                    r = e.matmul(pt[:, 0:n], lhsT=self.ONES, rhs=self.sqb[:, c, 0:n], start=(c == 0), stop=(c == NCH - 1))
                return r
            P.op("pe", f, reads=SQK + ("cb",), writes=(pk,))
            t, tkey = self.s32()
            P.op("act", lambda e, pt=pt, t=t, n=n: e.activation(out=t[:, 0:n], in_=pt[:, 0:n], func=AF.Ln, bias=EPS, scale=1.0 / D),
                 reads=(pk,), writes=(tkey,))
            P.op("act", lambda e, t=t, n=n: e.activation(out=t[:, 0:n], in_=t[:, 0:n], func=AF.Exp, scale=-0.5), reads=(tkey,), writes=(tkey,))

            def g(e, t=t, c0=c0, n=n):
                for c in range(NCH):
                    r = e.scalar_tensor_tensor(out=self.u[:, c, c0:c0 + n], in0=self.x[:, c, c0:c0 + n],
                                               scalar=self.pvs(gname, c), in1=t[:, 0:n], op0=ALU.mult, op1=ALU.mult)
                return r
            P.op("dve", g, reads=(tkey, f"x:{tk}", "pv"), writes=self.uk(tk))

    def conv_layer(self, L, isY, tiles=None):
        P = self.P
        if tiles is None:
            tiles = TILES_Y if isY else TILES_X
        w_in = self.din["a_w_in"]
        w_out = self.din["a_w_out"]
        G = self.G
        self.rmsnorm(tiles, f"a_norm{L}")
        if isY:
            P.op("dve", lambda e: e.tensor_copy(out=G[:, :, 0:30], in_=self.gstate[:, L, :, :]),
                 reads=("gstate",), writes=("G:state",))
            for s in range(4):
                st, sk = self.stage()
                P.dma("sp", sk, lambda e, st=st, s=s: [e.dma_start(out=st[0:30, :], in_=self.din["cconv"][L, s, :, :])], writes=(sk,))
                for half in range(2):
                    pt, pk = self.ps()

                    def f(e, half=half, pt=pt, st=st):
                        for cc in range(4):
                            c = half * 4 + cc
                            r = e.transpose(out=pt[:, cc * 128:cc * 128 + 30], in_=st[0:30, c * 128:(c + 1) * 128],
                                            identity=self.ident[0:30, 0:30])
                        return r
                    P.op("pe", f, reads=(sk, "ident"), writes=(pk,))
                    P.op("act", lambda e, half=half, pt=pt, s=s: e.activation(
                        out=G[:, half * 4:half * 4 + 4, GS0 + s * 38:GS0 + s * 38 + 30],
                        in_=pt[:, :].rearrange("p (a b) -> p a b", a=4)[:, :, 0:30], func=AF.Copy),
                        reads=(pk,), writes=("G:s",))
        for oc in range(NCH):
            wv, wk = self.wslot()
            wa = wv[:, 0:2048].rearrange("p (h k n) -> p h k n", h=2, k=NCH)
            src1 = w_in[L, :, oc * 128:(oc + 1) * 128].rearrange("(k p) n -> p k n", p=128)
            src2 = w_in[L, :, D + oc * 128:D + (oc + 1) * 128].rearrange("(k p) n -> p k n", p=128)
            P.dma("pool", wk, lambda e, wa=wa, src1=src1, src2=src2: [e.dma_start(out=wa[:, 0], in_=src1), e.dma_start(out=wa[:, 1], in_=src2)],
                  writes=(wk,), n=2)
            for (c0, n) in tiles:
                tk = c0 // 512
                p1, k1 = self.ps()
                p2, k2 = self.ps()

                def mm(e, p1=p1, p2=p2, wa=wa, c0=c0, n=n):
                    for k in range(NCH):
                        e.matmul(p1[:, 0:n], lhsT=wa[:, 0, k, :], rhs=self.u[:, k, c0:c0 + n], start=(k == 0), stop=(k == NCH - 1))
                    for k in range(NCH):
                        r = e.matmul(p2[:, 0:n], lhsT=wa[:, 1, k, :], rhs=self.u[:, k, c0:c0 + n], start=(k == 0), stop=(k == NCH - 1))
                    return r
                P.op("pe", mm, reads=(wk,) + self.uk(tk), writes=(k1, k2))
                sg, sgk = self.t32()
                P.op("act", lambda e, sg=sg, p2=p2, n=n, oc=oc: e.activation(out=sg[:, 0:n], in_=p2[:, 0:n], func=AF.Sigmoid,
                                                                       bias=self.pvs(f"b2_{L}", oc), scale=1.0),
                     reads=(k2, "pv"), writes=(sgk,))
                if n == 512:
                    gout = G[:, oc, 30 + c0:30 + c0 + n]
                    gkey = f"G:{tk}"
                else:
                    gout = G[:, oc, GS0:GS0 + 4 * 38].rearrange("p (s t) -> p s t", s=4)[:, :, 30:38]
                    gkey = "G:s"

                def glu(e, sg=sg, p1=p1, n=n, oc=oc, gout=gout, c0=c0):
                    if n == 512:
                        r = e.scalar_tensor_tensor(out=gout, in0=p1[:, 0:n], scalar=self.pvs(f"b1_{L}", oc), in1=sg[:, 0:n],
                                                   op0=ALU.add, op1=ALU.mult)
                        if c0 == 1536:
                            if isY:
                                r = e.scalar_tensor_tensor(out=self.g32[:, L, oc, :], in0=p1[:, 482:512], scalar=self.pvs(f"b1_{L}", oc),
                                                           in1=sg[:, 482:512], op0=ALU.add, op1=ALU.mult)
                            else:
                                r = e.scalar_tensor_tensor(out=self.gstate[:, L, oc, :], in0=p1[:, 482:512], scalar=self.pvs(f"b1_{L}", oc),
                                                           in1=sg[:, 482:512], op0=ALU.add, op1=ALU.mult)
                    else:
                        e.scalar_tensor_tensor(out=gout, in0=p1[:, 0:n].rearrange("p (s t) -> p s t", s=4), scalar=self.pvs(f"b1_{L}", oc),
                                               in1=sg[:, 0:n].rearrange("p (s t) -> p s t", s=4), op0=ALU.add, op1=ALU.mult)
                        r = e.scalar_tensor_tensor(out=self.gs32[:, L, oc, :], in0=p1[:, 0:n], scalar=self.pvs(f"b1_{L}", oc),
                                                   in1=sg[:, 0:n], op0=ALU.add, op1=ALU.mult)
                    return r
                wr = [gkey]
                if c0 == 1536:
                    wr.append("g32" if isY else "gstate")
                if n != 512:
                    wr.append("gs32")
                P.op("dve", glu, reads=(k1, sgk, "pv", "flag"), writes=tuple(wr))
                if c0 == 1536 and not isY:
                    P.op("dve", lambda e, oc=oc: e.tensor_scalar(out=self.gstate[:, L, oc, :], in0=self.gstate[:, L, oc, :],
                                                                 scalar1=self.flag[:, 0:1], scalar2=None, op0=ALU.mult),
                         reads=("gstate", "flag"), writes=("gstate",))
        gkeys_main = tuple(f"G:{i}" for i in range(4)) + ("G:state",)
        for c in range(NCH):
            i1 = 2 * self._di
            self._di = (self._di + 1) % 3
            sk1, sk2 = f"w{i1}", f"w{i1 + 1}"
            dg = self.wall[:, i1 * 2048:i1 * 2048 + CW * 128].rearrange("p (j n) -> p j n", j=CW)

            def mk(e, dg=dg, c=c):
                for j in range(CW):
                    r = e.activation(out=dg[:, j, :], in_=self.I_BF, func=AF.Identity, scale=self.pvs(f"wdw{L}", c, j))
                return r
            P.op("act", mk, reads=("cb", "pv"), writes=(sk1, sk2))
            for (c0, n) in tiles:
                tk = c0 // 512
                pt, pk = self.ps()
                if n == 512:
                    def cv(e, dg=dg, c=c, c0=c0, pt=pt):
                        for j in range(CW):
                            r = e.matmul(pt[:, 0:512], lhsT=dg[:, j, :], rhs=G[:, c, c0 + j:c0 + j + 512], start=(j == 0), stop=(j == CW - 1))
                        return r
                    rd = (sk1, sk2) + gkeys_main
                else:
                    def cv(e, dg=dg, c=c, pt=pt):
                        gsv = G[:, c, GS0:GS0 + 4 * 38].rearrange("p (s t) -> p s t", s=4)
                        for j in range(CW):
                            r = e.matmul(pt[:, 0:NS].rearrange("p (s t) -> p s t", s=4), lhsT=dg[:, j, :], rhs=gsv[:, :, j:j + 8],
                                         start=(j == 0), stop=(j == CW - 1))
                        return r
                    rd = (sk1, sk2, "G:s")
                P.op("pe", cv, reads=rd, writes=(pk,))
                P.op("act", lambda e, pt=pt, c=c, c0=c0, n=n: e.activation(out=self.u[:, c, c0:c0 + n], in_=pt[:, 0:n], func=AF.Identity,
                                                                     bias=self.pvs(f"b_dw{L}", c), scale=1.0),
                     reads=(pk, "pv"), writes=(f"u:{tk}:{c}",))
        if getattr(self, "stop_stage", None) == "B":
            return
        for (c0, n) in tiles:
            tk = c0 // 512
            ukey = self.uk(tk)
            P.op("act", lambda e, c0=c0, n=n: e.activation(out=self.sqb[:, :, 0:n], in_=self.u[:, :, c0:c0 + n], func=AF.Square),
                 reads=ukey, writes=SQK)
            psm, ksm = self.ps()
            psq, ksq = self.ps()

            def st(e, psm=psm, psq=psq, c0=c0, n=n):
                for c in range(NCH):
                    e.matmul(psm[:, 0:n], lhsT=self.ONES, rhs=self.u[:, c, c0:c0 + n], start=(c == 0), stop=(c == NCH - 1))
                for c in range(NCH):
                    r = e.matmul(psq[:, 0:n], lhsT=self.ONES, rhs=self.sqb[:, c, 0:n], start=(c == 0), stop=(c == NCH - 1))
                return r
            P.op("pe", st, reads=ukey + SQK + ("cb",), writes=(ksm, ksq))
            mu, kmu = self.s32()
            va, kva = self.s32()

            P.op("dve", lambda e, mu=mu, psm=psm, n=n: e.tensor_scalar(out=mu[:, 0:n], in0=psm[:, 0:n], scalar1=1.0 / D, scalar2=None, op0=ALU.mult),
                 reads=(ksm,), writes=(kmu,))
            P.op("dve", lambda e, mu=mu, va=va, n=n: e.tensor_tensor(out=va[:, 0:n], in0=mu[:, 0:n], in1=mu[:, 0:n], op=ALU.mult),
                 reads=(kmu,), writes=(kva,))
            P.op("dve", lambda e, va=va, psq=psq, n=n: e.scalar_tensor_tensor(out=va[:, 0:n], in0=psq[:, 0:n], scalar=1.0 / D, in1=va[:, 0:n],
                                                                         op0=ALU.mult, op1=ALU.subtract),
                 reads=(ksq, kva), writes=(kva,))
            P.op("act", lambda e, va=va, n=n: e.activation(out=va[:, 0:n], in_=va[:, 0:n], func=AF.Ln, bias=EPS, scale=1.0),
                 reads=(kva,), writes=(kva,))
            P.op("act", lambda e, va=va, n=n: e.activation(out=va[:, 0:n], in_=va[:, 0:n], func=AF.Exp, scale=-0.5), reads=(kva,), writes=(kva,))
            for c in range(NCH):
                t, tkey = self.t32()

                P.op("dve", lambda e, t=t, c=c, c0=c0, n=n, mu=mu: e.tensor_tensor(out=t[:, 0:n], in0=self.u[:, c, c0:c0 + n], in1=mu[:, 0:n], op=ALU.subtract),
                     reads=(ukey[c], kmu), writes=(tkey,))
                P.op("dve", lambda e, t=t, n=n, va=va: e.tensor_tensor(out=t[:, 0:n], in0=t[:, 0:n], in1=va[:, 0:n], op=ALU.mult),
                     reads=(tkey, kva), writes=(tkey,))
                P.op("act", lambda e, t=t, c=c, c0=c0, n=n: e.activation(out=self.u[:, c, c0:c0 + n], in_=t[:, 0:n], func=AF.Silu,
                                                                   bias=self.pvs(f"ln_b{L}", c), scale=self.pvs(f"ln_g{L}", c)),
                     reads=(tkey, "pv"), writes=(ukey[c],))
        if getattr(self, "stop_stage", None) == "C":
            return
        for oc in range(NCH):
            wv, wk = self.wslot()
            wa = wv[:, 0:1024].rearrange("p (k n) -> p k n", k=NCH)
            src = w_out[L, :, oc * 128:(oc + 1) * 128].rearrange("(k p) n -> p k n", p=128)
            P.dma("pool", wk, lambda e, wa=wa, src=src: [e.dma_start(out=wa, in_=src)], writes=(wk,))
            for (c0, n) in tiles:
                tk = c0 // 512
                pt, pk = self.ps()

                def mm(e, pt=pt, wa=wa, c0=c0, n=n):
                    for k in range(NCH):
                        r = e.matmul(pt[:, 0:n], lhsT=wa[:, k, :], rhs=self.u[:, k, c0:c0 + n], start=(k == 0), stop=(k == NCH - 1))
                    return r
                P.op("pe", mm, reads=(wk,) + self.uk(tk), writes=(pk,))
                P.op("dve", lambda e, pt=pt, oc=oc, c0=c0, n=n: e.scalar_tensor_tensor(
                    out=self.x[:, oc, c0:c0 + n], in0=pt[:, 0:n], scalar=self.pvs(f"b_out{L}", oc), in1=self.x[:, oc, c0:c0 + n],
                    op0=ALU.add, op1=ALU.add), reads=(pk, "pv", f"x:{tk}"), writes=(f"x:{tk}",))

    def ffn(self, L, tiles):
        P = self.P
        wgu = self.din["w_gate_up"]
        wdn = self.din["w_down"]
        self.rmsnorm(tiles, f"ffn_norm{L}")
        for fg in range(NFC // 2):
            f0 = fg * 256
            wgv, kg = self.wslot()
            wuv, ku = self.wslot()
            wdv, kd = self.wslot()
            wg = wgv[:, 0:2048].rearrange("p (k n) -> p k n", k=NCH)
            wu = wuv[:, 0:2048].rearrange("p (k n) -> p k n", k=NCH)
            wd = wdv[:, 0:2048].rearrange("p (j n) -> p j n", j=2)
            sg_ = wgu[L, :, f0:f0 + 256].rearrange("(k p) n -> p k n", p=128)
            su_ = wgu[L, :, DFF + f0:DFF + f0 + 256].rearrange("(k p) n -> p k n", p=128)
            sd_ = wdn[L, f0:f0 + 256, :].rearrange("(j p) n -> p j n", p=128)
            P.dma("pool", kg, lambda e, wg=wg, sg_=sg_: [e.dma_start(out=wg, in_=sg_)], writes=(kg,))
            P.dma("pool", ku, lambda e, wu=wu, su_=su_: [e.dma_start(out=wu, in_=su_)], writes=(ku,))
            P.dma("pool", kd, lambda e, wd=wd, sd_=sd_: [e.dma_start(out=wd, in_=sd_)], writes=(kd,))

            def gate_up_steps(c0, n):
                tk = c0 // 512
                hkeys = []
                hs = []
                steps = []
                for j in range(2):
                    sg, sgk = self.t32()
                    hb = self.hbuf[self._hi]
                    hk = f"h{self._hi}"
                    self._hi = (self._hi + 1) % len(self.hbuf)
                    hs.append(hb)
                    hkeys.append(hk)
                    bank = {}

                    def s_g(bank=bank, j=j):
                        pg, kpg = self.ps()
                        bank["g"] = (pg, kpg)

                        def mm(e, pg=pg, j=j, c0=c0, n=n, wg=wg):
                            for k in range(NCH):
                                r = e.matmul(pg[:, 0:n], lhsT=wg[:, k, j * 128:(j + 1) * 128], rhs=self.u[:, k, c0:c0 + n], start=(k == 0), stop=(k == NCH - 1))
                            return r
                        P.op("pe", mm, reads=(kg,) + self.uk(tk), writes=(kpg,))

                    def s_u(bank=bank, j=j):
                        pu, kpu = self.ps()
                        bank["u"] = (pu, kpu)

                        def mm(e, pu=pu, j=j, c0=c0, n=n, wu=wu):
                            for k in range(NCH):
                                r = e.matmul(pu[:, 0:n], lhsT=wu[:, k, j * 128:(j + 1) * 128], rhs=self.u[:, k, c0:c0 + n], start=(k == 0), stop=(k == NCH - 1))
                            return r
                        P.op("pe", mm, reads=(ku,) + self.uk(tk), writes=(kpu,))

                    def s_e(bank=bank, sg=sg, sgk=sgk, hb=hb, hk=hk):
                        pg, kpg = bank["g"]
                        pu, kpu = bank["u"]
                        P.op("act", lambda e, sg=sg, pg=pg, n=n: e.activation(out=sg[:, 0:n], in_=pg[:, 0:n], func=AF.Silu),
                             reads=(kpg,), writes=(sgk,))
                        P.op("dve", lambda e, hb=hb, sg=sg, pu=pu, n=n: e.tensor_tensor(out=hb[:, 0:n], in0=sg[:, 0:n], in1=pu[:, 0:n], op=ALU.mult),
                             reads=(sgk, kpu), writes=(hk,))
                    steps += [s_g, s_u, s_e]
                return steps, hs, hkeys

            def down_steps(c0, n, hs, hkeys):
                tk = c0 // 512
                steps = []
                for dc in range(NCH):
                    def s_d(dc=dc):
                        pd, kpd = self.ps()

                        def dn(e, pd=pd, dc=dc, hs=tuple(hs), n=n, wd=wd):
                            e.matmul(pd[:, 0:n], lhsT=wd[:, 0, dc * 128:(dc + 1) * 128], rhs=hs[0][:, 0:n], start=True, stop=False)
                            return e.matmul(pd[:, 0:n], lhsT=wd[:, 1, dc * 128:(dc + 1) * 128], rhs=hs[1][:, 0:n], start=False, stop=True)
                        P.op("pe", dn, reads=(kd,) + tuple(hkeys), writes=(kpd,))
                        P.op("dve", lambda e, pd=pd, dc=dc, c0=c0, n=n: e.tensor_tensor(out=self.x[:, dc, c0:c0 + n], in0=pd[:, 0:n],
                                                                                  in1=self.x[:, dc, c0:c0 + n], op=ALU.add),
                             reads=(kpd, f"x:{tk}"), writes=(f"x:{tk}",))
                    steps.append(s_d)
                return steps
            prev = None
            for (c0, n) in tiles:
                gu, hs, hkeys = gate_up_steps(c0, n)
                dn_ = down_steps(*prev) if prev is not None else []
                order = [gu[0]] + dn_[0:2] + [gu[1]] + dn_[2:4] + [gu[2], gu[3]] + dn_[4:6] + [gu[4]] + dn_[6:8] + [gu[5]]
                for st_ in order:
                    st_()
                prev = (c0, n, hs, hkeys)
            for st_ in down_steps(*prev):
                st_()

    def phase1_test(self):
        self.setup()
        xin = self.din["xin"]
        for sup in range(2):
            isY = sup == 1
            for tb in range(16):
                self.load_tokens(xin[sup * HALF + tb * 128: sup * HALF + (tb + 1) * 128, :], 128, tb * 128)
            if isY:
                self.load_tokens(self.din["xs"][:, :], NS, HALF)
            tiles = TILES_Y if isY else TILES_X
            for L in range(2):
                self.conv_layer(L, isY)
                self.ffn(L, tiles)
        for tb in range(16):
            self.store_tokens(self.dout["y"][tb * 128:(tb + 1) * 128, :], 128, tb * 128)
        self.store_tokens(self.dout["y"][HALF:HALF + NS, :], NS, HALF)

    def finish(self):
        nc = self.nc
        P = self.P
        chans = sorted(P.chan_n.keys())
        import contextlib
        with contextlib.ExitStack() as es:
            sems = {e: es.enter_context(nc.semaphore(f"sem_{e}")) for e in ENGS}
            csems = {ch: es.enter_context(nc.semaphore(f"semd_{ch}")) for ch in chans}
            block = es.enter_context(nc.Block())
            regs = {"pe": block.tensor, "act": block.scalar, "dve": block.vector, "pool": block.gpsimd, "sp": block.sync}
            P.emit(nc, regs, sems, csems)
        return nc


def _const_mats():
    p = np.arange(128)
    ident = np.eye(128, dtype=np.float32)
    ones = np.ones((128, 128), np.float32)
    bones = (p[:, None] // 64 == p[None, :] // 64).astype(np.float32)
    partner = (p // 64) * 64 + ((p % 64) + 32) % 64
    swap = np.zeros((128, 128), np.float32)
    swap[partner, p] = 1.0
    maskP = (p[None, :] <= p[:, None]).astype(np.float32)
    maskC = (p[:, None] <= p[None, :]).astype(np.float32)
    return np.stack([ident, ones, bones, swap, maskP, maskC]).astype(np.float32)


def _smask():
    m = np.zeros((128, 544), np.float32)
    k = np.arange(128)[:, None]
    q = np.arange(8)[None, :]
    for a in range(16):
        m[:, a * 8:(a + 1) * 8] = (k >= q)
        m[:, 128 + a * 2] = 1.0
        m[:, 128 + a * 2 + 1] = (np.arange(128) >= 1)
    t = np.arange(8)[:, None]
    M0 = (t <= q).astype(np.float32)
    M1 = ((t == q) | (t == q - 4)).astype(np.float32)
    M2 = (t == q).astype(np.float32)
    for g, Mg in enumerate((M0, M1, M2)):
        for a in range(16):
            m[0:8, 160 + g * 128 + a * 8:160 + g * 128 + (a + 1) * 8] = Mg
    return m


def _rot_tables(half_id):
    half = 32
    inv = np.float32(10000.0) ** (-(np.arange(half, dtype=np.float32) / np.float32(half)))
    pos = np.zeros(KTW, np.float32)
    l = np.arange(SEQ)
    if half_id == 1:
        pos[:SEQ] = l
    else:
        pos[:SEQ] = np.maximum(l - HALF, 0)
    pos[SEQ:] = np.tile(PAST + np.arange(8), 4)
    ang = (pos[:, None].astype(np.float32) * inv[None, :].astype(np.float32)).astype(np.float32)
    p = np.arange(128)
    idx = (p % 64) % 32
    cosT = np.cos(ang)[:, idx].T.astype(np.float32)
    sinT = np.sin(ang)[:, idx].T.astype(np.float32)
    sign = np.where((p % 64) < 32, -1.0, 1.0).astype(np.float32)
    return np.stack([cosT, sinT * sign[:, None]]).astype(np.float32)


def _pvec(inp):
    rows, n = Builder.pv_layout()
    pv = np.zeros((n, D), np.float32)
    for L in range(2):
        pv[rows[f"a_norm{L}"]] = inp["a_norm"][L]
        pv[rows[f"b1_{L}"]] = inp["a_b_in"][L][:D]
        pv[rows[f"b2_{L}"]] = inp["a_b_in"][L][D:]
        pv[rows[f"wdw{L}"]:rows[f"wdw{L}"] + CW] = inp["a_w_dw"][L]
        pv[rows[f"b_dw{L}"]] = inp["a_b_dw"][L]
        pv[rows[f"ln_g{L}"]] = inp["a_ln_g"][L]
        pv[rows[f"ln_b{L}"]] = inp["a_ln_b"][L]
        pv[rows[f"b_out{L}"]] = inp["a_b_out"][L]
    pv[rows["kv_norm"]] = inp["kv_norm"]
    for j in range(2):
        pv[rows[f"b_norm{j}"]] = inp["b_norm"][j]
    for L in range(4):
        pv[rows[f"ffn_norm{L}"]] = inp["ffn_norm"][L]
    for g in range(3):
        pv[rows[f"k_norm{g}"], :128] = np.tile(inp["k_norm"][g], 2)
    for j in range(2):
        for g in range(3):
            pv[rows[f"q_norm{j}_{g}"], :128] = np.tile(inp["q_norm"][j, g], 2)
    return pv


def make_in_maps(inp):
    inp = {k: np.asarray(v) for k, v in inp.items()}
    cmat = _const_mats()
    pvec = _pvec(inp)
    rots = [_rot_tables(0), _rot_tables(1)]
    shared = {
        "pvec": pvec, "cmat": cmat, "smaskin": _smask(),
        "a_w_in": np.ascontiguousarray(inp["a_w_in"], np.float32), "a_w_out": np.ascontiguousarray(inp["a_w_out"], np.float32),
        "w_kv": np.ascontiguousarray(inp["w_kv"], np.float32), "w_q": np.ascontiguousarray(inp["w_q"], np.float32),
        "w_o": np.ascontiguousarray(inp["w_o"], np.float32), "w_gate_up": np.ascontiguousarray(inp["w_gate_up"], np.float32),
        "w_down": np.ascontiguousarray(inp["w_down"], np.float32),
    }
    maps = []
    for c in range(8):
        b, h = c // 2, c % 2
        if h == 1:
            xin = np.ascontiguousarray(inp["x_prompt"][b], np.float32)
        else:
            xin = np.concatenate([np.zeros((HALF, D), np.float32), inp["x_prompt"][b, :HALF]], axis=0)
        m = dict(shared)
        m["xin"] = xin
        m["xs"] = np.ascontiguousarray(inp["x_sample"][4 * c:4 * c + 4].reshape(NS, D), np.float32)
        m["cconv"] = np.ascontiguousarray(inp["cache_conv"][:, 4 * c:4 * c + 4], np.float32)
        m["ckv0"] = np.ascontiguousarray(inp["cache_kv_w128"][4 * c:4 * c + 4].reshape(4, 128, 2048), np.float32)
        m["ckv1"] = np.ascontiguousarray(inp["cache_kv_w512"][4 * c:4 * c + 4].reshape(4, 512, 2048), np.float32)
        m["ckv2"] = np.ascontiguousarray(inp["cache_kv_w2048"][4 * c:4 * c + 4].reshape(4, 2048, 2048), np.float32)
        m["rot"] = rots[h]
        m["flag"] = np.full((128, 1), float(h), np.float32)
        maps.append(m)
    return maps


def _dbg_t0(self):
    self.setup()
    xin = self.din["xin"]
    self.load_tokens(xin[HALF:HALF + 128, :], 128, 0)
    self.store_tokens(self.dout["y"][0:128, :], 128, 0)


Builder.dbg_t0 = _dbg_t0


def _dbg_c0(self):
    self.setup()
    xin = self.din["xin"]
    tiles = [(0, 512)]
    for tb in range(4):
        self.load_tokens(xin[tb * 128:(tb + 1) * 128, :], 128, tb * 128)
    import os
    self.stop_stage = os.environ.get("STOP_STAGE")
    self.conv_layer(0, False, tiles=tiles)
    if self.stop_stage is None:
        self.ffn(0, tiles)
    for tb in range(4):
        self.store_tokens(self.dout["y"][tb * 128:(tb + 1) * 128, :], 128, tb * 128)
    self.dbg_sbuf = [self.M.names[k] for k in ("u", "glu", "pv", "x")]


Builder.dbg_c0 = _dbg_c0


def _ovl_G(self):
    off, nb = self.M.offs["glu"]
    return Mem(self.nc, off, nb)


def _load_rot(self, OM, col_src0, ncols_main, with_sample):
    P = self.P
    rc = OM.alloc([128, NT], F32, "rotc")
    rs = OM.alloc([128, NT], F32, "rots")
    rot = self.din["rot"]

    def f(e):
        r = [e.dma_start(out=rc[:, 0:HALF], in_=rot[0, :, col_src0:col_src0 + HALF]),
             e.dma_start(out=rs[:, 0:HALF], in_=rot[1, :, col_src0:col_src0 + HALF])]
        if with_sample:
            r += [e.dma_start(out=rc[:, HALF:NT], in_=rot[0, :, SEQ:SEQ + NS]),
                  e.dma_start(out=rs[:, HALF:NT], in_=rot[1, :, SEQ:SEQ + NS])]
        return r
    P.dma("sp", "rot", f, writes=("rot",), n=4 if with_sample else 2)
    return rc, rs


def _head_proj(self, tiles, w_src_fn, gain_fn, dst_fn, rc, rs, OM, out32_fn=None, need32_fn=None):
    P = self.P
    kst = [OM.alloc([128, 512], BF16, f"kst{i}") for i in range(2)]
    ksti = 0
    for g in range(3):
        for hp in range(8):
            wv, wk = self.wslot()
            wa = wv[:, 0:1024].rearrange("p (k n) -> p k n", k=NCH)
            src = w_src_fn(g, hp)
            P.dma("pool", wk, lambda e, wa=wa, src=src: [e.dma_start(out=wa, in_=src)], writes=(wk,))
            for (c0, n) in tiles:
                tk = c0 // 512
                pr, kpr = self.ps()

                def mm(e, pr=pr, wa=wa, c0=c0, n=n):
                    for k in range(NCH):
                        r = e.matmul(pr[:, 0:n], lhsT=wa[:, k, :], rhs=self.u[:, k, c0:c0 + n], start=(k == 0), stop=(k == NCH - 1))
                    return r
                P.op("pe", mm, reads=(wk,) + self.uk(tk), writes=(kpr,))
                si = self._sqi
                self._sqi = (si + 2) % 8
                sq = self.sqb[:, si, 0:n]
                xg = self.sqb[:, si + 1, 0:n]
                ksq, kxg = f"sqb{si}", f"sqb{si + 1}"
                P.op("act", lambda e, sq=sq, pr=pr, n=n: e.activation(out=sq, in_=pr[:, 0:n], func=AF.Square), reads=(kpr,), writes=(ksq,))
                P.op("act", lambda e, xg=xg, pr=pr, n=n, g=g: e.activation(out=xg, in_=pr[:, 0:n], func=AF.Identity, scale=gain_fn(g)),
                     reads=(kpr, "pv"), writes=(kxg,))
                p1, k1 = self.ps()
                p2, k2 = self.ps()
                P.op("pe", lambda e, p1=p1, sq=sq, n=n: e.matmul(p1[:, 0:n], lhsT=self.BONES, rhs=sq, start=True, stop=True),
                     reads=(ksq, "cb"), writes=(k1,))
                P.op("pe", lambda e, p2=p2, xg=xg, n=n: e.matmul(p2[:, 0:n], lhsT=self.SWAP, rhs=xg, start=True, stop=True),
                     reads=(kxg, "cb"), writes=(k2,))
                rstd, krs = self.s32()
                P.op("act", lambda e, rstd=rstd, p1=p1, n=n: e.activation(out=rstd[:, 0:n], in_=p1[:, 0:n], func=AF.Ln, bias=EPS, scale=1.0 / 64),
                     reads=(k1,), writes=(krs,))
                P.op("act", lambda e, rstd=rstd, n=n: e.activation(out=rstd[:, 0:n], in_=rstd[:, 0:n], func=AF.Exp, scale=-0.5), reads=(krs,), writes=(krs,))
                t1, kt1 = self.t32()
                t2, kt2 = self.t32()
                P.op("pool", lambda e, t1=t1, xg=xg, c0=c0, n=n: e.tensor_tensor(out=t1[:, 0:n], in0=xg, in1=rc[:, c0:c0 + n], op=ALU.mult),
                     reads=(kxg, "rot"), writes=(kt1,))
                P.op("dve", lambda e, t2=t2, p2=p2, c0=c0, n=n: e.tensor_tensor(out=t2[:, 0:n], in0=p2[:, 0:n], in1=rs[:, c0:c0 + n], op=ALU.mult),
                     reads=(k2, "rot"), writes=(kt2,))
                P.op("dve", lambda e, t1=t1, t2=t2, n=n: e.tensor_tensor(out=t1[:, 0:n], in0=t1[:, 0:n], in1=t2[:, 0:n], op=ALU.add),
                     reads=(kt1, kt2), writes=(kt1,))
                ks = kst[ksti]
                kk = f"kst{ksti}"
                ksti = (ksti + 1) % 2
                P.op("dve", lambda e, ks=ks, t1=t1, rstd=rstd, n=n: e.tensor_tensor(out=ks[:, 0:n], in0=t1[:, 0:n], in1=rstd[:, 0:n], op=ALU.mult),
                     reads=(kt1, krs), writes=(kk,))
                if out32_fn is not None and need32_fn is not None and need32_fn(g, c0, n):
                    P.op("dve", lambda e, t1=t1, rstd=rstd, n=n: e.tensor_tensor(out=t1[:, 0:n], in0=t1[:, 0:n], in1=rstd[:, 0:n], op=ALU.mult),
                         reads=(kt1, krs), writes=(kt1,))
                dst = dst_fn(g, hp, c0, n)
                P.dma("sp", kk, lambda e, ks=ks, dst=dst, n=n: [e.dma_start(out=dst, in_=ks[:, 0:n])], reads=(kk,))
                if out32_fn is not None and need32_fn is not None and need32_fn(g, c0, n):
                    out32_fn(g, hp, c0, n, t1, kt1)


def _kv_stage(self, isY):
    P = self.P
    P.fence()
    tiles = TILES_Y if isY else TILES_X
    self.rmsnorm(tiles, "kv_norm")
    OM = self.ovl_G()
    loc0 = HALF if isY else 0
    rc, rs = self.load_rot(OM, loc0, HALF, isY)
    w_kv = self.din["w_kv"]
    kt = self.kt_scr
    vs = self.v_scr

    def w_src(g, hp):
        c = g * 2048 + hp * 128
        return w_kv[:, c:c + 128].rearrange("(k p) n -> p k n", p=128)

    def gain(g):
        return self.pvs(f"k_norm{g}", 0)

    def dst(g, hp, c0, n):
        if n == 512:
            return kt[g, hp, :, loc0 + c0:loc0 + c0 + n]
        return kt[g, hp, :, SEQ:SEQ + NS]

    def out32(g, hp, c0, n, t1, kt1):
        import os
        if not isY or os.environ.get("NO_OUT32"):
            return
        W = WINS[g]
        if n == 512:
            blocks = [b for b in range(4) if c0 + b * 128 >= HALF - W]
            if not blocks:
                return
            pt, pk = self.ps()

            def tr(e, pt=pt, t1=t1, blocks=tuple(blocks)):
                for b in blocks:
                    r = e.transpose(out=pt[:, b * 128:(b + 1) * 128], in_=t1[:, b * 128:(b + 1) * 128], identity=self.ident[:, :])
                return r
            P.op("pe", tr, reads=(kt1, "ident"), writes=(pk,))
            st, sk = self.stage()
            b0, nb = blocks[0], len(blocks)
            P.op("act", lambda e, st=st, pt=pt, b0=b0, nb=nb: e.activation(out=st[:, b0 * 128:(b0 + nb) * 128], in_=pt[:, b0 * 128:(b0 + nb) * 128], func=AF.Copy),
                 reads=(pk,), writes=(sk,))
            row0 = c0 + b0 * 128 - (HALF - W)
            dsto = self.dout[f"kvp{g}"][row0:row0 + nb * 128, hp * 128:(hp + 1) * 128].rearrange("(b t) f -> t b f", t=128)
            P.dma("sp", sk, lambda e, st=st, dsto=dsto, b0=b0, nb=nb: [e.dma_start(out=dsto, in_=st[:, b0 * 128:(b0 + nb) * 128].rearrange("t (b f) -> t b f", b=nb))],
                  reads=(sk,))
        else:
            pt, pk = self.ps()
            P.op("pe", lambda e, pt=pt, t1=t1: e.transpose(out=pt[0:NS, 0:128], in_=t1[:, 0:NS], identity=self.ident[:, :]),
                 reads=(kt1, "ident"), writes=(pk,))
            st, sk = self.stage()
            P.op("act", lambda e, st=st, pt=pt: e.activation(out=st[0:NS, 0:128], in_=pt[0:NS, 0:128], func=AF.Copy), reads=(pk,), writes=(sk,))
            dsto = self.dout[f"kvs{g}"][:, W - 8:W, hp * 128:(hp + 1) * 128]

            def f(e, st=st, dsto=dsto):
                return [e.dma_start(out=dsto[s], in_=st[s * 8:(s + 1) * 8, 0:128]) for s in range(4)]
            P.dma("sp", sk, f, reads=(sk,), n=4)
    def need32(g, c0, n):
        return isY and (n != 512 or c0 + 512 > HALF - WINS[g])
    self.head_proj(tiles, w_src, gain, dst, rc, rs, OM, out32, need32)

    vst = [OM.alloc([128, 256], BF16, f"vst{i}") for i in range(2)]
    vi = 0
    blocks = [(tb * 128, 128) for tb in range(16)] + ([(HALF, NS)] if isY else [])
    for g in range(3):
        W = WINS[g]
        for q in range(4):
            wv, wk = self.wslot()
            wa = wv[:, 0:2048].rearrange("p (k n) -> p k n", k=NCH)
            c = g * 2048 + 1024 + q * 256
            src = w_kv[:, c:c + 256].rearrange("(k p) n -> p k n", p=128)
            P.dma("pool", wk, lambda e, wa=wa, src=src: [e.dma_start(out=wa, in_=src)], writes=(wk,))
            for (c0, m) in blocks:
                tk = c0 // 512
                pt, pk = self.ps()

                def mm(e, pt=pt, wa=wa, c0=c0, m=m):
                    for k in range(NCH):
                        r = e.matmul(pt[0:m, 0:256], lhsT=self.u[:, k, c0:c0 + m], rhs=wa[:, k, :], start=(k == 0), stop=(k == NCH - 1))
                    return r
                P.op("pe", mm, reads=(wk,) + self.uk(tk), writes=(pk,))
                vb = vst[vi]
                vk = f"vst{vi}"
                vi = (vi + 1) % 2
                P.op("act", lambda e, vb=vb, pt=pt, m=m: e.activation(out=vb[0:m, :], in_=pt[0:m, 0:256], func=AF.Copy), reads=(pk,), writes=(vk,))
                if m == 128:
                    l0 = loc0 + c0
                else:
                    l0 = SEQ
                dstv = vs[g, 2 * q:2 * q + 2, l0:l0 + m, :].rearrange("h t f -> t h f")
                P.dma("sp", vk, lambda e, vb=vb, dstv=dstv, m=m: [e.dma_start(out=dstv, in_=vb[0:m, :].rearrange("t (h f) -> t h f", h=2))],
                      reads=(vk,))
                import os
                if isY and not os.environ.get("NO_VOUT"):
                    need = (m == NS) or (c0 >= HALF - W)
                    if need:
                        st, sk = self.stage()
                        P.op("dve", lambda e, st=st, pt=pt, m=m: e.tensor_copy(out=st[0:m, 0:256], in_=pt[0:m, 0:256]), reads=(pk,), writes=(sk,))
                        if m == 128:
                            row0 = c0 - (HALF - W)
                            dsto = self.dout[f"kvp{g}"][row0:row0 + 128, 1024 + q * 256:1024 + (q + 1) * 256]
                            P.dma("sp", sk, lambda e, st=st, dsto=dsto: [e.dma_start(out=dsto, in_=st[:, 0:256])], reads=(sk,))
                        else:
                            dsto = self.dout[f"kvs{g}"][:, W - 8:W, 1024 + q * 256:1024 + (q + 1) * 256]

                            def f(e, st=st, dsto=dsto):
                                return [e.dma_start(out=dsto[s], in_=st[s * 8:(s + 1) * 8, 0:256]) for s in range(4)]
                            P.dma("sp", sk, f, reads=(sk,), n=4)
    P.fence()


Builder.ovl_G = _ovl_G
Builder.load_rot = _load_rot
Builder.head_proj = _head_proj
Builder.kv_stage = _kv_stage


SCALE = 0.125


def sl(start, n, step):
    return slice(start, start + step * (n - 1) + 1, step)

KBASE = (HALF - 128, HALF - 512, 0)
NBO = (16, 4, 1)


def _q_stage(self, j):
    P = self.P
    P.fence()
    self.rmsnorm(TILES_Y, f"b_norm{j}")
    OM = self.ovl_G()
    rc, rs = self.load_rot(OM, HALF, HALF, True)
    w_q = self.din["w_q"]

    def w_src(g, hp):
        c = g * 1024 + hp * 128
        return w_q[j, :, c:c + 128].rearrange("(k p) n -> p k n", p=128)

    def gain(g):
        return self.pvs(f"q_norm{j}_{g}", 0)

    def dst(g, hp, c0, n):
        return self.q_scr[g, hp, :, c0:c0 + n]
    self.head_proj(TILES_Y, w_src, gain, dst, rc, rs, OM, None)
    P.fence()


def _attn_layer(self, j, STOP=""):
    P = self.P
    self.q_stage(j)
    if STOP == "Q":
        return
    self.sample_stage(j)
    if STOP == "S":
        return
    uo, ub = self.M.offs["u"]
    go, gb = self.M.offs["glu"]
    assert uo + ub == go
    OM = Mem(self.nc, uo, ub + gb)
    QT = [OM.alloc([128, NT], BF16, f"QT{g}") for g in range(3)]
    kw = [SEQ - KBASE[g] + NS for g in range(3)]
    KT = [OM.alloc([128, kw[g]], BF16, f"KT{g}") for g in range(3)]
    nblk = [DILS[g] * (NBO[g] + 1) for g in range(3)]
    VT = [OM.alloc([128, nblk[g], 128], BF16, f"VT{g}") for g in range(3)]
    acc = OM.alloc([128, 2, NT], F32, "acc")
    PT = [OM.alloc([128, 512], BF16, f"PT{i}") for i in range(3)]
    qo, qb = OM.offs["QT0"]
    assert OM.offs["QT1"][0] == qo + qb and qb == NT * 2
    oT = self.nc.alloc_sbuf_tensor_at(f"oT_{j}", [128, 2, NT], BF16, offset=qo)
    self.att = dict(QT=QT, KT=KT, VT=VT, acc=acc, PT=PT, OM=OM)
    pti = 0
    w_o = self.din["w_o"]
    for hp in range(8):
        for g in range(3):
            r = DILS[g]
            nbo = NBO[g]
            kb0 = KBASE[g]
            P.dma("sp", f"QT{g}", lambda e, g=g, hp=hp: [e.dma_start(out=QT[g][:, :], in_=self.q_scr[g, hp, :, :])], writes=(f"QT{g}",))
            P.dma("sp", f"KT{g}", lambda e, g=g, hp=hp, kb0=kb0: [e.dma_start(out=KT[g][:, :], in_=self.kt_scr[g, hp, :, kb0:KTW])],
                  writes=(f"KT{g}",))
            vsrc = self.v_scr[g, hp, kb0:kb0 + r * 128 * (nbo + 1), :].rearrange("(m i c) f -> i c m f", i=128, c=r)
            vdst = VT[g][:, :, :].rearrange("p (c m) f -> p c m f", c=r)
            if r == 1:
                P.dma("sp", f"VT{g}", lambda e, vdst=vdst, vsrc=vsrc: [e.dma_start(out=vdst[:, 0], in_=vsrc[:, 0])], writes=(f"VT{g}",))
            else:
                def fv(e, vdst=vdst, vsrc=vsrc, r=r):
                    return [e.dma_start(out=vdst[:, c], in_=vsrc[:, c]) for c in range(r)]
                P.dma("sp", f"VT{g}", fv, writes=(f"VT{g}",), n=r)
            def front(c, nn, g=g, r=r, nbo=nbo, kb0=kb0):
                nonlocal pti
                q0 = c + r * 128 * nn
                kcur = (HALF - kb0) + q0
                kprev = kcur - r * 128
                bprev = c * (nbo + 1) + nn
                pSa, kSa = self.ps()
                pSb, kSb = self.ps()

                def st(e, pSa=pSa, pSb=pSb, g=g, r=r, q0=q0, kcur=kcur, kprev=kprev):
                    for h, pS in enumerate((pSa, pSb)):
                        for kb, k0 in enumerate((kprev, kcur)):
                            rr = e.matmul(pS[:, kb * 128:(kb + 1) * 128],
                                          lhsT=KT[g][h * 64:(h + 1) * 64, sl(k0, 128, r)],
                                          rhs=QT[g][h * 64:(h + 1) * 64, sl(q0, 128, r)], start=True, stop=True)
                    return rr
                P.op("pe", st, reads=(f"KT{g}", f"QT{g}"), writes=(kSa, kSb))
                E, kE = self.t32()
                E4 = E[:, :].rearrange("p (k h q) -> p k h q", k=2, h=2)
                P.op("act", lambda e, E4=E4, pSa=pSa: e.activation(out=E4[:, :, 0, :], in_=pSa[:, 0:256].rearrange("p (k q) -> p k q", k=2),
                                                                func=AF.Exp, scale=SCALE), reads=(kSa,), writes=(kE,))
                P.op("act", lambda e, E4=E4, pSb=pSb: e.activation(out=E4[:, :, 1, :], in_=pSb[:, 0:256].rearrange("p (k q) -> p k q", k=2),
                                                                func=AF.Exp, scale=SCALE), reads=(kSb, kE), writes=(kE,))
                pt = PT[pti]
                kpt = f"PT{pti}"
                pti = (pti + 1) % 3
                if nn == 0:
                    def mk(e, pt=pt, E=E):
                        e.scalar_tensor_tensor(out=pt[:, 0:256], in0=E[:, 0:256], scalar=self.flag[:, 0:1], in1=self.maskf[:, 0:256],
                                               op0=ALU.mult, op1=ALU.mult)
                        return e.tensor_tensor(out=pt[:, 256:512], in0=E[:, 256:512], in1=self.maskf[:, 256:512], op=ALU.mult)
                else:
                    def mk(e, pt=pt, E=E):
                        return e.tensor_tensor(out=pt[:, :], in0=E[:, :], in1=self.maskf[:, :], op=ALU.mult)
                P.op("dve", mk, reads=(kE, "maskf", "flag"), writes=(kpt,))
                return (pt, kpt, bprev, q0)

            def back(ctx, g=g, r=r):
                pt, kpt, bprev, q0 = ctx
                pU, kU = self.ps()

                def pv(e, pU=pU, pt=pt, g=g, bprev=bprev):
                    for h in range(2):
                        for kb in range(2):
                            rhs = pt[:, (kb * 2 + h) * 128:(kb * 2 + h + 1) * 128]
                            e.matmul(pU[0:64, h * 128:(h + 1) * 128], lhsT=VT[g][:, bprev + kb, h * 64:(h + 1) * 64], rhs=rhs,
                                     start=(kb == 0), stop=(kb == 1))
                            rr = e.matmul(pU[64:128, h * 128:(h + 1) * 128], lhsT=self.ONES[:, 0:64], rhs=rhs,
                                          start=(kb == 0), stop=(kb == 1))
                    return rr
                P.op("pe", pv, reads=(kpt, f"VT{g}", "cb"), writes=(kU,))
                av = acc[:, :, sl(q0, 128, r)]
                pv3 = pU[:, 0:256].rearrange("p (h q) -> p h q", h=2)
                if g == 0:
                    P.op("dve", lambda e, av=av, pv3=pv3: e.tensor_copy(out=av, in_=pv3), reads=(kU,), writes=("acc",))
                else:
                    P.op("dve", lambda e, av=av, pv3=pv3: e.tensor_tensor(out=av, in0=pv3, in1=av, op=ALU.add), reads=(kU, "acc"), writes=("acc",))
            pend = None
            for c in range(r):
                for nn in range(nbo):
                    ctx = front(c, nn)
                    if pend is not None:
                        back(pend)
                    pend = ctx
            back(pend)
        self.sample_attn(j, hp)
        P.op("act", lambda e: e.activation(out=acc[64:128, :, :], in_=acc[64:128, :, :], func=AF.Ln), reads=("acc",), writes=("acc",))
        P.op("act", lambda e: e.activation(out=acc[64:128, :, :], in_=acc[64:128, :, :], func=AF.Exp, scale=-1.0), reads=("acc",), writes=("acc",))
        wv, wk = self.wslot()
        wo = wv[0:64, 0:2048].rearrange("p (h n) -> p h n", h=2)
        src = w_o[j, hp * 128:(hp + 1) * 128, :].rearrange("(h p) n -> p h n", p=64)
        P.dma("pool", wk, lambda e, wo=wo, src=src: [e.dma_start(out=wo, in_=src)], writes=(wk,))
        for (c0, n) in TILES_Y:
            tk = c0 // 512
            for h in range(2):
                pR, kR = self.ps()
                P.op("pe", lambda e, pR=pR, h=h, c0=c0, n=n: e.matmul(pR[0:64, 0:n], lhsT=self.ident[:, 64:128], rhs=acc[:, h, c0:c0 + n], start=True, stop=True),
                     reads=("acc", "ident"), writes=(kR,))
                P.op("dve", lambda e, pR=pR, h=h, c0=c0, n=n: e.tensor_tensor(out=oT[0:64, h, c0:c0 + n], in0=acc[0:64, h, c0:c0 + n], in1=pR[0:64, 0:n], op=ALU.mult),
                     reads=(kR, "acc"), writes=("QT0", "QT1"))
            for dc in range(NCH):
                pd, kd = self.ps()

                def wm(e, pd=pd, dc=dc, c0=c0, n=n, wo=wo):
                    e.matmul(pd[:, 0:n], lhsT=wo[0:64, 0, dc * 128:(dc + 1) * 128], rhs=oT[0:64, 0, c0:c0 + n], start=True, stop=False)
                    return e.matmul(pd[:, 0:n], lhsT=wo[0:64, 1, dc * 128:(dc + 1) * 128], rhs=oT[0:64, 1, c0:c0 + n], start=False, stop=True)
                P.op("pe", wm, reads=(wk, "QT0", "QT1"), writes=(kd,))
                P.op("dve", lambda e, pd=pd, dc=dc, c0=c0, n=n: e.tensor_tensor(out=self.x[:, dc, c0:c0 + n], in0=pd[:, 0:n], in1=self.x[:, dc, c0:c0 + n], op=ALU.add),
                     reads=(kd, f"x:{tk}"), writes=(f"x:{tk}",))
    P.fence()


def _sample_attn_stub(self, j, hp):
    acc = self.att["acc"]
    self.P.op("dve", lambda e: e.memset(acc[:, :, HALF:NT], 1.0), writes=("acc",))


Builder.q_stage = _q_stage
Builder.attn_layer = _attn_layer
Builder.sample_attn = _sample_attn_stub


def _sample_stage(self, j):
    P = self.P
    P.fence()
    uo, ub = self.M.offs["u"]
    go, gb = self.M.offs["glu"]
    OM = Mem(self.nc, uo, ub + gb)
    QS = OM.alloc([128, 24, NS], BF16, "QS")
    KN = OM.alloc([128, 24, NS], BF16, "KN")
    VN = [OM.alloc([8, 32, 128], BF16, f"VN{g}") for g in range(3)]
    KC = [OM.alloc([128, 1024], F32, f"KC{i}") for i in range(2)]
    VC = [OM.alloc([128, 1024], BF16, f"VC{i}") for i in range(2)]
    KCT = [OM.alloc([128, 8, 128], BF16, f"KCT{i}") for i in range(2)]
    ES = [OM.alloc([128, 128], F32, f"ES{i}") for i in range(2)]
    PS_ = [OM.alloc([128, 128], BF16, f"PS{i}") for i in range(2)]
    sacc = self.sacc
    P.dma("sp", "QS", lambda e: [e.dma_start(out=QS[:, :, :], in_=self.q_scr[:, :, :, HALF:NT].rearrange("g h p n -> p (g h) n"))], writes=("QS",))
    P.dma("sp", "KN", lambda e: [e.dma_start(out=KN[:, :, :], in_=self.kt_scr[:, :, :, SEQ:KTW].rearrange("g h p n -> p (g h) n"))], writes=("KN",))
    for g in range(3):
        def fvn(e, g=g):
            return [e.dma_start(out=VN[g][:, hp * 4:(hp + 1) * 4, :], in_=self.v_scr[g, hp, SEQ:KTW, :].rearrange("(s t) f -> t s f", t=8))
                    for hp in range(8)]
        P.dma("sp", f"VN{g}", fvn, writes=(f"VN{g}",), n=8)
    P.op("dve", lambda e: e.memset(sacc[:, :, :, :], 0.0), writes=("sacc",))
    ckv = [self.din["ckv0"], self.din["ckv1"], self.din["ckv2"]]
    it = 0
    pend = None

    def run(front, back):
        nonlocal pend
        ctx = front()
        if pend is not None:
            pend[0](pend[1])
        pend = (back, ctx)
    for s in range(4):
        for g in range(3):
            r = DILS[g]
            ntile = (1, 4, 8)[g]
            for c in range(ntile):
                if g == 0:
                    qcols = list(range(8))
                elif g == 1:
                    qcols = [c, c + 4]
                else:
                    qcols = [c]
                nq = len(qcols)
                q0, qstep = s * 8 + qcols[0], (qcols[1] - qcols[0]) if nq > 1 else 1
                b = it % 2
                it += 1

                def front(s=s, g=g, c=c, r=r, nq=nq, q0=q0, qstep=qstep, b=b):
                    kc, vc, kct, es, pb = KC[b], VC[b], KCT[b], ES[b], PS_[b]
                    rows = ckv[g][s, sl(c, 128, r), :]
                    P.dma("sp", f"KC{b}", lambda e, kc=kc, rows=rows: [e.dma_start(out=kc[:, :], in_=rows[:, 0:1024])], writes=(f"KC{b}",))
                    P.dma("pool", f"VC{b}", lambda e, vc=vc, rows=rows: [e.dma_start(out=vc[:, :], in_=rows[:, 1024:2048])], writes=(f"VC{b}",))
                    for half in range(2):
                        pt, pk = self.ps()

                        def tr(e, pt=pt, kc=kc, half=half):
                            for cc in range(4):
                                hp = half * 4 + cc
                                rr = e.transpose(out=pt[:, cc * 128:(cc + 1) * 128], in_=kc[:, hp * 128:(hp + 1) * 128], identity=self.ident[:, :])
                            return rr
                        P.op("pe", tr, reads=(f"KC{b}", "ident"), writes=(pk,))
                        P.op("act", lambda e, pt=pt, kct=kct, half=half: e.activation(out=kct[:, half * 4:half * 4 + 4, :],
                                                                                   in_=pt[:, :].rearrange("p (a k) -> p a k", a=4), func=AF.Copy),
                             reads=(pk,), writes=(f"KCT{b}",))
                    pSa, kSa = self.ps()
                    pSb, kSb = self.ps()

                    def st(e, pSa=pSa, pSb=pSb, kct=kct):
                        for h, pS in enumerate((pSa, pSb)):
                            for hp in range(8):
                                rr = e.matmul(pS[:, hp * nq:(hp + 1) * nq], lhsT=kct[h * 64:(h + 1) * 64, hp, :],
                                              rhs=QS[h * 64:(h + 1) * 64, g * 8 + hp, sl(q0, nq, qstep)], start=True, stop=True)
                        return rr
                    P.op("pe", st, reads=(f"KCT{b}", "QS"), writes=(kSa, kSb))
                    es4 = es[:, 0:16 * nq].rearrange("p (a h q) -> p a h q", a=8, h=2)
                    P.op("act", lambda e, es4=es4, pSa=pSa: e.activation(out=es4[:, :, 0, :], in_=pSa[:, 0:8 * nq].rearrange("p (a q) -> p a q", a=8),
                                                                      func=AF.Exp, scale=SCALE), reads=(kSa,), writes=(f"ES{b}",))
                    P.op("act", lambda e, es4=es4, pSb=pSb: e.activation(out=es4[:, :, 1, :], in_=pSb[:, 0:8 * nq].rearrange("p (a q) -> p a q", a=8),
                                                                      func=AF.Exp, scale=SCALE), reads=(kSb, f"ES{b}"), writes=(f"ES{b}",))
                    if g == 0:
                        msk = self.smask[:, 0:128]
                    elif g == 1:
                        msk = self.smask[:, 128:160]
                    else:
                        msk = None
                    if msk is not None:
                        P.op("dve", lambda e, pb=pb, es=es, msk=msk: e.tensor_tensor(out=pb[:, 0:16 * nq], in0=es[:, 0:16 * nq], in1=msk, op=ALU.mult),
                             reads=(f"ES{b}", "smask"), writes=(f"PS{b}",))
                    else:
                        P.op("dve", lambda e, pb=pb, es=es: e.tensor_copy(out=pb[:, 0:16 * nq], in_=es[:, 0:16 * nq]),
                             reads=(f"ES{b}",), writes=(f"PS{b}",))
                    return None

                def back(ctx, s=s, g=g, nq=nq, q0=q0, qstep=qstep, b=b):
                    vc, pb = VC[b], PS_[b]
                    pU, kU = self.ps()

                    def pv(e, pU=pU, pb=pb, vc=vc):
                        for hp in range(8):
                            for h in range(2):
                                col = (hp * 2 + h) * nq
                                e.matmul(pU[0:64, col:col + nq], lhsT=vc[:, hp * 128 + h * 64:hp * 128 + (h + 1) * 64], rhs=pb[:, col:col + nq], start=True, stop=True)
                                rr = e.matmul(pU[64:128, col:col + nq], lhsT=self.ONES[:, 0:64], rhs=pb[:, col:col + nq], start=True, stop=True)
                        return rr
                    P.op("pe", pv, reads=(f"PS{b}", f"VC{b}", "cb"), writes=(kU,))
                    sv = sacc[:, :, :, sl(q0, nq, qstep)].rearrange("p a h q -> p (a h) q")
                    P.op("dve", lambda e, sv=sv, pU=pU: e.tensor_tensor(out=sv, in0=pU[:, 0:16 * nq].rearrange("p (a q) -> p a q", a=16), in1=sv, op=ALU.add),
                         reads=(kU, "sacc"), writes=("sacc",))
                run(front, back)
            b = it % 2
            it += 1

            def frontn(s=s, g=g, b=b):
                es, pb = ES[b], PS_[b]
                pSa, kSa = self.ps()
                pSb, kSb = self.ps()

                def stn(e, pSa=pSa, pSb=pSb):
                    for h, pS in enumerate((pSa, pSb)):
                        for hp in range(8):
                            rr = e.matmul(pS[0:8, hp * 8:(hp + 1) * 8], lhsT=KN[h * 64:(h + 1) * 64, g * 8 + hp, s * 8:s * 8 + 8],
                                          rhs=QS[h * 64:(h + 1) * 64, g * 8 + hp, s * 8:s * 8 + 8], start=True, stop=True)
                    return rr
                P.op("pe", stn, reads=("KN", "QS"), writes=(kSa, kSb))
                esn = es[0:8, :].rearrange("p (a h q) -> p a h q", a=8, h=2)
                P.op("act", lambda e, esn=esn, pSa=pSa: e.activation(out=esn[:, :, 0, :], in_=pSa[0:8, 0:64].rearrange("p (a q) -> p a q", a=8),
                                                                  func=AF.Exp, scale=SCALE), reads=(kSa,), writes=(f"ES{b}",))
                P.op("act", lambda e, esn=esn, pSb=pSb: e.activation(out=esn[:, :, 1, :], in_=pSb[0:8, 0:64].rearrange("p (a q) -> p a q", a=8),
                                                                  func=AF.Exp, scale=SCALE), reads=(kSb, f"ES{b}"), writes=(f"ES{b}",))
                mo = 160 + g * 128
                P.op("dve", lambda e, pb=pb, es=es, mo=mo: e.tensor_tensor(out=pb[0:8, :], in0=es[0:8, :], in1=self.smask[0:8, mo:mo + 128], op=ALU.mult),
                     reads=(f"ES{b}", "smask"), writes=(f"PS{b}",))
                return None

            def backn(ctx, s=s, g=g, b=b):
                pb = PS_[b]
                pU, kU = self.ps()

                def pvn(e, pU=pU, pb=pb):
                    for hp in range(8):
                        for h in range(2):
                            col = (hp * 2 + h) * 8
                            e.matmul(pU[0:64, col:col + 8], lhsT=VN[g][0:8, hp * 4 + s, h * 64:(h + 1) * 64], rhs=pb[0:8, col:col + 8], start=True, stop=True)
                            rr = e.matmul(pU[64:128, col:col + 8], lhsT=self.ONES[0:8, 0:64], rhs=pb[0:8, col:col + 8], start=True, stop=True)
                    return rr
                P.op("pe", pvn, reads=(f"PS{b}", f"VN{g}", "cb"), writes=(kU,))
                sv = sacc[:, :, :, s * 8:s * 8 + 8].rearrange("p a h q -> p (a h) q")
                P.op("dve", lambda e, sv=sv, pU=pU: e.tensor_tensor(out=sv, in0=pU[:, 0:128].rearrange("p (a q) -> p a q", a=16), in1=sv, op=ALU.add),
                     reads=(kU, "sacc"), writes=("sacc",))
            run(frontn, backn)
    if pend is not None:
        pend[0](pend[1])
    P.fence()


def _sample_attn(self, j, hp):
    acc = self.att["acc"]
    self.P.op("dve", lambda e, hp=hp: e.tensor_copy(out=acc[:, :, HALF:NT], in_=self.sacc[:, hp, :, :]), reads=("sacc",), writes=("acc",))


Builder.sample_stage = _sample_stage
Builder.sample_attn = _sample_attn


def _conv_outputs(self, L):
    P = self.P
    st, sk = self.stage()
    for half in range(2):
        pt, pk = self.ps()

        def f(e, half=half, pt=pt):
            for cc in range(4):
                c = half * 4 + cc
                r = e.transpose(out=pt[0:30, cc * 128:(cc + 1) * 128], in_=self.g32[:, L, c, :], identity=self.ident[:, :])
            return r
        P.op("pe", f, reads=("g32", "ident"), writes=(pk,))
        P.op("act", lambda e, half=half, pt=pt, st=st: e.activation(out=st[0:30, half * 512:(half + 1) * 512], in_=pt[0:30, :], func=AF.Copy),
             reads=(pk,), writes=(sk,))
    P.dma("sp", sk, lambda e, st=st: [e.dma_start(out=self.dout["convp"][L, :, :], in_=st[0:30, :])], reads=(sk,))
    P.dma("sp", "d2d", lambda e: [e.dma_start(out=self.dout["convs"][L, :, 0:22, :], in_=self.din["cconv"][L, :, 8:30, :])])
    st, sk = self.stage()
    for half in range(2):
        pt, pk = self.ps()

        def f2(e, half=half, pt=pt):
            for cc in range(4):
                c = half * 4 + cc
                r = e.transpose(out=pt[0:NS, cc * 128:(cc + 1) * 128], in_=self.gs32[:, L, c, :], identity=self.ident[:, :])
            return r
        P.op("pe", f2, reads=("gs32", "ident"), writes=(pk,))
        P.op("act", lambda e, half=half, pt=pt, st=st: e.activation(out=st[0:NS, half * 512:(half + 1) * 512], in_=pt[0:NS, :], func=AF.Copy),
             reads=(pk,), writes=(sk,))

    def fs(e, st=st):
        return [e.dma_start(out=self.dout["convs"][L, s, 22:30, :], in_=st[s * 8:(s + 1) * 8, :]) for s in range(4)]
    P.dma("sp", sk, fs, reads=(sk,), n=4)


def _build_full(self):
    import os
    STOP = os.environ.get("KSTOP", "")
    P = self.P
    self.setup()
    xin = self.din["xin"]
    ck = [self.din["ckv0"], self.din["ckv1"], self.din["ckv2"]]
    for g in range(3):
        W = WINS[g]
        if os.environ.get("NO_D2D"):
            break
        for s in range(4):
            for r0 in range(0, W - 8, 512):
                nr = min(512, W - 8 - r0)
                P.dma("act", "d2d", lambda e, g=g, s=s, r0=r0, nr=nr: [e.dma_start(out=self.dout[f"kvs{g}"][s, r0:r0 + nr, :],
                                                                                  in_=ck[g][s, 8 + r0:8 + r0 + nr, :])])
    for sup in range(2):
        isY = sup == 1 and STOP != "XX"
        tiles = TILES_Y if isY else TILES_X
        for tb in range(16):
            self.load_tokens(xin[sup * HALF + tb * 128: sup * HALF + (tb + 1) * 128, :], 128, tb * 128)
        if isY:
            self.load_tokens(self.din["xs"][:, :], NS, HALF)
        for L in range(2):
            self.conv_layer(L, isY)
            if isY and not os.environ.get("NO_CONVOUT"):
                self.conv_outputs(L)
            self.ffn(L, tiles)
        self.kv_stage(isY)
        if STOP == "X":
            break
    for j in range(2):
        if STOP in ("X", "KV", "XX"):
            break
        self.attn_layer(j, STOP)
        if STOP in ("Q", "S", "A"):
            break
        self.ffn(2 + j, TILES_Y)
    for tb in range(16):
        self.store_tokens(self.dout["y"][tb * 128:(tb + 1) * 128, :], 128, tb * 128)
    self.store_tokens(self.dout["y"][HALF:HALF + NS, :], NS, HALF)


Builder.conv_outputs = _conv_outputs
Builder.build_full = _build_full

_CACHE = {}


def kernel(**inputs):
    if "nc" not in _CACHE:
        B = Builder()
        B.build_full()
        _CACHE["nc"] = B.finish()
        _CACHE["names"] = set(B.din.keys())
    nc = _CACHE["nc"]
    maps = make_in_maps(inputs)
    maps = [{k: v for k, v in m.items() if k in _CACHE["names"]} for m in maps]
    res = run_bass_kernel_spmd(nc, maps, core_ids=list(range(8)))
    R = res.results
    f32 = np.float32
    y_prompt = np.zeros((4, SEQ, D), f32)
    y_sample = np.zeros((32, 8, D), f32)
    conv_p = np.zeros((2, 4, 30, D), f32)
    conv_s = np.zeros((2, 32, 30, D), f32)
    kvp = [np.zeros((4, W, 2, 16, 64), f32) for W in WINS]
    kvs = [np.zeros((32, W, 2, 16, 64), f32) for W in WINS]
    for c in range(8):
        b, h = c // 2, c % 2
        r = R[c]
        y_prompt[b, h * HALF:(h + 1) * HALF] = r["y"][:HALF]
        y_sample[4 * c:4 * c + 4] = r["y"][HALF:].reshape(4, 8, D)
        conv_s[:, 4 * c:4 * c + 4] = r["convs"]
        for g in range(3):
            kvs[g][4 * c:4 * c + 4] = r[f"kvs{g}"].reshape(4, WINS[g], 2, 16, 64)
        if h == 1:
            conv_p[:, b] = r["convp"]
            for g in range(3):
                kvp[g][b] = r[f"kvp{g}"].reshape(WINS[g], 2, 16, 64)
    return (y_prompt, y_sample, conv_p, conv_s, kvp[0], kvp[1], kvp[2], kvs[0], kvs[1], kvs[2])
```

```python
import numpy as np
import ml_dtypes
import concourse.bass as bass
import concourse.mybir as mybir
from concourse.bass_utils import run_bass_kernel_spmd

F32 = mybir.dt.float32
BF16 = mybir.dt.bfloat16
AF = mybir.ActivationFunctionType
ALU = mybir.AluOpType

D = 1024
NCH = 8
SEQ = 4096
HALF = 2048
NS = 32
NT = HALF + NS
DFF = 2816
NFC = 22
CW = 31
EPS = 1e-6
PAST = 8192
WINS = (128, 512, 2048)
DILS = (1, 4, 16)
KTW = SEQ + NS

ENGS = ["pe", "act", "dve", "pool", "sp"]
SQK = tuple(f"sqb{i}" for i in range(8))


class Prog:
    def __init__(self):
        self.ops = {e: [] for e in ENGS}
        self.last_w = {}
        self.readers = {}
        self.chan_n = {}

    def _deps(self, reads, writes):
        deps = []
        for k in reads:
            t = self.last_w.get(k)
            if t is not None:
                deps.append(t)
        for k in writes:
            t = self.last_w.get(k)
            if t is not None:
                deps.append(t)
            deps.extend(self.readers.get(k, ()))
        return deps

    def _commit(self, tok, reads, writes):
        for k in reads:
            lst = self.readers.setdefault(k, [])
            src = tok[:2]
            lst[:] = [t for t in lst if t[:2] != src]
            lst.append(tok)
        for k in writes:
            self.last_w[k] = tok
            self.readers[k] = []

    def op(self, eng, fn, reads=(), writes=()):
        reads = tuple(reads)
        writes = tuple(writes) + tuple(k for k in reads if k.startswith("ps") and k[2:].isdigit())
        idx = len(self.ops[eng])
        tok = ("e", eng, idx)
        deps = [t for t in self._deps(reads, writes) if not (t[0] == "e" and t[1] == eng and (eng == "pe" or t[2] == idx))]
        deps += self._take_fence(eng, idx)
        for t in deps:
            if t[0] == "e":
                self.ops[t[1]][t[2]]["signal"] = True
        self.ops[eng].append({"fn": fn, "deps": deps, "signal": False, "dma": None})
        self._commit(tok, reads, writes)
        return tok

    def dma(self, queue, chan, fn, reads=(), writes=(), n=1):
        cnt = self.chan_n.get(chan, 0) + n
        self.chan_n[chan] = cnt
        tok = ("d", chan, cnt)
        deps = list(self._deps(reads, writes))
        deps += self._take_fence(queue, len(self.ops[queue]))
        for t in deps:
            if t[0] == "e":
                self.ops[t[1]][t[2]]["signal"] = True
        self.ops[queue].append({"fn": fn, "deps": deps, "signal": False, "dma": chan})
        self._commit(tok, reads, writes)
        return tok

    def fence(self):
        deps = [("e", E, len(self.ops[E]) - 1) for E in ENGS if self.ops[E] and self.ops[E][-1]["dma"] is None]
        for E in ENGS:
            if self.ops[E] and self.ops[E][-1]["dma"] is not None:
                for i in range(len(self.ops[E]) - 1, -1, -1):
                    if self.ops[E][i]["dma"] is None:
                        deps.append(("e", E, i))
                        break
        deps += [("d", ch, n) for ch, n in self.chan_n.items()]
        self.pending = {E: list(deps) for E in ENGS}

    def _take_fence(self, eng, idx):
        pend = getattr(self, "pending", None)
        if not pend or not pend.get(eng):
            return []
        d = pend[eng]
        pend[eng] = []
        return [t for t in d if not (t[0] == "e" and t[1] == eng and eng == "pe")]

    def emit(self, nc, block_engines, sems, chan_sems):
        sigcount = {}
        for e in ENGS:
            c = 0
            arr = []
            for o in self.ops[e]:
                if o["signal"] and o["dma"] is None:
                    c += 1
                arr.append(c)
            sigcount[e] = arr
        prog = self

        def make(e):
            def body(eng):
                seen = {}
                for o in prog.ops[e]:
                    for t in o["deps"]:
                        if t[0] == "e":
                            src = ("e", t[1])
                            cnt = sigcount[t[1]][t[2]]
                            sem = sems[t[1]]
                        else:
                            src = ("d", t[1])
                            cnt = 16 * t[2]
                            sem = chan_sems[t[1]]
                        if seen.get(src, 0) >= cnt:
                            continue
                        seen[src] = cnt
                        eng.wait_ge(sem, cnt)
                    r = o["fn"](eng)
                    if o["dma"] is not None:
                        for ins in r:
                            ins.then_inc(chan_sems[o["dma"]], 16)
                    elif o["signal"]:
                        r.then_inc(sems[e], 1)
                if e == "sp":
                    for ch, n in prog.chan_n.items():
                        if seen.get(("d", ch), 0) < 16 * n:
                            eng.wait_ge(chan_sems[ch], 16 * n)
            return body

        for e in ENGS:
            block_engines[e](make(e))


class Mem:
    def __init__(self, nc, base, size):
        self.nc = nc
        self.base = base
        self.size = size
        self.cur = 0
        self.n = 0
        Mem._gn = getattr(Mem, "_gn", 0) + 1000
        self.n = Mem._gn

    def alloc(self, shape, dtype, name=None):
        nbytes = int(np.prod(shape[1:])) * (4 if dtype == F32 else 2)
        nbytes = (nbytes + 63) // 64 * 64
        off = self.cur
        self.cur += nbytes
        assert self.cur <= self.size, f"SBUF overflow {self.cur} > {self.size} at {name}"
        self.n += 1
        self.offs = getattr(self, "offs", {})
        self.offs[name] = (self.base + off, nbytes)
        return self._reg(name, self.nc.alloc_sbuf_tensor_at(f"{name or 't'}_{self.n}", list(shape), dtype, offset=self.base + off))

    def _reg(self, name, h):
        self.names = getattr(self, "names", {})
        self.names[name] = h.name
        return h

    def mark(self):
        return self.cur

    def reset(self, m):
        self.cur = m


TILES_X = [(i * 512, 512) for i in range(4)]
TILES_Y = TILES_X + [(HALF, NS)]
GW = 30 + HALF + 4 * 38
GS0 = 30 + HALF


class Builder:
    def __init__(self, stop_after=None):
        self.stop_after = stop_after
        nc = self.nc = bass.Bass("TRN2", target_bir_lowering=False)
        self.P = Prog()
        self.din = {}
        self.dout = {}
        self._uid = 0

    def inp(self, name, shape, dtype=F32):
        self.din[name] = self.nc.dram_tensor(name, list(shape), dtype, kind="ExternalInput").ap()
        return self.din[name]

    def outp(self, name, shape, dtype=F32):
        self.dout[name] = self.nc.dram_tensor(name, list(shape), dtype, kind="ExternalOutput").ap()
        return self.dout[name]

    @staticmethod
    def uk(tk):
        return tuple(f"u:{tk}:{c}" for c in range(NCH))

    def uid(self):
        self._uid += 1
        return self._uid

    def ps(self):
        i = self._psi
        self._psi = (i + 1) % 8
        return self.psb[i], f"ps{i}"

    def wslot(self):
        i = self._wi
        self._wi = (i + 1) % len(self.wbufs)
        return self.wbufs[i], f"w{i}"

    def t32(self):
        i = self._t32i
        self._t32i = (i + 1) % len(self.T32)
        return self.T32[i], f"t32_{i}"

    def s32(self):
        i = self._s32i
        self._s32i = (i + 1) % len(self.S32)
        return self.S32[i], f"s32_{i}"

    def stage(self):
        i = self._stgi
        self._stgi = (i + 1) % 2
        return self.stg[i], f"stg{i}"

    def load_w(self, src_ap, shape_view):
        buf, key = self.wslot()
        a, b = shape_view
        dst = buf[:, 0:a * b].rearrange("p (a b) -> p a b", a=a)
        self.P.dma("pool", key, lambda e, dst=dst, src=src_ap: [e.dma_start(out=dst, in_=src)],
                   reads=(), writes=(key,))
        return dst, key

    PV_ROWS = {}

    @staticmethod
    def pv_layout():
        rows = {}
        n = 0

        def add(name, cnt=1):
            nonlocal n
            rows[name] = n
            n += cnt
        for L in range(2):
            add(f"a_norm{L}")
            add(f"b1_{L}")
            add(f"b2_{L}")
            add(f"wdw{L}", CW)
            add(f"b_dw{L}")
            add(f"ln_g{L}")
            add(f"ln_b{L}")
            add(f"b_out{L}")
        add("kv_norm")
        for j in range(2):
            add(f"b_norm{j}")
        for L in range(4):
            add(f"ffn_norm{L}")
        for g in range(3):
            add(f"k_norm{g}")
        for j in range(2):
            for g in range(3):
                add(f"q_norm{j}_{g}")
        return rows, n

    def pvs(self, name, c, off=0):
        i = self.pvrows[name] + off
        return self.pv[:, c, i:i + 1]

    def setup(self):
        nc = self.nc
        P = self.P
        self.pvrows, self.npv = self.pv_layout()
        self.inp("xin", [SEQ, D])
        self.inp("xs", [NS, D])
        self.inp("cconv", [2, 4, 30, D])
        self.inp("ckv0", [4, 128, 2048])
        self.inp("ckv1", [4, 512, 2048])
        self.inp("ckv2", [4, 2048, 2048])
        self.inp("pvec", [self.npv, D])
        self.inp("a_w_in", [2, D, 2 * D])
        self.inp("a_w_out", [2, D, D])
        self.inp("w_kv", [D, 6 * D])
        self.inp("w_q", [2, D, 3 * D])
        self.inp("w_o", [2, D, D])
        self.inp("w_gate_up", [4, D, 2 * DFF])
        self.inp("w_down", [4, DFF, D])
        self.inp("cmat", [6, 128, 128])
        self.inp("rot", [2, 128, KTW])
        self.inp("flag", [128, 1])
        self.inp("smaskin", [128, 544])
        self.outp("y", [NT, D])
        self.outp("convp", [2, 30, D])
        self.outp("convs", [2, 4, 30, D])
        self.outp("kvp0", [128, 2048])
        self.outp("kvp1", [512, 2048])
        self.outp("kvp2", [2048, 2048])
        self.outp("kvs0", [4, 128, 2048])
        self.outp("kvs1", [4, 512, 2048])
        self.outp("kvs2", [4, 2048, 2048])
        self.kt_scr = nc.dram_tensor("kt_scr", [3, 8, 128, KTW], BF16, kind="Internal").ap()
        self.v_scr = nc.dram_tensor("v_scr", [3, 8, KTW, 128], BF16, kind="Internal").ap()
        self.q_scr = nc.dram_tensor("q_scr", [3, 8, 128, NT], BF16, kind="Internal").ap()

        total = nc.sbuf_bytes_remaining - 64
        arena = nc.alloc_sbuf_tensor("arena", [128, total // 4], F32)
        base = nc.lookup_mloc(arena).addr
        self.M = M = Mem(nc, base, (total // 4) * 4)
        self.x = M.alloc([128, NCH, NT], F32, "x")
        self.u = M.alloc([128, NCH, NT], BF16, "u")
        self.G = M.alloc([128, NCH, GW], BF16, "glu")
        self.pv = M.alloc([128, NCH, self.npv], F32, "pv")
        self.ident = M.alloc([128, 128], F32, "ident")
        self.cb = M.alloc([128, 6, 128], BF16, "cb")
        self.maskf = M.alloc([128, 512], F32, "maskf")
        self.flag = M.alloc([128, 1], F32, "flag")
        self.sacc = M.alloc([128, 8, 2, NS], F32, "sacc")
        self.smask = M.alloc([128, 544], BF16, "smask")
        self.gstate = M.alloc([128, 2, NCH, 30], BF16, "gstate")
        self.g32 = M.alloc([128, 2, NCH, 30], F32, "g32")
        self.gs32 = M.alloc([128, 2, NCH, NS], F32, "gs32")
        self.wall = M.alloc([128, 6 * 2048], BF16, "wall")
        self.wbufs = [self.wall[:, i * 2048:(i + 1) * 2048] for i in range(6)]
        self.sqb = M.alloc([128, NCH, 512], BF16, "sqb")
        self.T32 = [M.alloc([128, 512], F32, f"t32_{i}") for i in range(4)]
        self._t32i = 0
        self.S32 = [M.alloc([128, 512], F32, f"s32_{i}") for i in range(4)]
        self._s32i = 0
        self.stg = [M.alloc([128, D], F32, f"stg{i}") for i in range(2)]
        self._stgi = 0
        self.hbuf = [M.alloc([128, 512], BF16, f"hbuf{i}") for i in range(4)]
        self._hi = 0
        self._di = 0
        self._sqi = 0
        self.persist_mark = M.mark()
        print("SBUF used", M.cur, "of", M.size)

        self.psb = [nc.alloc_psum_tensor(f"psb{i}", [128, 512], F32) for i in range(8)]
        self._psi = 0
        self._wi = 0

        cst = self.stg[1][:, 0:768].rearrange("p (a b) -> p a b", a=6)
        P.dma("sp", "stg1", lambda e: [e.dma_start(out=cst, in_=self.din["cmat"].rearrange("a p n -> p a n"))],
              writes=("stg1",))
        P.dma("sp", "c1", lambda e: [e.dma_start(out=self.flag[:], in_=self.din["flag"][:, :])], writes=("flag",))
        P.dma("pool", "c3", lambda e: [e.dma_start(out=self.smask[:], in_=self.din["smaskin"][:, :])], writes=("smask",))
        P.op("dve", lambda e: e.tensor_copy(out=self.cb[:], in_=cst), reads=("stg1",), writes=("cb",))
        P.op("dve", lambda e: e.tensor_copy(out=self.ident[:], in_=cst[:, 0, :]), reads=("stg1",), writes=("ident",))
        for q, src in enumerate([4, 4, 5, 5]):
            P.op("dve", lambda e, q=q, src=src: e.tensor_copy(out=self.maskf[:, q * 128:(q + 1) * 128], in_=cst[:, src, :]),
                 reads=("stg1",), writes=("maskf",))
        self.I_BF = self.cb[:, 0, :]
        self.ONES = self.cb[:, 1, :]
        self.BONES = self.cb[:, 2, :]
        self.SWAP = self.cb[:, 3, :]
        pst = self.stg[0]
        npv = self.npv
        P.dma("sp", "stg0", lambda e: [e.dma_start(out=pst[0:npv, :], in_=self.din["pvec"][:, :])], writes=("stg0",))
        for half in range(2):
            pt, pk = self.ps()

            def f(e, half=half, pt=pt):
                for cc in range(4):
                    c = half * 4 + cc
                    r = e.transpose(out=pt[:, cc * 128:cc * 128 + npv], in_=pst[0:npv, c * 128:(c + 1) * 128],
                                    identity=self.ident[0:npv, 0:npv])
                return r
            P.op("pe", f, reads=("stg0", "ident"), writes=(pk,))
            P.op("act", lambda e, half=half, pt=pt: e.activation(
                out=self.pv[:, half * 4:half * 4 + 4, :],
                in_=pt[:, :].rearrange("p (a b) -> p a b", a=4)[:, :, 0:npv], func=AF.Copy),
                reads=(pk,), writes=("pv",))
        P.op("dve", lambda e: e.memset(self.G[:, :, 0:30], 0.0), writes=("G:state",))

    def load_tokens(self, src_ap, nrows, col0):
        P = self.P
        st, sk = self.stage()
        P.dma("sp", sk, lambda e: [e.dma_start(out=st[0:nrows, :], in_=src_ap)], writes=(sk,))
        xkey = f"x:{col0 // 512}"
        for half in range(2):
            pt, pk = self.ps()

            def f(e, half=half, pt=pt):
                for cc in range(4):
                    c = half * 4 + cc
                    r = e.transpose(out=pt[:, cc * 128:cc * 128 + nrows], in_=st[0:nrows, c * 128:(c + 1) * 128],
                                    identity=self.ident[0:nrows, 0:nrows])
                return r
            P.op("pe", f, reads=(sk, "ident"), writes=(pk,))
            P.op("act", lambda e, half=half, pt=pt: e.activation(
                out=self.x[:, half * 4:half * 4 + 4, col0:col0 + nrows],
                in_=pt[:, :].rearrange("p (a b) -> p a b", a=4)[:, :, 0:nrows], func=AF.Copy),
                reads=(pk,), writes=(xkey,))

    def store_tokens(self, dst_ap, nrows, col0, src=None, skey=None):
        P = self.P
        src = self.x if src is None else src
        skey = f"x:{col0 // 512}" if skey is None else skey
        st, sk = self.stage()
        for half in range(2):
            pt, pk = self.ps()

            def f(e, half=half, pt=pt):
                for cc in range(4):
                    c = half * 4 + cc
                    r = e.transpose(out=pt[0:nrows, cc * 128:(cc + 1) * 128], in_=src[:, c, col0:col0 + nrows],
                                    identity=self.ident[:, :])
                return r
            P.op("pe", f, reads=(skey, "ident"), writes=(pk,))
            P.op("act", lambda e, half=half, pt=pt: e.activation(
                out=st[0:nrows, half * 512:(half + 1) * 512], in_=pt[0:nrows, :], func=AF.Copy),
                reads=(pk,), writes=(sk,))
        P.dma("sp", sk, lambda e: [e.dma_start(out=dst_ap, in_=st[0:nrows, :])], reads=(sk,))

    def rmsnorm(self, tiles, gname):
        P = self.P
        rs = {}
        for (c0, n) in tiles:
            tk = c0 // 512
            P.op("act", lambda e, c0=c0, n=n: e.activation(out=self.sqb[:, :, 0:n], in_=self.x[:, :, c0:c0 + n], func=AF.Square),
                 reads=(f"x:{tk}",), writes=SQK)
            pt, pk = self.ps()

            def f(e, pt=pt, n=n):
                for c in range(NCH):
                    r = e.matmul(pt[:, 0:n], lhsT=self.ONES, rhs=self.sqb[:, c, 0:n], start=(c == 0), stop=(c == NCH - 1))
                return r
            P.op("pe", f, reads=SQK + ("cb",), writes=(pk,))
            t, tkey = self.s32()
            P.op("act", lambda e, pt=pt, t=t, n=n: e.activation(out=t[:, 0:n], in_=pt[:, 0:n], func=AF.Ln, bias=EPS, scale=1.0 / D),
                 reads=(pk,), writes=(tkey,))
            P.op("act", lambda e, t=t, n=n: e.activation(out=t[:, 0:n], in_=t[:, 0:n], func=AF.Exp, scale=-0.5), reads=(tkey,), writes=(tkey,))

            def g(e, t=t, c0=c0, n=n):
                for c in range(NCH):
                    r = e.scalar_tensor_tensor(out=self.u[:, c, c0:c0 + n], in0=self.x[:, c, c0:c0 + n],
                                               scalar=self.pvs(gname, c), in1=t[:, 0:n], op0=ALU.mult, op1=ALU.mult)
                return r
            P.op("dve", g, reads=(tkey, f"x:{tk}", "pv"), writes=self.uk(tk))

    def conv_layer(self, L, isY, tiles=None):
        P = self.P
        if tiles is None:
            tiles = TILES_Y if isY else TILES_X
        w_in = self.din["a_w_in"]
        w_out = self.din["a_w_out"]
        G = self.G
        self.rmsnorm(tiles, f"a_norm{L}")
        if isY:
            P.op("dve", lambda e: e.tensor_copy(out=G[:, :, 0:30], in_=self.gstate[:, L, :, :]),
                 reads=("gstate",), writes=("G:state",))
            for s in range(4):
                st, sk = self.stage()
                P.dma("sp", sk, lambda e, st=st, s=s: [e.dma_start(out=st[0:30, :], in_=self.din["cconv"][L, s, :, :])], writes=(sk,))
                for half in range(2):
                    pt, pk = self.ps()

                    def f(e, half=half, pt=pt, st=st):
                        for cc in range(4):
                            c = half * 4 + cc
                            r = e.transpose(out=pt[:, cc * 128:cc * 128 + 30], in_=st[0:30, c * 128:(c + 1) * 128],
                                            identity=self.ident[0:30, 0:30])
                        return r
                    P.op("pe", f, reads=(sk, "ident"), writes=(pk,))
                    P.op("act", lambda e, half=half, pt=pt, s=s: e.activation(
                        out=G[:, half * 4:half * 4 + 4, GS0 + s * 38:GS0 + s * 38 + 30],
                        in_=pt[:, :].rearrange("p (a b) -> p a b", a=4)[:, :, 0:30], func=AF.Copy),
                        reads=(pk,), writes=("G:s",))
        for oc in range(NCH):
            wv, wk = self.wslot()
            wa = wv[:, 0:2048].rearrange("p (h k n) -> p h k n", h=2, k=NCH)
            src1 = w_in[L, :, oc * 128:(oc + 1) * 128].rearrange("(k p) n -> p k n", p=128)
            src2 = w_in[L, :, D + oc * 128:D + (oc + 1) * 128].rearrange("(k p) n -> p k n", p=128)
            P.dma("pool", wk, lambda e, wa=wa, src1=src1, src2=src2: [e.dma_start(out=wa[:, 0], in_=src1), e.dma_start(out=wa[:, 1], in_=src2)],
                  writes=(wk,), n=2)
            for (c0, n) in tiles:
                tk = c0 // 512
                p1, k1 = self.ps()
                p2, k2 = self.ps()

                def mm(e, p1=p1, p2=p2, wa=wa, c0=c0, n=n):
                    for k in range(NCH):
                        e.matmul(p1[:, 0:n], lhsT=wa[:, 0, k, :], rhs=self.u[:, k, c0:c0 + n], start=(k == 0), stop=(k == NCH - 1))
                    for k in range(NCH):
                        r = e.matmul(p2[:, 0:n], lhsT=wa[:, 1, k, :], rhs=self.u[:, k, c0:c0 + n], start=(k == 0), stop=(k == NCH - 1))
                    return r
                P.op("pe", mm, reads=(wk,) + self.uk(tk), writes=(k1, k2))
                sg, sgk = self.t32()
                P.op("act", lambda e, sg=sg, p2=p2, n=n, oc=oc: e.activation(out=sg[:, 0:n], in_=p2[:, 0:n], func=AF.Sigmoid,
                                                                       bias=self.pvs(f"b2_{L}", oc), scale=1.0),
                     reads=(k2, "pv"), writes=(sgk,))
                if n == 512:
                    gout = G[:, oc, 30 + c0:30 + c0 + n]
                    gkey = f"G:{tk}"
                else:
                    gout = G[:, oc, GS0:GS0 + 4 * 38].rearrange("p (s t) -> p s t", s=4)[:, :, 30:38]
                    gkey = "G:s"

                def glu(e, sg=sg, p1=p1, n=n, oc=oc, gout=gout, c0=c0):
                    if n == 512:
                        r = e.scalar_tensor_tensor(out=gout, in0=p1[:, 0:n], scalar=self.pvs(f"b1_{L}", oc), in1=sg[:, 0:n],
                                                   op0=ALU.add, op1=ALU.mult)
                        if c0 == 1536:
                            if isY:
                                r = e.scalar_tensor_tensor(out=self.g32[:, L, oc, :], in0=p1[:, 482:512], scalar=self.pvs(f"b1_{L}", oc),
                                                           in1=sg[:, 482:512], op0=ALU.add, op1=ALU.mult)
                            else:
                                r = e.scalar_tensor_tensor(out=self.gstate[:, L, oc, :], in0=p1[:, 482:512], scalar=self.pvs(f"b1_{L}", oc),
                                                           in1=sg[:, 482:512], op0=ALU.add, op1=ALU.mult)
                    else:
                        e.scalar_tensor_tensor(out=gout, in0=p1[:, 0:n].rearrange("p (s t) -> p s t", s=4), scalar=self.pvs(f"b1_{L}", oc),
                                               in1=sg[:, 0:n].rearrange("p (s t) -> p s t", s=4), op0=ALU.add, op1=ALU.mult)
                        r = e.scalar_tensor_tensor(out=self.gs32[:, L, oc, :], in0=p1[:, 0:n], scalar=self.pvs(f"b1_{L}", oc),
                                                   in1=sg[:, 0:n], op0=ALU.add, op1=ALU.mult)
                    return r
                wr = [gkey]
                if c0 == 1536:
                    wr.append("g32" if isY else "gstate")
                if n != 512:
                    wr.append("gs32")
                P.op("dve", glu, reads=(k1, sgk, "pv", "flag"), writes=tuple(wr))
                if c0 == 1536 and not isY:
                    P.op("dve", lambda e, oc=oc: e.tensor_scalar(out=self.gstate[:, L, oc, :], in0=self.gstate[:, L, oc, :],
                                                                 scalar1=self.flag[:, 0:1], scalar2=None, op0=ALU.mult),
                         reads=("gstate", "flag"), writes=("gstate",))
        gkeys_main = tuple(f"G:{i}" for i in range(4)) + ("G:state",)
        for c in range(NCH):
            i1 = 2 * self._di
            self._di = (self._di + 1) % 3
            sk1, sk2 = f"w{i1}", f"w{i1 + 1}"
            dg = self.wall[:, i1 * 2048:i1 * 2048 + CW * 128].rearrange("p (j n) -> p j n", j=CW)

            def mk(e, dg=dg, c=c):
                for j in range(CW):
                    r = e.activation(out=dg[:, j, :], in_=self.I_BF, func=AF.Identity, scale=self.pvs(f"wdw{L}", c, j))
                return r
            P.op("act", mk, reads=("cb", "pv"), writes=(sk1, sk2))
            for (c0, n) in tiles:
                tk = c0 // 512
                pt, pk = self.ps()
                if n == 512:
                    def cv(e, dg=dg, c=c, c0=c0, pt=pt):
                        for j in range(CW):
                            r = e.matmul(pt[:, 0:512], lhsT=dg[:, j, :], rhs=G[:, c, c0 + j:c0 + j + 512], start=(j == 0), stop=(j == CW - 1))
                        return r
                    rd = (sk1, sk2) + gkeys_main
                else:
                    def cv(e, dg=dg, c=c, pt=pt):
                        gsv = G[:, c, GS0:GS0 + 4 * 38].rearrange("p (s t) -> p s t", s=4)
                        for j in range(CW):
                            r = e.matmul(pt[:, 0:NS].rearrange("p (s t) -> p s t", s=4), lhsT=dg[:, j, :], rhs=gsv[:, :, j:j + 8],
                                         start=(j == 0), stop=(j == CW - 1))
                        return r
                    rd = (sk1, sk2, "G:s")
                P.op("pe", cv, reads=rd, writes=(pk,))
                P.op("act", lambda e, pt=pt, c=c, c0=c0, n=n: e.activation(out=self.u[:, c, c0:c0 + n], in_=pt[:, 0:n], func=AF.Identity,
                                                                     bias=self.pvs(f"b_dw{L}", c), scale=1.0),
                     reads=(pk, "pv"), writes=(f"u:{tk}:{c}",))
        if getattr(self, "stop_stage", None) == "B":
            return
        for (c0, n) in tiles:
            tk = c0 // 512
            ukey = self.uk(tk)
            P.op("act", lambda e, c0=c0, n=n: e.activation(out=self.sqb[:, :, 0:n], in_=self.u[:, :, c0:c0 + n], func=AF.Square),
                 reads=ukey, writes=SQK)
            psm, ksm = self.ps()
            psq, ksq = self.ps()

            def st(e, psm=psm, psq=psq, c0=c0, n=n):
                for c in range(NCH):
                    e.matmul(psm[:, 0:n], lhsT=self.ONES, rhs=self.u[:, c, c0:c0 + n], start=(c == 0), stop=(c == NCH - 1))
                for c in range(NCH):
                    r = e.matmul(psq[:, 0:n], lhsT=self.ONES, rhs=self.sqb[:, c, 0:n], start=(c == 0), stop=(c == NCH - 1))
                return r
            P.op("pe", st, reads=ukey + SQK + ("cb",), writes=(ksm, ksq))
            mu, kmu = self.s32()
            va, kva = self.s32()

            P.op("dve", lambda e, mu=mu, psm=psm, n=n: e.tensor_scalar(out=mu[:, 0:n], in0=psm[:, 0:n], scalar1=1.0 / D, scalar2=None, op0=ALU.mult),
                 reads=(ksm,), writes=(kmu,))
            P.op("dve", lambda e, mu=mu, va=va, n=n: e.tensor_tensor(out=va[:, 0:n], in0=mu[:, 0:n], in1=mu[:, 0:n], op=ALU.mult),
                 reads=(kmu,), writes=(kva,))
            P.op("dve", lambda e, va=va, psq=psq, n=n: e.scalar_tensor_tensor(out=va[:, 0:n], in0=psq[:, 0:n], scalar=1.0 / D, in1=va[:, 0:n],
                                                                         op0=ALU.mult, op1=ALU.subtract),
                 reads=(ksq, kva), writes=(kva,))
            P.op("act", lambda e, va=va, n=n: e.activation(out=va[:, 0:n], in_=va[:, 0:n], func=AF.Ln, bias=EPS, scale=1.0),
                 reads=(kva,), writes=(kva,))
            P.op("act", lambda e, va=va, n=n: e.activation(out=va[:, 0:n], in_=va[:, 0:n], func=AF.Exp, scale=-0.5), reads=(kva,), writes=(kva,))
            for c in range(NCH):
                t, tkey = self.t32()

                P.op("dve", lambda e, t=t, c=c, c0=c0, n=n, mu=mu: e.tensor_tensor(out=t[:, 0:n], in0=self.u[:, c, c0:c0 + n], in1=mu[:, 0:n], op=ALU.subtract),
                     reads=(ukey[c], kmu), writes=(tkey,))
                P.op("dve", lambda e, t=t, n=n, va=va: e.tensor_tensor(out=t[:, 0:n], in0=t[:, 0:n], in1=va[:, 0:n], op=ALU.mult),
                     reads=(tkey, kva), writes=(tkey,))
                P.op("act", lambda e, t=t, c=c, c0=c0, n=n: e.activation(out=self.u[:, c, c0:c0 + n], in_=t[:, 0:n], func=AF.Silu,
                                                                   bias=self.pvs(f"ln_b{L}", c), scale=self.pvs(f"ln_g{L}", c)),
                     reads=(tkey, "pv"), writes=(ukey[c],))
        if getattr(self, "stop_stage", None) == "C":
            return
        for oc in range(NCH):
            wv, wk = self.wslot()
            wa = wv[:, 0:1024].rearrange("p (k n) -> p k n", k=NCH)
            src = w_out[L, :, oc * 128:(oc + 1) * 128].rearrange("(k p) n -> p k n", p=128)
            P.dma("pool", wk, lambda e, wa=wa, src=src: [e.dma_start(out=wa, in_=src)], writes=(wk,))
            for (c0, n) in tiles:
                tk = c0 // 512
                pt, pk = self.ps()

                def mm(e, pt=pt, wa=wa, c0=c0, n=n):
                    for k in range(NCH):
                        r = e.matmul(pt[:, 0:n], lhsT=wa[:, k, :], rhs=self.u[:, k, c0:c0 + n], start=(k == 0), stop=(k == NCH - 1))
                    return r
                P.op("pe", mm, reads=(wk,) + self.uk(tk), writes=(pk,))
                P.op("dve", lambda e, pt=pt, oc=oc, c0=c0, n=n: e.scalar_tensor_tensor(
                    out=self.x[:, oc, c0:c0 + n], in0=pt[:, 0:n], scalar=self.pvs(f"b_out{L}", oc), in1=self.x[:, oc, c0:c0 + n],
                    op0=ALU.add, op1=ALU.add), reads=(pk, "pv", f"x:{tk}"), writes=(f"x:{tk}",))

    def ffn(self, L, tiles):
        P = self.P
        wgu = self.din["w_gate_up"]
        wdn = self.din["w_down"]
        self.rmsnorm(tiles, f"ffn_norm{L}")
        for fg in range(NFC // 2):
            f0 = fg * 256
            wgv, kg = self.wslot()
            wuv, ku = self.wslot()
            wdv, kd = self.wslot()
            wg = wgv[:, 0:2048].rearrange("p (k n) -> p k n", k=NCH)
            wu = wuv[:, 0:2048].rearrange("p (k n) -> p k n", k=NCH)
            wd = wdv[:, 0:2048].rearrange("p (j n) -> p j n", j=2)
            sg_ = wgu[L, :, f0:f0 + 256].rearrange("(k p) n -> p k n", p=128)
            su_ = wgu[L, :, DFF + f0:DFF + f0 + 256].rearrange("(k p) n -> p k n", p=128)
            sd_ = wdn[L, f0:f0 + 256, :].rearrange("(j p) n -> p j n", p=128)
            P.dma("pool", kg, lambda e, wg=wg, sg_=sg_: [e.dma_start(out=wg, in_=sg_)], writes=(kg,))
            P.dma("pool", ku, lambda e, wu=wu, su_=su_: [e.dma_start(out=wu, in_=su_)], writes=(ku,))
            P.dma("pool", kd, lambda e, wd=wd, sd_=sd_: [e.dma_start(out=wd, in_=sd_)], writes=(kd,))

            def gate_up_steps(c0, n):
                tk = c0 // 512
                hkeys = []
                hs = []
                steps = []
                for j in range(2):
                    sg, sgk = self.t32()
                    hb = self.hbuf[self._hi]
                    hk = f"h{self._hi}"
                    self._hi = (self._hi + 1) % len(self.hbuf)
                    hs.append(hb)
                    hkeys.append(hk)
                    bank = {}

                    def s_g(bank=bank, j=j):
                        pg, kpg = self.ps()
                        bank["g"] = (pg, kpg)

                        def mm(e, pg=pg, j=j, c0=c0, n=n, wg=wg):
                            for k in range(NCH):
                                r = e.matmul(pg[:, 0:n], lhsT=wg[:, k, j * 128:(j + 1) * 128], rhs=self.u[:, k, c0:c0 + n], start=(k == 0), stop=(k == NCH - 1))
                            return r
                        P.op("pe", mm, reads=(kg,) + self.uk(tk), writes=(kpg,))

                    def s_u(bank=bank, j=j):
                        pu, kpu = self.ps()
                        bank["u"] = (pu, kpu)

                        def mm(e, pu=pu, j=j, c0=c0, n=n, wu=wu):
                            for k in range(NCH):
                                r = e.matmul(pu[:, 0:n], lhsT=wu[:, k, j * 128:(j + 1) * 128], rhs=self.u[:, k, c0:c0 + n], start=(k == 0), stop=(k == NCH - 1))
                            return r
                        P.op("pe", mm, reads=(ku,) + self.uk(tk), writes=(kpu,))

                    def s_e(bank=bank, sg=sg, sgk=sgk, hb=hb, hk=hk):
                        pg, kpg = bank["g"]
                        pu, kpu = bank["u"]
                        P.op("act", lambda e, sg=sg, pg=pg, n=n: e.activation(out=sg[:, 0:n], in_=pg[:, 0:n], func=AF.Silu),
                             reads=(kpg,), writes=(sgk,))
                        P.op("dve", lambda e, hb=hb, sg=sg, pu=pu, n=n: e.tensor_tensor(out=hb[:, 0:n], in0=sg[:, 0:n], in1=pu[:, 0:n], op=ALU.mult),
                             reads=(sgk, kpu), writes=(hk,))
                    steps += [s_g, s_u, s_e]
                return steps, hs, hkeys

            def down_steps(c0, n, hs, hkeys):
                tk = c0 // 512
                steps = []
                for dc in range(NCH):
                    def s_d(dc=dc):
                        pd, kpd = self.ps()

                        def dn(e, pd=pd, dc=dc, hs=tuple(hs), n=n, wd=wd):
                            e.matmul(pd[:, 0:n], lhsT=wd[:, 0, dc * 128:(dc + 1) * 128], rhs=hs[0][:, 0:n], start=True, stop=False)
                            return e.matmul(pd[:, 0:n], lhsT=wd[:, 1, dc * 128:(dc + 1) * 128], rhs=hs[1][:, 0:n], start=False, stop=True)
                        P.op("pe", dn, reads=(kd,) + tuple(hkeys), writes=(kpd,))
                        P.op("dve", lambda e, pd=pd, dc=dc, c0=c0, n=n: e.tensor_tensor(out=self.x[:, dc, c0:c0 + n], in0=pd[:, 0:n],
                                                                                  in1=self.x[:, dc, c0:c0 + n], op=ALU.add),
                             reads=(kpd, f"x:{tk}"), writes=(f"x:{tk}",))
                    steps.append(s_d)
                return steps
            prev = None
            for (c0, n) in tiles:
                gu, hs, hkeys = gate_up_steps(c0, n)
                dn_ = down_steps(*prev) if prev is not None else []
                order = [gu[0]] + dn_[0:2] + [gu[1]] + dn_[2:4] + [gu[2], gu[3]] + dn_[4:6] + [gu[4]] + dn_[6:8] + [gu[5]]
                for st_ in order:
                    st_()
                prev = (c0, n, hs, hkeys)
            for st_ in down_steps(*prev):
                st_()

    def phase1_test(self):
        self.setup()
        xin = self.din["xin"]
        for sup in range(2):
            isY = sup == 1
            for tb in range(16):
                self.load_tokens(xin[sup * HALF + tb * 128: sup * HALF + (tb + 1) * 128, :], 128, tb * 128)
            if isY:
                self.load_tokens(self.din["xs"][:, :], NS, HALF)
            tiles = TILES_Y if isY else TILES_X
            for L in range(2):
                self.conv_layer(L, isY)
                self.ffn(L, tiles)
        for tb in range(16):
            self.store_tokens(self.dout["y"][tb * 128:(tb + 1) * 128, :], 128, tb * 128)
        self.store_tokens(self.dout["y"][HALF:HALF + NS, :], NS, HALF)

    def finish(self):
        nc = self.nc
        P = self.P
        chans = sorted(P.chan_n.keys())
        import contextlib
        with contextlib.ExitStack() as es:
            sems = {e: es.enter_context(nc.semaphore(f"sem_{e}")) for e in ENGS}
            csems = {ch: es.enter_context(nc.semaphore(f"semd_{ch}")) for ch in chans}
            block = es.enter_context(nc.Block())
            regs = {"pe": block.tensor, "act": block.scalar, "dve": block.vector, "pool": block.gpsimd, "sp": block.sync}
            P.emit(nc, regs, sems, csems)
        return nc


def _const_mats():
    p = np.arange(128)
    ident = np.eye(128, dtype=np.float32)
    ones = np.ones((128, 128), np.float32)
    bones = (p[:, None] // 64 == p[None, :] // 64).astype(np.float32)
    partner = (p // 64) * 64 + ((p % 64) + 32) % 64
    swap = np.zeros((128, 128), np.float32)
    swap[partner, p] = 1.0
    maskP = (p[None, :] <= p[:, None]).astype(np.float32)
    maskC = (p[:, None] <= p[None, :]).astype(np.float32)
    return np.stack([ident, ones, bones, swap, maskP, maskC]).astype(np.float32)


def _smask():
    m = np.zeros((128, 544), np.float32)
    k = np.arange(128)[:, None]
    q = np.arange(8)[None, :]
    for a in range(16):
        m[:, a * 8:(a + 1) * 8] = (k >= q)
        m[:, 128 + a * 2] = 1.0
        m[:, 128 + a * 2 + 1] = (np.arange(128) >= 1)
    t = np.arange(8)[:, None]
    M0 = (t <= q).astype(np.float32)
    M1 = ((t == q) | (t == q - 4)).astype(np.float32)
    M2 = (t == q).astype(np.float32)
    for g, Mg in enumerate((M0, M1, M2)):
        for a in range(16):
            m[0:8, 160 + g * 128 + a * 8:160 + g * 128 + (a + 1) * 8] = Mg
    return m


def _rot_tables(half_id):
    half = 32
    inv = np.float32(10000.0) ** (-(np.arange(half, dtype=np.float32) / np.float32(half)))
    pos = np.zeros(KTW, np.float32)
    l = np.arange(SEQ)
    if half_id == 1:
        pos[:SEQ] = l
    else:
        pos[:SEQ] = np.maximum(l - HALF, 0)
    pos[SEQ:] = np.tile(PAST + np.arange(8), 4)
    ang = (pos[:, None].astype(np.float32) * inv[None, :].astype(np.float32)).astype(np.float32)
    p = np.arange(128)
    idx = (p % 64) % 32
    cosT = np.cos(ang)[:, idx].T.astype(np.float32)
    sinT = np.sin(ang)[:, idx].T.astype(np.float32)
    sign = np.where((p % 64) < 32, -1.0, 1.0).astype(np.float32)
    return np.stack([cosT, sinT * sign[:, None]]).astype(np.float32)


def _pvec(inp):
    rows, n = Builder.pv_layout()
    pv = np.zeros((n, D), np.float32)
    for L in range(2):
        pv[rows[f"a_norm{L}"]] = inp["a_norm"][L]
        pv[rows[f"b1_{L}"]] = inp["a_b_in"][L][:D]
        pv[rows[f"b2_{L}"]] = inp["a_b_in"][L][D:]
        pv[rows[f"wdw{L}"]:rows[f"wdw{L}"] + CW] = inp["a_w_dw"][L]
        pv[rows[f"b_dw{L}"]] = inp["a_b_dw"][L]
        pv[rows[f"ln_g{L}"]] = inp["a_ln_g"][L]
        pv[rows[f"ln_b{L}"]] = inp["a_ln_b"][L]
        pv[rows[f"b_out{L}"]] = inp["a_b_out"][L]
    pv[rows["kv_norm"]] = inp["kv_norm"]
    for j in range(2):
        pv[rows[f"b_norm{j}"]] = inp["b_norm"][j]
    for L in range(4):
        pv[rows[f"ffn_norm{L}"]] = inp["ffn_norm"][L]
    for g in range(3):
        pv[rows[f"k_norm{g}"], :128] = np.tile(inp["k_norm"][g], 2)
    for j in range(2):
        for g in range(3):
            pv[rows[f"q_norm{j}_{g}"], :128] = np.tile(inp["q_norm"][j, g], 2)
    return pv


def make_in_maps(inp):
    inp = {k: np.asarray(v) for k, v in inp.items()}
    cmat = _const_mats()
    pvec = _pvec(inp)
    rots = [_rot_tables(0), _rot_tables(1)]
    shared = {
        "pvec": pvec, "cmat": cmat, "smaskin": _smask(),
        "a_w_in": np.ascontiguousarray(inp["a_w_in"], np.float32), "a_w_out": np.ascontiguousarray(inp["a_w_out"], np.float32),
        "w_kv": np.ascontiguousarray(inp["w_kv"], np.float32), "w_q": np.ascontiguousarray(inp["w_q"], np.float32),
        "w_o": np.ascontiguousarray(inp["w_o"], np.float32), "w_gate_up": np.ascontiguousarray(inp["w_gate_up"], np.float32),
        "w_down": np.ascontiguousarray(inp["w_down"], np.float32),
    }
    maps = []
    for c in range(8):
        b, h = c // 2, c % 2
        if h == 1:
            xin = np.ascontiguousarray(inp["x_prompt"][b], np.float32)
        else:
            xin = np.concatenate([np.zeros((HALF, D), np.float32), inp["x_prompt"][b, :HALF]], axis=0)
        m = dict(shared)
        m["xin"] = xin
        m["xs"] = np.ascontiguousarray(inp["x_sample"][4 * c:4 * c + 4].reshape(NS, D), np.float32)
        m["cconv"] = np.ascontiguousarray(inp["cache_conv"][:, 4 * c:4 * c + 4], np.float32)
        m["ckv0"] = np.ascontiguousarray(inp["cache_kv_w128"][4 * c:4 * c + 4].reshape(4, 128, 2048), np.float32)
        m["ckv1"] = np.ascontiguousarray(inp["cache_kv_w512"][4 * c:4 * c + 4].reshape(4, 512, 2048), np.float32)
        m["ckv2"] = np.ascontiguousarray(inp["cache_kv_w2048"][4 * c:4 * c + 4].reshape(4, 2048, 2048), np.float32)
        m["rot"] = rots[h]
        m["flag"] = np.full((128, 1), float(h), np.float32)
        maps.append(m)
    return maps


def _dbg_t0(self):
    self.setup()
    xin = self.din["xin"]
    self.load_tokens(xin[HALF:HALF + 128, :], 128, 0)
    self.store_tokens(self.dout["y"][0:128, :], 128, 0)


Builder.dbg_t0 = _dbg_t0


def _dbg_c0(self):
    self.setup()
    xin = self.din["xin"]
    tiles = [(0, 512)]
    for tb in range(4):
        self.load_tokens(xin[tb * 128:(tb + 1) * 128, :], 128, tb * 128)
    import os
    self.stop_stage = os.environ.get("STOP_STAGE")
    self.conv_layer(0, False, tiles=tiles)
    if self.stop_stage is None:
        self.ffn(0, tiles)
    for tb in range(4):
        self.store_tokens(self.dout["y"][tb * 128:(tb + 1) * 128, :], 128, tb * 128)
    self.dbg_sbuf = [self.M.names[k] for k in ("u", "glu", "pv", "x")]


Builder.dbg_c0 = _dbg_c0


def _ovl_G(self):
    off, nb = self.M.offs["glu"]
    return Mem(self.nc, off, nb)


def _load_rot(self, OM, col_src0, ncols_main, with_sample):
    P = self.P
    rc = OM.alloc([128, NT], F32, "rotc")
    rs = OM.alloc([128, NT], F32, "rots")
    rot = self.din["rot"]

    def f(e):
        r = [e.dma_start(out=rc[:, 0:HALF], in_=rot[0, :, col_src0:col_src0 + HALF]),
             e.dma_start(out=rs[:, 0:HALF], in_=rot[1, :, col_src0:col_src0 + HALF])]
        if with_sample:
            r += [e.dma_start(out=rc[:, HALF:NT], in_=rot[0, :, SEQ:SEQ + NS]),
                  e.dma_start(out=rs[:, HALF:NT], in_=rot[1, :, SEQ:SEQ + NS])]
        return r
    P.dma("sp", "rot", f, writes=("rot",), n=4 if with_sample else 2)
    return rc, rs


def _head_proj(self, tiles, w_src_fn, gain_fn, dst_fn, rc, rs, OM, out32_fn=None, need32_fn=None, tiles_fn=None):
    P = self.P
    kst = [OM.alloc([128, 512], BF16, f"kst{i}") for i in range(2)]
    ksti = 0
    for g in range(3):
        for hp in range(8):
            wv, wk = self.wslot()
            wa = wv[:, 0:1024].rearrange("p (k n) -> p k n", k=NCH)
            src = w_src_fn(g, hp)
            P.dma("pool", wk, lambda e, wa=wa, src=src: [e.dma_start(out=wa, in_=src)], writes=(wk,))
            for (c0, n) in (tiles if tiles_fn is None else tiles_fn(g)):
                tk = c0 // 512
                pr, kpr = self.ps()

                def mm(e, pr=pr, wa=wa, c0=c0, n=n):
                    for k in range(NCH):
                        r = e.matmul(pr[:, 0:n], lhsT=wa[:, k, :], rhs=self.u[:, k, c0:c0 + n], start=(k == 0), stop=(k == NCH - 1))
                    return r
                P.op("pe", mm, reads=(wk,) + self.uk(tk), writes=(kpr,))
                si = self._sqi
                self._sqi = (si + 2) % 8
                sq = self.sqb[:, si, 0:n]
                xg = self.sqb[:, si + 1, 0:n]
                ksq, kxg = f"sqb{si}", f"sqb{si + 1}"
                P.op("act", lambda e, sq=sq, pr=pr, n=n: e.activation(out=sq, in_=pr[:, 0:n], func=AF.Square), reads=(kpr,), writes=(ksq,))
                P.op("act", lambda e, xg=xg, pr=pr, n=n, g=g: e.activation(out=xg, in_=pr[:, 0:n], func=AF.Identity, scale=gain_fn(g)),
                     reads=(kpr, "pv"), writes=(kxg,))
                p1, k1 = self.ps()
                p2, k2 = self.ps()
                P.op("pe", lambda e, p1=p1, sq=sq, n=n: e.matmul(p1[:, 0:n], lhsT=self.BONES, rhs=sq, start=True, stop=True),
                     reads=(ksq, "cb"), writes=(k1,))
                P.op("pe", lambda e, p2=p2, xg=xg, n=n: e.matmul(p2[:, 0:n], lhsT=self.SWAP, rhs=xg, start=True, stop=True),
                     reads=(kxg, "cb"), writes=(k2,))
                rstd, krs = self.s32()
                P.op("act", lambda e, rstd=rstd, p1=p1, n=n: e.activation(out=rstd[:, 0:n], in_=p1[:, 0:n], func=AF.Ln, bias=EPS, scale=1.0 / 64),
                     reads=(k1,), writes=(krs,))
                P.op("act", lambda e, rstd=rstd, n=n: e.activation(out=rstd[:, 0:n], in_=rstd[:, 0:n], func=AF.Exp, scale=-0.5), reads=(krs,), writes=(krs,))
                t1, kt1 = self.t32()
                t2, kt2 = self.t32()
                P.op("pool", lambda e, t1=t1, xg=xg, c0=c0, n=n: e.tensor_tensor(out=t1[:, 0:n], in0=xg, in1=rc[:, c0:c0 + n], op=ALU.mult),
                     reads=(kxg, "rot"), writes=(kt1,))
                P.op("dve", lambda e, t2=t2, p2=p2, c0=c0, n=n: e.tensor_tensor(out=t2[:, 0:n], in0=p2[:, 0:n], in1=rs[:, c0:c0 + n], op=ALU.mult),
                     reads=(k2, "rot"), writes=(kt2,))
                P.op("dve", lambda e, t1=t1, t2=t2, n=n: e.tensor_tensor(out=t1[:, 0:n], in0=t1[:, 0:n], in1=t2[:, 0:n], op=ALU.add),
                     reads=(kt1, kt2), writes=(kt1,))
                ks = kst[ksti]
                kk = f"kst{ksti}"
                ksti = (ksti + 1) % 2
                P.op("dve", lambda e, ks=ks, t1=t1, rstd=rstd, n=n: e.tensor_tensor(out=ks[:, 0:n], in0=t1[:, 0:n], in1=rstd[:, 0:n], op=ALU.mult),
                     reads=(kt1, krs), writes=(kk,))
                if out32_fn is not None and need32_fn is not None and need32_fn(g, c0, n):
                    P.op("dve", lambda e, t1=t1, rstd=rstd, n=n: e.tensor_tensor(out=t1[:, 0:n], in0=t1[:, 0:n], in1=rstd[:, 0:n], op=ALU.mult),
                         reads=(kt1, krs), writes=(kt1,))
                dst = dst_fn(g, hp, c0, n)
                P.dma("sp", kk, lambda e, ks=ks, dst=dst, n=n: [e.dma_start(out=dst, in_=ks[:, 0:n])], reads=(kk,))
                if out32_fn is not None and need32_fn is not None and need32_fn(g, c0, n):
                    out32_fn(g, hp, c0, n, t1, kt1)


def _kv_stage(self, isY):
    P = self.P
    P.fence()
    tiles = TILES_Y if isY else TILES_X
    self.rmsnorm(tiles, "kv_norm")
    OM = self.ovl_G()
    loc0 = HALF if isY else 0
    rc, rs = self.load_rot(OM, loc0, HALF, isY)
    w_kv = self.din["w_kv"]
    kt = self.kt_scr
    vs = self.v_scr

    def w_src(g, hp):
        c = g * 2048 + hp * 128
        return w_kv[:, c:c + 128].rearrange("(k p) n -> p k n", p=128)

    def gain(g):
        return self.pvs(f"k_norm{g}", 0)

    def dst(g, hp, c0, n):
        if n == 512:
            return kt[g, hp, :, loc0 + c0:loc0 + c0 + n]
        return kt[g, hp, :, SEQ:SEQ + NS]

    def out32(g, hp, c0, n, t1, kt1):
        import os
        if not isY or os.environ.get("NO_OUT32"):
            return
        W = WINS[g]
        if n == 512:
            blocks = [b for b in range(4) if c0 + b * 128 >= HALF - W]
            if not blocks:
                return
            pt, pk = self.ps()

            def tr(e, pt=pt, t1=t1, blocks=tuple(blocks)):
                for b in blocks:
                    r = e.transpose(out=pt[:, b * 128:(b + 1) * 128], in_=t1[:, b * 128:(b + 1) * 128], identity=self.ident[:, :])
                return r
            P.op("pe", tr, reads=(kt1, "ident"), writes=(pk,))
            st, sk = self.stage()
            b0, nb = blocks[0], len(blocks)
            P.op("act", lambda e, st=st, pt=pt, b0=b0, nb=nb: e.activation(out=st[:, b0 * 128:(b0 + nb) * 128], in_=pt[:, b0 * 128:(b0 + nb) * 128], func=AF.Copy),
                 reads=(pk,), writes=(sk,))
            row0 = c0 + b0 * 128 - (HALF - W)
            dsto = self.dout[f"kvp{g}"][row0:row0 + nb * 128, hp * 128:(hp + 1) * 128].rearrange("(b t) f -> t b f", t=128)
            P.dma("sp", sk, lambda e, st=st, dsto=dsto, b0=b0, nb=nb: [e.dma_start(out=dsto, in_=st[:, b0 * 128:(b0 + nb) * 128].rearrange("t (b f) -> t b f", b=nb))],
                  reads=(sk,))
        else:
            pt, pk = self.ps()
            P.op("pe", lambda e, pt=pt, t1=t1: e.transpose(out=pt[0:NS, 0:128], in_=t1[:, 0:NS], identity=self.ident[:, :]),
                 reads=(kt1, "ident"), writes=(pk,))
            st, sk = self.stage()
            P.op("act", lambda e, st=st, pt=pt: e.activation(out=st[0:NS, 0:128], in_=pt[0:NS, 0:128], func=AF.Copy), reads=(pk,), writes=(sk,))
            dsto = self.dout[f"kvs{g}"][:, W - 8:W, hp * 128:(hp + 1) * 128]

            def f(e, st=st, dsto=dsto):
                return [e.dma_start(out=dsto[s], in_=st[s * 8:(s + 1) * 8, 0:128]) for s in range(4)]
            P.dma("sp", sk, f, reads=(sk,), n=4)
    def need32(g, c0, n):
        return isY and (n != 512 or c0 + 512 > HALF - WINS[g])
    def tiles_for(g):
        if isY:
            return tiles
        return [(c0, n) for (c0, n) in tiles if c0 + n > HALF - WINS[g]]
    self.head_proj(tiles, w_src, gain, dst, rc, rs, OM, out32, need32, tiles_for)

    vst = [OM.alloc([128, 256], BF16, f"vst{i}") for i in range(2)]
    vi = 0
    blocks = [(tb * 128, 128) for tb in range(16)] + ([(HALF, NS)] if isY else [])
    for g in range(3):
        W = WINS[g]
        for q in range(4):
            wv, wk = self.wslot()
            wa = wv[:, 0:2048].rearrange("p (k n) -> p k n", k=NCH)
            c = g * 2048 + 1024 + q * 256
            src = w_kv[:, c:c + 256].rearrange("(k p) n -> p k n", p=128)
            P.dma("pool", wk, lambda e, wa=wa, src=src: [e.dma_start(out=wa, in_=src)], writes=(wk,))
            for (c0, m) in blocks:
                if (not isY) and c0 + m <= HALF - W:
                    continue
                tk = c0 // 512
                pt, pk = self.ps()

                def mm(e, pt=pt, wa=wa, c0=c0, m=m):
                    for k in range(NCH):
                        r = e.matmul(pt[0:m, 0:256], lhsT=self.u[:, k, c0:c0 + m], rhs=wa[:, k, :], start=(k == 0), stop=(k == NCH - 1))
                    return r
                P.op("pe", mm, reads=(wk,) + self.uk(tk), writes=(pk,))
                vb = vst[vi]
                vk = f"vst{vi}"
                vi = (vi + 1) % 2
                P.op("act", lambda e, vb=vb, pt=pt, m=m: e.activation(out=vb[0:m, :], in_=pt[0:m, 0:256], func=AF.Copy), reads=(pk,), writes=(vk,))
                if m == 128:
                    l0 = loc0 + c0
                else:
                    l0 = SEQ
                dstv = vs[g, 2 * q:2 * q + 2, l0:l0 + m, :].rearrange("h t f -> t h f")
                P.dma("sp", vk, lambda e, vb=vb, dstv=dstv, m=m: [e.dma_start(out=dstv, in_=vb[0:m, :].rearrange("t (h f) -> t h f", h=2))],
                      reads=(vk,))
                import os
                if isY and not os.environ.get("NO_VOUT"):
                    need = (m == NS) or (c0 >= HALF - W)
                    if need:
                        st, sk = self.stage()
                        P.op("dve", lambda e, st=st, pt=pt, m=m: e.tensor_copy(out=st[0:m, 0:256], in_=pt[0:m, 0:256]), reads=(pk,), writes=(sk,))
                        if m == 128:
                            row0 = c0 - (HALF - W)
                            dsto = self.dout[f"kvp{g}"][row0:row0 + 128, 1024 + q * 256:1024 + (q + 1) * 256]
                            P.dma("sp", sk, lambda e, st=st, dsto=dsto: [e.dma_start(out=dsto, in_=st[:, 0:256])], reads=(sk,))
                        else:
                            dsto = self.dout[f"kvs{g}"][:, W - 8:W, 1024 + q * 256:1024 + (q + 1) * 256]

                            def f(e, st=st, dsto=dsto):
                                return [e.dma_start(out=dsto[s], in_=st[s * 8:(s + 1) * 8, 0:256]) for s in range(4)]
                            P.dma("sp", sk, f, reads=(sk,), n=4)
    P.fence()


Builder.ovl_G = _ovl_G
Builder.load_rot = _load_rot
Builder.head_proj = _head_proj
Builder.kv_stage = _kv_stage


SCALE = 0.125


def sl(start, n, step):
    return slice(start, start + step * (n - 1) + 1, step)

KBASE = (HALF - 128, HALF - 512, 0)
NBO = (16, 4, 1)


def _q_stage(self, j):
    P = self.P
    P.fence()
    self.rmsnorm(TILES_Y, f"b_norm{j}")
    OM = self.ovl_G()
    rc, rs = self.load_rot(OM, HALF, HALF, True)
    w_q = self.din["w_q"]

    def w_src(g, hp):
        c = g * 1024 + hp * 128
        return w_q[j, :, c:c + 128].rearrange("(k p) n -> p k n", p=128)

    def gain(g):
        return self.pvs(f"q_norm{j}_{g}", 0)

    def dst(g, hp, c0, n):
        return self.q_scr[g, hp, :, c0:c0 + n]
    self.head_proj(TILES_Y, w_src, gain, dst, rc, rs, OM, None)
    P.fence()


def _attn_layer(self, j, STOP=""):
    P = self.P
    self.q_stage(j)
    if STOP == "Q":
        return
    self.sample_stage(j)
    if STOP == "S":
        return
    uo, ub = self.M.offs["u"]
    go, gb = self.M.offs["glu"]
    assert uo + ub == go
    OM = Mem(self.nc, uo, ub + gb)
    QT = [OM.alloc([128, NT], BF16, f"QT{g}") for g in range(3)]
    kw = [SEQ - KBASE[g] + NS for g in range(3)]
    KT = [OM.alloc([128, kw[g]], BF16, f"KT{g}") for g in range(3)]
    nblk = [DILS[g] * (NBO[g] + 1) for g in range(3)]
    VT = [OM.alloc([128, nblk[g], 128], BF16, f"VT{g}") for g in range(3)]
    acc = OM.alloc([128, 2, NT], F32, "acc")
    PT = [OM.alloc([128, 512], BF16, f"PT{i}") for i in range(3)]
    qo, qb = OM.offs["QT0"]
    assert OM.offs["QT1"][0] == qo + qb and qb == NT * 2
    oT = self.nc.alloc_sbuf_tensor_at(f"oT_{j}", [128, 2, NT], BF16, offset=qo)
    self.att = dict(QT=QT, KT=KT, VT=VT, acc=acc, PT=PT, OM=OM)
    pti = 0
    w_o = self.din["w_o"]
    for hp in range(8):
        for g in range(3):
            r = DILS[g]
            nbo = NBO[g]
            kb0 = KBASE[g]
            P.dma("sp", f"QT{g}", lambda e, g=g, hp=hp: [e.dma_start(out=QT[g][:, :], in_=self.q_scr[g, hp, :, :])], writes=(f"QT{g}",))
            P.dma("sp", f"KT{g}", lambda e, g=g, hp=hp, kb0=kb0: [e.dma_start(out=KT[g][:, :], in_=self.kt_scr[g, hp, :, kb0:KTW])],
                  writes=(f"KT{g}",))
            vsrc = self.v_scr[g, hp, kb0:kb0 + r * 128 * (nbo + 1), :].rearrange("(m i c) f -> i c m f", i=128, c=r)
            vdst = VT[g][:, :, :].rearrange("p (c m) f -> p c m f", c=r)
            if r == 1:
                P.dma("sp", f"VT{g}", lambda e, vdst=vdst, vsrc=vsrc: [e.dma_start(out=vdst[:, 0], in_=vsrc[:, 0])], writes=(f"VT{g}",))
            else:
                def fv(e, vdst=vdst, vsrc=vsrc, r=r):
                    return [e.dma_start(out=vdst[:, c], in_=vsrc[:, c]) for c in range(r)]
                P.dma("sp", f"VT{g}", fv, writes=(f"VT{g}",), n=r)
            def front(c, nn, g=g, r=r, nbo=nbo, kb0=kb0):
                nonlocal pti
                q0 = c + r * 128 * nn
                kcur = (HALF - kb0) + q0
                kprev = kcur - r * 128
                bprev = c * (nbo + 1) + nn
                pSa, kSa = self.ps()
                pSb, kSb = self.ps()

                def st(e, pSa=pSa, pSb=pSb, g=g, r=r, q0=q0, kcur=kcur, kprev=kprev):
                    for h, pS in enumerate((pSa, pSb)):
                        for kb, k0 in enumerate((kprev, kcur)):
                            rr = e.matmul(pS[:, kb * 128:(kb + 1) * 128],
                                          lhsT=KT[g][h * 64:(h + 1) * 64, sl(k0, 128, r)],
                                          rhs=QT[g][h * 64:(h + 1) * 64, sl(q0, 128, r)], start=True, stop=True)
                    return rr
                P.op("pe", st, reads=(f"KT{g}", f"QT{g}"), writes=(kSa, kSb))
                E, kE = self.t32()
                E4 = E[:, :].rearrange("p (k h q) -> p k h q", k=2, h=2)
                P.op("act", lambda e, E4=E4, pSa=pSa: e.activation(out=E4[:, :, 0, :], in_=pSa[:, 0:256].rearrange("p (k q) -> p k q", k=2),
                                                                func=AF.Exp, scale=SCALE), reads=(kSa,), writes=(kE,))
                P.op("act", lambda e, E4=E4, pSb=pSb: e.activation(out=E4[:, :, 1, :], in_=pSb[:, 0:256].rearrange("p (k q) -> p k q", k=2),
                                                                func=AF.Exp, scale=SCALE), reads=(kSb, kE), writes=(kE,))
                pt = PT[pti]
                kpt = f"PT{pti}"
                pti = (pti + 1) % 3
                if nn == 0:
                    def mk(e, pt=pt, E=E):
                        e.scalar_tensor_tensor(out=pt[:, 0:256], in0=E[:, 0:256], scalar=self.flag[:, 0:1], in1=self.maskf[:, 0:256],
                                               op0=ALU.mult, op1=ALU.mult)
                        return e.tensor_tensor(out=pt[:, 256:512], in0=E[:, 256:512], in1=self.maskf[:, 256:512], op=ALU.mult)
                else:
                    def mk(e, pt=pt, E=E):
                        return e.tensor_tensor(out=pt[:, :], in0=E[:, :], in1=self.maskf[:, :], op=ALU.mult)
                P.op("dve", mk, reads=(kE, "maskf", "flag"), writes=(kpt,))
                return (pt, kpt, bprev, q0)

            def back(ctx, g=g, r=r):
                pt, kpt, bprev, q0 = ctx
                pU, kU = self.ps()

                def pv(e, pU=pU, pt=pt, g=g, bprev=bprev):
                    for h in range(2):
                        for kb in range(2):
                            rhs = pt[:, (kb * 2 + h) * 128:(kb * 2 + h + 1) * 128]
                            e.matmul(pU[0:64, h * 128:(h + 1) * 128], lhsT=VT[g][:, bprev + kb, h * 64:(h + 1) * 64], rhs=rhs,
                                     start=(kb == 0), stop=(kb == 1))
                            rr = e.matmul(pU[64:128, h * 128:(h + 1) * 128], lhsT=self.ONES[:, 0:64], rhs=rhs,
                                          start=(kb == 0), stop=(kb == 1))
                    return rr
                P.op("pe", pv, reads=(kpt, f"VT{g}", "cb"), writes=(kU,))
                av = acc[:, :, sl(q0, 128, r)]
                pv3 = pU[:, 0:256].rearrange("p (h q) -> p h q", h=2)
                if g == 0:
                    P.op("dve", lambda e, av=av, pv3=pv3: e.tensor_copy(out=av, in_=pv3), reads=(kU,), writes=("acc",))
                else:
                    P.op("dve", lambda e, av=av, pv3=pv3: e.tensor_tensor(out=av, in0=pv3, in1=av, op=ALU.add), reads=(kU, "acc"), writes=("acc",))
            pend = None
            for c in range(r):
                for nn in range(nbo):
                    ctx = front(c, nn)
                    if pend is not None:
                        back(pend)
                    pend = ctx
            back(pend)
        self.sample_attn(j, hp)
        P.op("act", lambda e: e.activation(out=acc[64:128, :, :], in_=acc[64:128, :, :], func=AF.Ln), reads=("acc",), writes=("acc",))
        P.op("act", lambda e: e.activation(out=acc[64:128, :, :], in_=acc[64:128, :, :], func=AF.Exp, scale=-1.0), reads=("acc",), writes=("acc",))
        wv, wk = self.wslot()
        wo = wv[0:64, 0:2048].rearrange("p (h n) -> p h n", h=2)
        src = w_o[j, hp * 128:(hp + 1) * 128, :].rearrange("(h p) n -> p h n", p=64)
        P.dma("pool", wk, lambda e, wo=wo, src=src: [e.dma_start(out=wo, in_=src)], writes=(wk,))
        for (c0, n) in TILES_Y:
            tk = c0 // 512
            for h in range(2):
                pR, kR = self.ps()
                P.op("pe", lambda e, pR=pR, h=h, c0=c0, n=n: e.matmul(pR[0:64, 0:n], lhsT=self.ident[:, 64:128], rhs=acc[:, h, c0:c0 + n], start=True, stop=True),
                     reads=("acc", "ident"), writes=(kR,))
                P.op("dve", lambda e, pR=pR, h=h, c0=c0, n=n: e.tensor_tensor(out=oT[0:64, h, c0:c0 + n], in0=acc[0:64, h, c0:c0 + n], in1=pR[0:64, 0:n], op=ALU.mult),
                     reads=(kR, "acc"), writes=("QT0", "QT1"))
            for dc in range(NCH):
                pd, kd = self.ps()

                def wm(e, pd=pd, dc=dc, c0=c0, n=n, wo=wo):
                    e.matmul(pd[:, 0:n], lhsT=wo[0:64, 0, dc * 128:(dc + 1) * 128], rhs=oT[0:64, 0, c0:c0 + n], start=True, stop=False)
                    return e.matmul(pd[:, 0:n], lhsT=wo[0:64, 1, dc * 128:(dc + 1) * 128], rhs=oT[0:64, 1, c0:c0 + n], start=False, stop=True)
                P.op("pe", wm, reads=(wk, "QT0", "QT1"), writes=(kd,))
                P.op("dve", lambda e, pd=pd, dc=dc, c0=c0, n=n: e.tensor_tensor(out=self.x[:, dc, c0:c0 + n], in0=pd[:, 0:n], in1=self.x[:, dc, c0:c0 + n], op=ALU.add),
                     reads=(kd, f"x:{tk}"), writes=(f"x:{tk}",))
    P.fence()


def _sample_attn_stub(self, j, hp):
    acc = self.att["acc"]
    self.P.op("dve", lambda e: e.memset(acc[:, :, HALF:NT], 1.0), writes=("acc",))


Builder.q_stage = _q_stage
Builder.attn_layer = _attn_layer
Builder.sample_attn = _sample_attn_stub


def _sample_stage(self, j):
    P = self.P
    P.fence()
    uo, ub = self.M.offs["u"]
    go, gb = self.M.offs["glu"]
    OM = Mem(self.nc, uo, ub + gb)
    QS = OM.alloc([128, 24, NS], BF16, "QS")
    KN = OM.alloc([128, 24, NS], BF16, "KN")
    VN = [OM.alloc([8, 32, 128], BF16, f"VN{g}") for g in range(3)]
    KC = [OM.alloc([128, 1024], F32, f"KC{i}") for i in range(2)]
    VC = [OM.alloc([128, 1024], BF16, f"VC{i}") for i in range(2)]
    KCT = [OM.alloc([128, 8, 128], BF16, f"KCT{i}") for i in range(2)]
    ES = [OM.alloc([128, 128], F32, f"ES{i}") for i in range(2)]
    PS_ = [OM.alloc([128, 128], BF16, f"PS{i}") for i in range(2)]
    sacc = self.sacc
    P.dma("sp", "QS", lambda e: [e.dma_start(out=QS[:, :, :], in_=self.q_scr[:, :, :, HALF:NT].rearrange("g h p n -> p (g h) n"))], writes=("QS",))
    P.dma("sp", "KN", lambda e: [e.dma_start(out=KN[:, :, :], in_=self.kt_scr[:, :, :, SEQ:KTW].rearrange("g h p n -> p (g h) n"))], writes=("KN",))
    for g in range(3):
        def fvn(e, g=g):
            return [e.dma_start(out=VN[g][:, hp * 4:(hp + 1) * 4, :], in_=self.v_scr[g, hp, SEQ:KTW, :].rearrange("(s t) f -> t s f", t=8))
                    for hp in range(8)]
        P.dma("sp", f"VN{g}", fvn, writes=(f"VN{g}",), n=8)
    P.op("dve", lambda e: e.memset(sacc[:, :, :, :], 0.0), writes=("sacc",))
    ckv = [self.din["ckv0"], self.din["ckv1"], self.din["ckv2"]]
    it = 0
    pend = None

    def run(front, back):
        nonlocal pend
        ctx = front()
        if pend is not None:
            pend[0](pend[1])
        pend = (back, ctx)
    for s in range(4):
        for g in range(3):
            r = DILS[g]
            ntile = (1, 4, 8)[g]
            for c in range(ntile):
                if g == 0:
                    qcols = list(range(8))
                elif g == 1:
                    qcols = [c, c + 4]
                else:
                    qcols = [c]
                nq = len(qcols)
                q0, qstep = s * 8 + qcols[0], (qcols[1] - qcols[0]) if nq > 1 else 1
                b = it % 2
                it += 1

                def front(s=s, g=g, c=c, r=r, nq=nq, q0=q0, qstep=qstep, b=b):
                    kc, vc, kct, es, pb = KC[b], VC[b], KCT[b], ES[b], PS_[b]
                    rows = ckv[g][s, sl(c, 128, r), :]
                    P.dma("sp", f"KC{b}", lambda e, kc=kc, rows=rows: [e.dma_start(out=kc[:, :], in_=rows[:, 0:1024])], writes=(f"KC{b}",))
                    P.dma("pool", f"VC{b}", lambda e, vc=vc, rows=rows: [e.dma_start(out=vc[:, :], in_=rows[:, 1024:2048])], writes=(f"VC{b}",))
                    for half in range(2):
                        pt, pk = self.ps()

                        def tr(e, pt=pt, kc=kc, half=half):
                            for cc in range(4):
                                hp = half * 4 + cc
                                rr = e.transpose(out=pt[:, cc * 128:(cc + 1) * 128], in_=kc[:, hp * 128:(hp + 1) * 128], identity=self.ident[:, :])
                            return rr
                        P.op("pe", tr, reads=(f"KC{b}", "ident"), writes=(pk,))
                        P.op("act", lambda e, pt=pt, kct=kct, half=half: e.activation(out=kct[:, half * 4:half * 4 + 4, :],
                                                                                   in_=pt[:, :].rearrange("p (a k) -> p a k", a=4), func=AF.Copy),
                             reads=(pk,), writes=(f"KCT{b}",))
                    pSa, kSa = self.ps()
                    pSb, kSb = self.ps()

                    def st(e, pSa=pSa, pSb=pSb, kct=kct):
                        for h, pS in enumerate((pSa, pSb)):
                            for hp in range(8):
                                rr = e.matmul(pS[:, hp * nq:(hp + 1) * nq], lhsT=kct[h * 64:(h + 1) * 64, hp, :],
                                              rhs=QS[h * 64:(h + 1) * 64, g * 8 + hp, sl(q0, nq, qstep)], start=True, stop=True)
                        return rr
                    P.op("pe", st, reads=(f"KCT{b}", "QS"), writes=(kSa, kSb))
                    es4 = es[:, 0:16 * nq].rearrange("p (a h q) -> p a h q", a=8, h=2)
                    P.op("act", lambda e, es4=es4, pSa=pSa: e.activation(out=es4[:, :, 0, :], in_=pSa[:, 0:8 * nq].rearrange("p (a q) -> p a q", a=8),
                                                                      func=AF.Exp, scale=SCALE), reads=(kSa,), writes=(f"ES{b}",))
                    P.op("act", lambda e, es4=es4, pSb=pSb: e.activation(out=es4[:, :, 1, :], in_=pSb[:, 0:8 * nq].rearrange("p (a q) -> p a q", a=8),
                                                                      func=AF.Exp, scale=SCALE), reads=(kSb, f"ES{b}"), writes=(f"ES{b}",))
                    if g == 0:
                        msk = self.smask[:, 0:128]
                    elif g == 1:
                        msk = self.smask[:, 128:160]
                    else:
                        msk = None
                    if msk is not None:
                        P.op("dve", lambda e, pb=pb, es=es, msk=msk: e.tensor_tensor(out=pb[:, 0:16 * nq], in0=es[:, 0:16 * nq], in1=msk, op=ALU.mult),
                             reads=(f"ES{b}", "smask"), writes=(f"PS{b}",))
                    else:
                        P.op("dve", lambda e, pb=pb, es=es: e.tensor_copy(out=pb[:, 0:16 * nq], in_=es[:, 0:16 * nq]),
                             reads=(f"ES{b}",), writes=(f"PS{b}",))
                    return None

                def back(ctx, s=s, g=g, nq=nq, q0=q0, qstep=qstep, b=b):
                    vc, pb = VC[b], PS_[b]
                    pU, kU = self.ps()

                    def pv(e, pU=pU, pb=pb, vc=vc):
                        for hp in range(8):
                            for h in range(2):
                                col = (hp * 2 + h) * nq
                                e.matmul(pU[0:64, col:col + nq], lhsT=vc[:, hp * 128 + h * 64:hp * 128 + (h + 1) * 64], rhs=pb[:, col:col + nq], start=True, stop=True)
                                rr = e.matmul(pU[64:128, col:col + nq], lhsT=self.ONES[:, 0:64], rhs=pb[:, col:col + nq], start=True, stop=True)
                        return rr
                    P.op("pe", pv, reads=(f"PS{b}", f"VC{b}", "cb"), writes=(kU,))
                    sv = sacc[:, :, :, sl(q0, nq, qstep)].rearrange("p a h q -> p (a h) q")
                    P.op("dve", lambda e, sv=sv, pU=pU: e.tensor_tensor(out=sv, in0=pU[:, 0:16 * nq].rearrange("p (a q) -> p a q", a=16), in1=sv, op=ALU.add),
                         reads=(kU, "sacc"), writes=("sacc",))
                run(front, back)
            b = it % 2
            it += 1

            def frontn(s=s, g=g, b=b):
                es, pb = ES[b], PS_[b]
                pSa, kSa = self.ps()
                pSb, kSb = self.ps()

                def stn(e, pSa=pSa, pSb=pSb):
                    for h, pS in enumerate((pSa, pSb)):
                        for hp in range(8):
                            rr = e.matmul(pS[0:8, hp * 8:(hp + 1) * 8], lhsT=KN[h * 64:(h + 1) * 64, g * 8 + hp, s * 8:s * 8 + 8],
                                          rhs=QS[h * 64:(h + 1) * 64, g * 8 + hp, s * 8:s * 8 + 8], start=True, stop=True)
                    return rr
                P.op("pe", stn, reads=("KN", "QS"), writes=(kSa, kSb))
                esn = es[0:8, :].rearrange("p (a h q) -> p a h q", a=8, h=2)
                P.op("act", lambda e, esn=esn, pSa=pSa: e.activation(out=esn[:, :, 0, :], in_=pSa[0:8, 0:64].rearrange("p (a q) -> p a q", a=8),
                                                                  func=AF.Exp, scale=SCALE), reads=(kSa,), writes=(f"ES{b}",))
                P.op("act", lambda e, esn=esn, pSb=pSb: e.activation(out=esn[:, :, 1, :], in_=pSb[0:8, 0:64].rearrange("p (a q) -> p a q", a=8),
                                                                  func=AF.Exp, scale=SCALE), reads=(kSb, f"ES{b}"), writes=(f"ES{b}",))
                mo = 160 + g * 128
                P.op("dve", lambda e, pb=pb, es=es, mo=mo: e.tensor_tensor(out=pb[0:8, :], in0=es[0:8, :], in1=self.smask[0:8, mo:mo + 128], op=ALU.mult),
                     reads=(f"ES{b}", "smask"), writes=(f"PS{b}",))
                return None

            def backn(ctx, s=s, g=g, b=b):
                pb = PS_[b]
                pU, kU = self.ps()

                def pvn(e, pU=pU, pb=pb):
                    for hp in range(8):
                        for h in range(2):
                            col = (hp * 2 + h) * 8
                            e.matmul(pU[0:64, col:col + 8], lhsT=VN[g][0:8, hp * 4 + s, h * 64:(h + 1) * 64], rhs=pb[0:8, col:col + 8], start=True, stop=True)
                            rr = e.matmul(pU[64:128, col:col + 8], lhsT=self.ONES[0:8, 0:64], rhs=pb[0:8, col:col + 8], start=True, stop=True)
                    return rr
                P.op("pe", pvn, reads=(f"PS{b}", f"VN{g}", "cb"), writes=(kU,))
                sv = sacc[:, :, :, s * 8:s * 8 + 8].rearrange("p a h q -> p (a h) q")
                P.op("dve", lambda e, sv=sv, pU=pU: e.tensor_tensor(out=sv, in0=pU[:, 0:128].rearrange("p (a q) -> p a q", a=16), in1=sv, op=ALU.add),
                     reads=(kU, "sacc"), writes=("sacc",))
            run(frontn, backn)
    if pend is not None:
        pend[0](pend[1])
    P.fence()


def _sample_attn(self, j, hp):
    acc = self.att["acc"]
    self.P.op("dve", lambda e, hp=hp: e.tensor_copy(out=acc[:, :, HALF:NT], in_=self.sacc[:, hp, :, :]), reads=("sacc",), writes=("acc",))


Builder.sample_stage = _sample_stage
Builder.sample_attn = _sample_attn


def _conv_outputs(self, L):
    P = self.P
    st, sk = self.stage()
    for half in range(2):
        pt, pk = self.ps()

        def f(e, half=half, pt=pt):
            for cc in range(4):
                c = half * 4 + cc
                r = e.transpose(out=pt[0:30, cc * 128:(cc + 1) * 128], in_=self.g32[:, L, c, :], identity=self.ident[:, :])
            return r
        P.op("pe", f, reads=("g32", "ident"), writes=(pk,))
        P.op("act", lambda e, half=half, pt=pt, st=st: e.activation(out=st[0:30, half * 512:(half + 1) * 512], in_=pt[0:30, :], func=AF.Copy),
             reads=(pk,), writes=(sk,))
    P.dma("sp", sk, lambda e, st=st: [e.dma_start(out=self.dout["convp"][L, :, :], in_=st[0:30, :])], reads=(sk,))
    P.dma("sp", "d2d", lambda e: [e.dma_start(out=self.dout["convs"][L, :, 0:22, :], in_=self.din["cconv"][L, :, 8:30, :])])
    st, sk = self.stage()
    for half in range(2):
        pt, pk = self.ps()

        def f2(e, half=half, pt=pt):
            for cc in range(4):
                c = half * 4 + cc
                r = e.transpose(out=pt[0:NS, cc * 128:(cc + 1) * 128], in_=self.gs32[:, L, c, :], identity=self.ident[:, :])
            return r
        P.op("pe", f2, reads=("gs32", "ident"), writes=(pk,))
        P.op("act", lambda e, half=half, pt=pt, st=st: e.activation(out=st[0:NS, half * 512:(half + 1) * 512], in_=pt[0:NS, :], func=AF.Copy),
             reads=(pk,), writes=(sk,))

    def fs(e, st=st):
        return [e.dma_start(out=self.dout["convs"][L, s, 22:30, :], in_=st[s * 8:(s + 1) * 8, :]) for s in range(4)]
    P.dma("sp", sk, fs, reads=(sk,), n=4)


def _build_full(self):
    import os
    STOP = os.environ.get("KSTOP", "")
    P = self.P
    self.setup()
    xin = self.din["xin"]
    ck = [self.din["ckv0"], self.din["ckv1"], self.din["ckv2"]]
    for g in range(3):
        W = WINS[g]
        if os.environ.get("NO_D2D"):
            break
        for s in range(4):
            for r0 in range(0, W - 8, 512):
                nr = min(512, W - 8 - r0)
                P.dma("act", "d2d", lambda e, g=g, s=s, r0=r0, nr=nr: [e.dma_start(out=self.dout[f"kvs{g}"][s, r0:r0 + nr, :],
                                                                                  in_=ck[g][s, 8 + r0:8 + r0 + nr, :])])
    for sup in range(2):
        isY = sup == 1 and STOP != "XX"
        tiles = TILES_Y if isY else TILES_X
        for tb in range(16):
            self.load_tokens(xin[sup * HALF + tb * 128: sup * HALF + (tb + 1) * 128, :], 128, tb * 128)
        if isY:
            self.load_tokens(self.din["xs"][:, :], NS, HALF)
        for L in range(2):
            self.conv_layer(L, isY)
            if isY and not os.environ.get("NO_CONVOUT"):
                self.conv_outputs(L)
            self.ffn(L, tiles)
        self.kv_stage(isY)
        if STOP == "X":
            break
    for j in range(2):
        if STOP in ("X", "KV", "XX"):
            break
        self.attn_layer(j, STOP)
        if STOP in ("Q", "S", "A"):
            break
        self.ffn(2 + j, TILES_Y)
    for tb in range(16):
        self.store_tokens(self.dout["y"][tb * 128:(tb + 1) * 128, :], 128, tb * 128)
    self.store_tokens(self.dout["y"][HALF:HALF + NS, :], NS, HALF)


Builder.conv_outputs = _conv_outputs
Builder.build_full = _build_full

_CACHE = {}


def kernel(**inputs):
    if "nc" not in _CACHE:
        B = Builder()
        B.build_full()
        _CACHE["nc"] = B.finish()
        _CACHE["names"] = set(B.din.keys())
    nc = _CACHE["nc"]
    maps = make_in_maps(inputs)
    maps = [{k: v for k, v in m.items() if k in _CACHE["names"]} for m in maps]
    res = run_bass_kernel_spmd(nc, maps, core_ids=list(range(8)))
    R = res.results
    f32 = np.float32
    y_prompt = np.zeros((4, SEQ, D), f32)
    y_sample = np.zeros((32, 8, D), f32)
    conv_p = np.zeros((2, 4, 30, D), f32)
    conv_s = np.zeros((2, 32, 30, D), f32)
    kvp = [np.zeros((4, W, 2, 16, 64), f32) for W in WINS]
    kvs = [np.zeros((32, W, 2, 16, 64), f32) for W in WINS]
    for c in range(8):
        b, h = c // 2, c % 2
        r = R[c]
        y_prompt[b, h * HALF:(h + 1) * HALF] = r["y"][:HALF]
        y_sample[4 * c:4 * c + 4] = r["y"][HALF:].reshape(4, 8, D)
        conv_s[:, 4 * c:4 * c + 4] = r["convs"]
        for g in range(3):
            kvs[g][4 * c:4 * c + 4] = r[f"kvs{g}"].reshape(4, WINS[g], 2, 16, 64)
        if h == 1:
            conv_p[:, b] = r["convp"]
            for g in range(3):
                kvp[g][b] = r[f"kvp{g}"].reshape(WINS[g], 2, 16, 64)
    return (y_prompt, y_sample, conv_p, conv_s, kvp[0], kvp[1], kvp[2], kvs[0], kvs[1], kvs[2])
```

```python
import numpy as np
import ml_dtypes
import concourse.bass as bass
import concourse.mybir as mybir
from concourse.bass_utils import run_bass_kernel_spmd

F32 = mybir.dt.float32
BF16 = mybir.dt.bfloat16
AF = mybir.ActivationFunctionType
ALU = mybir.AluOpType

D = 1024
NCH = 8
SEQ = 4096
HALF = 2048
NS = 32
NT = HALF + NS
DFF = 2816
NFC = 22
CW = 31
EPS = 1e-6
PAST = 8192
WINS = (128, 512, 2048)
DILS = (1, 4, 16)
KTW = SEQ + NS

ENGS = ["pe", "act", "dve", "pool", "sp"]
SQK = tuple(f"sqb{i}" for i in range(8))


class Prog:
    def __init__(self):
        self.ops = {e: [] for e in ENGS}
        self.last_w = {}
        self.readers = {}
        self.chan_n = {}

    def _deps(self, reads, writes):
        deps = []
        for k in reads:
            t = self.last_w.get(k)
            if t is not None:
                deps.append(t)
        for k in writes:
            t = self.last_w.get(k)
            if t is not None:
                deps.append(t)
            deps.extend(self.readers.get(k, ()))
        return deps

    def _commit(self, tok, reads, writes):
        for k in reads:
            lst = self.readers.setdefault(k, [])
            src = tok[:2]
            lst[:] = [t for t in lst if t[:2] != src]
            lst.append(tok)
        for k in writes:
            self.last_w[k] = tok
            self.readers[k] = []

    def op(self, eng, fn, reads=(), writes=()):
        reads = tuple(reads)
        writes = tuple(writes) + tuple(k for k in reads if k.startswith("ps") and k[2:].isdigit())
        idx = len(self.ops[eng])
        tok = ("e", eng, idx)
        deps = [t for t in self._deps(reads, writes) if not (t[0] == "e" and t[1] == eng and (eng == "pe" or t[2] == idx))]
        deps += self._take_fence(eng, idx)
        for t in deps:
            if t[0] == "e":
                self.ops[t[1]][t[2]]["signal"] = True
        self.ops[eng].append({"fn": fn, "deps": deps, "signal": False, "dma": None})
        self._commit(tok, reads, writes)
        return tok

    def dma(self, queue, chan, fn, reads=(), writes=(), n=1):
        cnt = self.chan_n.get(chan, 0) + n
        self.chan_n[chan] = cnt
        tok = ("d", chan, cnt)
        deps = list(self._deps(reads, writes))
        deps += self._take_fence(queue, len(self.ops[queue]))
        for t in deps:
            if t[0] == "e":
                self.ops[t[1]][t[2]]["signal"] = True
        self.ops[queue].append({"fn": fn, "deps": deps, "signal": False, "dma": chan})
        self._commit(tok, reads, writes)
        return tok

    def fence(self):
        deps = [("e", E, len(self.ops[E]) - 1) for E in ENGS if self.ops[E] and self.ops[E][-1]["dma"] is None]
        for E in ENGS:
            if self.ops[E] and self.ops[E][-1]["dma"] is not None:
                for i in range(len(self.ops[E]) - 1, -1, -1):
                    if self.ops[E][i]["dma"] is None:
                        deps.append(("e", E, i))
                        break
        deps += [("d", ch, n) for ch, n in self.chan_n.items()]
        self.pending = {E: list(deps) for E in ENGS}

    def _take_fence(self, eng, idx):
        pend = getattr(self, "pending", None)
        if not pend or not pend.get(eng):
            return []
        d = pend[eng]
        pend[eng] = []
        return [t for t in d if not (t[0] == "e" and t[1] == eng and eng == "pe")]

    def emit(self, nc, block_engines, sems, chan_sems):
        sigcount = {}
        for e in ENGS:
            c = 0
            arr = []
            for o in self.ops[e]:
                if o["signal"] and o["dma"] is None:
                    c += 1
                arr.append(c)
            sigcount[e] = arr
        prog = self

        def make(e):
            def body(eng):
                seen = {}
                for o in prog.ops[e]:
                    for t in o["deps"]:
                        if t[0] == "e":
                            src = ("e", t[1])
                            cnt = sigcount[t[1]][t[2]]
                            sem = sems[t[1]]
                        else:
                            src = ("d", t[1])
                            cnt = 16 * t[2]
                            sem = chan_sems[t[1]]
                        if seen.get(src, 0) >= cnt:
                            continue
                        seen[src] = cnt
                        eng.wait_ge(sem, cnt)
                    r = o["fn"](eng)
                    if o["dma"] is not None:
                        for ins in r:
                            ins.then_inc(chan_sems[o["dma"]], 16)
                    elif o["signal"]:
                        r.then_inc(sems[e], 1)
                if e == "sp":
                    for ch, n in prog.chan_n.items():
                        if seen.get(("d", ch), 0) < 16 * n:
                            eng.wait_ge(chan_sems[ch], 16 * n)
            return body

        for e in ENGS:
            block_engines[e](make(e))


class Mem:
    def __init__(self, nc, base, size):
        self.nc = nc
        self.base = base
        self.size = size
        self.cur = 0
        self.n = 0
        Mem._gn = getattr(Mem, "_gn", 0) + 1000
        self.n = Mem._gn

    def alloc(self, shape, dtype, name=None):
        nbytes = int(np.prod(shape[1:])) * (4 if dtype == F32 else 2)
        nbytes = (nbytes + 63) // 64 * 64
        off = self.cur
        self.cur += nbytes
        assert self.cur <= self.size, f"SBUF overflow {self.cur} > {self.size} at {name}"
        self.n += 1
        self.offs = getattr(self, "offs", {})
        self.offs[name] = (self.base + off, nbytes)
        return self._reg(name, self.nc.alloc_sbuf_tensor_at(f"{name or 't'}_{self.n}", list(shape), dtype, offset=self.base + off))

    def _reg(self, name, h):
        self.names = getattr(self, "names", {})
        self.names[name] = h.name
        return h

    def mark(self):
        return self.cur

    def reset(self, m):
        self.cur = m


TILES_X = [(i * 512, 512) for i in range(4)]
TILES_Y = TILES_X + [(HALF, NS)]
GW = 30 + HALF + 4 * 38
GS0 = 30 + HALF


class Builder:
    def __init__(self, stop_after=None):
        self.stop_after = stop_after
        nc = self.nc = bass.Bass("TRN2", target_bir_lowering=False)
        self.P = Prog()
        self.din = {}
        self.dout = {}
        self._uid = 0

    def inp(self, name, shape, dtype=F32):
        self.din[name] = self.nc.dram_tensor(name, list(shape), dtype, kind="ExternalInput").ap()
        return self.din[name]

    def outp(self, name, shape, dtype=F32):
        self.dout[name] = self.nc.dram_tensor(name, list(shape), dtype, kind="ExternalOutput").ap()
        return self.dout[name]

    @staticmethod
    def uk(tk):
        return tuple(f"u:{tk}:{c}" for c in range(NCH))

    def uid(self):
        self._uid += 1
        return self._uid

    def ps(self):
        i = self._psi
        self._psi = (i + 1) % 8
        return self.psb[i], f"ps{i}"

    def wslot(self):
        i = self._wi
        self._wi = (i + 1) % len(self.wbufs)
        return self.wbufs[i], f"w{i}"

    def t32(self):
        i = self._t32i
        self._t32i = (i + 1) % len(self.T32)
        return self.T32[i], f"t32_{i}"

    def s32(self):
        i = self._s32i
        self._s32i = (i + 1) % len(self.S32)
        return self.S32[i], f"s32_{i}"

    def stage(self):
        i = self._stgi
        self._stgi = (i + 1) % 2
        return self.stg[i], f"stg{i}"

    def load_w(self, src_ap, shape_view):
        buf, key = self.wslot()
        a, b = shape_view
        dst = buf[:, 0:a * b].rearrange("p (a b) -> p a b", a=a)
        self.P.dma("pool", key, lambda e, dst=dst, src=src_ap: [e.dma_start(out=dst, in_=src)],
                   reads=(), writes=(key,))
        return dst, key

    PV_ROWS = {}

    @staticmethod
    def pv_layout():
        rows = {}
        n = 0

        def add(name, cnt=1):
            nonlocal n
            rows[name] = n
            n += cnt
        for L in range(2):
            add(f"a_norm{L}")
            add(f"b1_{L}")
            add(f"b2_{L}")
            add(f"wdw{L}", CW)
            add(f"b_dw{L}")
            add(f"ln_g{L}")
            add(f"ln_b{L}")
            add(f"b_out{L}")
        add("kv_norm")
        for j in range(2):
            add(f"b_norm{j}")
        for L in range(4):
            add(f"ffn_norm{L}")
        for g in range(3):
            add(f"k_norm{g}")
        for j in range(2):
            for g in range(3):
                add(f"q_norm{j}_{g}")
        return rows, n

    def pvs(self, name, c, off=0):
        i = self.pvrows[name] + off
        return self.pv[:, c, i:i + 1]

    def setup(self):
        nc = self.nc
        P = self.P
        self.pvrows, self.npv = self.pv_layout()
        self.inp("xin", [SEQ, D])
        self.inp("xs", [NS, D])
        self.inp("cconv", [2, 4, 30, D])
        self.inp("ckv0", [4, 128, 2048])
        self.inp("ckv1", [4, 512, 2048])
        self.inp("ckv2", [4, 2048, 2048])
        self.inp("pvec", [self.npv, D])
        self.inp("a_w_in", [2, D, 2 * D])
        self.inp("a_w_out", [2, D, D])
        self.inp("w_kv", [D, 6 * D])
        self.inp("w_q", [2, D, 3 * D])
        self.inp("w_o", [2, D, D])
        self.inp("w_gate_up", [4, D, 2 * DFF])
        self.inp("w_down", [4, DFF, D])
        self.inp("cmat", [6, 128, 128])
        self.inp("rot", [2, 128, KTW])
        self.inp("flag", [128, 1])
        self.inp("smaskin", [128, 544])
        self.outp("y", [NT, D])
        self.outp("convp", [2, 30, D])
        self.outp("convs", [2, 4, 30, D])
        self.outp("kvp0", [128, 2048])
        self.outp("kvp1", [512, 2048])
        self.outp("kvp2", [2048, 2048])
        self.outp("kvs0", [4, 128, 2048])
        self.outp("kvs1", [4, 512, 2048])
        self.outp("kvs2", [4, 2048, 2048])
        self.kt_scr = nc.dram_tensor("kt_scr", [3, 8, 128, KTW], BF16, kind="Internal").ap()
        self.v_scr = nc.dram_tensor("v_scr", [3, 8, KTW, 128], BF16, kind="Internal").ap()
        self.q_scr = nc.dram_tensor("q_scr", [3, 8, 128, NT], BF16, kind="Internal").ap()

        total = nc.sbuf_bytes_remaining - 64
        arena = nc.alloc_sbuf_tensor("arena", [128, total // 4], F32)
        base = nc.lookup_mloc(arena).addr
        self.M = M = Mem(nc, base, (total // 4) * 4)
        self.x = M.alloc([128, NCH, NT], F32, "x")
        self.u = M.alloc([128, NCH, NT], BF16, "u")
        self.G = M.alloc([128, NCH, GW], BF16, "glu")
        self.pv = M.alloc([128, NCH, self.npv], F32, "pv")
        self.ident = M.alloc([128, 128], F32, "ident")
        self.cb = M.alloc([128, 6, 128], BF16, "cb")
        self.maskf = M.alloc([128, 512], F32, "maskf")
        self.flag = M.alloc([128, 1], F32, "flag")
        self.sacc = M.alloc([128, 8, 2, NS], F32, "sacc")
        self.smask = M.alloc([128, 544], BF16, "smask")
        self.gstate = M.alloc([128, 2, NCH, 30], BF16, "gstate")
        self.g32 = M.alloc([128, 2, NCH, 30], F32, "g32")
        self.gs32 = M.alloc([128, 2, NCH, NS], F32, "gs32")
        self.wall = M.alloc([128, 6 * 2048], BF16, "wall")
        self.wbufs = [self.wall[:, i * 2048:(i + 1) * 2048] for i in range(6)]
        self.sqb = M.alloc([128, NCH, 512], BF16, "sqb")
        self.T32 = [M.alloc([128, 512], F32, f"t32_{i}") for i in range(4)]
        self._t32i = 0
        self.S32 = [M.alloc([128, 512], F32, f"s32_{i}") for i in range(4)]
        self._s32i = 0
        self.stg = [M.alloc([128, D], F32, f"stg{i}") for i in range(2)]
        self._stgi = 0
        self.hbuf = [M.alloc([128, 512], BF16, f"hbuf{i}") for i in range(4)]
        self._hi = 0
        self._di = 0
        self._sqi = 0
        self.persist_mark = M.mark()
        print("SBUF used", M.cur, "of", M.size)

        self.psb = [nc.alloc_psum_tensor(f"psb{i}", [128, 512], F32) for i in range(8)]
        self._psi = 0
        self._wi = 0

        cst = self.stg[1][:, 0:768].rearrange("p (a b) -> p a b", a=6)
        P.dma("sp", "stg1", lambda e: [e.dma_start(out=cst, in_=self.din["cmat"].rearrange("a p n -> p a n"))],
              writes=("stg1",))
        P.dma("sp", "c1", lambda e: [e.dma_start(out=self.flag[:], in_=self.din["flag"][:, :])], writes=("flag",))
        P.dma("pool", "c3", lambda e: [e.dma_start(out=self.smask[:], in_=self.din["smaskin"][:, :])], writes=("smask",))
        P.op("dve", lambda e: e.tensor_copy(out=self.cb[:], in_=cst), reads=("stg1",), writes=("cb",))
        P.op("dve", lambda e: e.tensor_copy(out=self.ident[:], in_=cst[:, 0, :]), reads=("stg1",), writes=("ident",))
        for q, src in enumerate([4, 4, 5, 5]):
            P.op("dve", lambda e, q=q, src=src: e.tensor_copy(out=self.maskf[:, q * 128:(q + 1) * 128], in_=cst[:, src, :]),
                 reads=("stg1",), writes=("maskf",))
        self.I_BF = self.cb[:, 0, :]
        self.ONES = self.cb[:, 1, :]
        self.BONES = self.cb[:, 2, :]
        self.SWAP = self.cb[:, 3, :]
        pst = self.stg[0]
        npv = self.npv
        P.dma("sp", "stg0", lambda e: [e.dma_start(out=pst[0:npv, :], in_=self.din["pvec"][:, :])], writes=("stg0",))
        for half in range(2):
            pt, pk = self.ps()

            def f(e, half=half, pt=pt):
                for cc in range(4):
                    c = half * 4 + cc
                    r = e.transpose(out=pt[:, cc * 128:cc * 128 + npv], in_=pst[0:npv, c * 128:(c + 1) * 128],
                                    identity=self.ident[0:npv, 0:npv])
                return r
            P.op("pe", f, reads=("stg0", "ident"), writes=(pk,))
            P.op("act", lambda e, half=half, pt=pt: e.activation(
                out=self.pv[:, half * 4:half * 4 + 4, :],
                in_=pt[:, :].rearrange("p (a b) -> p a b", a=4)[:, :, 0:npv], func=AF.Copy),
                reads=(pk,), writes=("pv",))
        P.op("dve", lambda e: e.memset(self.G[:, :, 0:30], 0.0), writes=("G:state",))

    def load_tokens(self, src_ap, nrows, col0):
        P = self.P
        st, sk = self.stage()
        P.dma("sp", sk, lambda e: [e.dma_start(out=st[0:nrows, :], in_=src_ap)], writes=(sk,))
        xkey = f"x:{col0 // 512}"
        for half in range(2):
            pt, pk = self.ps()

            def f(e, half=half, pt=pt):
                for cc in range(4):
                    c = half * 4 + cc
                    r = e.transpose(out=pt[:, cc * 128:cc * 128 + nrows], in_=st[0:nrows, c * 128:(c + 1) * 128],
                                    identity=self.ident[0:nrows, 0:nrows])
                return r
            P.op("pe", f, reads=(sk, "ident"), writes=(pk,))
            P.op("act", lambda e, half=half, pt=pt: e.activation(
                out=self.x[:, half * 4:half * 4 + 4, col0:col0 + nrows],
                in_=pt[:, :].rearrange("p (a b) -> p a b", a=4)[:, :, 0:nrows], func=AF.Copy),
                reads=(pk,), writes=(xkey,))

    def store_tokens(self, dst_ap, nrows, col0, src=None, skey=None):
        P = self.P
        src = self.x if src is None else src
        skey = f"x:{col0 // 512}" if skey is None else skey
        st, sk = self.stage()
        for half in range(2):
            pt, pk = self.ps()

            def f(e, half=half, pt=pt):
                for cc in range(4):
                    c = half * 4 + cc
                    r = e.transpose(out=pt[0:nrows, cc * 128:(cc + 1) * 128], in_=src[:, c, col0:col0 + nrows],
                                    identity=self.ident[:, :])
                return r
            P.op("pe", f, reads=(skey, "ident"), writes=(pk,))
            P.op("act", lambda e, half=half, pt=pt: e.activation(
                out=st[0:nrows, half * 512:(half + 1) * 512], in_=pt[0:nrows, :], func=AF.Copy),
                reads=(pk,), writes=(sk,))
        P.dma("sp", sk, lambda e: [e.dma_start(out=dst_ap, in_=st[0:nrows, :])], reads=(sk,))

    def rmsnorm(self, tiles, gname):
        P = self.P
        rs = {}
        for (c0, n) in tiles:
            tk = c0 // 512
            P.op("act", lambda e, c0=c0, n=n: e.activation(out=self.sqb[:, :, 0:n], in_=self.x[:, :, c0:c0 + n], func=AF.Square),
                 reads=(f"x:{tk}",), writes=SQK)
            pt, pk = self.ps()

            def f(e, pt=pt, n=n):
                for c in range(NCH):
                    r = e.matmul(pt[:, 0:n], lhsT=self.ONES, rhs=self.sqb[:, c, 0:n], start=(c == 0), stop=(c == NCH - 1))
                return r
            P.op("pe", f, reads=SQK + ("cb",), writes=(pk,))
            t, tkey = self.s32()
            P.op("act", lambda e, pt=pt, t=t, n=n: e.activation(out=t[:, 0:n], in_=pt[:, 0:n], func=AF.Ln, bias=EPS, scale=1.0 / D),
                 reads=(pk,), writes=(tkey,))
            P.op("act", lambda e, t=t, n=n: e.activation(out=t[:, 0:n], in_=t[:, 0:n], func=AF.Exp, scale=-0.5), reads=(tkey,), writes=(tkey,))

            def g(e, t=t, c0=c0, n=n):
                for c in range(NCH):
                    r = e.scalar_tensor_tensor(out=self.u[:, c, c0:c0 + n], in0=self.x[:, c, c0:c0 + n],
                                               scalar=self.pvs(gname, c), in1=t[:, 0:n], op0=ALU.mult, op1=ALU.mult)
                return r
            P.op("dve", g, reads=(tkey, f"x:{tk}", "pv"), writes=self.uk(tk))

    def conv_layer(self, L, isY, tiles=None):
        P = self.P
        if tiles is None:
            tiles = TILES_Y if isY else TILES_X
        w_in = self.din["a_w_in"]
        w_out = self.din["a_w_out"]
        G = self.G
        self.rmsnorm(tiles, f"a_norm{L}")
        if isY:
            P.op("dve", lambda e: e.tensor_copy(out=G[:, :, 0:30], in_=self.gstate[:, L, :, :]),
                 reads=("gstate",), writes=("G:state",))
            for s in range(4):
                st, sk = self.stage()
                P.dma("sp", sk, lambda e, st=st, s=s: [e.dma_start(out=st[0:30, :], in_=self.din["cconv"][L, s, :, :])], writes=(sk,))
                for half in range(2):
                    pt, pk = self.ps()

                    def f(e, half=half, pt=pt, st=st):
                        for cc in range(4):
                            c = half * 4 + cc
                            r = e.transpose(out=pt[:, cc * 128:cc * 128 + 30], in_=st[0:30, c * 128:(c + 1) * 128],
                                            identity=self.ident[0:30, 0:30])
                        return r
                    P.op("pe", f, reads=(sk, "ident"), writes=(pk,))
                    P.op("act", lambda e, half=half, pt=pt, s=s: e.activation(
                        out=G[:, half * 4:half * 4 + 4, GS0 + s * 38:GS0 + s * 38 + 30],
                        in_=pt[:, :].rearrange("p (a b) -> p a b", a=4)[:, :, 0:30], func=AF.Copy),
                        reads=(pk,), writes=("G:s",))
        for oc in range(NCH):
            wv, wk = self.wslot()
            wa = wv[:, 0:2048].rearrange("p (h k n) -> p h k n", h=2, k=NCH)
            src1 = w_in[L, :, oc * 128:(oc + 1) * 128].rearrange("(k p) n -> p k n", p=128)
            src2 = w_in[L, :, D + oc * 128:D + (oc + 1) * 128].rearrange("(k p) n -> p k n", p=128)
            P.dma("pool", wk, lambda e, wa=wa, src1=src1, src2=src2: [e.dma_start(out=wa[:, 0], in_=src1), e.dma_start(out=wa[:, 1], in_=src2)],
                  writes=(wk,), n=2)
            for (c0, n) in tiles:
                tk = c0 // 512
                p1, k1 = self.ps()
                p2, k2 = self.ps()

                def mm(e, p1=p1, p2=p2, wa=wa, c0=c0, n=n):
                    for k in range(NCH):
                        e.matmul(p1[:, 0:n], lhsT=wa[:, 0, k, :], rhs=self.u[:, k, c0:c0 + n], start=(k == 0), stop=(k == NCH - 1))
                    for k in range(NCH):
                        r = e.matmul(p2[:, 0:n], lhsT=wa[:, 1, k, :], rhs=self.u[:, k, c0:c0 + n], start=(k == 0), stop=(k == NCH - 1))
                    return r
                P.op("pe", mm, reads=(wk,) + self.uk(tk), writes=(k1, k2))
                sg, sgk = self.t32()
                P.op("act", lambda e, sg=sg, p2=p2, n=n, oc=oc: e.activation(out=sg[:, 0:n], in_=p2[:, 0:n], func=AF.Sigmoid,
                                                                       bias=self.pvs(f"b2_{L}", oc), scale=1.0),
                     reads=(k2, "pv"), writes=(sgk,))
                if n == 512:
                    gout = G[:, oc, 30 + c0:30 + c0 + n]
                    gkey = f"G:{tk}"
                else:
                    gout = G[:, oc, GS0:GS0 + 4 * 38].rearrange("p (s t) -> p s t", s=4)[:, :, 30:38]
                    gkey = "G:s"

                def glu(e, sg=sg, p1=p1, n=n, oc=oc, gout=gout, c0=c0):
                    if n == 512:
                        r = e.scalar_tensor_tensor(out=gout, in0=p1[:, 0:n], scalar=self.pvs(f"b1_{L}", oc), in1=sg[:, 0:n],
                                                   op0=ALU.add, op1=ALU.mult)
                        if c0 == 1536:
                            if isY:
                                r = e.scalar_tensor_tensor(out=self.g32[:, L, oc, :], in0=p1[:, 482:512], scalar=self.pvs(f"b1_{L}", oc),
                                                           in1=sg[:, 482:512], op0=ALU.add, op1=ALU.mult)
                            else:
                                r = e.scalar_tensor_tensor(out=self.gstate[:, L, oc, :], in0=p1[:, 482:512], scalar=self.pvs(f"b1_{L}", oc),
                                                           in1=sg[:, 482:512], op0=ALU.add, op1=ALU.mult)
                    else:
                        e.scalar_tensor_tensor(out=gout, in0=p1[:, 0:n].rearrange("p (s t) -> p s t", s=4), scalar=self.pvs(f"b1_{L}", oc),
                                               in1=sg[:, 0:n].rearrange("p (s t) -> p s t", s=4), op0=ALU.add, op1=ALU.mult)
                        r = e.scalar_tensor_tensor(out=self.gs32[:, L, oc, :], in0=p1[:, 0:n], scalar=self.pvs(f"b1_{L}", oc),
                                                   in1=sg[:, 0:n], op0=ALU.add, op1=ALU.mult)
                    return r
                wr = [gkey]
                if c0 == 1536:
                    wr.append("g32" if isY else "gstate")
                if n != 512:
                    wr.append("gs32")
                P.op("dve", glu, reads=(k1, sgk, "pv", "flag"), writes=tuple(wr))
                if c0 == 1536 and not isY:
                    P.op("dve", lambda e, oc=oc: e.tensor_scalar(out=self.gstate[:, L, oc, :], in0=self.gstate[:, L, oc, :],
                                                                 scalar1=self.flag[:, 0:1], scalar2=None, op0=ALU.mult),
                         reads=("gstate", "flag"), writes=("gstate",))
        gkeys_main = tuple(f"G:{i}" for i in range(4)) + ("G:state",)
        for c in range(NCH):
            i1 = 2 * self._di
            self._di = (self._di + 1) % 3
            sk1, sk2 = f"w{i1}", f"w{i1 + 1}"
            dg = self.wall[:, i1 * 2048:i1 * 2048 + CW * 128].rearrange("p (j n) -> p j n", j=CW)

            def mk(e, dg=dg, c=c):
                for j in range(CW):
                    r = e.activation(out=dg[:, j, :], in_=self.I_BF, func=AF.Identity, scale=self.pvs(f"wdw{L}", c, j))
                return r
            P.op("act", mk, reads=("cb", "pv"), writes=(sk1, sk2))
            for (c0, n) in tiles:
                tk = c0 // 512
                pt, pk = self.ps()
                if n == 512:
                    def cv(e, dg=dg, c=c, c0=c0, pt=pt):
                        for j in range(CW):
                            r = e.matmul(pt[:, 0:512], lhsT=dg[:, j, :], rhs=G[:, c, c0 + j:c0 + j + 512], start=(j == 0), stop=(j == CW - 1))
                        return r
                    rd = (sk1, sk2) + gkeys_main
                else:
                    def cv(e, dg=dg, c=c, pt=pt):
                        gsv = G[:, c, GS0:GS0 + 4 * 38].rearrange("p (s t) -> p s t", s=4)
                        for j in range(CW):
                            r = e.matmul(pt[:, 0:NS].rearrange("p (s t) -> p s t", s=4), lhsT=dg[:, j, :], rhs=gsv[:, :, j:j + 8],
                                         start=(j == 0), stop=(j == CW - 1))
                        return r
                    rd = (sk1, sk2, "G:s")
                P.op("pe", cv, reads=rd, writes=(pk,))
                P.op("act", lambda e, pt=pt, c=c, c0=c0, n=n: e.activation(out=self.u[:, c, c0:c0 + n], in_=pt[:, 0:n], func=AF.Identity,
                                                                     bias=self.pvs(f"b_dw{L}", c), scale=1.0),
                     reads=(pk, "pv"), writes=(f"u:{tk}:{c}",))
        if getattr(self, "stop_stage", None) == "B":
            return
        def ln_tile(c0, n):
            tk = c0 // 512
            ukey = self.uk(tk)
            P.op("act", lambda e, c0=c0, n=n: e.activation(out=self.sqb[:, :, 0:n], in_=self.u[:, :, c0:c0 + n], func=AF.Square),
                 reads=ukey, writes=SQK)
            psm, ksm = self.ps()
            psq, ksq = self.ps()

            def st(e, psm=psm, psq=psq, c0=c0, n=n):
                for c in range(NCH):
                    e.matmul(psm[:, 0:n], lhsT=self.ONES, rhs=self.u[:, c, c0:c0 + n], start=(c == 0), stop=(c == NCH - 1))
                for c in range(NCH):
                    r = e.matmul(psq[:, 0:n], lhsT=self.ONES, rhs=self.sqb[:, c, 0:n], start=(c == 0), stop=(c == NCH - 1))
                return r
            P.op("pe", st, reads=ukey + SQK + ("cb",), writes=(ksm, ksq))
            mu, kmu = self.s32()
            va, kva = self.s32()

            P.op("dve", lambda e, mu=mu, psm=psm, n=n: e.tensor_scalar(out=mu[:, 0:n], in0=psm[:, 0:n], scalar1=1.0 / D, scalar2=None, op0=ALU.mult),
                 reads=(ksm,), writes=(kmu,))
            P.op("dve", lambda e, mu=mu, va=va, n=n: e.tensor_tensor(out=va[:, 0:n], in0=mu[:, 0:n], in1=mu[:, 0:n], op=ALU.mult),
                 reads=(kmu,), writes=(kva,))
            P.op("dve", lambda e, va=va, psq=psq, n=n: e.scalar_tensor_tensor(out=va[:, 0:n], in0=psq[:, 0:n], scalar=1.0 / D, in1=va[:, 0:n],
                                                                         op0=ALU.mult, op1=ALU.subtract),
                 reads=(ksq, kva), writes=(kva,))
            P.op("act", lambda e, va=va, n=n: e.activation(out=va[:, 0:n], in_=va[:, 0:n], func=AF.Ln, bias=EPS, scale=1.0),
                 reads=(kva,), writes=(kva,))
            P.op("act", lambda e, va=va, n=n: e.activation(out=va[:, 0:n], in_=va[:, 0:n], func=AF.Exp, scale=-0.5), reads=(kva,), writes=(kva,))
            for c in range(NCH):
                t, tkey = self.t32()

                P.op("dve", lambda e, t=t, c=c, c0=c0, n=n, mu=mu: e.tensor_tensor(out=t[:, 0:n], in0=self.u[:, c, c0:c0 + n], in1=mu[:, 0:n], op=ALU.subtract),
                     reads=(ukey[c], kmu), writes=(tkey,))
                P.op("dve", lambda e, t=t, n=n, va=va: e.tensor_tensor(out=t[:, 0:n], in0=t[:, 0:n], in1=va[:, 0:n], op=ALU.mult),
                     reads=(tkey, kva), writes=(tkey,))
                P.op("act", lambda e, t=t, c=c, c0=c0, n=n: e.activation(out=self.u[:, c, c0:c0 + n], in_=t[:, 0:n], func=AF.Silu,
                                                                   bias=self.pvs(f"ln_b{L}", c), scale=self.pvs(f"ln_g{L}", c)),
                     reads=(tkey, "pv"), writes=(ukey[c],))
        if getattr(self, "stop_stage", None) == "C":
            for (c0, n) in tiles:
                ln_tile(c0, n)
            return
        wq = []
        for q in range(4):
            wv, wk = self.wslot()
            wa = wv[:, 0:2048].rearrange("p (k n) -> p k n", k=NCH)
            src = w_out[L, :, q * 256:(q + 1) * 256].rearrange("(k p) n -> p k n", p=128)
            P.dma("pool", wk, lambda e, wa=wa, src=src: [e.dma_start(out=wa, in_=src)], writes=(wk,))
            wq.append((wa, wk))

        def wo_tile(c0, n):
            tk = c0 // 512
            for oc in range(NCH):
                wa, wk = wq[oc // 2]
                off = (oc % 2) * 128
                pt, pk = self.ps()

                def mm(e, pt=pt, wa=wa, off=off, c0=c0, n=n):
                    for k in range(NCH):
                        r = e.matmul(pt[:, 0:n], lhsT=wa[:, k, off:off + 128], rhs=self.u[:, k, c0:c0 + n], start=(k == 0), stop=(k == NCH - 1))
                    return r
                P.op("pe", mm, reads=(wk,) + self.uk(tk), writes=(pk,))
                P.op("dve", lambda e, pt=pt, oc=oc, c0=c0, n=n: e.scalar_tensor_tensor(
                    out=self.x[:, oc, c0:c0 + n], in0=pt[:, 0:n], scalar=self.pvs(f"b_out{L}", oc), in1=self.x[:, oc, c0:c0 + n],
                    op0=ALU.add, op1=ALU.add), reads=(pk, "pv", f"x:{tk}"), writes=(f"x:{tk}",))
        prev_t = None
        for (c0, n) in tiles:
            ln_tile(c0, n)
            if prev_t is not None:
                wo_tile(*prev_t)
            prev_t = (c0, n)
        wo_tile(*prev_t)

    def ffn(self, L, tiles):
        P = self.P
        wgu = self.din["w_gate_up"]
        wdn = self.din["w_down"]
        self.rmsnorm(tiles, f"ffn_norm{L}")
        for fg in range(NFC // 2):
            f0 = fg * 256
            wgv, kg = self.wslot()
            wuv, ku = self.wslot()
            wdv, kd = self.wslot()
            wg = wgv[:, 0:2048].rearrange("p (k n) -> p k n", k=NCH)
            wu = wuv[:, 0:2048].rearrange("p (k n) -> p k n", k=NCH)
            wd = wdv[:, 0:2048].rearrange("p (j n) -> p j n", j=2)
            sg_ = wgu[L, :, f0:f0 + 256].rearrange("(k p) n -> p k n", p=128)
            su_ = wgu[L, :, DFF + f0:DFF + f0 + 256].rearrange("(k p) n -> p k n", p=128)
            sd_ = wdn[L, f0:f0 + 256, :].rearrange("(j p) n -> p j n", p=128)
            P.dma("pool", kg, lambda e, wg=wg, sg_=sg_: [e.dma_start(out=wg, in_=sg_)], writes=(kg,))
            P.dma("pool", ku, lambda e, wu=wu, su_=su_: [e.dma_start(out=wu, in_=su_)], writes=(ku,))
            P.dma("pool", kd, lambda e, wd=wd, sd_=sd_: [e.dma_start(out=wd, in_=sd_)], writes=(kd,))

            def gate_up_steps(c0, n):
                tk = c0 // 512
                hkeys = []
                hs = []
                steps = []
                for j in range(2):
                    sg, sgk = self.t32()
                    hb = self.hbuf[self._hi]
                    hk = f"h{self._hi}"
                    self._hi = (self._hi + 1) % len(self.hbuf)
                    hs.append(hb)
                    hkeys.append(hk)
                    bank = {}

                    def s_g(bank=bank, j=j):
                        pg, kpg = self.ps()
                        bank["g"] = (pg, kpg)

                        def mm(e, pg=pg, j=j, c0=c0, n=n, wg=wg):
                            for k in range(NCH):
                                r = e.matmul(pg[:, 0:n], lhsT=wg[:, k, j * 128:(j + 1) * 128], rhs=self.u[:, k, c0:c0 + n], start=(k == 0), stop=(k == NCH - 1))
                            return r
                        P.op("pe", mm, reads=(kg,) + self.uk(tk), writes=(kpg,))

                    def s_u(bank=bank, j=j):
                        pu, kpu = self.ps()
                        bank["u"] = (pu, kpu)

                        def mm(e, pu=pu, j=j, c0=c0, n=n, wu=wu):
                            for k in range(NCH):
                                r = e.matmul(pu[:, 0:n], lhsT=wu[:, k, j * 128:(j + 1) * 128], rhs=self.u[:, k, c0:c0 + n], start=(k == 0), stop=(k == NCH - 1))
                            return r
                        P.op("pe", mm, reads=(ku,) + self.uk(tk), writes=(kpu,))

                    def s_e(bank=bank, sg=sg, sgk=sgk, hb=hb, hk=hk):
                        pg, kpg = bank["g"]
                        pu, kpu = bank["u"]
                        P.op("act", lambda e, sg=sg, pg=pg, n=n: e.activation(out=sg[:, 0:n], in_=pg[:, 0:n], func=AF.Silu),
                             reads=(kpg,), writes=(sgk,))
                        P.op("dve", lambda e, hb=hb, sg=sg, pu=pu, n=n: e.tensor_tensor(out=hb[:, 0:n], in0=sg[:, 0:n], in1=pu[:, 0:n], op=ALU.mult),
                             reads=(sgk, kpu), writes=(hk,))
                    steps += [s_g, s_u, s_e]
                return steps, hs, hkeys

            def down_steps(c0, n, hs, hkeys):
                tk = c0 // 512
                steps = []
                for dc in range(NCH):
                    def s_d(dc=dc):
                        pd, kpd = self.ps()

                        def dn(e, pd=pd, dc=dc, hs=tuple(hs), n=n, wd=wd):
                            e.matmul(pd[:, 0:n], lhsT=wd[:, 0, dc * 128:(dc + 1) * 128], rhs=hs[0][:, 0:n], start=True, stop=False)
                            return e.matmul(pd[:, 0:n], lhsT=wd[:, 1, dc * 128:(dc + 1) * 128], rhs=hs[1][:, 0:n], start=False, stop=True)
                        P.op("pe", dn, reads=(kd,) + tuple(hkeys), writes=(kpd,))
                        P.op("dve", lambda e, pd=pd, dc=dc, c0=c0, n=n: e.tensor_tensor(out=self.x[:, dc, c0:c0 + n], in0=pd[:, 0:n],
                                                                                  in1=self.x[:, dc, c0:c0 + n], op=ALU.add),
                             reads=(kpd, f"x:{tk}"), writes=(f"x:{tk}",))
                    steps.append(s_d)
                return steps
            prev = None
            for (c0, n) in tiles:
                gu, hs, hkeys = gate_up_steps(c0, n)
                dn_ = down_steps(*prev) if prev is not None else []
                order = [gu[0]] + dn_[0:2] + [gu[1]] + dn_[2:4] + [gu[2], gu[3]] + dn_[4:6] + [gu[4]] + dn_[6:8] + [gu[5]]
                for st_ in order:
                    st_()
                prev = (c0, n, hs, hkeys)
            for st_ in down_steps(*prev):
                st_()

    def phase1_test(self):
        self.setup()
        xin = self.din["xin"]
        for sup in range(2):
            isY = sup == 1
            for tb in range(16):
                self.load_tokens(xin[sup * HALF + tb * 128: sup * HALF + (tb + 1) * 128, :], 128, tb * 128)
            if isY:
                self.load_tokens(self.din["xs"][:, :], NS, HALF)
            tiles = TILES_Y if isY else TILES_X
            for L in range(2):
                self.conv_layer(L, isY)
                self.ffn(L, tiles)
        for tb in range(16):
            self.store_tokens(self.dout["y"][tb * 128:(tb + 1) * 128, :], 128, tb * 128)
        self.store_tokens(self.dout["y"][HALF:HALF + NS, :], NS, HALF)

    def finish(self):
        nc = self.nc
        P = self.P
        chans = sorted(P.chan_n.keys())
        import contextlib
        with contextlib.ExitStack() as es:
            sems = {e: es.enter_context(nc.semaphore(f"sem_{e}")) for e in ENGS}
            csems = {ch: es.enter_context(nc.semaphore(f"semd_{ch}")) for ch in chans}
            block = es.enter_context(nc.Block())
            regs = {"pe": block.tensor, "act": block.scalar, "dve": block.vector, "pool": block.gpsimd, "sp": block.sync}
            P.emit(nc, regs, sems, csems)
        return nc


def _const_mats():
    p = np.arange(128)
    ident = np.eye(128, dtype=np.float32)
    ones = np.ones((128, 128), np.float32)
    bones = (p[:, None] // 64 == p[None, :] // 64).astype(np.float32)
    partner = (p // 64) * 64 + ((p % 64) + 32) % 64
    swap = np.zeros((128, 128), np.float32)
    swap[partner, p] = 1.0
    maskP = (p[None, :] <= p[:, None]).astype(np.float32)
    maskC = (p[:, None] <= p[None, :]).astype(np.float32)
    return np.stack([ident, ones, bones, swap, maskP, maskC]).astype(np.float32)


def _smask():
    m = np.zeros((128, 544), np.float32)
    k = np.arange(128)[:, None]
    q = np.arange(8)[None, :]
    for a in range(16):
        m[:, a * 8:(a + 1) * 8] = (k >= q)
        m[:, 128 + a * 2] = 1.0
        m[:, 128 + a * 2 + 1] = (np.arange(128) >= 1)
    t = np.arange(8)[:, None]
    M0 = (t <= q).astype(np.float32)
    M1 = ((t == q) | (t == q - 4)).astype(np.float32)
    M2 = (t == q).astype(np.float32)
    for g, Mg in enumerate((M0, M1, M2)):
        for a in range(16):
            m[0:8, 160 + g * 128 + a * 8:160 + g * 128 + (a + 1) * 8] = Mg
    return m


def _rot_tables(half_id):
    half = 32
    inv = np.float32(10000.0) ** (-(np.arange(half, dtype=np.float32) / np.float32(half)))
    pos = np.zeros(KTW, np.float32)
    l = np.arange(SEQ)
    if half_id == 1:
        pos[:SEQ] = l
    else:
        pos[:SEQ] = np.maximum(l - HALF, 0)
    pos[SEQ:] = np.tile(PAST + np.arange(8), 4)
    ang = (pos[:, None].astype(np.float32) * inv[None, :].astype(np.float32)).astype(np.float32)
    p = np.arange(128)
    idx = (p % 64) % 32
    cosT = np.cos(ang)[:, idx].T.astype(np.float32)
    sinT = np.sin(ang)[:, idx].T.astype(np.float32)
    sign = np.where((p % 64) < 32, -1.0, 1.0).astype(np.float32)
    return np.stack([cosT, sinT * sign[:, None]]).astype(np.float32)


def _pvec(inp):
    rows, n = Builder.pv_layout()
    pv = np.zeros((n, D), np.float32)
    for L in range(2):
        pv[rows[f"a_norm{L}"]] = inp["a_norm"][L]
        pv[rows[f"b1_{L}"]] = inp["a_b_in"][L][:D]
        pv[rows[f"b2_{L}"]] = inp["a_b_in"][L][D:]
        pv[rows[f"wdw{L}"]:rows[f"wdw{L}"] + CW] = inp["a_w_dw"][L]
        pv[rows[f"b_dw{L}"]] = inp["a_b_dw"][L]
        pv[rows[f"ln_g{L}"]] = inp["a_ln_g"][L]
        pv[rows[f"ln_b{L}"]] = inp["a_ln_b"][L]
        pv[rows[f"b_out{L}"]] = inp["a_b_out"][L]
    pv[rows["kv_norm"]] = inp["kv_norm"]
    for j in range(2):
        pv[rows[f"b_norm{j}"]] = inp["b_norm"][j]
    for L in range(4):
        pv[rows[f"ffn_norm{L}"]] = inp["ffn_norm"][L]
    for g in range(3):
        pv[rows[f"k_norm{g}"], :128] = np.tile(inp["k_norm"][g], 2)
    for j in range(2):
        for g in range(3):
            pv[rows[f"q_norm{j}_{g}"], :128] = np.tile(inp["q_norm"][j, g], 2)
    return pv


def make_in_maps(inp):
    inp = {k: np.asarray(v) for k, v in inp.items()}
    cmat = _const_mats()
    pvec = _pvec(inp)
    rots = [_rot_tables(0), _rot_tables(1)]
    shared = {
        "pvec": pvec, "cmat": cmat, "smaskin": _smask(),
        "a_w_in": np.ascontiguousarray(inp["a_w_in"], np.float32), "a_w_out": np.ascontiguousarray(inp["a_w_out"], np.float32),
        "w_kv": np.ascontiguousarray(inp["w_kv"], np.float32), "w_q": np.ascontiguousarray(inp["w_q"], np.float32),
        "w_o": np.ascontiguousarray(inp["w_o"], np.float32), "w_gate_up": np.ascontiguousarray(inp["w_gate_up"], np.float32),
        "w_down": np.ascontiguousarray(inp["w_down"], np.float32),
    }
    maps = []
    for c in range(8):
        b, h = c // 2, c % 2
        if h == 1:
            xin = np.ascontiguousarray(inp["x_prompt"][b], np.float32)
        else:
            xin = np.concatenate([np.zeros((HALF, D), np.float32), inp["x_prompt"][b, :HALF]], axis=0)
        m = dict(shared)
        m["xin"] = xin
        m["xs"] = np.ascontiguousarray(inp["x_sample"][4 * c:4 * c + 4].reshape(NS, D), np.float32)
        m["cconv"] = np.ascontiguousarray(inp["cache_conv"][:, 4 * c:4 * c + 4], np.float32)
        m["ckv0"] = np.ascontiguousarray(inp["cache_kv_w128"][4 * c:4 * c + 4].reshape(4, 128, 2048), np.float32)
        m["ckv1"] = np.ascontiguousarray(inp["cache_kv_w512"][4 * c:4 * c + 4].reshape(4, 512, 2048), np.float32)
        m["ckv2"] = np.ascontiguousarray(inp["cache_kv_w2048"][4 * c:4 * c + 4].reshape(4, 2048, 2048), np.float32)
        m["rot"] = rots[h]
        m["flag"] = np.full((128, 1), float(h), np.float32)
        maps.append(m)
    return maps


def _dbg_t0(self):
    self.setup()
    xin = self.din["xin"]
    self.load_tokens(xin[HALF:HALF + 128, :], 128, 0)
    self.store_tokens(self.dout["y"][0:128, :], 128, 0)


Builder.dbg_t0 = _dbg_t0


def _dbg_c0(self):
    self.setup()
    xin = self.din["xin"]
    tiles = [(0, 512)]
    for tb in range(4):
        self.load_tokens(xin[tb * 128:(tb + 1) * 128, :], 128, tb * 128)
    import os
    self.stop_stage = os.environ.get("STOP_STAGE")
    self.conv_layer(0, False, tiles=tiles)
    if self.stop_stage is None:
        self.ffn(0, tiles)
    for tb in range(4):
        self.store_tokens(self.dout["y"][tb * 128:(tb + 1) * 128, :], 128, tb * 128)
    self.dbg_sbuf = [self.M.names[k] for k in ("u", "glu", "pv", "x")]


Builder.dbg_c0 = _dbg_c0


def _ovl_G(self):
    off, nb = self.M.offs["glu"]
    return Mem(self.nc, off, nb)


def _load_rot(self, OM, col_src0, ncols_main, with_sample):
    P = self.P
    rc = OM.alloc([128, NT], F32, "rotc")
    rs = OM.alloc([128, NT], F32, "rots")
    rot = self.din["rot"]

    def f(e):
        r = [e.dma_start(out=rc[:, 0:HALF], in_=rot[0, :, col_src0:col_src0 + HALF]),
             e.dma_start(out=rs[:, 0:HALF], in_=rot[1, :, col_src0:col_src0 + HALF])]
        if with_sample:
            r += [e.dma_start(out=rc[:, HALF:NT], in_=rot[0, :, SEQ:SEQ + NS]),
                  e.dma_start(out=rs[:, HALF:NT], in_=rot[1, :, SEQ:SEQ + NS])]
        return r
    P.dma("sp", "rot", f, writes=("rot",), n=4 if with_sample else 2)
    return rc, rs


def _head_proj(self, tiles, w_src_fn, gain_fn, dst_fn, rc, rs, OM, out32_fn=None, need32_fn=None, tiles_fn=None):
    P = self.P
    kst = [OM.alloc([128, 512], BF16, f"kst{i}") for i in range(2)]
    ksti = 0
    for g in range(3):
        for hp in range(8):
            wv, wk = self.wslot()
            wa = wv[:, 0:1024].rearrange("p (k n) -> p k n", k=NCH)
            src = w_src_fn(g, hp)
            P.dma("pool", wk, lambda e, wa=wa, src=src: [e.dma_start(out=wa, in_=src)], writes=(wk,))
            for (c0, n) in (tiles if tiles_fn is None else tiles_fn(g)):
                tk = c0 // 512
                pr, kpr = self.ps()

                def mm(e, pr=pr, wa=wa, c0=c0, n=n):
                    for k in range(NCH):
                        r = e.matmul(pr[:, 0:n], lhsT=wa[:, k, :], rhs=self.u[:, k, c0:c0 + n], start=(k == 0), stop=(k == NCH - 1))
                    return r
                P.op("pe", mm, reads=(wk,) + self.uk(tk), writes=(kpr,))
                si = self._sqi
                self._sqi = (si + 2) % 8
                sq = self.sqb[:, si, 0:n]
                xg = self.sqb[:, si + 1, 0:n]
                ksq, kxg = f"sqb{si}", f"sqb{si + 1}"
                P.op("act", lambda e, sq=sq, pr=pr, n=n: e.activation(out=sq, in_=pr[:, 0:n], func=AF.Square), reads=(kpr,), writes=(ksq,))
                P.op("act", lambda e, xg=xg, pr=pr, n=n, g=g: e.activation(out=xg, in_=pr[:, 0:n], func=AF.Identity, scale=gain_fn(g)),
                     reads=(kpr, "pv"), writes=(kxg,))
                p1, k1 = self.ps()
                p2, k2 = self.ps()
                P.op("pe", lambda e, p1=p1, sq=sq, n=n: e.matmul(p1[:, 0:n], lhsT=self.BONES, rhs=sq, start=True, stop=True),
                     reads=(ksq, "cb"), writes=(k1,))
                P.op("pe", lambda e, p2=p2, xg=xg, n=n: e.matmul(p2[:, 0:n], lhsT=self.SWAP, rhs=xg, start=True, stop=True),
                     reads=(kxg, "cb"), writes=(k2,))
                rstd, krs = self.s32()
                P.op("act", lambda e, rstd=rstd, p1=p1, n=n: e.activation(out=rstd[:, 0:n], in_=p1[:, 0:n], func=AF.Ln, bias=EPS, scale=1.0 / 64),
                     reads=(k1,), writes=(krs,))
                P.op("act", lambda e, rstd=rstd, n=n: e.activation(out=rstd[:, 0:n], in_=rstd[:, 0:n], func=AF.Exp, scale=-0.5), reads=(krs,), writes=(krs,))
                t1, kt1 = self.t32()
                t2, kt2 = self.t32()
                P.op("pool", lambda e, t1=t1, xg=xg, c0=c0, n=n: e.tensor_tensor(out=t1[:, 0:n], in0=xg, in1=rc[:, c0:c0 + n], op=ALU.mult),
                     reads=(kxg, "rot"), writes=(kt1,))
                P.op("dve", lambda e, t2=t2, p2=p2, c0=c0, n=n: e.tensor_tensor(out=t2[:, 0:n], in0=p2[:, 0:n], in1=rs[:, c0:c0 + n], op=ALU.mult),
                     reads=(k2, "rot"), writes=(kt2,))
                P.op("dve", lambda e, t1=t1, t2=t2, n=n: e.tensor_tensor(out=t1[:, 0:n], in0=t1[:, 0:n], in1=t2[:, 0:n], op=ALU.add),
                     reads=(kt1, kt2), writes=(kt1,))
                ks = kst[ksti]
                kk = f"kst{ksti}"
                ksti = (ksti + 1) % 2
                P.op("dve", lambda e, ks=ks, t1=t1, rstd=rstd, n=n: e.tensor_tensor(out=ks[:, 0:n], in0=t1[:, 0:n], in1=rstd[:, 0:n], op=ALU.mult),
                     reads=(kt1, krs), writes=(kk,))
                if out32_fn is not None and need32_fn is not None and need32_fn(g, c0, n):
                    P.op("dve", lambda e, t1=t1, rstd=rstd, n=n: e.tensor_tensor(out=t1[:, 0:n], in0=t1[:, 0:n], in1=rstd[:, 0:n], op=ALU.mult),
                         reads=(kt1, krs), writes=(kt1,))
                dst = dst_fn(g, hp, c0, n)
                P.dma("sp", kk, lambda e, ks=ks, dst=dst, n=n: [e.dma_start(out=dst, in_=ks[:, 0:n])], reads=(kk,))
                if out32_fn is not None and need32_fn is not None and need32_fn(g, c0, n):
                    out32_fn(g, hp, c0, n, t1, kt1)


def _kv_stage(self, isY):
    P = self.P
    P.fence()
    tiles = TILES_Y if isY else TILES_X
    self.rmsnorm(tiles, "kv_norm")
    OM = self.ovl_G()
    loc0 = HALF if isY else 0
    rc, rs = self.load_rot(OM, loc0, HALF, isY)
    w_kv = self.din["w_kv"]
    kt = self.kt_scr
    vs = self.v_scr

    def w_src(g, hp):
        c = g * 2048 + hp * 128
        return w_kv[:, c:c + 128].rearrange("(k p) n -> p k n", p=128)

    def gain(g):
        return self.pvs(f"k_norm{g}", 0)

    def dst(g, hp, c0, n):
        if n == 512:
            return kt[g, hp, :, loc0 + c0:loc0 + c0 + n]
        return kt[g, hp, :, SEQ:SEQ + NS]

    def out32(g, hp, c0, n, t1, kt1):
        import os
        if not isY or os.environ.get("NO_OUT32"):
            return
        W = WINS[g]
        if n == 512:
            blocks = [b for b in range(4) if c0 + b * 128 >= HALF - W]
            if not blocks:
                return
            pt, pk = self.ps()

            def tr(e, pt=pt, t1=t1, blocks=tuple(blocks)):
                for b in blocks:
                    r = e.transpose(out=pt[:, b * 128:(b + 1) * 128], in_=t1[:, b * 128:(b + 1) * 128], identity=self.ident[:, :])
                return r
            P.op("pe", tr, reads=(kt1, "ident"), writes=(pk,))
            st, sk = self.stage()
            b0, nb = blocks[0], len(blocks)
            P.op("act", lambda e, st=st, pt=pt, b0=b0, nb=nb: e.activation(out=st[:, b0 * 128:(b0 + nb) * 128], in_=pt[:, b0 * 128:(b0 + nb) * 128], func=AF.Copy),
                 reads=(pk,), writes=(sk,))
            row0 = c0 + b0 * 128 - (HALF - W)
            dsto = self.dout[f"kvp{g}"][row0:row0 + nb * 128, hp * 128:(hp + 1) * 128].rearrange("(b t) f -> t b f", t=128)
            P.dma("sp", sk, lambda e, st=st, dsto=dsto, b0=b0, nb=nb: [e.dma_start(out=dsto, in_=st[:, b0 * 128:(b0 + nb) * 128].rearrange("t (b f) -> t b f", b=nb))],
                  reads=(sk,))
        else:
            pt, pk = self.ps()
            P.op("pe", lambda e, pt=pt, t1=t1: e.transpose(out=pt[0:NS, 0:128], in_=t1[:, 0:NS], identity=self.ident[:, :]),
                 reads=(kt1, "ident"), writes=(pk,))
            st, sk = self.stage()
            P.op("act", lambda e, st=st, pt=pt: e.activation(out=st[0:NS, 0:128], in_=pt[0:NS, 0:128], func=AF.Copy), reads=(pk,), writes=(sk,))
            dsto = self.dout[f"kvs{g}"][:, W - 8:W, hp * 128:(hp + 1) * 128]

            def f(e, st=st, dsto=dsto):
                return [e.dma_start(out=dsto[s], in_=st[s * 8:(s + 1) * 8, 0:128]) for s in range(4)]
            P.dma("sp", sk, f, reads=(sk,), n=4)
    def need32(g, c0, n):
        return isY and (n != 512 or c0 + 512 > HALF - WINS[g])
    def tiles_for(g):
        if isY:
            return tiles
        return [(c0, n) for (c0, n) in tiles if c0 + n > HALF - WINS[g]]
    self.head_proj(tiles, w_src, gain, dst, rc, rs, OM, out32, need32, tiles_for)

    vst = [OM.alloc([128, 256], BF16, f"vst{i}") for i in range(2)]
    vi = 0
    blocks = [(tb * 128, 128) for tb in range(16)] + ([(HALF, NS)] if isY else [])
    for g in range(3):
        W = WINS[g]
        for q in range(4):
            wv, wk = self.wslot()
            wa = wv[:, 0:2048].rearrange("p (k n) -> p k n", k=NCH)
            c = g * 2048 + 1024 + q * 256
            src = w_kv[:, c:c + 256].rearrange("(k p) n -> p k n", p=128)
            P.dma("pool", wk, lambda e, wa=wa, src=src: [e.dma_start(out=wa, in_=src)], writes=(wk,))
            for (c0, m) in blocks:
                if (not isY) and c0 + m <= HALF - W:
                    continue
                tk = c0 // 512
                pt, pk = self.ps()

                def mm(e, pt=pt, wa=wa, c0=c0, m=m):
                    for k in range(NCH):
                        r = e.matmul(pt[0:m, 0:256], lhsT=self.u[:, k, c0:c0 + m], rhs=wa[:, k, :], start=(k == 0), stop=(k == NCH - 1))
                    return r
                P.op("pe", mm, reads=(wk,) + self.uk(tk), writes=(pk,))
                vb = vst[vi]
                vk = f"vst{vi}"
                vi = (vi + 1) % 2
                P.op("act", lambda e, vb=vb, pt=pt, m=m: e.activation(out=vb[0:m, :], in_=pt[0:m, 0:256], func=AF.Copy), reads=(pk,), writes=(vk,))
                if m == 128:
                    l0 = loc0 + c0
                else:
                    l0 = SEQ
                dstv = vs[g, 2 * q:2 * q + 2, l0:l0 + m, :].rearrange("h t f -> t h f")
                P.dma("sp", vk, lambda e, vb=vb, dstv=dstv, m=m: [e.dma_start(out=dstv, in_=vb[0:m, :].rearrange("t (h f) -> t h f", h=2))],
                      reads=(vk,))
                import os
                if isY and not os.environ.get("NO_VOUT"):
                    need = (m == NS) or (c0 >= HALF - W)
                    if need:
                        st, sk = self.stage()
                        P.op("dve", lambda e, st=st, pt=pt, m=m: e.tensor_copy(out=st[0:m, 0:256], in_=pt[0:m, 0:256]), reads=(pk,), writes=(sk,))
                        if m == 128:
                            row0 = c0 - (HALF - W)
                            dsto = self.dout[f"kvp{g}"][row0:row0 + 128, 1024 + q * 256:1024 + (q + 1) * 256]
                            P.dma("sp", sk, lambda e, st=st, dsto=dsto: [e.dma_start(out=dsto, in_=st[:, 0:256])], reads=(sk,))
                        else:
                            dsto = self.dout[f"kvs{g}"][:, W - 8:W, 1024 + q * 256:1024 + (q + 1) * 256]

                            def f(e, st=st, dsto=dsto):
                                return [e.dma_start(out=dsto[s], in_=st[s * 8:(s + 1) * 8, 0:256]) for s in range(4)]
                            P.dma("sp", sk, f, reads=(sk,), n=4)
    P.fence()


Builder.ovl_G = _ovl_G
Builder.load_rot = _load_rot
Builder.head_proj = _head_proj
Builder.kv_stage = _kv_stage


SCALE = 0.125


def sl(start, n, step):
    return slice(start, start + step * (n - 1) + 1, step)

KBASE = (HALF - 128, HALF - 512, 0)
NBO = (16, 4, 1)


def _q_stage(self, j):
    P = self.P
    P.fence()
    self.rmsnorm(TILES_Y, f"b_norm{j}")
    OM = self.ovl_G()
    rc, rs = self.load_rot(OM, HALF, HALF, True)
    w_q = self.din["w_q"]

    def w_src(g, hp):
        c = g * 1024 + hp * 128
        return w_q[j, :, c:c + 128].rearrange("(k p) n -> p k n", p=128)

    def gain(g):
        return self.pvs(f"q_norm{j}_{g}", 0)

    def dst(g, hp, c0, n):
        return self.q_scr[g, hp, :, c0:c0 + n]
    self.head_proj(TILES_Y, w_src, gain, dst, rc, rs, OM, None)
    P.fence()


def _attn_layer(self, j, STOP=""):
    P = self.P
    self.q_stage(j)
    if STOP == "Q":
        return
    self.sample_stage(j)
    if STOP == "S":
        return
    uo, ub = self.M.offs["u"]
    go, gb = self.M.offs["glu"]
    assert uo + ub == go
    OM = Mem(self.nc, uo, ub + gb)
    QT = [OM.alloc([128, NT], BF16, f"QT{g}") for g in range(3)]
    kw = [SEQ - KBASE[g] + NS for g in range(3)]
    KT = [OM.alloc([128, kw[g]], BF16, f"KT{g}") for g in range(3)]
    nblk = [DILS[g] * (NBO[g] + 1) for g in range(3)]
    VT = [OM.alloc([128, nblk[g], 128], BF16, f"VT{g}") for g in range(3)]
    acc = OM.alloc([128, 2, NT], F32, "acc")
    PT = [OM.alloc([128, 512], BF16, f"PT{i}") for i in range(3)]
    qo, qb = OM.offs["QT0"]
    assert OM.offs["QT1"][0] == qo + qb and qb == NT * 2
    oT = self.nc.alloc_sbuf_tensor_at(f"oT_{j}", [128, 2, NT], BF16, offset=qo)
    self.att = dict(QT=QT, KT=KT, VT=VT, acc=acc, PT=PT, OM=OM)
    pti = 0
    w_o = self.din["w_o"]
    for hp in range(8):
        for g in range(3):
            r = DILS[g]
            nbo = NBO[g]
            kb0 = KBASE[g]
            P.dma("sp", f"QT{g}", lambda e, g=g, hp=hp: [e.dma_start(out=QT[g][:, :], in_=self.q_scr[g, hp, :, :])], writes=(f"QT{g}",))
            P.dma("sp", f"KT{g}", lambda e, g=g, hp=hp, kb0=kb0: [e.dma_start(out=KT[g][:, :], in_=self.kt_scr[g, hp, :, kb0:KTW])],
                  writes=(f"KT{g}",))
            vsrc = self.v_scr[g, hp, kb0:kb0 + r * 128 * (nbo + 1), :].rearrange("(m i c) f -> i c m f", i=128, c=r)
            vdst = VT[g][:, :, :].rearrange("p (c m) f -> p c m f", c=r)
            if r == 1:
                P.dma("sp", f"VT{g}", lambda e, vdst=vdst, vsrc=vsrc: [e.dma_start(out=vdst[:, 0], in_=vsrc[:, 0])], writes=(f"VT{g}",))
            else:
                def fv(e, vdst=vdst, vsrc=vsrc, r=r):
                    return [e.dma_start(out=vdst[:, c], in_=vsrc[:, c]) for c in range(r)]
                P.dma("sp", f"VT{g}", fv, writes=(f"VT{g}",), n=r)
            def front(c, nn, g=g, r=r, nbo=nbo, kb0=kb0):
                nonlocal pti
                q0 = c + r * 128 * nn
                kcur = (HALF - kb0) + q0
                kprev = kcur - r * 128
                bprev = c * (nbo + 1) + nn
                pSa, kSa = self.ps()
                pSb, kSb = self.ps()

                def st(e, pSa=pSa, pSb=pSb, g=g, r=r, q0=q0, kcur=kcur, kprev=kprev):
                    for h, pS in enumerate((pSa, pSb)):
                        for kb, k0 in enumerate((kprev, kcur)):
                            rr = e.matmul(pS[:, kb * 128:(kb + 1) * 128],
                                          lhsT=KT[g][h * 64:(h + 1) * 64, sl(k0, 128, r)],
                                          rhs=QT[g][h * 64:(h + 1) * 64, sl(q0, 128, r)], start=True, stop=True)
                    return rr
                P.op("pe", st, reads=(f"KT{g}", f"QT{g}"), writes=(kSa, kSb))
                E, kE = self.t32()
                E4 = E[:, :].rearrange("p (k h q) -> p k h q", k=2, h=2)
                P.op("act", lambda e, E4=E4, pSa=pSa: e.activation(out=E4[:, :, 0, :], in_=pSa[:, 0:256].rearrange("p (k q) -> p k q", k=2),
                                                                func=AF.Exp, scale=SCALE), reads=(kSa,), writes=(kE,))
                P.op("act", lambda e, E4=E4, pSb=pSb: e.activation(out=E4[:, :, 1, :], in_=pSb[:, 0:256].rearrange("p (k q) -> p k q", k=2),
                                                                func=AF.Exp, scale=SCALE), reads=(kSb, kE), writes=(kE,))
                pt = PT[pti]
                kpt = f"PT{pti}"
                pti = (pti + 1) % 3
                if nn == 0:
                    def mk(e, pt=pt, E=E):
                        e.scalar_tensor_tensor(out=pt[:, 0:256], in0=E[:, 0:256], scalar=self.flag[:, 0:1], in1=self.maskf[:, 0:256],
                                               op0=ALU.mult, op1=ALU.mult)
                        return e.tensor_tensor(out=pt[:, 256:512], in0=E[:, 256:512], in1=self.maskf[:, 256:512], op=ALU.mult)
                else:
                    def mk(e, pt=pt, E=E):
                        return e.tensor_tensor(out=pt[:, :], in0=E[:, :], in1=self.maskf[:, :], op=ALU.mult)
                P.op("dve", mk, reads=(kE, "maskf", "flag"), writes=(kpt,))
                return (pt, kpt, bprev, q0)

            def back(ctx, g=g, r=r):
                pt, kpt, bprev, q0 = ctx
                pU, kU = self.ps()

                def pv(e, pU=pU, pt=pt, g=g, bprev=bprev):
                    for h in range(2):
                        for kb in range(2):
                            rhs = pt[:, (kb * 2 + h) * 128:(kb * 2 + h + 1) * 128]
                            e.matmul(pU[0:64, h * 128:(h + 1) * 128], lhsT=VT[g][:, bprev + kb, h * 64:(h + 1) * 64], rhs=rhs,
                                     start=(kb == 0), stop=(kb == 1))
                            rr = e.matmul(pU[64:128, h * 128:(h + 1) * 128], lhsT=self.ONES[:, 0:64], rhs=rhs,
                                          start=(kb == 0), stop=(kb == 1))
                    return rr
                P.op("pe", pv, reads=(kpt, f"VT{g}", "cb"), writes=(kU,))
                av = acc[:, :, sl(q0, 128, r)]
                pv3 = pU[:, 0:256].rearrange("p (h q) -> p h q", h=2)
                if g == 0:
                    P.op("dve", lambda e, av=av, pv3=pv3: e.tensor_copy(out=av, in_=pv3), reads=(kU,), writes=("acc",))
                else:
                    P.op("dve", lambda e, av=av, pv3=pv3: e.tensor_tensor(out=av, in0=pv3, in1=av, op=ALU.add), reads=(kU, "acc"), writes=("acc",))
            pend = None
            for c in range(r):
                for nn in range(nbo):
                    ctx = front(c, nn)
                    if pend is not None:
                        back(pend)
                    pend = ctx
            back(pend)
        self.sample_attn(j, hp)
        P.op("act", lambda e: e.activation(out=acc[64:128, :, :], in_=acc[64:128, :, :], func=AF.Ln), reads=("acc",), writes=("acc",))
        P.op("act", lambda e: e.activation(out=acc[64:128, :, :], in_=acc[64:128, :, :], func=AF.Exp, scale=-1.0), reads=("acc",), writes=("acc",))
        wv, wk = self.wslot()
        wo = wv[0:64, 0:2048].rearrange("p (h n) -> p h n", h=2)
        src = w_o[j, hp * 128:(hp + 1) * 128, :].rearrange("(h p) n -> p h n", p=64)
        P.dma("pool", wk, lambda e, wo=wo, src=src: [e.dma_start(out=wo, in_=src)], writes=(wk,))
        for (c0, n) in TILES_Y:
            tk = c0 // 512
            for h in range(2):
                pR, kR = self.ps()
                P.op("pe", lambda e, pR=pR, h=h, c0=c0, n=n: e.matmul(pR[0:64, 0:n], lhsT=self.ident[:, 64:128], rhs=acc[:, h, c0:c0 + n], start=True, stop=True),
                     reads=("acc", "ident"), writes=(kR,))
                P.op("dve", lambda e, pR=pR, h=h, c0=c0, n=n: e.tensor_tensor(out=oT[0:64, h, c0:c0 + n], in0=acc[0:64, h, c0:c0 + n], in1=pR[0:64, 0:n], op=ALU.mult),
                     reads=(kR, "acc"), writes=("QT0", "QT1"))
            for dc in range(NCH):
                pd, kd = self.ps()

                def wm(e, pd=pd, dc=dc, c0=c0, n=n, wo=wo):
                    e.matmul(pd[:, 0:n], lhsT=wo[0:64, 0, dc * 128:(dc + 1) * 128], rhs=oT[0:64, 0, c0:c0 + n], start=True, stop=False)
                    return e.matmul(pd[:, 0:n], lhsT=wo[0:64, 1, dc * 128:(dc + 1) * 128], rhs=oT[0:64, 1, c0:c0 + n], start=False, stop=True)
                P.op("pe", wm, reads=(wk, "QT0", "QT1"), writes=(kd,))
                P.op("dve", lambda e, pd=pd, dc=dc, c0=c0, n=n: e.tensor_tensor(out=self.x[:, dc, c0:c0 + n], in0=pd[:, 0:n], in1=self.x[:, dc, c0:c0 + n], op=ALU.add),
                     reads=(kd, f"x:{tk}"), writes=(f"x:{tk}",))
    P.fence()


def _sample_attn_stub(self, j, hp):
    acc = self.att["acc"]
    self.P.op("dve", lambda e: e.memset(acc[:, :, HALF:NT], 1.0), writes=("acc",))


Builder.q_stage = _q_stage
Builder.attn_layer = _attn_layer
Builder.sample_attn = _sample_attn_stub


def _sample_stage(self, j):
    P = self.P
    P.fence()
    uo, ub = self.M.offs["u"]
    go, gb = self.M.offs["glu"]
    OM = Mem(self.nc, uo, ub + gb)
    QS = OM.alloc([128, 24, NS], BF16, "QS")
    KN = OM.alloc([128, 24, NS], BF16, "KN")
    VN = [OM.alloc([8, 32, 128], BF16, f"VN{g}") for g in range(3)]
    KC = [OM.alloc([128, 1024], F32, f"KC{i}") for i in range(2)]
    VC = [OM.alloc([128, 1024], BF16, f"VC{i}") for i in range(2)]
    KCT = [OM.alloc([128, 8, 128], BF16, f"KCT{i}") for i in range(2)]
    ES = [OM.alloc([128, 128], F32, f"ES{i}") for i in range(2)]
    PS_ = [OM.alloc([128, 128], BF16, f"PS{i}") for i in range(2)]
    sacc = self.sacc
    P.dma("sp", "QS", lambda e: [e.dma_start(out=QS[:, :, :], in_=self.q_scr[:, :, :, HALF:NT].rearrange("g h p n -> p (g h) n"))], writes=("QS",))
    P.dma("sp", "KN", lambda e: [e.dma_start(out=KN[:, :, :], in_=self.kt_scr[:, :, :, SEQ:KTW].rearrange("g h p n -> p (g h) n"))], writes=("KN",))
    for g in range(3):
        def fvn(e, g=g):
            return [e.dma_start(out=VN[g][:, hp * 4:(hp + 1) * 4, :], in_=self.v_scr[g, hp, SEQ:KTW, :].rearrange("(s t) f -> t s f", t=8))
                    for hp in range(8)]
        P.dma("sp", f"VN{g}", fvn, writes=(f"VN{g}",), n=8)
    P.op("dve", lambda e: e.memset(sacc[:, :, :, :], 0.0), writes=("sacc",))
    ckv = [self.din["ckv0"], self.din["ckv1"], self.din["ckv2"]]
    it = 0
    pend = None

    def run(front, back):
        nonlocal pend
        ctx = front()
        if pend is not None:
            pend[0](pend[1])
        pend = (back, ctx)
    for s in range(4):
        for g in range(3):
            r = DILS[g]
            ntile = (1, 4, 8)[g]
            for c in range(ntile):
                if g == 0:
                    qcols = list(range(8))
                elif g == 1:
                    qcols = [c, c + 4]
                else:
                    qcols = [c]
                nq = len(qcols)
                q0, qstep = s * 8 + qcols[0], (qcols[1] - qcols[0]) if nq > 1 else 1
                b = it % 2
                it += 1

                def front(s=s, g=g, c=c, r=r, nq=nq, q0=q0, qstep=qstep, b=b):
                    kc, vc, kct, es, pb = KC[b], VC[b], KCT[b], ES[b], PS_[b]
                    rows = ckv[g][s, sl(c, 128, r), :]
                    P.dma("sp", f"KC{b}", lambda e, kc=kc, rows=rows: [e.dma_start(out=kc[:, :], in_=rows[:, 0:1024])], writes=(f"KC{b}",))
                    P.dma("pool", f"VC{b}", lambda e, vc=vc, rows=rows: [e.dma_start(out=vc[:, :], in_=rows[:, 1024:2048])], writes=(f"VC{b}",))
                    for half in range(2):
                        pt, pk = self.ps()

                        def tr(e, pt=pt, kc=kc, half=half):
                            for cc in range(4):
                                hp = half * 4 + cc
                                rr = e.transpose(out=pt[:, cc * 128:(cc + 1) * 128], in_=kc[:, hp * 128:(hp + 1) * 128], identity=self.ident[:, :])
                            return rr
                        P.op("pe", tr, reads=(f"KC{b}", "ident"), writes=(pk,))
                        P.op("act", lambda e, pt=pt, kct=kct, half=half: e.activation(out=kct[:, half * 4:half * 4 + 4, :],
                                                                                   in_=pt[:, :].rearrange("p (a k) -> p a k", a=4), func=AF.Copy),
                             reads=(pk,), writes=(f"KCT{b}",))
                    pSa, kSa = self.ps()
                    pSb, kSb = self.ps()

                    def st(e, pSa=pSa, pSb=pSb, kct=kct):
                        for h, pS in enumerate((pSa, pSb)):
                            for hp in range(8):
                                rr = e.matmul(pS[:, hp * nq:(hp + 1) * nq], lhsT=kct[h * 64:(h + 1) * 64, hp, :],
                                              rhs=QS[h * 64:(h + 1) * 64, g * 8 + hp, sl(q0, nq, qstep)], start=True, stop=True)
                        return rr
                    P.op("pe", st, reads=(f"KCT{b}", "QS"), writes=(kSa, kSb))
                    es4 = es[:, 0:16 * nq].rearrange("p (a h q) -> p a h q", a=8, h=2)
                    P.op("act", lambda e, es4=es4, pSa=pSa: e.activation(out=es4[:, :, 0, :], in_=pSa[:, 0:8 * nq].rearrange("p (a q) -> p a q", a=8),
                                                                      func=AF.Exp, scale=SCALE), reads=(kSa,), writes=(f"ES{b}",))
                    P.op("act", lambda e, es4=es4, pSb=pSb: e.activation(out=es4[:, :, 1, :], in_=pSb[:, 0:8 * nq].rearrange("p (a q) -> p a q", a=8),
                                                                      func=AF.Exp, scale=SCALE), reads=(kSb, f"ES{b}"), writes=(f"ES{b}",))
                    if g == 0:
                        msk = self.smask[:, 0:128]
                    elif g == 1:
                        msk = self.smask[:, 128:160]
                    else:
                        msk = None
                    if msk is not None:
                        P.op("dve", lambda e, pb=pb, es=es, msk=msk: e.tensor_tensor(out=pb[:, 0:16 * nq], in0=es[:, 0:16 * nq], in1=msk, op=ALU.mult),
                             reads=(f"ES{b}", "smask"), writes=(f"PS{b}",))
                    else:
                        P.op("dve", lambda e, pb=pb, es=es: e.tensor_copy(out=pb[:, 0:16 * nq], in_=es[:, 0:16 * nq]),
                             reads=(f"ES{b}",), writes=(f"PS{b}",))
                    return None

                def back(ctx, s=s, g=g, nq=nq, q0=q0, qstep=qstep, b=b):
                    vc, pb = VC[b], PS_[b]
                    pU, kU = self.ps()

                    def pv(e, pU=pU, pb=pb, vc=vc):
                        for hp in range(8):
                            for h in range(2):
                                col = (hp * 2 + h) * nq
                                e.matmul(pU[0:64, col:col + nq], lhsT=vc[:, hp * 128 + h * 64:hp * 128 + (h + 1) * 64], rhs=pb[:, col:col + nq], start=True, stop=True)
                                rr = e.matmul(pU[64:128, col:col + nq], lhsT=self.ONES[:, 0:64], rhs=pb[:, col:col + nq], start=True, stop=True)
                        return rr
                    P.op("pe", pv, reads=(f"PS{b}", f"VC{b}", "cb"), writes=(kU,))
                    sv = sacc[:, :, :, sl(q0, nq, qstep)].rearrange("p a h q -> p (a h) q")
                    P.op("dve", lambda e, sv=sv, pU=pU: e.tensor_tensor(out=sv, in0=pU[:, 0:16 * nq].rearrange("p (a q) -> p a q", a=16), in1=sv, op=ALU.add),
                         reads=(kU, "sacc"), writes=("sacc",))
                run(front, back)
            b = it % 2
            it += 1

            def frontn(s=s, g=g, b=b):
                es, pb = ES[b], PS_[b]
                pSa, kSa = self.ps()
                pSb, kSb = self.ps()

                def stn(e, pSa=pSa, pSb=pSb):
                    for h, pS in enumerate((pSa, pSb)):
                        for hp in range(8):
                            rr = e.matmul(pS[0:8, hp * 8:(hp + 1) * 8], lhsT=KN[h * 64:(h + 1) * 64, g * 8 + hp, s * 8:s * 8 + 8],
                                          rhs=QS[h * 64:(h + 1) * 64, g * 8 + hp, s * 8:s * 8 + 8], start=True, stop=True)
                    return rr
                P.op("pe", stn, reads=("KN", "QS"), writes=(kSa, kSb))
                esn = es[0:8, :].rearrange("p (a h q) -> p a h q", a=8, h=2)
                P.op("act", lambda e, esn=esn, pSa=pSa: e.activation(out=esn[:, :, 0, :], in_=pSa[0:8, 0:64].rearrange("p (a q) -> p a q", a=8),
                                                                  func=AF.Exp, scale=SCALE), reads=(kSa,), writes=(f"ES{b}",))
                P.op("act", lambda e, esn=esn, pSb=pSb: e.activation(out=esn[:, :, 1, :], in_=pSb[0:8, 0:64].rearrange("p (a q) -> p a q", a=8),
                                                                  func=AF.Exp, scale=SCALE), reads=(kSb, f"ES{b}"), writes=(f"ES{b}",))
                mo = 160 + g * 128
                P.op("dve", lambda e, pb=pb, es=es, mo=mo: e.tensor_tensor(out=pb[0:8, :], in0=es[0:8, :], in1=self.smask[0:8, mo:mo + 128], op=ALU.mult),
                     reads=(f"ES{b}", "smask"), writes=(f"PS{b}",))
                return None

            def backn(ctx, s=s, g=g, b=b):
                pb = PS_[b]
                pU, kU = self.ps()

                def pvn(e, pU=pU, pb=pb):
                    for hp in range(8):
                        for h in range(2):
                            col = (hp * 2 + h) * 8
                            e.matmul(pU[0:64, col:col + 8], lhsT=VN[g][0:8, hp * 4 + s, h * 64:(h + 1) * 64], rhs=pb[0:8, col:col + 8], start=True, stop=True)
                            rr = e.matmul(pU[64:128, col:col + 8], lhsT=self.ONES[0:8, 0:64], rhs=pb[0:8, col:col + 8], start=True, stop=True)
                    return rr
                P.op("pe", pvn, reads=(f"PS{b}", f"VN{g}", "cb"), writes=(kU,))
                sv = sacc[:, :, :, s * 8:s * 8 + 8].rearrange("p a h q -> p (a h) q")
                P.op("dve", lambda e, sv=sv, pU=pU: e.tensor_tensor(out=sv, in0=pU[:, 0:128].rearrange("p (a q) -> p a q", a=16), in1=sv, op=ALU.add),
                     reads=(kU, "sacc"), writes=("sacc",))
            run(frontn, backn)
    if pend is not None:
        pend[0](pend[1])
    P.fence()


def _sample_attn(self, j, hp):
    acc = self.att["acc"]
    self.P.op("dve", lambda e, hp=hp: e.tensor_copy(out=acc[:, :, HALF:NT], in_=self.sacc[:, hp, :, :]), reads=("sacc",), writes=("acc",))


Builder.sample_stage = _sample_stage
Builder.sample_attn = _sample_attn


def _conv_outputs(self, L):
    P = self.P
    st, sk = self.stage()
    for half in range(2):
        pt, pk = self.ps()

        def f(e, half=half, pt=pt):
            for cc in range(4):
                c = half * 4 + cc
                r = e.transpose(out=pt[0:30, cc * 128:(cc + 1) * 128], in_=self.g32[:, L, c, :], identity=self.ident[:, :])
            return r
        P.op("pe", f, reads=("g32", "ident"), writes=(pk,))
        P.op("act", lambda e, half=half, pt=pt, st=st: e.activation(out=st[0:30, half * 512:(half + 1) * 512], in_=pt[0:30, :], func=AF.Copy),
             reads=(pk,), writes=(sk,))
    P.dma("sp", sk, lambda e, st=st: [e.dma_start(out=self.dout["convp"][L, :, :], in_=st[0:30, :])], reads=(sk,))
    P.dma("sp", "d2d", lambda e: [e.dma_start(out=self.dout["convs"][L, :, 0:22, :], in_=self.din["cconv"][L, :, 8:30, :])])
    st, sk = self.stage()
    for half in range(2):
        pt, pk = self.ps()

        def f2(e, half=half, pt=pt):
            for cc in range(4):
                c = half * 4 + cc
                r = e.transpose(out=pt[0:NS, cc * 128:(cc + 1) * 128], in_=self.gs32[:, L, c, :], identity=self.ident[:, :])
            return r
        P.op("pe", f2, reads=("gs32", "ident"), writes=(pk,))
        P.op("act", lambda e, half=half, pt=pt, st=st: e.activation(out=st[0:NS, half * 512:(half + 1) * 512], in_=pt[0:NS, :], func=AF.Copy),
             reads=(pk,), writes=(sk,))

    def fs(e, st=st):
        return [e.dma_start(out=self.dout["convs"][L, s, 22:30, :], in_=st[s * 8:(s + 1) * 8, :]) for s in range(4)]
    P.dma("sp", sk, fs, reads=(sk,), n=4)


def _build_full(self):
    import os
    STOP = os.environ.get("KSTOP", "")
    P = self.P
    self.setup()
    xin = self.din["xin"]
    ck = [self.din["ckv0"], self.din["ckv1"], self.din["ckv2"]]
    for g in range(3):
        W = WINS[g]
        if os.environ.get("NO_D2D"):
            break
        for s in range(4):
            for r0 in range(0, W - 8, 512):
                nr = min(512, W - 8 - r0)
                P.dma("act", "d2d", lambda e, g=g, s=s, r0=r0, nr=nr: [e.dma_start(out=self.dout[f"kvs{g}"][s, r0:r0 + nr, :],
                                                                                  in_=ck[g][s, 8 + r0:8 + r0 + nr, :])])
    for sup in range(2):
        isY = sup == 1 and STOP != "XX"
        tiles = TILES_Y if isY else TILES_X
        for tb in range(16):
            self.load_tokens(xin[sup * HALF + tb * 128: sup * HALF + (tb + 1) * 128, :], 128, tb * 128)
        if isY:
            self.load_tokens(self.din["xs"][:, :], NS, HALF)
        for L in range(2):
            self.conv_layer(L, isY)
            if isY and not os.environ.get("NO_CONVOUT"):
                self.conv_outputs(L)
            self.ffn(L, tiles)
        self.kv_stage(isY)
        if STOP == "X":
            break
    for j in range(2):
        if STOP in ("X", "KV", "XX"):
            break
        self.attn_layer(j, STOP)
        if STOP in ("Q", "S", "A"):
            break
        self.ffn(2 + j, TILES_Y)
    for tb in range(16):
        self.store_tokens(self.dout["y"][tb * 128:(tb + 1) * 128, :], 128, tb * 128)
    self.store_tokens(self.dout["y"][HALF:HALF + NS, :], NS, HALF)


Builder.conv_outputs = _conv_outputs
Builder.build_full = _build_full

_CACHE = {}


def kernel(**inputs):
    if "nc" not in _CACHE:
        B = Builder()
        B.build_full()
        _CACHE["nc"] = B.finish()
        _CACHE["names"] = set(B.din.keys())
    nc = _CACHE["nc"]
    maps = make_in_maps(inputs)
    maps = [{k: v for k, v in m.items() if k in _CACHE["names"]} for m in maps]
    res = run_bass_kernel_spmd(nc, maps, core_ids=list(range(8)))
    R = res.results
    f32 = np.float32
    y_prompt = np.zeros((4, SEQ, D), f32)
    y_sample = np.zeros((32, 8, D), f32)
    conv_p = np.zeros((2, 4, 30, D), f32)
    conv_s = np.zeros((2, 32, 30, D), f32)
    kvp = [np.zeros((4, W, 2, 16, 64), f32) for W in WINS]
    kvs = [np.zeros((32, W, 2, 16, 64), f32) for W in WINS]
    for c in range(8):
        b, h = c // 2, c % 2
        r = R[c]
        y_prompt[b, h * HALF:(h + 1) * HALF] = r["y"][:HALF]
        y_sample[4 * c:4 * c + 4] = r["y"][HALF:].reshape(4, 8, D)
        conv_s[:, 4 * c:4 * c + 4] = r["convs"]
        for g in range(3):
            kvs[g][4 * c:4 * c + 4] = r[f"kvs{g}"].reshape(4, WINS[g], 2, 16, 64)
        if h == 1:
            conv_p[:, b] = r["convp"]
            for g in range(3):
                kvp[g][b] = r[f"kvp{g}"].reshape(WINS[g], 2, 16, 64)
    return (y_prompt, y_sample, conv_p, conv_s, kvp[0], kvp[1], kvp[2], kvs[0], kvs[1], kvs[2])
```

```python
import numpy as np
import ml_dtypes
import concourse.bass as bass
import concourse.mybir as mybir
from concourse.bass_utils import run_bass_kernel_spmd

F32 = mybir.dt.float32
BF16 = mybir.dt.bfloat16
AF = mybir.ActivationFunctionType
ALU = mybir.AluOpType

D = 1024
NCH = 8
SEQ = 4096
HALF = 2048
NS = 32
NT = HALF + NS
DFF = 2816
NFC = 22
CW = 31
EPS = 1e-6
PAST = 8192
WINS = (128, 512, 2048)
DILS = (1, 4, 16)
KTW = SEQ + NS

ENGS = ["pe", "act", "dve", "pool", "sp"]
SQK = tuple(f"sqb{i}" for i in range(8))


class Prog:
    def __init__(self):
        self.ops = {e: [] for e in ENGS}
        self.last_w = {}
        self.readers = {}
        self.chan_n = {}

    def _deps(self, reads, writes):
        deps = []
        for k in reads:
            t = self.last_w.get(k)
            if t is not None:
                deps.append(t)
        for k in writes:
            t = self.last_w.get(k)
            if t is not None:
                deps.append(t)
            deps.extend(self.readers.get(k, ()))
        return deps

    def _commit(self, tok, reads, writes):
        for k in reads:
            lst = self.readers.setdefault(k, [])
            src = tok[:2]
            lst[:] = [t for t in lst if t[:2] != src]
            lst.append(tok)
        for k in writes:
            self.last_w[k] = tok
            self.readers[k] = []

    def op(self, eng, fn, reads=(), writes=()):
        reads = tuple(reads)
        writes = tuple(writes) + tuple(k for k in reads if k.startswith("ps") and k[2:].isdigit())
        idx = len(self.ops[eng])
        tok = ("e", eng, idx)
        deps = [t for t in self._deps(reads, writes) if not (t[0] == "e" and t[1] == eng and (eng == "pe" or t[2] == idx))]
        deps += self._take_fence(eng, idx)
        for t in deps:
            if t[0] == "e":
                self.ops[t[1]][t[2]]["signal"] = True
        self.ops[eng].append({"fn": fn, "deps": deps, "signal": False, "dma": None})
        self._commit(tok, reads, writes)
        return tok

    def dma(self, queue, chan, fn, reads=(), writes=(), n=1):
        cnt = self.chan_n.get(chan, 0) + n
        self.chan_n[chan] = cnt
        tok = ("d", chan, cnt)
        deps = list(self._deps(reads, writes))
        deps += self._take_fence(queue, len(self.ops[queue]))
        for t in deps:
            if t[0] == "e":
                self.ops[t[1]][t[2]]["signal"] = True
        self.ops[queue].append({"fn": fn, "deps": deps, "signal": False, "dma": chan})
        self._commit(tok, reads, writes)
        return tok

    def fence(self):
        deps = [("e", E, len(self.ops[E]) - 1) for E in ENGS if self.ops[E] and self.ops[E][-1]["dma"] is None]
        for E in ENGS:
            if self.ops[E] and self.ops[E][-1]["dma"] is not None:
                for i in range(len(self.ops[E]) - 1, -1, -1):
                    if self.ops[E][i]["dma"] is None:
                        deps.append(("e", E, i))
                        break
        deps += [("d", ch, n) for ch, n in self.chan_n.items()]
        self.pending = {E: list(deps) for E in ENGS}

    def _take_fence(self, eng, idx):
        pend = getattr(self, "pending", None)
        if not pend or not pend.get(eng):
            return []
        d = pend[eng]
        pend[eng] = []
        return [t for t in d if not (t[0] == "e" and t[1] == eng and eng == "pe")]

    def emit(self, nc, block_engines, sems, chan_sems):
        sigcount = {}
        for e in ENGS:
            c = 0
            arr = []
            for o in self.ops[e]:
                if o["signal"] and o["dma"] is None:
                    c += 1
                arr.append(c)
            sigcount[e] = arr
        prog = self

        def make(e):
            def body(eng):
                seen = {}
                for o in prog.ops[e]:
                    for t in o["deps"]:
                        if t[0] == "e":
                            src = ("e", t[1])
                            cnt = sigcount[t[1]][t[2]]
                            sem = sems[t[1]]
                        else:
                            src = ("d", t[1])
                            cnt = 16 * t[2]
                            sem = chan_sems[t[1]]
                        if seen.get(src, 0) >= cnt:
                            continue
                        seen[src] = cnt
                        eng.wait_ge(sem, cnt)
                    r = o["fn"](eng)
                    if o["dma"] is not None:
                        for ins in r:
                            ins.then_inc(chan_sems[o["dma"]], 16)
                    elif o["signal"]:
                        r.then_inc(sems[e], 1)
                if e == "sp":
                    for ch, n in prog.chan_n.items():
                        if seen.get(("d", ch), 0) < 16 * n:
                            eng.wait_ge(chan_sems[ch], 16 * n)
            return body

        for e in ENGS:
            block_engines[e](make(e))


class Mem:
    def __init__(self, nc, base, size):
        self.nc = nc
        self.base = base
        self.size = size
        self.cur = 0
        self.n = 0
        Mem._gn = getattr(Mem, "_gn", 0) + 1000
        self.n = Mem._gn

    def alloc(self, shape, dtype, name=None):
        nbytes = int(np.prod(shape[1:])) * (4 if dtype == F32 else 2)
        nbytes = (nbytes + 63) // 64 * 64
        off = self.cur
        self.cur += nbytes
        assert self.cur <= self.size, f"SBUF overflow {self.cur} > {self.size} at {name}"
        self.n += 1
        self.offs = getattr(self, "offs", {})
        self.offs[name] = (self.base + off, nbytes)
        return self._reg(name, self.nc.alloc_sbuf_tensor_at(f"{name or 't'}_{self.n}", list(shape), dtype, offset=self.base + off))

    def _reg(self, name, h):
        self.names = getattr(self, "names", {})
        self.names[name] = h.name
        return h

    def mark(self):
        return self.cur

    def reset(self, m):
        self.cur = m


TILES_X = [(i * 512, 512) for i in range(4)]
TILES_Y = TILES_X + [(HALF, NS)]
GW = 30 + HALF + 4 * 38
GS0 = 30 + HALF


class Builder:
    def __init__(self, stop_after=None):
        self.stop_after = stop_after
        nc = self.nc = bass.Bass("TRN2", target_bir_lowering=False)
        self.P = Prog()
        self.din = {}
        self.dout = {}
        self._uid = 0

    def inp(self, name, shape, dtype=F32):
        self.din[name] = self.nc.dram_tensor(name, list(shape), dtype, kind="ExternalInput").ap()
        return self.din[name]

    def outp(self, name, shape, dtype=F32):
        self.dout[name] = self.nc.dram_tensor(name, list(shape), dtype, kind="ExternalOutput").ap()
        return self.dout[name]

    @staticmethod
    def uk(tk):
        return tuple(f"u:{tk}:{c}" for c in range(NCH))

    def uid(self):
        self._uid += 1
        return self._uid

    def ps(self):
        i = self._psi
        self._psi = (i + 1) % 8
        return self.psb[i], f"ps{i}"

    def wslot(self):
        i = self._wi
        self._wi = (i + 1) % len(self.wbufs)
        return self.wbufs[i], f"w{i}"

    def t32(self):
        i = self._t32i
        self._t32i = (i + 1) % len(self.T32)
        return self.T32[i], f"t32_{i}"

    def s32(self):
        i = self._s32i
        self._s32i = (i + 1) % len(self.S32)
        return self.S32[i], f"s32_{i}"

    def stage(self):
        i = self._stgi
        self._stgi = (i + 1) % 2
        return self.stg[i], f"stg{i}"

    def load_w(self, src_ap, shape_view):
        buf, key = self.wslot()
        a, b = shape_view
        dst = buf[:, 0:a * b].rearrange("p (a b) -> p a b", a=a)
        self.P.dma("pool", key, lambda e, dst=dst, src=src_ap: [e.dma_start(out=dst, in_=src)],
                   reads=(), writes=(key,))
        return dst, key

    PV_ROWS = {}

    @staticmethod
    def pv_layout():
        rows = {}
        n = 0

        def add(name, cnt=1):
            nonlocal n
            rows[name] = n
            n += cnt
        for L in range(2):
            add(f"a_norm{L}")
            add(f"b1_{L}")
            add(f"b2_{L}")
            add(f"wdw{L}", CW)
            add(f"b_dw{L}")
            add(f"ln_g{L}")
            add(f"ln_b{L}")
            add(f"b_out{L}")
        add("kv_norm")
        for j in range(2):
            add(f"b_norm{j}")
        for L in range(4):
            add(f"ffn_norm{L}")
        for g in range(3):
            add(f"k_norm{g}")
        for j in range(2):
            for g in range(3):
                add(f"q_norm{j}_{g}")
        return rows, n

    def pvs(self, name, c, off=0):
        i = self.pvrows[name] + off
        return self.pv[:, c, i:i + 1]

    def setup(self):
        nc = self.nc
        P = self.P
        self.pvrows, self.npv = self.pv_layout()
        self.inp("xin", [SEQ, D])
        self.inp("xs", [NS, D])
        self.inp("cconv", [2, 4, 30, D])
        self.inp("ckv0", [4, 128, 2048])
        self.inp("ckv1", [4, 512, 2048])
        self.inp("ckv2", [4, 2048, 2048])
        self.inp("pvec", [self.npv, D])
        self.inp("a_w_in", [2, D, 2 * D])
        self.inp("a_w_out", [2, D, D])
        self.inp("w_kv", [D, 6 * D])
        self.inp("w_q", [2, D, 3 * D])
        self.inp("w_o", [2, D, D])
        self.inp("w_gate_up", [4, D, 2 * DFF])
        self.inp("w_down", [4, DFF, D])
        self.inp("cmat", [6, 128, 128])
        self.inp("rot", [2, 128, KTW])
        self.inp("flag", [128, 1])
        self.inp("smaskin", [128, 544])
        self.outp("y", [NT, D])
        self.outp("convp", [2, 30, D])
        self.outp("convs", [2, 4, 30, D])
        self.outp("kvp0", [128, 2048])
        self.outp("kvp1", [512, 2048])
        self.outp("kvp2", [2048, 2048])
        self.outp("kvs0", [4, 128, 2048])
        self.outp("kvs1", [4, 512, 2048])
        self.outp("kvs2", [4, 2048, 2048])
        self.kt_scr = nc.dram_tensor("kt_scr", [3, 8, 128, KTW], BF16, kind="Internal").ap()
        self.v_scr = nc.dram_tensor("v_scr", [3, 8, KTW, 128], BF16, kind="Internal").ap()
        self.q_scr = nc.dram_tensor("q_scr", [3, 8, 128, NT], BF16, kind="Internal").ap()

        total = nc.sbuf_bytes_remaining - 64
        arena = nc.alloc_sbuf_tensor("arena", [128, total // 4], F32)
        base = nc.lookup_mloc(arena).addr
        self.M = M = Mem(nc, base, (total // 4) * 4)
        self.x = M.alloc([128, NCH, NT], F32, "x")
        self.u = M.alloc([128, NCH, NT], BF16, "u")
        self.G = M.alloc([128, NCH, GW], BF16, "glu")
        self.pv = M.alloc([128, NCH, self.npv], F32, "pv")
        self.ident = M.alloc([128, 128], F32, "ident")
        self.cb = M.alloc([128, 6, 128], BF16, "cb")
        self.maskf = M.alloc([128, 512], F32, "maskf")
        self.flag = M.alloc([128, 1], F32, "flag")
        self.sacc = M.alloc([128, 8, 2, NS], F32, "sacc")
        self.smask = M.alloc([128, 544], BF16, "smask")
        self.gstate = M.alloc([128, 2, NCH, 30], BF16, "gstate")
        self.g32 = M.alloc([128, 2, NCH, 30], F32, "g32")
        self.gs32 = M.alloc([128, 2, NCH, NS], F32, "gs32")
        self.wall = M.alloc([128, 6 * 2048], BF16, "wall")
        self.wbufs = [self.wall[:, i * 2048:(i + 1) * 2048] for i in range(6)]
        self.sqb = M.alloc([128, NCH, 512], BF16, "sqb")
        self.T32 = [M.alloc([128, 512], F32, f"t32_{i}") for i in range(4)]
        self._t32i = 0
        self.S32 = [M.alloc([128, 512], F32, f"s32_{i}") for i in range(4)]
        self._s32i = 0
        self.stg = [M.alloc([128, D], F32, f"stg{i}") for i in range(2)]
        self._stgi = 0
        self.hbuf = [M.alloc([128, 512], BF16, f"hbuf{i}") for i in range(4)]
        self._hi = 0
        self._di = 0
        self._sqi = 0
        self.persist_mark = M.mark()
        print("SBUF used", M.cur, "of", M.size)

        self.psb = [nc.alloc_psum_tensor(f"psb{i}", [128, 512], F32) for i in range(8)]
        self._psi = 0
        self._wi = 0

        cst = self.stg[1][:, 0:768].rearrange("p (a b) -> p a b", a=6)
        P.dma("sp", "stg1", lambda e: [e.dma_start(out=cst, in_=self.din["cmat"].rearrange("a p n -> p a n"))],
              writes=("stg1",))
        P.dma("sp", "c1", lambda e: [e.dma_start(out=self.flag[:], in_=self.din["flag"][:, :])], writes=("flag",))
        P.dma("pool", "c3", lambda e: [e.dma_start(out=self.smask[:], in_=self.din["smaskin"][:, :])], writes=("smask",))
        P.op("dve", lambda e: e.tensor_copy(out=self.cb[:], in_=cst), reads=("stg1",), writes=("cb",))
        P.op("dve", lambda e: e.tensor_copy(out=self.ident[:], in_=cst[:, 0, :]), reads=("stg1",), writes=("ident",))
        for q, src in enumerate([4, 4, 5, 5]):
            P.op("dve", lambda e, q=q, src=src: e.tensor_copy(out=self.maskf[:, q * 128:(q + 1) * 128], in_=cst[:, src, :]),
                 reads=("stg1",), writes=("maskf",))
        self.I_BF = self.cb[:, 0, :]
        self.ONES = self.cb[:, 1, :]
        self.BONES = self.cb[:, 2, :]
        self.SWAP = self.cb[:, 3, :]
        pst = self.stg[0]
        npv = self.npv
        P.dma("sp", "stg0", lambda e: [e.dma_start(out=pst[0:npv, :], in_=self.din["pvec"][:, :])], writes=("stg0",))
        for half in range(2):
            pt, pk = self.ps()

            def f(e, half=half, pt=pt):
                for cc in range(4):
                    c = half * 4 + cc
                    r = e.transpose(out=pt[:, cc * 128:cc * 128 + npv], in_=pst[0:npv, c * 128:(c + 1) * 128],
                                    identity=self.ident[0:npv, 0:npv])
                return r
            P.op("pe", f, reads=("stg0", "ident"), writes=(pk,))
            P.op("act", lambda e, half=half, pt=pt: e.activation(
                out=self.pv[:, half * 4:half * 4 + 4, :],
                in_=pt[:, :].rearrange("p (a b) -> p a b", a=4)[:, :, 0:npv], func=AF.Copy),
                reads=(pk,), writes=("pv",))
        P.op("dve", lambda e: e.memset(self.G[:, :, 0:30], 0.0), writes=("G:state",))

    def load_tokens(self, src_ap, nrows, col0):
        P = self.P
        st, sk = self.stage()
        P.dma("sp", sk, lambda e: [e.dma_start(out=st[0:nrows, :], in_=src_ap)], writes=(sk,))
        xkey = f"x:{col0 // 512}"
        for half in range(2):
            pt, pk = self.ps()

            def f(e, half=half, pt=pt):
                for cc in range(4):
                    c = half * 4 + cc
                    r = e.transpose(out=pt[:, cc * 128:cc * 128 + nrows], in_=st[0:nrows, c * 128:(c + 1) * 128],
                                    identity=self.ident[0:nrows, 0:nrows])
                return r
            P.op("pe", f, reads=(sk, "ident"), writes=(pk,))
            P.op("act", lambda e, half=half, pt=pt: e.activation(
                out=self.x[:, half * 4:half * 4 + 4, col0:col0 + nrows],
                in_=pt[:, :].rearrange("p (a b) -> p a b", a=4)[:, :, 0:nrows], func=AF.Copy),
                reads=(pk,), writes=(xkey,))

    def store_tokens(self, dst_ap, nrows, col0, src=None, skey=None):
        P = self.P
        src = self.x if src is None else src
        skey = f"x:{col0 // 512}" if skey is None else skey
        st, sk = self.stage()
        for half in range(2):
            pt, pk = self.ps()

            def f(e, half=half, pt=pt):
                for cc in range(4):
                    c = half * 4 + cc
                    r = e.transpose(out=pt[0:nrows, cc * 128:(cc + 1) * 128], in_=src[:, c, col0:col0 + nrows],
                                    identity=self.ident[:, :])
                return r
            P.op("pe", f, reads=(skey, "ident"), writes=(pk,))
            P.op("act", lambda e, half=half, pt=pt: e.activation(
                out=st[0:nrows, half * 512:(half + 1) * 512], in_=pt[0:nrows, :], func=AF.Copy),
                reads=(pk,), writes=(sk,))
        P.dma("sp", sk, lambda e: [e.dma_start(out=dst_ap, in_=st[0:nrows, :])], reads=(sk,))

    def rmsnorm(self, tiles, gname):
        P = self.P
        rs = {}
        for (c0, n) in tiles:
            tk = c0 // 512
            P.op("act", lambda e, c0=c0, n=n: e.activation(out=self.sqb[:, :, 0:n], in_=self.x[:, :, c0:c0 + n], func=AF.Square),
                 reads=(f"x:{tk}",), writes=SQK)
            pt, pk = self.ps()

            def f(e, pt=pt, n=n):
                for c in range(NCH):
                    r = e.matmul(pt[:, 0:n], lhsT=self.ONES, rhs=self.sqb[:, c, 0:n], start=(c == 0), stop=(c == NCH - 1))
                return r
            P.op("pe", f, reads=SQK + ("cb",), writes=(pk,))
            t, tkey = self.s32()
            P.op("act", lambda e, pt=pt, t=t, n=n: e.activation(out=t[:, 0:n], in_=pt[:, 0:n], func=AF.Ln, bias=EPS, scale=1.0 / D),
                 reads=(pk,), writes=(tkey,))
            P.op("act", lambda e, t=t, n=n: e.activation(out=t[:, 0:n], in_=t[:, 0:n], func=AF.Exp, scale=-0.5), reads=(tkey,), writes=(tkey,))

            def g(e, t=t, c0=c0, n=n):
                for c in range(NCH):
                    r = e.scalar_tensor_tensor(out=self.u[:, c, c0:c0 + n], in0=self.x[:, c, c0:c0 + n],
                                               scalar=self.pvs(gname, c), in1=t[:, 0:n], op0=ALU.mult, op1=ALU.mult)
                return r
            P.op("dve", g, reads=(tkey, f"x:{tk}", "pv"), writes=self.uk(tk))

    def conv_layer(self, L, isY, tiles=None):
        P = self.P
        if tiles is None:
            tiles = TILES_Y if isY else TILES_X
        w_in = self.din["a_w_in"]
        w_out = self.din["a_w_out"]
        G = self.G
        self.rmsnorm(tiles, f"a_norm{L}")
        if isY:
            P.op("dve", lambda e: e.tensor_copy(out=G[:, :, 0:30], in_=self.gstate[:, L, :, :]),
                 reads=("gstate",), writes=("G:state",))
            for s in range(4):
                st, sk = self.stage()
                P.dma("sp", sk, lambda e, st=st, s=s: [e.dma_start(out=st[0:30, :], in_=self.din["cconv"][L, s, :, :])], writes=(sk,))
                for half in range(2):
                    pt, pk = self.ps()

                    def f(e, half=half, pt=pt, st=st):
                        for cc in range(4):
                            c = half * 4 + cc
                            r = e.transpose(out=pt[:, cc * 128:cc * 128 + 30], in_=st[0:30, c * 128:(c + 1) * 128],
                                            identity=self.ident[0:30, 0:30])
                        return r
                    P.op("pe", f, reads=(sk, "ident"), writes=(pk,))
                    P.op("act", lambda e, half=half, pt=pt, s=s: e.activation(
                        out=G[:, half * 4:half * 4 + 4, GS0 + s * 38:GS0 + s * 38 + 30],
                        in_=pt[:, :].rearrange("p (a b) -> p a b", a=4)[:, :, 0:30], func=AF.Copy),
                        reads=(pk,), writes=("G:s",))
        for oc in range(NCH):
            wv, wk = self.wslot()
            wa = wv[:, 0:2048].rearrange("p (h k n) -> p h k n", h=2, k=NCH)
            src1 = w_in[L, :, oc * 128:(oc + 1) * 128].rearrange("(k p) n -> p k n", p=128)
            src2 = w_in[L, :, D + oc * 128:D + (oc + 1) * 128].rearrange("(k p) n -> p k n", p=128)
            P.dma("pool", wk, lambda e, wa=wa, src1=src1, src2=src2: [e.dma_start(out=wa[:, 0], in_=src1), e.dma_start(out=wa[:, 1], in_=src2)],
                  writes=(wk,), n=2)
            for (c0, n) in tiles:
                tk = c0 // 512
                p1, k1 = self.ps()
                p2, k2 = self.ps()

                def mm(e, p1=p1, p2=p2, wa=wa, c0=c0, n=n):
                    for k in range(NCH):
                        e.matmul(p1[:, 0:n], lhsT=wa[:, 0, k, :], rhs=self.u[:, k, c0:c0 + n], start=(k == 0), stop=(k == NCH - 1))
                    for k in range(NCH):
                        r = e.matmul(p2[:, 0:n], lhsT=wa[:, 1, k, :], rhs=self.u[:, k, c0:c0 + n], start=(k == 0), stop=(k == NCH - 1))
                    return r
                P.op("pe", mm, reads=(wk,) + self.uk(tk), writes=(k1, k2))
                sg, sgk = self.t32()
                P.op("act", lambda e, sg=sg, p2=p2, n=n, oc=oc: e.activation(out=sg[:, 0:n], in_=p2[:, 0:n], func=AF.Sigmoid,
                                                                       bias=self.pvs(f"b2_{L}", oc), scale=1.0),
                     reads=(k2, "pv"), writes=(sgk,))
                if n == 512:
                    gout = G[:, oc, 30 + c0:30 + c0 + n]
                    gkey = f"G:{tk}"
                else:
                    gout = G[:, oc, GS0:GS0 + 4 * 38].rearrange("p (s t) -> p s t", s=4)[:, :, 30:38]
                    gkey = "G:s"

                def glu(e, sg=sg, p1=p1, n=n, oc=oc, gout=gout, c0=c0):
                    if n == 512:
                        r = e.scalar_tensor_tensor(out=gout, in0=p1[:, 0:n], scalar=self.pvs(f"b1_{L}", oc), in1=sg[:, 0:n],
                                                   op0=ALU.add, op1=ALU.mult)
                        if c0 == 1536:
                            if isY:
                                r = e.scalar_tensor_tensor(out=self.g32[:, L, oc, :], in0=p1[:, 482:512], scalar=self.pvs(f"b1_{L}", oc),
                                                           in1=sg[:, 482:512], op0=ALU.add, op1=ALU.mult)
                            else:
                                r = e.scalar_tensor_tensor(out=self.gstate[:, L, oc, :], in0=p1[:, 482:512], scalar=self.pvs(f"b1_{L}", oc),
                                                           in1=sg[:, 482:512], op0=ALU.add, op1=ALU.mult)
                    else:
                        e.scalar_tensor_tensor(out=gout, in0=p1[:, 0:n].rearrange("p (s t) -> p s t", s=4), scalar=self.pvs(f"b1_{L}", oc),
                                               in1=sg[:, 0:n].rearrange("p (s t) -> p s t", s=4), op0=ALU.add, op1=ALU.mult)
                        r = e.scalar_tensor_tensor(out=self.gs32[:, L, oc, :], in0=p1[:, 0:n], scalar=self.pvs(f"b1_{L}", oc),
                                                   in1=sg[:, 0:n], op0=ALU.add, op1=ALU.mult)
                    return r
                wr = [gkey]
                if c0 == 1536:
                    wr.append("g32" if isY else "gstate")
                if n != 512:
                    wr.append("gs32")
                P.op("dve", glu, reads=(k1, sgk, "pv", "flag"), writes=tuple(wr))
                if c0 == 1536 and not isY:
                    P.op("dve", lambda e, oc=oc: e.tensor_scalar(out=self.gstate[:, L, oc, :], in0=self.gstate[:, L, oc, :],
                                                                 scalar1=self.flag[:, 0:1], scalar2=None, op0=ALU.mult),
                         reads=("gstate", "flag"), writes=("gstate",))
        gkeys_main = tuple(f"G:{i}" for i in range(4)) + ("G:state",)
        for c in range(NCH):
            i1 = 2 * self._di
            self._di = (self._di + 1) % 3
            sk1, sk2 = f"w{i1}", f"w{i1 + 1}"
            dg = self.wall[:, i1 * 2048:i1 * 2048 + CW * 128].rearrange("p (j n) -> p j n", j=CW)

            def mk(e, dg=dg, c=c):
                for j in range(CW):
                    r = e.activation(out=dg[:, j, :], in_=self.I_BF, func=AF.Identity, scale=self.pvs(f"wdw{L}", c, j))
                return r
            P.op("act", mk, reads=("cb", "pv"), writes=(sk1, sk2))
            for (c0, n) in tiles:
                tk = c0 // 512
                pt, pk = self.ps()
                if n == 512:
                    def cv(e, dg=dg, c=c, c0=c0, pt=pt):
                        for j in range(CW):
                            r = e.matmul(pt[:, 0:512], lhsT=dg[:, j, :], rhs=G[:, c, c0 + j:c0 + j + 512], start=(j == 0), stop=(j == CW - 1))
                        return r
                    rd = (sk1, sk2) + gkeys_main
                else:
                    def cv(e, dg=dg, c=c, pt=pt):
                        gsv = G[:, c, GS0:GS0 + 4 * 38].rearrange("p (s t) -> p s t", s=4)
                        for j in range(CW):
                            r = e.matmul(pt[:, 0:NS].rearrange("p (s t) -> p s t", s=4), lhsT=dg[:, j, :], rhs=gsv[:, :, j:j + 8],
                                         start=(j == 0), stop=(j == CW - 1))
                        return r
                    rd = (sk1, sk2, "G:s")
                P.op("pe", cv, reads=rd, writes=(pk,))
                P.op("act", lambda e, pt=pt, c=c, c0=c0, n=n: e.activation(out=self.u[:, c, c0:c0 + n], in_=pt[:, 0:n], func=AF.Identity,
                                                                     bias=self.pvs(f"b_dw{L}", c), scale=1.0),
                     reads=(pk, "pv"), writes=(f"u:{tk}:{c}",))
        if getattr(self, "stop_stage", None) == "B":
            return
        def ln_tile(c0, n):
            tk = c0 // 512
            ukey = self.uk(tk)
            P.op("act", lambda e, c0=c0, n=n: e.activation(out=self.sqb[:, :, 0:n], in_=self.u[:, :, c0:c0 + n], func=AF.Square),
                 reads=ukey, writes=SQK)
            psm, ksm = self.ps()
            psq, ksq = self.ps()

            def st(e, psm=psm, psq=psq, c0=c0, n=n):
                for c in range(NCH):
                    e.matmul(psm[:, 0:n], lhsT=self.ONES, rhs=self.u[:, c, c0:c0 + n], start=(c == 0), stop=(c == NCH - 1))
                for c in range(NCH):
                    r = e.matmul(psq[:, 0:n], lhsT=self.ONES, rhs=self.sqb[:, c, 0:n], start=(c == 0), stop=(c == NCH - 1))
                return r
            P.op("pe", st, reads=ukey + SQK + ("cb",), writes=(ksm, ksq))
            mu, kmu = self.s32()
            va, kva = self.s32()

            P.op("dve", lambda e, mu=mu, psm=psm, n=n: e.tensor_scalar(out=mu[:, 0:n], in0=psm[:, 0:n], scalar1=1.0 / D, scalar2=None, op0=ALU.mult),
                 reads=(ksm,), writes=(kmu,))
            P.op("dve", lambda e, mu=mu, va=va, n=n: e.tensor_tensor(out=va[:, 0:n], in0=mu[:, 0:n], in1=mu[:, 0:n], op=ALU.mult),
                 reads=(kmu,), writes=(kva,))
            P.op("dve", lambda e, va=va, psq=psq, n=n: e.scalar_tensor_tensor(out=va[:, 0:n], in0=psq[:, 0:n], scalar=1.0 / D, in1=va[:, 0:n],
                                                                         op0=ALU.mult, op1=ALU.subtract),
                 reads=(ksq, kva), writes=(kva,))
            P.op("act", lambda e, va=va, n=n: e.activation(out=va[:, 0:n], in_=va[:, 0:n], func=AF.Ln, bias=EPS, scale=1.0),
                 reads=(kva,), writes=(kva,))
            P.op("act", lambda e, va=va, n=n: e.activation(out=va[:, 0:n], in_=va[:, 0:n], func=AF.Exp, scale=-0.5), reads=(kva,), writes=(kva,))
            for c in range(NCH):
                t, tkey = self.t32()

                P.op("dve", lambda e, t=t, c=c, c0=c0, n=n, mu=mu: e.tensor_tensor(out=t[:, 0:n], in0=self.u[:, c, c0:c0 + n], in1=mu[:, 0:n], op=ALU.subtract),
                     reads=(ukey[c], kmu), writes=(tkey,))
                P.op("dve", lambda e, t=t, n=n, va=va: e.tensor_tensor(out=t[:, 0:n], in0=t[:, 0:n], in1=va[:, 0:n], op=ALU.mult),
                     reads=(tkey, kva), writes=(tkey,))
                P.op("act", lambda e, t=t, c=c, c0=c0, n=n: e.activation(out=self.u[:, c, c0:c0 + n], in_=t[:, 0:n], func=AF.Silu,
                                                                   bias=self.pvs(f"ln_b{L}", c), scale=self.pvs(f"ln_g{L}", c)),
                     reads=(tkey, "pv"), writes=(ukey[c],))
        if getattr(self, "stop_stage", None) == "C":
            for (c0, n) in tiles:
                ln_tile(c0, n)
            return
        wq = []
        for q in range(4):
            wv, wk = self.wslot()
            wa = wv[:, 0:2048].rearrange("p (k n) -> p k n", k=NCH)
            src = w_out[L, :, q * 256:(q + 1) * 256].rearrange("(k p) n -> p k n", p=128)
            P.dma("pool", wk, lambda e, wa=wa, src=src: [e.dma_start(out=wa, in_=src)], writes=(wk,))
            wq.append((wa, wk))

        def wo_tile(c0, n):
            tk = c0 // 512
            for oc in range(NCH):
                wa, wk = wq[oc // 2]
                off = (oc % 2) * 128
                pt, pk = self.ps()

                def mm(e, pt=pt, wa=wa, off=off, c0=c0, n=n):
                    for k in range(NCH):
                        r = e.matmul(pt[:, 0:n], lhsT=wa[:, k, off:off + 128], rhs=self.u[:, k, c0:c0 + n], start=(k == 0), stop=(k == NCH - 1))
                    return r
                P.op("pe", mm, reads=(wk,) + self.uk(tk), writes=(pk,))
                P.op("dve", lambda e, pt=pt, oc=oc, c0=c0, n=n: e.scalar_tensor_tensor(
                    out=self.x[:, oc, c0:c0 + n], in0=pt[:, 0:n], scalar=self.pvs(f"b_out{L}", oc), in1=self.x[:, oc, c0:c0 + n],
                    op0=ALU.add, op1=ALU.add), reads=(pk, "pv", f"x:{tk}"), writes=(f"x:{tk}",))
        prev_t = None
        for (c0, n) in tiles:
            ln_tile(c0, n)
            if prev_t is not None:
                wo_tile(*prev_t)
            prev_t = (c0, n)
        wo_tile(*prev_t)

    def ffn(self, L, tiles):
        P = self.P
        wgu = self.din["w_gate_up"]
        wdn = self.din["w_down"]
        self.rmsnorm(tiles, f"ffn_norm{L}")
        for fg in range(NFC // 2):
            f0 = fg * 256
            wgv, kg = self.wslot()
            wuv, ku = self.wslot()
            wdv, kd = self.wslot()
            wg = wgv[:, 0:2048].rearrange("p (k n) -> p k n", k=NCH)
            wu = wuv[:, 0:2048].rearrange("p (k n) -> p k n", k=NCH)
            wd = wdv[:, 0:2048].rearrange("p (j n) -> p j n", j=2)
            sg_ = wgu[L, :, f0:f0 + 256].rearrange("(k p) n -> p k n", p=128)
            su_ = wgu[L, :, DFF + f0:DFF + f0 + 256].rearrange("(k p) n -> p k n", p=128)
            sd_ = wdn[L, f0:f0 + 256, :].rearrange("(j p) n -> p j n", p=128)
            P.dma("pool", kg, lambda e, wg=wg, sg_=sg_: [e.dma_start(out=wg, in_=sg_)], writes=(kg,))
            P.dma("pool", ku, lambda e, wu=wu, su_=su_: [e.dma_start(out=wu, in_=su_)], writes=(ku,))
            P.dma("pool", kd, lambda e, wd=wd, sd_=sd_: [e.dma_start(out=wd, in_=sd_)], writes=(kd,))

            def gate_up_steps(c0, n):
                tk = c0 // 512
                hkeys = []
                hs = []
                steps = []
                for j in range(2):
                    sg, sgk = self.t32()
                    hb = self.hbuf[self._hi]
                    hk = f"h{self._hi}"
                    self._hi = (self._hi + 1) % len(self.hbuf)
                    hs.append(hb)
                    hkeys.append(hk)
                    bank = {}

                    def s_g(bank=bank, j=j):
                        pg, kpg = self.ps()
                        bank["g"] = (pg, kpg)

                        def mm(e, pg=pg, j=j, c0=c0, n=n, wg=wg):
                            for k in range(NCH):
                                r = e.matmul(pg[:, 0:n], lhsT=wg[:, k, j * 128:(j + 1) * 128], rhs=self.u[:, k, c0:c0 + n], start=(k == 0), stop=(k == NCH - 1))
                            return r
                        P.op("pe", mm, reads=(kg,) + self.uk(tk), writes=(kpg,))

                    def s_u(bank=bank, j=j):
                        pu, kpu = self.ps()
                        bank["u"] = (pu, kpu)

                        def mm(e, pu=pu, j=j, c0=c0, n=n, wu=wu):
                            for k in range(NCH):
                                r = e.matmul(pu[:, 0:n], lhsT=wu[:, k, j * 128:(j + 1) * 128], rhs=self.u[:, k, c0:c0 + n], start=(k == 0), stop=(k == NCH - 1))
                            return r
                        P.op("pe", mm, reads=(ku,) + self.uk(tk), writes=(kpu,))

                    def s_e(bank=bank, sg=sg, sgk=sgk, hb=hb, hk=hk):
                        pg, kpg = bank["g"]
                        pu, kpu = bank["u"]
                        P.op("act", lambda e, sg=sg, pg=pg, n=n: e.activation(out=sg[:, 0:n], in_=pg[:, 0:n], func=AF.Silu),
                             reads=(kpg,), writes=(sgk,))
                        P.op("dve", lambda e, hb=hb, sg=sg, pu=pu, n=n: e.tensor_tensor(out=hb[:, 0:n], in0=sg[:, 0:n], in1=pu[:, 0:n], op=ALU.mult),
                             reads=(sgk, kpu), writes=(hk,))
                    steps += [s_g, s_u, s_e]
                return steps, hs, hkeys

            def down_steps(c0, n, hs, hkeys):
                tk = c0 // 512
                steps = []
                for dc in range(NCH):
                    def s_d(dc=dc):
                        pd, kpd = self.ps()

                        def dn(e, pd=pd, dc=dc, hs=tuple(hs), n=n, wd=wd):
                            e.matmul(pd[:, 0:n], lhsT=wd[:, 0, dc * 128:(dc + 1) * 128], rhs=hs[0][:, 0:n], start=True, stop=False)
                            return e.matmul(pd[:, 0:n], lhsT=wd[:, 1, dc * 128:(dc + 1) * 128], rhs=hs[1][:, 0:n], start=False, stop=True)
                        P.op("pe", dn, reads=(kd,) + tuple(hkeys), writes=(kpd,))
                        P.op("dve", lambda e, pd=pd, dc=dc, c0=c0, n=n: e.tensor_tensor(out=self.x[:, dc, c0:c0 + n], in0=pd[:, 0:n],
                                                                                  in1=self.x[:, dc, c0:c0 + n], op=ALU.add),
                             reads=(kpd, f"x:{tk}"), writes=(f"x:{tk}",))
                    steps.append(s_d)
                return steps
            prev = None
            for (c0, n) in tiles:
                gu, hs, hkeys = gate_up_steps(c0, n)
                dn_ = down_steps(*prev) if prev is not None else []
                order = [gu[0]] + dn_[0:2] + [gu[1]] + dn_[2:4] + [gu[2], gu[3]] + dn_[4:6] + [gu[4]] + dn_[6:8] + [gu[5]]
                for st_ in order:
                    st_()
                prev = (c0, n, hs, hkeys)
            for st_ in down_steps(*prev):
                st_()

    def phase1_test(self):
        self.setup()
        xin = self.din["xin"]
        for sup in range(2):
            isY = sup == 1
            for tb in range(16):
                self.load_tokens(xin[sup * HALF + tb * 128: sup * HALF + (tb + 1) * 128, :], 128, tb * 128)
            if isY:
                self.load_tokens(self.din["xs"][:, :], NS, HALF)
            tiles = TILES_Y if isY else TILES_X
            for L in range(2):
                self.conv_layer(L, isY)
                self.ffn(L, tiles)
        for tb in range(16):
            self.store_tokens(self.dout["y"][tb * 128:(tb + 1) * 128, :], 128, tb * 128)
        self.store_tokens(self.dout["y"][HALF:HALF + NS, :], NS, HALF)

    def finish(self):
        nc = self.nc
        P = self.P
        chans = sorted(P.chan_n.keys())
        import contextlib
        with contextlib.ExitStack() as es:
            sems = {e: es.enter_context(nc.semaphore(f"sem_{e}")) for e in ENGS}
            csems = {ch: es.enter_context(nc.semaphore(f"semd_{ch}")) for ch in chans}
            block = es.enter_context(nc.Block())
            regs = {"pe": block.tensor, "act": block.scalar, "dve": block.vector, "pool": block.gpsimd, "sp": block.sync}
            P.emit(nc, regs, sems, csems)
        return nc


def _const_mats():
    p = np.arange(128)
    ident = np.eye(128, dtype=np.float32)
    ones = np.ones((128, 128), np.float32)
    bones = (p[:, None] // 64 == p[None, :] // 64).astype(np.float32)
    partner = (p // 64) * 64 + ((p % 64) + 32) % 64
    swap = np.zeros((128, 128), np.float32)
    swap[partner, p] = 1.0
    maskP = (p[None, :] <= p[:, None]).astype(np.float32)
    maskC = (p[:, None] <= p[None, :]).astype(np.float32)
    return np.stack([ident, ones, bones, swap, maskP, maskC]).astype(np.float32)


def _smask():
    m = np.zeros((128, 544), np.float32)
    k = np.arange(128)[:, None]
    q = np.arange(8)[None, :]
    for a in range(16):
        m[:, a * 8:(a + 1) * 8] = (k >= q)
        m[:, 128 + a * 2] = 1.0
        m[:, 128 + a * 2 + 1] = (np.arange(128) >= 1)
    t = np.arange(8)[:, None]
    M0 = (t <= q).astype(np.float32)
    M1 = ((t == q) | (t == q - 4)).astype(np.float32)
    M2 = (t == q).astype(np.float32)
    for g, Mg in enumerate((M0, M1, M2)):
        for a in range(16):
            m[0:8, 160 + g * 128 + a * 8:160 + g * 128 + (a + 1) * 8] = Mg
    return m


def _rot_tables(half_id):
    half = 32
    inv = np.float32(10000.0) ** (-(np.arange(half, dtype=np.float32) / np.float32(half)))
    pos = np.zeros(KTW, np.float32)
    l = np.arange(SEQ)
    if half_id == 1:
        pos[:SEQ] = l
    else:
        pos[:SEQ] = np.maximum(l - HALF, 0)
    pos[SEQ:] = np.tile(PAST + np.arange(8), 4)
    ang = (pos[:, None].astype(np.float32) * inv[None, :].astype(np.float32)).astype(np.float32)
    p = np.arange(128)
    idx = (p % 64) % 32
    cosT = np.cos(ang)[:, idx].T.astype(np.float32)
    sinT = np.sin(ang)[:, idx].T.astype(np.float32)
    sign = np.where((p % 64) < 32, -1.0, 1.0).astype(np.float32)
    return np.stack([cosT, sinT * sign[:, None]]).astype(np.float32)


def _pvec(inp):
    rows, n = Builder.pv_layout()
    pv = np.zeros((n, D), np.float32)
    for L in range(2):
        pv[rows[f"a_norm{L}"]] = inp["a_norm"][L]
        pv[rows[f"b1_{L}"]] = inp["a_b_in"][L][:D]
        pv[rows[f"b2_{L}"]] = inp["a_b_in"][L][D:]
        pv[rows[f"wdw{L}"]:rows[f"wdw{L}"] + CW] = inp["a_w_dw"][L]
        pv[rows[f"b_dw{L}"]] = inp["a_b_dw"][L]
        pv[rows[f"ln_g{L}"]] = inp["a_ln_g"][L]
        pv[rows[f"ln_b{L}"]] = inp["a_ln_b"][L]
        pv[rows[f"b_out{L}"]] = inp["a_b_out"][L]
    pv[rows["kv_norm"]] = inp["kv_norm"]
    for j in range(2):
        pv[rows[f"b_norm{j}"]] = inp["b_norm"][j]
    for L in range(4):
        pv[rows[f"ffn_norm{L}"]] = inp["ffn_norm"][L]
    for g in range(3):
        pv[rows[f"k_norm{g}"], :128] = np.tile(inp["k_norm"][g], 2)
    for j in range(2):
        for g in range(3):
            pv[rows[f"q_norm{j}_{g}"], :128] = np.tile(inp["q_norm"][j, g], 2)
    return pv


def make_in_maps(inp):
    inp = {k: np.asarray(v) for k, v in inp.items()}
    cmat = _const_mats()
    pvec = _pvec(inp)
    rots = [_rot_tables(0), _rot_tables(1)]
    shared = {
        "pvec": pvec, "cmat": cmat, "smaskin": _smask(),
        "a_w_in": np.ascontiguousarray(inp["a_w_in"], np.float32), "a_w_out": np.ascontiguousarray(inp["a_w_out"], np.float32),
        "w_kv": np.ascontiguousarray(inp["w_kv"], np.float32), "w_q": np.ascontiguousarray(inp["w_q"], np.float32),
        "w_o": np.ascontiguousarray(inp["w_o"], np.float32), "w_gate_up": np.ascontiguousarray(inp["w_gate_up"], np.float32),
        "w_down": np.ascontiguousarray(inp["w_down"], np.float32),
    }
    maps = []
    for c in range(8):
        b, h = c // 2, c % 2
        if h == 1:
            xin = np.ascontiguousarray(inp["x_prompt"][b], np.float32)
        else:
            xin = np.concatenate([np.zeros((HALF, D), np.float32), inp["x_prompt"][b, :HALF]], axis=0)
        m = dict(shared)
        m["xin"] = xin
        m["xs"] = np.ascontiguousarray(inp["x_sample"][4 * c:4 * c + 4].reshape(NS, D), np.float32)
        m["cconv"] = np.ascontiguousarray(inp["cache_conv"][:, 4 * c:4 * c + 4], np.float32)
        m["ckv0"] = np.ascontiguousarray(inp["cache_kv_w128"][4 * c:4 * c + 4].reshape(4, 128, 2048), np.float32)
        m["ckv1"] = np.ascontiguousarray(inp["cache_kv_w512"][4 * c:4 * c + 4].reshape(4, 512, 2048), np.float32)
        m["ckv2"] = np.ascontiguousarray(inp["cache_kv_w2048"][4 * c:4 * c + 4].reshape(4, 2048, 2048), np.float32)
        m["rot"] = rots[h]
        m["flag"] = np.full((128, 1), float(h), np.float32)
        maps.append(m)
    return maps


def _dbg_t0(self):
    self.setup()
    xin = self.din["xin"]
    self.load_tokens(xin[HALF:HALF + 128, :], 128, 0)
    self.store_tokens(self.dout["y"][0:128, :], 128, 0)


Builder.dbg_t0 = _dbg_t0


def _dbg_c0(self):
    self.setup()
    xin = self.din["xin"]
    tiles = [(0, 512)]
    for tb in range(4):
        self.load_tokens(xin[tb * 128:(tb + 1) * 128, :], 128, tb * 128)
    import os
    self.stop_stage = os.environ.get("STOP_STAGE")
    self.conv_layer(0, False, tiles=tiles)
    if self.stop_stage is None:
        self.ffn(0, tiles)
    for tb in range(4):
        self.store_tokens(self.dout["y"][tb * 128:(tb + 1) * 128, :], 128, tb * 128)
    self.dbg_sbuf = [self.M.names[k] for k in ("u", "glu", "pv", "x")]


Builder.dbg_c0 = _dbg_c0


def _ovl_G(self):
    off, nb = self.M.offs["glu"]
    return Mem(self.nc, off, nb)


def _load_rot(self, OM, col_src0, ncols_main, with_sample):
    P = self.P
    rc = OM.alloc([128, NT], F32, "rotc")
    rs = OM.alloc([128, NT], F32, "rots")
    rot = self.din["rot"]

    def f(e):
        r = [e.dma_start(out=rc[:, 0:HALF], in_=rot[0, :, col_src0:col_src0 + HALF]),
             e.dma_start(out=rs[:, 0:HALF], in_=rot[1, :, col_src0:col_src0 + HALF])]
        if with_sample:
            r += [e.dma_start(out=rc[:, HALF:NT], in_=rot[0, :, SEQ:SEQ + NS]),
                  e.dma_start(out=rs[:, HALF:NT], in_=rot[1, :, SEQ:SEQ + NS])]
        return r
    P.dma("sp", "rot", f, writes=("rot",), n=4 if with_sample else 2)
    return rc, rs


def _head_proj(self, tiles, w_src_fn, gain_fn, dst_fn, rc, rs, OM, out32_fn=None, need32_fn=None, tiles_fn=None):
    P = self.P
    kst = [OM.alloc([128, 512], BF16, f"kst{i}") for i in range(2)]
    ksti = 0
    for g in range(3):
        for hp in range(8):
            wv, wk = self.wslot()
            wa = wv[:, 0:1024].rearrange("p (k n) -> p k n", k=NCH)
            src = w_src_fn(g, hp)
            P.dma("pool", wk, lambda e, wa=wa, src=src: [e.dma_start(out=wa, in_=src)], writes=(wk,))
            for (c0, n) in (tiles if tiles_fn is None else tiles_fn(g)):
                tk = c0 // 512
                pr, kpr = self.ps()

                def mm(e, pr=pr, wa=wa, c0=c0, n=n):
                    for k in range(NCH):
                        r = e.matmul(pr[:, 0:n], lhsT=wa[:, k, :], rhs=self.u[:, k, c0:c0 + n], start=(k == 0), stop=(k == NCH - 1))
                    return r
                P.op("pe", mm, reads=(wk,) + self.uk(tk), writes=(kpr,))
                si = self._sqi
                self._sqi = (si + 2) % 8
                sq = self.sqb[:, si, 0:n]
                xg = self.sqb[:, si + 1, 0:n]
                ksq, kxg = f"sqb{si}", f"sqb{si + 1}"
                P.op("act", lambda e, sq=sq, pr=pr, n=n: e.activation(out=sq, in_=pr[:, 0:n], func=AF.Square), reads=(kpr,), writes=(ksq,))
                P.op("act", lambda e, xg=xg, pr=pr, n=n, g=g: e.activation(out=xg, in_=pr[:, 0:n], func=AF.Identity, scale=gain_fn(g)),
                     reads=(kpr, "pv"), writes=(kxg,))
                p1, k1 = self.ps()
                p2, k2 = self.ps()
                P.op("pe", lambda e, p1=p1, sq=sq, n=n: e.matmul(p1[:, 0:n], lhsT=self.BONES, rhs=sq, start=True, stop=True),
                     reads=(ksq, "cb"), writes=(k1,))
                P.op("pe", lambda e, p2=p2, xg=xg, n=n: e.matmul(p2[:, 0:n], lhsT=self.SWAP, rhs=xg, start=True, stop=True),
                     reads=(kxg, "cb"), writes=(k2,))
                rstd, krs = self.s32()
                P.op("act", lambda e, rstd=rstd, p1=p1, n=n: e.activation(out=rstd[:, 0:n], in_=p1[:, 0:n], func=AF.Ln, bias=EPS, scale=1.0 / 64),
                     reads=(k1,), writes=(krs,))
                P.op("act", lambda e, rstd=rstd, n=n: e.activation(out=rstd[:, 0:n], in_=rstd[:, 0:n], func=AF.Exp, scale=-0.5), reads=(krs,), writes=(krs,))
                t1, kt1 = self.t32()
                t2, kt2 = self.t32()
                P.op("pool", lambda e, t1=t1, xg=xg, c0=c0, n=n: e.tensor_tensor(out=t1[:, 0:n], in0=xg, in1=rc[:, c0:c0 + n], op=ALU.mult),
                     reads=(kxg, "rot"), writes=(kt1,))
                P.op("dve", lambda e, t2=t2, p2=p2, c0=c0, n=n: e.tensor_tensor(out=t2[:, 0:n], in0=p2[:, 0:n], in1=rs[:, c0:c0 + n], op=ALU.mult),
                     reads=(k2, "rot"), writes=(kt2,))
                P.op("dve", lambda e, t1=t1, t2=t2, n=n: e.tensor_tensor(out=t1[:, 0:n], in0=t1[:, 0:n], in1=t2[:, 0:n], op=ALU.add),
                     reads=(kt1, kt2), writes=(kt1,))
                ks = kst[ksti]
                kk = f"kst{ksti}"
                ksti = (ksti + 1) % 2
                P.op("dve", lambda e, ks=ks, t1=t1, rstd=rstd, n=n: e.tensor_tensor(out=ks[:, 0:n], in0=t1[:, 0:n], in1=rstd[:, 0:n], op=ALU.mult),
                     reads=(kt1, krs), writes=(kk,))
                if out32_fn is not None and need32_fn is not None and need32_fn(g, c0, n):
                    P.op("dve", lambda e, t1=t1, rstd=rstd, n=n: e.tensor_tensor(out=t1[:, 0:n], in0=t1[:, 0:n], in1=rstd[:, 0:n], op=ALU.mult),
                         reads=(kt1, krs), writes=(kt1,))
                dst = dst_fn(g, hp, c0, n)
                P.dma("sp", kk, lambda e, ks=ks, dst=dst, n=n: [e.dma_start(out=dst, in_=ks[:, 0:n])], reads=(kk,))
                if out32_fn is not None and need32_fn is not None and need32_fn(g, c0, n):
                    out32_fn(g, hp, c0, n, t1, kt1)


def _kv_stage(self, isY):
    P = self.P
    P.fence()
    tiles = TILES_Y if isY else TILES_X
    self.rmsnorm(tiles, "kv_norm")
    OM = self.ovl_G()
    loc0 = HALF if isY else 0
    rc, rs = self.load_rot(OM, loc0, HALF, isY)
    w_kv = self.din["w_kv"]
    kt = self.kt_scr
    vs = self.v_scr

    def w_src(g, hp):
        c = g * 2048 + hp * 128
        return w_kv[:, c:c + 128].rearrange("(k p) n -> p k n", p=128)

    def gain(g):
        return self.pvs(f"k_norm{g}", 0)

    def dst(g, hp, c0, n):
        if n == 512:
            return kt[g, hp, :, loc0 + c0:loc0 + c0 + n]
        return kt[g, hp, :, SEQ:SEQ + NS]

    def out32(g, hp, c0, n, t1, kt1):
        import os
        if not isY or os.environ.get("NO_OUT32"):
            return
        W = WINS[g]
        if n == 512:
            blocks = [b for b in range(4) if c0 + b * 128 >= HALF - W]
            if not blocks:
                return
            pt, pk = self.ps()

            def tr(e, pt=pt, t1=t1, blocks=tuple(blocks)):
                for b in blocks:
                    r = e.transpose(out=pt[:, b * 128:(b + 1) * 128], in_=t1[:, b * 128:(b + 1) * 128], identity=self.ident[:, :])
                return r
            P.op("pe", tr, reads=(kt1, "ident"), writes=(pk,))
            st, sk = self.stage()
            b0, nb = blocks[0], len(blocks)
            P.op("act", lambda e, st=st, pt=pt, b0=b0, nb=nb: e.activation(out=st[:, b0 * 128:(b0 + nb) * 128], in_=pt[:, b0 * 128:(b0 + nb) * 128], func=AF.Copy),
                 reads=(pk,), writes=(sk,))
            row0 = c0 + b0 * 128 - (HALF - W)
            dsto = self.dout[f"kvp{g}"][row0:row0 + nb * 128, hp * 128:(hp + 1) * 128].rearrange("(b t) f -> t b f", t=128)
            P.dma("sp", sk, lambda e, st=st, dsto=dsto, b0=b0, nb=nb: [e.dma_start(out=dsto, in_=st[:, b0 * 128:(b0 + nb) * 128].rearrange("t (b f) -> t b f", b=nb))],
                  reads=(sk,))
        else:
            pt, pk = self.ps()
            P.op("pe", lambda e, pt=pt, t1=t1: e.transpose(out=pt[0:NS, 0:128], in_=t1[:, 0:NS], identity=self.ident[:, :]),
                 reads=(kt1, "ident"), writes=(pk,))
            st, sk = self.stage()
            P.op("act", lambda e, st=st, pt=pt: e.activation(out=st[0:NS, 0:128], in_=pt[0:NS, 0:128], func=AF.Copy), reads=(pk,), writes=(sk,))
            dsto = self.dout[f"kvs{g}"][:, W - 8:W, hp * 128:(hp + 1) * 128]

            def f(e, st=st, dsto=dsto):
                return [e.dma_start(out=dsto[s], in_=st[s * 8:(s + 1) * 8, 0:128]) for s in range(4)]
            P.dma("sp", sk, f, reads=(sk,), n=4)
    def need32(g, c0, n):
        return isY and (n != 512 or c0 + 512 > HALF - WINS[g])
    def tiles_for(g):
        if isY:
            return tiles
        return [(c0, n) for (c0, n) in tiles if c0 + n > HALF - WINS[g]]
    self.head_proj(tiles, w_src, gain, dst, rc, rs, OM, out32, need32, tiles_for)

    vst = [OM.alloc([128, 256], BF16, f"vst{i}") for i in range(2)]
    vi = 0
    blocks = [(tb * 128, 128) for tb in range(16)] + ([(HALF, NS)] if isY else [])
    for g in range(3):
        W = WINS[g]
        for q in range(4):
            wv, wk = self.wslot()
            wa = wv[:, 0:2048].rearrange("p (k n) -> p k n", k=NCH)
            c = g * 2048 + 1024 + q * 256
            src = w_kv[:, c:c + 256].rearrange("(k p) n -> p k n", p=128)
            P.dma("pool", wk, lambda e, wa=wa, src=src: [e.dma_start(out=wa, in_=src)], writes=(wk,))
            for (c0, m) in blocks:
                if (not isY) and c0 + m <= HALF - W:
                    continue
                tk = c0 // 512
                pt, pk = self.ps()

                def mm(e, pt=pt, wa=wa, c0=c0, m=m):
                    for k in range(NCH):
                        r = e.matmul(pt[0:m, 0:256], lhsT=self.u[:, k, c0:c0 + m], rhs=wa[:, k, :], start=(k == 0), stop=(k == NCH - 1))
                    return r
                P.op("pe", mm, reads=(wk,) + self.uk(tk), writes=(pk,))
                vb = vst[vi]
                vk = f"vst{vi}"
                vi = (vi + 1) % 2
                P.op("act", lambda e, vb=vb, pt=pt, m=m: e.activation(out=vb[0:m, :], in_=pt[0:m, 0:256], func=AF.Copy), reads=(pk,), writes=(vk,))
                if m == 128:
                    l0 = loc0 + c0
                else:
                    l0 = SEQ
                dstv = vs[g, 2 * q:2 * q + 2, l0:l0 + m, :].rearrange("h t f -> t h f")
                P.dma("sp", vk, lambda e, vb=vb, dstv=dstv, m=m: [e.dma_start(out=dstv, in_=vb[0:m, :].rearrange("t (h f) -> t h f", h=2))],
                      reads=(vk,))
                import os
                if isY and not os.environ.get("NO_VOUT"):
                    need = (m == NS) or (c0 >= HALF - W)
                    if need:
                        st, sk = self.stage()
                        P.op("dve", lambda e, st=st, pt=pt, m=m: e.tensor_copy(out=st[0:m, 0:256], in_=pt[0:m, 0:256]), reads=(pk,), writes=(sk,))
                        if m == 128:
                            row0 = c0 - (HALF - W)
                            dsto = self.dout[f"kvp{g}"][row0:row0 + 128, 1024 + q * 256:1024 + (q + 1) * 256]
                            P.dma("sp", sk, lambda e, st=st, dsto=dsto: [e.dma_start(out=dsto, in_=st[:, 0:256])], reads=(sk,))
                        else:
                            dsto = self.dout[f"kvs{g}"][:, W - 8:W, 1024 + q * 256:1024 + (q + 1) * 256]

                            def f(e, st=st, dsto=dsto):
                                return [e.dma_start(out=dsto[s], in_=st[s * 8:(s + 1) * 8, 0:256]) for s in range(4)]
                            P.dma("sp", sk, f, reads=(sk,), n=4)
    P.fence()


Builder.ovl_G = _ovl_G
Builder.load_rot = _load_rot
Builder.head_proj = _head_proj
Builder.kv_stage = _kv_stage


SCALE = 0.125


def sl(start, n, step):
    return slice(start, start + step * (n - 1) + 1, step)

KBASE = (HALF - 128, HALF - 512, 0)
NBO = (16, 4, 1)


def _q_stage(self, j):
    P = self.P
    P.fence()
    self.rmsnorm(TILES_Y, f"b_norm{j}")
    OM = self.ovl_G()
    rc, rs = self.load_rot(OM, HALF, HALF, True)
    w_q = self.din["w_q"]

    def w_src(g, hp):
        c = g * 1024 + hp * 128
        return w_q[j, :, c:c + 128].rearrange("(k p) n -> p k n", p=128)

    def gain(g):
        return self.pvs(f"q_norm{j}_{g}", 0)

    def dst(g, hp, c0, n):
        return self.q_scr[g, hp, :, c0:c0 + n]
    self.head_proj(TILES_Y, w_src, gain, dst, rc, rs, OM, None)
    P.fence()


def _attn_layer(self, j, STOP=""):
    P = self.P
    self.q_stage(j)
    if STOP == "Q":
        return
    self.sample_stage(j)
    if STOP == "S":
        return
    uo, ub = self.M.offs["u"]
    go, gb = self.M.offs["glu"]
    assert uo + ub == go
    OM = Mem(self.nc, uo, ub + gb)
    QT = [OM.alloc([128, NT], BF16, f"QT{g}") for g in range(3)]
    kw = [SEQ - KBASE[g] + NS for g in range(3)]
    KT = [OM.alloc([128, kw[g]], BF16, f"KT{g}") for g in range(3)]
    nblk = [DILS[g] * (NBO[g] + 1) for g in range(3)]
    VT = [OM.alloc([128, nblk[g], 128], BF16, f"VT{g}") for g in range(3)]
    acc = OM.alloc([128, 2, NT], F32, "acc")
    PT = [OM.alloc([128, 512], BF16, f"PT{i}") for i in range(3)]
    qo, qb = OM.offs["QT0"]
    assert OM.offs["QT1"][0] == qo + qb and qb == NT * 2
    oT = self.nc.alloc_sbuf_tensor_at(f"oT_{j}", [128, 2, NT], BF16, offset=qo)
    self.att = dict(QT=QT, KT=KT, VT=VT, acc=acc, PT=PT, OM=OM)
    pti = 0
    w_o = self.din["w_o"]
    for hp in range(8):
        for g in range(3):
            r = DILS[g]
            nbo = NBO[g]
            kb0 = KBASE[g]
            P.dma("sp", f"QT{g}", lambda e, g=g, hp=hp: [e.dma_start(out=QT[g][:, :], in_=self.q_scr[g, hp, :, :])], writes=(f"QT{g}",))
            P.dma("sp", f"KT{g}", lambda e, g=g, hp=hp, kb0=kb0: [e.dma_start(out=KT[g][:, :], in_=self.kt_scr[g, hp, :, kb0:KTW])],
                  writes=(f"KT{g}",))
            vsrc = self.v_scr[g, hp, kb0:kb0 + r * 128 * (nbo + 1), :].rearrange("(m i c) f -> i c m f", i=128, c=r)
            vdst = VT[g][:, :, :].rearrange("p (c m) f -> p c m f", c=r)
            if r == 1:
                P.dma("sp", f"VT{g}", lambda e, vdst=vdst, vsrc=vsrc: [e.dma_start(out=vdst[:, 0], in_=vsrc[:, 0])], writes=(f"VT{g}",))
            else:
                def fv(e, vdst=vdst, vsrc=vsrc, r=r):
                    return [e.dma_start(out=vdst[:, c], in_=vsrc[:, c]) for c in range(r)]
                P.dma("sp", f"VT{g}", fv, writes=(f"VT{g}",), n=r)
            def front(c, nn, g=g, r=r, nbo=nbo, kb0=kb0):
                nonlocal pti
                q0 = c + r * 128 * nn
                kcur = (HALF - kb0) + q0
                kprev = kcur - r * 128
                bprev = c * (nbo + 1) + nn
                pSa, kSa = self.ps()
                pSb, kSb = self.ps()

                def st(e, pSa=pSa, pSb=pSb, g=g, r=r, q0=q0, kcur=kcur, kprev=kprev):
                    for h, pS in enumerate((pSa, pSb)):
                        for kb, k0 in enumerate((kprev, kcur)):
                            rr = e.matmul(pS[:, kb * 128:(kb + 1) * 128],
                                          lhsT=KT[g][h * 64:(h + 1) * 64, sl(k0, 128, r)],
                                          rhs=QT[g][h * 64:(h + 1) * 64, sl(q0, 128, r)], start=True, stop=True)
                    return rr
                P.op("pe", st, reads=(f"KT{g}", f"QT{g}"), writes=(kSa, kSb))
                E, kE = self.t32()
                E4 = E[:, :].rearrange("p (k h q) -> p k h q", k=2, h=2)
                P.op("act", lambda e, E4=E4, pSa=pSa: e.activation(out=E4[:, :, 0, :], in_=pSa[:, 0:256].rearrange("p (k q) -> p k q", k=2),
                                                                func=AF.Exp, scale=SCALE), reads=(kSa,), writes=(kE,))
                P.op("act", lambda e, E4=E4, pSb=pSb: e.activation(out=E4[:, :, 1, :], in_=pSb[:, 0:256].rearrange("p (k q) -> p k q", k=2),
                                                                func=AF.Exp, scale=SCALE), reads=(kSb, kE), writes=(kE,))
                pt = PT[pti]
                kpt = f"PT{pti}"
                pti = (pti + 1) % 3
                if nn == 0:
                    def mk(e, pt=pt, E=E):
                        e.scalar_tensor_tensor(out=pt[:, 0:256], in0=E[:, 0:256], scalar=self.flag[:, 0:1], in1=self.maskf[:, 0:256],
                                               op0=ALU.mult, op1=ALU.mult)
                        return e.tensor_tensor(out=pt[:, 256:512], in0=E[:, 256:512], in1=self.maskf[:, 256:512], op=ALU.mult)
                else:
                    def mk(e, pt=pt, E=E):
                        return e.tensor_tensor(out=pt[:, :], in0=E[:, :], in1=self.maskf[:, :], op=ALU.mult)
                P.op("dve", mk, reads=(kE, "maskf", "flag"), writes=(kpt,))
                return (pt, kpt, bprev, q0)

            def back(ctx, g=g, r=r):
                pt, kpt, bprev, q0 = ctx
                pU, kU = self.ps()

                def pv(e, pU=pU, pt=pt, g=g, bprev=bprev):
                    for h in range(2):
                        for kb in range(2):
                            rhs = pt[:, (kb * 2 + h) * 128:(kb * 2 + h + 1) * 128]
                            e.matmul(pU[0:64, h * 128:(h + 1) * 128], lhsT=VT[g][:, bprev + kb, h * 64:(h + 1) * 64], rhs=rhs,
                                     start=(kb == 0), stop=(kb == 1))
                            rr = e.matmul(pU[64:128, h * 128:(h + 1) * 128], lhsT=self.ONES[:, 0:64], rhs=rhs,
                                          start=(kb == 0), stop=(kb == 1))
                    return rr
                P.op("pe", pv, reads=(kpt, f"VT{g}", "cb"), writes=(kU,))
                av = acc[:, :, sl(q0, 128, r)]
                pv3 = pU[:, 0:256].rearrange("p (h q) -> p h q", h=2)
                if g == 0:
                    P.op("dve", lambda e, av=av, pv3=pv3: e.tensor_copy(out=av, in_=pv3), reads=(kU,), writes=("acc",))
                else:
                    P.op("dve", lambda e, av=av, pv3=pv3: e.tensor_tensor(out=av, in0=pv3, in1=av, op=ALU.add), reads=(kU, "acc"), writes=("acc",))
            pend = []
            for c in range(r):
                for nn in range(nbo):
                    pend.append(front(c, nn))
                    if len(pend) > 2:
                        back(pend.pop(0))
            while pend:
                back(pend.pop(0))
        self.sample_attn(j, hp)
        P.op("act", lambda e: e.activation(out=acc[64:128, :, :], in_=acc[64:128, :, :], func=AF.Ln), reads=("acc",), writes=("acc",))
        P.op("act", lambda e: e.activation(out=acc[64:128, :, :], in_=acc[64:128, :, :], func=AF.Exp, scale=-1.0), reads=("acc",), writes=("acc",))
        wv, wk = self.wslot()
        wo = wv[0:64, 0:2048].rearrange("p (h n) -> p h n", h=2)
        src = w_o[j, hp * 128:(hp + 1) * 128, :].rearrange("(h p) n -> p h n", p=64)
        P.dma("pool", wk, lambda e, wo=wo, src=src: [e.dma_start(out=wo, in_=src)], writes=(wk,))
        for (c0, n) in TILES_Y:
            tk = c0 // 512
            for h in range(2):
                pR, kR = self.ps()
                P.op("pe", lambda e, pR=pR, h=h, c0=c0, n=n: e.matmul(pR[0:64, 0:n], lhsT=self.ident[:, 64:128], rhs=acc[:, h, c0:c0 + n], start=True, stop=True),
                     reads=("acc", "ident"), writes=(kR,))
                P.op("dve", lambda e, pR=pR, h=h, c0=c0, n=n: e.tensor_tensor(out=oT[0:64, h, c0:c0 + n], in0=acc[0:64, h, c0:c0 + n], in1=pR[0:64, 0:n], op=ALU.mult),
                     reads=(kR, "acc"), writes=("QT0", "QT1"))
            for dc in range(NCH):
                pd, kd = self.ps()

                def wm(e, pd=pd, dc=dc, c0=c0, n=n, wo=wo):
                    e.matmul(pd[:, 0:n], lhsT=wo[0:64, 0, dc * 128:(dc + 1) * 128], rhs=oT[0:64, 0, c0:c0 + n], start=True, stop=False)
                    return e.matmul(pd[:, 0:n], lhsT=wo[0:64, 1, dc * 128:(dc + 1) * 128], rhs=oT[0:64, 1, c0:c0 + n], start=False, stop=True)
                P.op("pe", wm, reads=(wk, "QT0", "QT1"), writes=(kd,))
                P.op("dve", lambda e, pd=pd, dc=dc, c0=c0, n=n: e.tensor_tensor(out=self.x[:, dc, c0:c0 + n], in0=pd[:, 0:n], in1=self.x[:, dc, c0:c0 + n], op=ALU.add),
                     reads=(kd, f"x:{tk}"), writes=(f"x:{tk}",))
    P.fence()


def _sample_attn_stub(self, j, hp):
    acc = self.att["acc"]
    self.P.op("dve", lambda e: e.memset(acc[:, :, HALF:NT], 1.0), writes=("acc",))


Builder.q_stage = _q_stage
Builder.attn_layer = _attn_layer
Builder.sample_attn = _sample_attn_stub


def _sample_stage(self, j):
    P = self.P
    P.fence()
    uo, ub = self.M.offs["u"]
    go, gb = self.M.offs["glu"]
    OM = Mem(self.nc, uo, ub + gb)
    QS = OM.alloc([128, 24, NS], BF16, "QS")
    KN = OM.alloc([128, 24, NS], BF16, "KN")
    VN = [OM.alloc([8, 32, 128], BF16, f"VN{g}") for g in range(3)]
    KC = [OM.alloc([128, 1024], F32, f"KC{i}") for i in range(2)]
    VC = [OM.alloc([128, 1024], BF16, f"VC{i}") for i in range(2)]
    KCT = [OM.alloc([128, 8, 128], BF16, f"KCT{i}") for i in range(2)]
    ES = [OM.alloc([128, 128], F32, f"ES{i}") for i in range(2)]
    PS_ = [OM.alloc([128, 128], BF16, f"PS{i}") for i in range(2)]
    sacc = self.sacc
    P.dma("sp", "QS", lambda e: [e.dma_start(out=QS[:, :, :], in_=self.q_scr[:, :, :, HALF:NT].rearrange("g h p n -> p (g h) n"))], writes=("QS",))
    P.dma("sp", "KN", lambda e: [e.dma_start(out=KN[:, :, :], in_=self.kt_scr[:, :, :, SEQ:KTW].rearrange("g h p n -> p (g h) n"))], writes=("KN",))
    for g in range(3):
        def fvn(e, g=g):
            return [e.dma_start(out=VN[g][:, hp * 4:(hp + 1) * 4, :], in_=self.v_scr[g, hp, SEQ:KTW, :].rearrange("(s t) f -> t s f", t=8))
                    for hp in range(8)]
        P.dma("sp", f"VN{g}", fvn, writes=(f"VN{g}",), n=8)
    P.op("dve", lambda e: e.memset(sacc[:, :, :, :], 0.0), writes=("sacc",))
    ckv = [self.din["ckv0"], self.din["ckv1"], self.din["ckv2"]]
    it = 0
    pend = None

    def run(front, back):
        nonlocal pend
        ctx = front()
        if pend is not None:
            pend[0](pend[1])
        pend = (back, ctx)
    for s in range(4):
        for g in range(3):
            r = DILS[g]
            ntile = (1, 4, 8)[g]
            for c in range(ntile):
                if g == 0:
                    qcols = list(range(8))
                elif g == 1:
                    qcols = [c, c + 4]
                else:
                    qcols = [c]
                nq = len(qcols)
                q0, qstep = s * 8 + qcols[0], (qcols[1] - qcols[0]) if nq > 1 else 1
                b = it % 2
                it += 1

                def front(s=s, g=g, c=c, r=r, nq=nq, q0=q0, qstep=qstep, b=b):
                    kc, vc, kct, es, pb = KC[b], VC[b], KCT[b], ES[b], PS_[b]
                    rows = ckv[g][s, sl(c, 128, r), :]
                    P.dma("sp", f"KC{b}", lambda e, kc=kc, rows=rows: [e.dma_start(out=kc[:, :], in_=rows[:, 0:1024])], writes=(f"KC{b}",))
                    P.dma("pool", f"VC{b}", lambda e, vc=vc, rows=rows: [e.dma_start(out=vc[:, :], in_=rows[:, 1024:2048])], writes=(f"VC{b}",))
                    for half in range(2):
                        pt, pk = self.ps()

                        def tr(e, pt=pt, kc=kc, half=half):
                            for cc in range(4):
                                hp = half * 4 + cc
                                rr = e.transpose(out=pt[:, cc * 128:(cc + 1) * 128], in_=kc[:, hp * 128:(hp + 1) * 128], identity=self.ident[:, :])
                            return rr
                        P.op("pe", tr, reads=(f"KC{b}", "ident"), writes=(pk,))
                        P.op("act", lambda e, pt=pt, kct=kct, half=half: e.activation(out=kct[:, half * 4:half * 4 + 4, :],
                                                                                   in_=pt[:, :].rearrange("p (a k) -> p a k", a=4), func=AF.Copy),
                             reads=(pk,), writes=(f"KCT{b}",))
                    pSa, kSa = self.ps()
                    pSb, kSb = self.ps()

                    def st(e, pSa=pSa, pSb=pSb, kct=kct):
                        for h, pS in enumerate((pSa, pSb)):
                            for hp in range(8):
                                rr = e.matmul(pS[:, hp * nq:(hp + 1) * nq], lhsT=kct[h * 64:(h + 1) * 64, hp, :],
                                              rhs=QS[h * 64:(h + 1) * 64, g * 8 + hp, sl(q0, nq, qstep)], start=True, stop=True)
                        return rr
                    P.op("pe", st, reads=(f"KCT{b}", "QS"), writes=(kSa, kSb))
                    es4 = es[:, 0:16 * nq].rearrange("p (a h q) -> p a h q", a=8, h=2)
                    P.op("act", lambda e, es4=es4, pSa=pSa: e.activation(out=es4[:, :, 0, :], in_=pSa[:, 0:8 * nq].rearrange("p (a q) -> p a q", a=8),
                                                                      func=AF.Exp, scale=SCALE), reads=(kSa,), writes=(f"ES{b}",))
                    P.op("act", lambda e, es4=es4, pSb=pSb: e.activation(out=es4[:, :, 1, :], in_=pSb[:, 0:8 * nq].rearrange("p (a q) -> p a q", a=8),
                                                                      func=AF.Exp, scale=SCALE), reads=(kSb, f"ES{b}"), writes=(f"ES{b}",))
                    if g == 0:
                        msk = self.smask[:, 0:128]
                    elif g == 1:
                        msk = self.smask[:, 128:160]
                    else:
                        msk = None
                    if msk is not None:
                        P.op("dve", lambda e, pb=pb, es=es, msk=msk: e.tensor_tensor(out=pb[:, 0:16 * nq], in0=es[:, 0:16 * nq], in1=msk, op=ALU.mult),
                             reads=(f"ES{b}", "smask"), writes=(f"PS{b}",))
                    else:
                        P.op("dve", lambda e, pb=pb, es=es: e.tensor_copy(out=pb[:, 0:16 * nq], in_=es[:, 0:16 * nq]),
                             reads=(f"ES{b}",), writes=(f"PS{b}",))
                    return None

                def back(ctx, s=s, g=g, nq=nq, q0=q0, qstep=qstep, b=b):
                    vc, pb = VC[b], PS_[b]
                    pU, kU = self.ps()

                    def pv(e, pU=pU, pb=pb, vc=vc):
                        for hp in range(8):
                            for h in range(2):
                                col = (hp * 2 + h) * nq
                                e.matmul(pU[0:64, col:col + nq], lhsT=vc[:, hp * 128 + h * 64:hp * 128 + (h + 1) * 64], rhs=pb[:, col:col + nq], start=True, stop=True)
                                rr = e.matmul(pU[64:128, col:col + nq], lhsT=self.ONES[:, 0:64], rhs=pb[:, col:col + nq], start=True, stop=True)
                        return rr
                    P.op("pe", pv, reads=(f"PS{b}", f"VC{b}", "cb"), writes=(kU,))
                    sv = sacc[:, :, :, sl(q0, nq, qstep)].rearrange("p a h q -> p (a h) q")
                    P.op("dve", lambda e, sv=sv, pU=pU: e.tensor_tensor(out=sv, in0=pU[:, 0:16 * nq].rearrange("p (a q) -> p a q", a=16), in1=sv, op=ALU.add),
                         reads=(kU, "sacc"), writes=("sacc",))
                run(front, back)
            b = it % 2
            it += 1

            def frontn(s=s, g=g, b=b):
                es, pb = ES[b], PS_[b]
                pSa, kSa = self.ps()
                pSb, kSb = self.ps()

                def stn(e, pSa=pSa, pSb=pSb):
                    for h, pS in enumerate((pSa, pSb)):
                        for hp in range(8):
                            rr = e.matmul(pS[0:8, hp * 8:(hp + 1) * 8], lhsT=KN[h * 64:(h + 1) * 64, g * 8 + hp, s * 8:s * 8 + 8],
                                          rhs=QS[h * 64:(h + 1) * 64, g * 8 + hp, s * 8:s * 8 + 8], start=True, stop=True)
                    return rr
                P.op("pe", stn, reads=("KN", "QS"), writes=(kSa, kSb))
                esn = es[0:8, :].rearrange("p (a h q) -> p a h q", a=8, h=2)
                P.op("act", lambda e, esn=esn, pSa=pSa: e.activation(out=esn[:, :, 0, :], in_=pSa[0:8, 0:64].rearrange("p (a q) -> p a q", a=8),
                                                                  func=AF.Exp, scale=SCALE), reads=(kSa,), writes=(f"ES{b}",))
                P.op("act", lambda e, esn=esn, pSb=pSb: e.activation(out=esn[:, :, 1, :], in_=pSb[0:8, 0:64].rearrange("p (a q) -> p a q", a=8),
                                                                  func=AF.Exp, scale=SCALE), reads=(kSb, f"ES{b}"), writes=(f"ES{b}",))
                mo = 160 + g * 128
                P.op("dve", lambda e, pb=pb, es=es, mo=mo: e.tensor_tensor(out=pb[0:8, :], in0=es[0:8, :], in1=self.smask[0:8, mo:mo + 128], op=ALU.mult),
                     reads=(f"ES{b}", "smask"), writes=(f"PS{b}",))
                return None

            def backn(ctx, s=s, g=g, b=b):
                pb = PS_[b]
                pU, kU = self.ps()

                def pvn(e, pU=pU, pb=pb):
                    for hp in range(8):
                        for h in range(2):
                            col = (hp * 2 + h) * 8
                            e.matmul(pU[0:64, col:col + 8], lhsT=VN[g][0:8, hp * 4 + s, h * 64:(h + 1) * 64], rhs=pb[0:8, col:col + 8], start=True, stop=True)
                            rr = e.matmul(pU[64:128, col:col + 8], lhsT=self.ONES[0:8, 0:64], rhs=pb[0:8, col:col + 8], start=True, stop=True)
                    return rr
                P.op("pe", pvn, reads=(f"PS{b}", f"VN{g}", "cb"), writes=(kU,))
                sv = sacc[:, :, :, s * 8:s * 8 + 8].rearrange("p a h q -> p (a h) q")
                P.op("dve", lambda e, sv=sv, pU=pU: e.tensor_tensor(out=sv, in0=pU[:, 0:128].rearrange("p (a q) -> p a q", a=16), in1=sv, op=ALU.add),
                     reads=(kU, "sacc"), writes=("sacc",))
            run(frontn, backn)
    if pend is not None:
        pend[0](pend[1])
    P.fence()


def _sample_attn(self, j, hp):
    acc = self.att["acc"]
    self.P.op("dve", lambda e, hp=hp: e.tensor_copy(out=acc[:, :, HALF:NT], in_=self.sacc[:, hp, :, :]), reads=("sacc",), writes=("acc",))


Builder.sample_stage = _sample_stage
Builder.sample_attn = _sample_attn


def _conv_outputs(self, L):
    P = self.P
    st, sk = self.stage()
    for half in range(2):
        pt, pk = self.ps()

        def f(e, half=half, pt=pt):
            for cc in range(4):
                c = half * 4 + cc
                r = e.transpose(out=pt[0:30, cc * 128:(cc + 1) * 128], in_=self.g32[:, L, c, :], identity=self.ident[:, :])
            return r
        P.op("pe", f, reads=("g32", "ident"), writes=(pk,))
        P.op("act", lambda e, half=half, pt=pt, st=st: e.activation(out=st[0:30, half * 512:(half + 1) * 512], in_=pt[0:30, :], func=AF.Copy),
             reads=(pk,), writes=(sk,))
    P.dma("sp", sk, lambda e, st=st: [e.dma_start(out=self.dout["convp"][L, :, :], in_=st[0:30, :])], reads=(sk,))
    P.dma("sp", "d2d", lambda e: [e.dma_start(out=self.dout["convs"][L, :, 0:22, :], in_=self.din["cconv"][L, :, 8:30, :])])
    st, sk = self.stage()
    for half in range(2):
        pt, pk = self.ps()

        def f2(e, half=half, pt=pt):
            for cc in range(4):
                c = half * 4 + cc
                r = e.transpose(out=pt[0:NS, cc * 128:(cc + 1) * 128], in_=self.gs32[:, L, c, :], identity=self.ident[:, :])
            return r
        P.op("pe", f2, reads=("gs32", "ident"), writes=(pk,))
        P.op("act", lambda e, half=half, pt=pt, st=st: e.activation(out=st[0:NS, half * 512:(half + 1) * 512], in_=pt[0:NS, :], func=AF.Copy),
             reads=(pk,), writes=(sk,))

    def fs(e, st=st):
        return [e.dma_start(out=self.dout["convs"][L, s, 22:30, :], in_=st[s * 8:(s + 1) * 8, :]) for s in range(4)]
    P.dma("sp", sk, fs, reads=(sk,), n=4)


def _build_full(self):
    import os
    STOP = os.environ.get("KSTOP", "")
    P = self.P
    self.setup()
    xin = self.din["xin"]
    ck = [self.din["ckv0"], self.din["ckv1"], self.din["ckv2"]]
    for g in range(3):
        W = WINS[g]
        if os.environ.get("NO_D2D"):
            break
        for s in range(4):
            for r0 in range(0, W - 8, 512):
                nr = min(512, W - 8 - r0)
                P.dma("act", "d2d", lambda e, g=g, s=s, r0=r0, nr=nr: [e.dma_start(out=self.dout[f"kvs{g}"][s, r0:r0 + nr, :],
                                                                                  in_=ck[g][s, 8 + r0:8 + r0 + nr, :])])
    for sup in range(2):
        isY = sup == 1 and STOP != "XX"
        tiles = TILES_Y if isY else TILES_X
        for tb in range(16):
            self.load_tokens(xin[sup * HALF + tb * 128: sup * HALF + (tb + 1) * 128, :], 128, tb * 128)
        if isY:
            self.load_tokens(self.din["xs"][:, :], NS, HALF)
        for L in range(2):
            self.conv_layer(L, isY)
            if isY and not os.environ.get("NO_CONVOUT"):
                self.conv_outputs(L)
            self.ffn(L, tiles)
        self.kv_stage(isY)
        if STOP == "X":
            break
    for j in range(2):
        if STOP in ("X", "KV", "XX"):
            break
        self.attn_layer(j, STOP)
        if STOP in ("Q", "S", "A"):
            break
        self.ffn(2 + j, TILES_Y)
    for tb in range(16):
        self.store_tokens(self.dout["y"][tb * 128:(tb + 1) * 128, :], 128, tb * 128)
    self.store_tokens(self.dout["y"][HALF:HALF + NS, :], NS, HALF)


Builder.conv_outputs = _conv_outputs
Builder.build_full = _build_full

_CACHE = {}


def kernel(**inputs):
    if "nc" not in _CACHE:
        B = Builder()
        B.build_full()
        _CACHE["nc"] = B.finish()
        _CACHE["names"] = set(B.din.keys())
    nc = _CACHE["nc"]
    maps = make_in_maps(inputs)
    maps = [{k: v for k, v in m.items() if k in _CACHE["names"]} for m in maps]
    res = run_bass_kernel_spmd(nc, maps, core_ids=list(range(8)))
    R = res.results
    f32 = np.float32
    y_prompt = np.zeros((4, SEQ, D), f32)
    y_sample = np.zeros((32, 8, D), f32)
    conv_p = np.zeros((2, 4, 30, D), f32)
    conv_s = np.zeros((2, 32, 30, D), f32)
    kvp = [np.zeros((4, W, 2, 16, 64), f32) for W in WINS]
    kvs = [np.zeros((32, W, 2, 16, 64), f32) for W in WINS]
    for c in range(8):
        b, h = c // 2, c % 2
        r = R[c]
        y_prompt[b, h * HALF:(h + 1) * HALF] = r["y"][:HALF]
        y_sample[4 * c:4 * c + 4] = r["y"][HALF:].reshape(4, 8, D)
        conv_s[:, 4 * c:4 * c + 4] = r["convs"]
        for g in range(3):
            kvs[g][4 * c:4 * c + 4] = r[f"kvs{g}"].reshape(4, WINS[g], 2, 16, 64)
        if h == 1:
            conv_p[:, b] = r["convp"]
            for g in range(3):
                kvp[g][b] = r[f"kvp{g}"].reshape(WINS[g], 2, 16, 64)
    return (y_prompt, y_sample, conv_p, conv_s, kvp[0], kvp[1], kvp[2], kvs[0], kvs[1], kvs[2])
```
